# Optimizing a Trainium2 kernel written in Bass

```python
import math
import numpy as np
import jax
import jax.numpy as jnp
from jax import lax


D_MODEL = 1024
BATCH = 4
SEQ = 8192
DEPTH = 2

GRID_W = 64
CTX_LEN = 256
HEAD_DIM = 64
N_GROUP_HEADS = D_MODEL // (4 * HEAD_DIM)
N_HEADS_NA = N_GROUP_HEADS
N_BLOCKS_LRU = N_GROUP_HEADS
LRU_BLOCK = HEAD_DIM
N_HEADS_RET = N_GROUP_HEADS
N_HEADS_DIFF = N_GROUP_HEADS
W_NA = N_HEADS_NA * HEAD_DIM
W_LRU = N_BLOCKS_LRU * LRU_BLOCK
W_RET = N_HEADS_RET * HEAD_DIM
W_DIFF = N_HEADS_DIFF * HEAD_DIM
MIX_WIDTH = W_NA + W_LRU + W_RET + W_DIFF
DIFF_DIM = HEAD_DIM // 2
IN_SPLITS = (W_NA,) * 3 + (W_LRU,) * 2 + (W_RET,) * 4 + (W_DIFF,) * 3
IN_WIDTH = sum(IN_SPLITS)
SPLIT_POINTS = tuple(int(s) for s in np.cumsum(IN_SPLITS)[:-1])
NA_WIN_ROWS = 8
NA_WIN_COLS = 16
LRU_CONV = 4
LRU_C = 8.0
RET_CHUNK = 128
Q_BLOCK = 128
ROPE_BASE = 10000.0
EPS = 1e-6
D_FF = ((8 * D_MODEL + 3 * 256 - 1) // (3 * 256)) * 256
NA_SCALE = HEAD_DIM ** -0.5
RET_SCALE = HEAD_DIM ** -0.5

kernel_name = 'hybrid_grid_diffusion_block'


def rms_norm(x, g):
    xf = x.astype(jnp.float32)
    y = xf * lax.rsqrt(jnp.mean(xf * xf, axis=-1, keepdims=True) + EPS)
    return (y * g.astype(jnp.float32)).astype(x.dtype)


def head_rms(y):
    yf = y.astype(jnp.float32)
    return (yf * lax.rsqrt(jnp.mean(yf * yf, axis=-1, keepdims=True) + EPS)).astype(y.dtype)


def modulate(h, shift, scale):
    return h * (1.0 + scale) + shift


def to_heads(t, h):
    return t.reshape(t.shape[:2] + (h, -1))


def to_maps(t):
    return t.reshape(t.shape[:2] + (N_HEADS_DIFF, 2, DIFF_DIM))


def axial_rope(n, dim):
    t = jnp.arange(n)
    row = (t // GRID_W).astype(jnp.float32)
    col = (t % GRID_W).astype(jnp.float32)
    nf = dim // 4
    inv = ROPE_BASE ** (-jnp.arange(nf, dtype=jnp.float32) / nf)
    ang = jnp.concatenate([row[:, None] * inv, col[:, None] * inv], axis=-1)
    return jnp.cos(ang), jnp.sin(ang)


def apply_rope(x, cos, sin):
    shape = (1, x.shape[1]) + (1,) * (x.ndim - 3) + (cos.shape[-1],)
    cos = cos.reshape(shape).astype(x.dtype)
    sin = sin.reshape(shape).astype(x.dtype)
    half = x.shape[-1] // 2
    x1, x2 = x[..., :half], x[..., half:]
    return jnp.concatenate([x1 * cos - x2 * sin, x2 * cos + x1 * sin], axis=-1)


def softmax_attention(q, k, v):
    s = jnp.einsum('bqhd,bkhd->bhqk', q, k).astype(jnp.float32)
    p = jax.nn.softmax(s, axis=-1).astype(v.dtype)
    o = jnp.einsum('bhqk,bkhd->bqhd', p, v)
    return o.reshape(o.shape[:2] + (-1,))


def neighbourhood_attention(q, k, v, k_ctx, v_ctx, rpb, rows):
    b, n, h, d = q.shape
    wr = min(NA_WIN_ROWS, rows)
    qg = q.reshape(b, rows, GRID_W, h, d)
    kg = k.reshape(b, rows, GRID_W, h, d)
    vg = v.reshape(b, rows, GRID_W, h, d)
    col_start = np.clip(np.arange(GRID_W) - NA_WIN_COLS // 2, 0, GRID_W - NA_WIN_COLS)
    col_idx = col_start[:, None] + np.arange(NA_WIN_COLS)[None, :]
    dc = col_idx - np.arange(GRID_W)[:, None] + NA_WIN_COLS - 1
    n_win = wr * NA_WIN_COLS

    def one_row(r):
        r0 = jnp.clip(r - wr // 2, 0, rows - wr)
        kw = lax.dynamic_slice_in_dim(kg, r0, wr, axis=1)[:, :, col_idx]
        vw = lax.dynamic_slice_in_dim(vg, r0, wr, axis=1)[:, :, col_idx]
        qr = lax.dynamic_index_in_dim(qg, r, axis=1, keepdims=False)
        dr = r0 + jnp.arange(wr) - r + NA_WIN_ROWS - 1
        bias = jnp.transpose(rpb[:, dr][:, :, dc], (0, 2, 1, 3))
        s_win = (jnp.einsum('bqhd,brqjhd->bhqrj', qr, kw).astype(jnp.float32)
                 + bias[None].astype(jnp.float32))
        s_ctx = jnp.einsum('bqhd,bchd->bhqc', qr, k_ctx).astype(jnp.float32)
        s = jnp.concatenate([s_win.reshape(b, h, GRID_W, n_win), s_ctx], axis=-1)
        p = jax.nn.softmax(s, axis=-1).astype(v.dtype)
        p_win = p[..., :n_win].reshape(b, h, GRID_W, wr, NA_WIN_COLS)
        p_ctx = p[..., n_win:]
        return (jnp.einsum('bhqrj,brqjhd->bqhd', p_win, vw)
                + jnp.einsum('bhqc,bchd->bqhd', p_ctx, v_ctx))

    out = lax.map(one_row, jnp.arange(rows))
    return jnp.moveaxis(out, 0, 1).reshape(b, n, h * d)


def depthwise_conv(x, w, bias):
    k = w.shape[0]
    left = (k - 1) // 2
    y = lax.conv_general_dilated(x, w[:, None, :].astype(x.dtype), window_strides=(1,),
                                 padding=[(left, k - 1 - left)],
                                 dimension_numbers=('NWC', 'WIO', 'NWC'),
                                 feature_group_count=x.shape[-1])
    return y + bias


def rglru_coeffs(xb, w_g, b_g, lam):
    bsz, n = xb.shape[:2]
    xk = xb.reshape(bsz, n, N_BLOCKS_LRU, LRU_BLOCK)
    gates = jax.nn.sigmoid(jnp.einsum('bnkc,gkcd->gbnkd', xk, w_g) + b_g[:, None, None])
    r = gates[0].reshape(bsz, n, W_LRU).astype(jnp.float32)
    i = gates[1].reshape(bsz, n, W_LRU).astype(jnp.float32)
    log_a = -LRU_C * r * jax.nn.softplus(-lam.astype(jnp.float32))
    a = jnp.exp(log_a)
    u = jnp.sqrt(-jnp.expm1(2.0 * log_a)) * (i * xb.astype(jnp.float32))
    return a, u


def linear_scan(a, u, h0, reverse):
    def comb(left, right):
        al, ul = left
        ar, ur = right
        return al * ar, ar * ul + ur
    a_cum, u_cum = lax.associative_scan(comb, (a, u), axis=1, reverse=reverse)
    h = a_cum * h0[:, None, :] + u_cum
    final = h[:, 0] if reverse else h[:, -1]
    return h, final


def rglru_bidir(xb, w_g, b_g, lam, h0_fwd, h0_bwd):
    af, uf = rglru_coeffs(xb, w_g[0], b_g[0], lam[0])
    ab, ub = rglru_coeffs(xb, w_g[1], b_g[1], lam[1])
    hf, hf_last = linear_scan(af, uf, h0_fwd, reverse=False)
    hb, hb_last = linear_scan(ab, ub, h0_bwd, reverse=True)
    return (hf + hb).astype(xb.dtype), hf_last, hb_last


def retention_scan(q, k, v, log_gamma, s0, inclusive, reverse):
    dtype = v.dtype
    q, k, v = (t.astype(jnp.float32) for t in (q, k, v))
    if reverse:
        q, k, v = (jnp.flip(t, axis=1) for t in (q, k, v))
    b, n, h, d = q.shape
    cs = min(RET_CHUNK, n)
    nc = n // cs
    shift = 0.0 if inclusive else 1.0
    j = jnp.arange(cs, dtype=jnp.float32)
    rel = j[:, None] - j[None, :] - shift
    mask = rel >= 0
    dmat = jnp.where(mask[None], jnp.exp(log_gamma[:, None, None] * jnp.where(mask, rel, 0.0)[None]), 0.0)
    q_decay = jnp.exp(log_gamma[None, :] * (j[:, None] + 1.0 - shift))
    k_decay = jnp.exp(log_gamma[None, :] * (cs - 1.0 - j[:, None]))
    c_decay = jnp.exp(log_gamma * cs)

    def chunks(t):
        return jnp.moveaxis(t.reshape(b, nc, cs, h, d), 1, 0)

    def step(s, inp):
        qc, kc, vc = inp
        sc = jnp.einsum('bihd,bjhd->bhij', qc, kc) * dmat
        o = (jnp.einsum('bhij,bjhe->bihe', sc, vc)
             + jnp.einsum('bihd,bhde->bihe', qc * q_decay[:, :, None], s))
        s = s * c_decay[:, None, None] + jnp.einsum('bjhd,bjhe->bhde', kc * k_decay[:, :, None], vc)
        return s, o

    s_fin, o = lax.scan(step, s0, (chunks(q), chunks(k), chunks(v)))
    o = jnp.moveaxis(o, 0, 1).reshape(b, n, h, d)
    if reverse:
        o = jnp.flip(o, axis=1)
    return o.astype(dtype), s_fin


def diff_core(q, k, v, lam):
    s = jnp.einsum('bqhmd,bkhmd->bhmqk', q, k).astype(jnp.float32) * (DIFF_DIM ** -0.5)
    p = jax.nn.softmax(s, axis=-1)
    a = (p[:, :, 0] - lam * p[:, :, 1]).astype(v.dtype)
    return jnp.einsum('bhqk,bkhd->bqhd', a, v)


def diff_attention_latent(q, k, v, k_ctx, v_ctx, lam):
    b, n = q.shape[:2]
    keys = jnp.concatenate([k_ctx, k], axis=1)
    vals = jnp.concatenate([v_ctx, v], axis=1)
    qb = jnp.moveaxis(q.reshape((b, n // Q_BLOCK, Q_BLOCK) + q.shape[2:]), 1, 0)
    out = lax.map(lambda qi: diff_core(qi, keys, vals, lam), qb)
    return jnp.moveaxis(out, 0, 1).reshape((b, n) + v.shape[2:])


def diff_output(y, g, lam_init):
    y = head_rms(y) * g.astype(y.dtype) * (1.0 - lam_init)
    return y.reshape(y.shape[:2] + (-1,))


def swiglu(h, w1, w2):
    g, u = jnp.split(h @ w1, 2, axis=-1)
    return (jax.nn.silu(g) * u) @ w2


def setup_inputs(seed: int = 0) -> dict:
    key = jax.random.key(seed)
    ks = jax.random.split(key, 24)
    f32 = jnp.float32

    def nrm(k, shape, std):
        return jax.random.normal(k, shape, f32) * std

    gains = 1.0 + nrm(ks[6], (4, DEPTH, D_MODEL), 0.01)
    u = jax.random.uniform(ks[13], (DEPTH, 2, W_LRU), f32, 0.9, 0.999)
    gamma0 = 1.0 - 2.0 ** (-5.0 - jnp.arange(N_HEADS_RET, dtype=f32))
    return {
        'x': nrm(ks[0], (BATCH, SEQ, D_MODEL), 1.0),
        'c': nrm(ks[1], (BATCH, D_MODEL), 1.0),
        'ctx': nrm(ks[2], (BATCH, CTX_LEN, D_MODEL), 1.0),
        'c_ctx': nrm(ks[3], (D_MODEL,), 1.0),
        'w_mod': nrm(ks[4], (DEPTH, D_MODEL, 6 * D_MODEL), 0.5 * D_MODEL ** -0.5),
        'b_mod': nrm(ks[5], (DEPTH, 6 * D_MODEL), 0.01),
        'g_pre_mix': gains[0],
        'g_post_mix': gains[1],
        'g_pre_ffn': gains[2],
        'g_post_ffn': gains[3],
        'w_in': nrm(ks[7], (DEPTH, D_MODEL, IN_WIDTH), D_MODEL ** -0.5),
        'na_rpb': nrm(ks[8], (DEPTH, N_HEADS_NA, 2 * NA_WIN_ROWS - 1, 2 * NA_WIN_COLS - 1), 0.02),
        'lru_conv_w': nrm(ks[9], (DEPTH, LRU_CONV, W_LRU), LRU_CONV ** -0.5),
        'lru_conv_b': nrm(ks[10], (DEPTH, W_LRU), 0.01),
        'lru_gate_w': nrm(ks[11], (DEPTH, 2, 2, N_BLOCKS_LRU, LRU_BLOCK, LRU_BLOCK), LRU_BLOCK ** -0.5),
        'lru_gate_b': nrm(ks[12], (DEPTH, 2, 2, N_BLOCKS_LRU, LRU_BLOCK), 0.01),
        'lru_lambda': jnp.log(u) - jnp.log1p(-u),
        'ret_decay': (jnp.log(gamma0) - jnp.log1p(-gamma0)) + nrm(ks[14], (DEPTH, 2, N_HEADS_RET), 0.1),
        'diff_lambda': nrm(ks[15], (DEPTH, 4, DIFF_DIM), 0.1),
        'diff_subln': 1.0 + nrm(ks[16], (DEPTH, HEAD_DIM), 0.01),
        'w_out': nrm(ks[17], (DEPTH, MIX_WIDTH, D_MODEL), MIX_WIDTH ** -0.5),
        'w_ffn_in': nrm(ks[18], (DEPTH, D_MODEL, 2 * D_FF), D_MODEL ** -0.5),
        'w_ffn_out': nrm(ks[19], (DEPTH, D_FF, D_MODEL), D_FF ** -0.5),
    }


def reference(x, c, ctx, c_ctx, w_mod, b_mod, g_pre_mix, g_post_mix, g_pre_ffn, g_post_ffn,
              w_in, na_rpb, lru_conv_w, lru_conv_b, lru_gate_w, lru_gate_b, lru_lambda,
              ret_decay, diff_lambda, diff_subln, w_out, w_ffn_in, w_ffn_out):
    f32 = jnp.float32
    b, n_lat = x.shape[0], x.shape[1]
    rows = n_lat // GRID_W
    cos_r, sin_r = axial_rope(n_lat, HEAD_DIM)
    cos_d, sin_d = axial_rope(n_lat, DIFF_DIM)
    zeros_h = jnp.zeros((b, W_LRU), f32)
    zeros_s = jnp.zeros((b, N_HEADS_RET, HEAD_DIM, HEAD_DIM), f32)
    xl, xc = x, ctx
    for l in range(DEPTH):
        last = l == DEPTH - 1
        lam_init = 0.8 - 0.6 * math.exp(-0.3 * l)
        mod_l = jnp.split((jax.nn.silu(c) @ w_mod[l] + b_mod[l])[:, None, :], 6, axis=-1)
        mod_c = jnp.split(jax.nn.silu(c_ctx) @ w_mod[l] + b_mod[l], 6, axis=-1)

        hl = modulate(rms_norm(xl, g_pre_mix[l]), mod_l[0], mod_l[1])
        hc = modulate(rms_norm(xc, g_pre_mix[l]), mod_c[0], mod_c[1])
        pl = jnp.split(hl @ w_in[l], SPLIT_POINTS, axis=-1)
        pc = jnp.split(hc @ w_in[l], SPLIT_POINTS, axis=-1)

        qa_l, ka_l, va_l = (to_heads(t, N_HEADS_NA) for t in pl[0:3])
        qa_c, ka_c, va_c = (to_heads(t, N_HEADS_NA) for t in pc[0:3])
        ya_l = neighbourhood_attention(qa_l * NA_SCALE, ka_l, va_l, ka_c, va_c, na_rpb[l], rows)

        xb_c = depthwise_conv(pc[3], lru_conv_w[l], lru_conv_b[l])
        hb_c, hf_last, hb_last = rglru_bidir(xb_c, lru_gate_w[l], lru_gate_b[l], lru_lambda[l], zeros_h, zeros_h)
        xb_l = depthwise_conv(pl[3], lru_conv_w[l], lru_conv_b[l])
        hb_l, _, _ = rglru_bidir(xb_l, lru_gate_w[l], lru_gate_b[l], lru_lambda[l], hf_last, hb_last)
        yb_l = hb_l * jax.nn.gelu(pl[4])

        log_g = jax.nn.log_sigmoid(ret_decay[l].astype(f32))
        qr_c = to_heads(pc[5], N_HEADS_RET)
        kr_c = to_heads(pc[6], N_HEADS_RET) * RET_SCALE
        vr_c = to_heads(pc[7], N_HEADS_RET)
        of_c, s_fwd = retention_scan(qr_c, kr_c, vr_c, log_g[0], zeros_s, True, False)
        ob_c, s_bwd = retention_scan(qr_c, kr_c, vr_c, log_g[1], zeros_s, False, True)
        qr_l = apply_rope(to_heads(pl[5], N_HEADS_RET), cos_r, sin_r)
        kr_l = apply_rope(to_heads(pl[6], N_HEADS_RET) * RET_SCALE, cos_r, sin_r)
        vr_l = to_heads(pl[7], N_HEADS_RET)
        of_l, _ = retention_scan(qr_l, kr_l, vr_l, log_g[0], s_fwd, True, False)
        ob_l, _ = retention_scan(qr_l, kr_l, vr_l, log_g[1], s_bwd, False, True)
        yc_l = (head_rms(of_l + ob_l) * jax.nn.silu(to_heads(pl[8], N_HEADS_RET))).reshape(b, n_lat, W_RET)

        lq1, lk1, lq2, lk2 = diff_lambda[l].astype(f32)
        lam = jnp.exp(jnp.sum(lq1 * lk1)) - jnp.exp(jnp.sum(lq2 * lk2)) + lam_init
        qd_c, kd_c = to_maps(pc[9]), to_maps(pc[10])
        vd_c = to_heads(pc[11], N_HEADS_DIFF)
        qd_l = apply_rope(to_maps(pl[9]), cos_d, sin_d)
        kd_l = apply_rope(to_maps(pl[10]), cos_d, sin_d)
        vd_l = to_heads(pl[11], N_HEADS_DIFF)
        yd_l = diff_output(diff_attention_latent(qd_l, kd_l, vd_l, kd_c, vd_c, lam), diff_subln[l], lam_init)

        y_l = jnp.concatenate([ya_l, yb_l, yc_l, yd_l], axis=-1) @ w_out[l]
        xl_mid = xl + mod_l[2] * rms_norm(y_l, g_post_mix[l])

        h2 = modulate(rms_norm(xl_mid, g_pre_ffn[l]), mod_l[3], mod_l[4])
        xl_new = xl_mid + mod_l[5] * rms_norm(swiglu(h2, w_ffn_in[l], w_ffn_out[l]), g_post_ffn[l])

        if not last:
            ya_c = softmax_attention(qa_c * NA_SCALE, ka_c, va_c)
            yb_c = hb_c * jax.nn.gelu(pc[4])
            yc_c = (head_rms(of_c + ob_c) * jax.nn.silu(to_heads(pc[8], N_HEADS_RET))).reshape(b, -1, W_RET)
            yd_c = diff_output(diff_core(qd_c, kd_c, vd_c, lam), diff_subln[l], lam_init)
            y_c = jnp.concatenate([ya_c, yb_c, yc_c, yd_c], axis=-1) @ w_out[l]
            xc_mid = xc + mod_c[2] * rms_norm(y_c, g_post_mix[l])
            h2c = modulate(rms_norm(xc_mid, g_pre_ffn[l]), mod_c[3], mod_c[4])
            xc = xc_mid + mod_c[5] * rms_norm(swiglu(h2c, w_ffn_in[l], w_ffn_out[l]), g_post_ffn[l])
        xl = xl_new
    return xl
```

```python
import contextlib
import math
import numpy as np
import concourse.bass as bass
import concourse.mybir as mybir
from concourse.bass_utils import run_bass_kernel_spmd

F32 = mybir.dt.float32
BF16 = mybir.dt.bfloat16
AF = mybir.ActivationFunctionType
ALU = mybir.AluOpType
AX = mybir.AxisListType

D = 1024
NLAT = 8192
NCTX = 256
T = NLAT + NCTX
DEPTH = 2
DFF = 2816
EPS = 1e-6
NWCOL = 4096
GRID_W = 64


class Buf:
    __slots__ = ("w", "r", "name")

    def __init__(self, name=""):
        self.w = None
        self.r = []
        self.name = name


class FW:
    N_DMA_SEMS = 24

    def __init__(self, nc, stack):
        self.nc = nc
        self.eng = {"pe": nc.tensor, "act": nc.scalar, "dve": nc.vector, "pool": nc.gpsimd, "sp": nc.sync}
        self.sems = {}
        self.count = {}
        for e in self.eng:
            self.sems[e] = stack.enter_context(nc.semaphore("s_" + e))
            self.count[e] = 0
        self.dma_sems = {"hw": [], "sw": []}
        for kind, n in (("hw", 24), ("sw", 24)):
            for i in range(n):
                k = "d%s%d" % (kind, i)
                self.sems[k] = stack.enter_context(nc.semaphore("s_" + k))
                self.count[k] = 0
                self.dma_sems[kind].append(k)
        self.dma_rr = {"hw": 0, "sw": 0}
        self.waited = {e: {} for e in self.eng}
        self.pending_pe = []
        self.n_inst = 0
        self.n_wait = 0

    def _wait(self, e, ev):
        if ev is None:
            return
        k, v = ev
        if self.waited[e].get(k, 0) >= v:
            return
        self.eng[e].wait_ge(self.sems[k], v)
        self.waited[e][k] = v
        self.n_wait += 1

    def _deps(self, e, reads, writes):
        for b in reads:
            self._wait(e, b.w)
        for b in writes:
            self._wait(e, b.w)
            for ev in b.r:
                self._wait(e, ev)

    def _mark(self, ev, reads, writes):
        for b in reads:
            b.r.append(ev)
            if len(b.r) > 40:
                best = {}
                for k, v in b.r:
                    if best.get(k, 0) < v:
                        best[k] = v
                b.r = list(best.items())
        for b in writes:
            b.w = ev
            b.r = []

    def op(self, e, fn, reads=(), writes=(), inc=True):
        self._deps(e, reads, writes)
        inst = fn()
        self.n_inst += 1
        if e == "pe" and not inc:
            self.pending_pe.append((tuple(reads), tuple(writes)))
            return inst
        self.count[e] += 1
        ev = (e, self.count[e])
        inst.then_inc(self.sems[e], 1)
        if e == "pe" and self.pending_pe:
            for r, w in self.pending_pe:
                self._mark(ev, r, w)
            self.pending_pe = []
        self._mark(ev, reads, writes)
        return inst

    def dma(self, q, out, in_, reads=(), writes=(), **kw):
        kind = "sw" if q == "pool" else "hw"
        k = self.dma_sems[kind][self.dma_rr[kind]]
        self.dma_rr[kind] = (self.dma_rr[kind] + 1) % len(self.dma_sems[kind])
        self._wait(q, (k, self.count[k]))
        self._deps(q, reads, writes)
        inst = self.eng[q].dma_start(out=out, in_=in_, **kw)
        self.count[k] += 16
        inst.then_inc(self.sems[k], 16)
        ev = (k, self.count[k])
        self._mark(ev, reads, writes)
        self.n_inst += 1
        return ev

    def barrier(self):
        for e in self.eng:
            for k in self.sems:
                if k != e and self.count[k] > 0:
                    self._wait(e, (k, self.count[k]))

    def finish(self, e="sp"):
        for k in self.sems:
            if self.count[k] > 0:
                self._wait(e, (k, self.count[k]))


class Ctx:
    _n = 0

    def un(self, name):
        Ctx._n += 1
        return "%s_%d" % (name, Ctx._n)


def _w_in_perm():
    def split(i):
        return np.arange(i * 256, (i + 1) * 256)

    def swap_ret(cols):
        c = cols.reshape(4, 64)
        return np.concatenate([c[:, 32:], c[:, :32]], axis=1).reshape(-1)

    def swap_diff(cols):
        c = cols.reshape(4, 2, 32)
        return np.concatenate([c[:, :, 16:], c[:, :, :16]], axis=2).reshape(-1)

    order = [split(0), split(1), split(3), split(4), split(8),
             split(5), swap_ret(split(5)), split(6), swap_ret(split(6)),
             split(9), swap_diff(split(9)), split(10), swap_diff(split(10)),
             split(2), split(7), split(11)]
    return np.concatenate(order)


def _rope_tables():
    tabs = np.zeros((6, 128, 192), np.float64)
    rows = np.arange(128, dtype=np.float64)
    cols = np.arange(64, dtype=np.float64)
    for p in range(128):
        d = p % 64
        j = d % 32
        sign = -1.0 if d < 32 else 1.0
        inv = 10000.0 ** (-(j % 16) / 16.0)
        if j < 16:
            ang, sl = rows * np.float32(inv), slice(0, 128)
        else:
            ang, sl = cols * np.float32(inv), slice(128, 192)
        tabs[0, p, sl] = np.cos(ang)
        tabs[1, p, sl] = sign * np.sin(ang)
        tabs[2, p, sl] = np.cos(ang) * 0.125
        tabs[3, p, sl] = sign * np.sin(ang) * 0.125
        d = p % 32
        j = d % 16
        sign = -1.0 if d < 16 else 1.0
        inv = 10000.0 ** (-(j % 8) / 8.0)
        if j < 8:
            ang, sl = rows * np.float32(inv), slice(0, 128)
        else:
            ang, sl = cols * np.float32(inv), slice(128, 192)
        tabs[4, p, sl] = np.cos(ang)
        tabs[5, p, sl] = sign * np.sin(ang)
    return tabs.astype(np.float32)


def _p0_modvec(K):
    nc, fw = K.nc, K.fw
    with contextlib.ExitStack() as st:
        sb = lambda n, s, d: st.enter_context(nc.sbuf_tensor(K.un(n), s, d))
        cT = sb("p0_cT", [128, 2, 8], F32); b_cT = Buf()
        sT = sb("p0_sT", [128, 2, 8], F32); b_sT = Buf()
        wm = [sb("p0_wm%d" % i, [128, 8, 512], F32) for i in range(2)]; b_wm = [Buf(), Buf()]
        bm = sb("p0_bm", [2, 6144], F32); b_bm = Buf()
        gn = sb("p0_gn", [2, 4, 1024], F32); b_gn = Buf()
        mv = sb("p0_mv", [2, 6144], F32); b_mv = Buf()
        cv = sb("p0_cv", [2, 6, 1024], F32); b_cv = Buf()
        ps = [st.enter_context(nc.psum_tensor(K.un("p0_ps%d" % i), [2, 512], F32)) for i in range(2)]
        b_ps = [Buf(), Buf()]
        for j in range(2):
            fw.dma("sp", cT[:, j, :], K.cvec[j, :].rearrange("(k p) -> p k", p=128), writes=[b_cT],
                   allow_slow_non_contiguous=True)
        fw.op("act", lambda: nc.scalar.activation(out=sT[:], in_=cT[:], func=AF.Silu), reads=[b_cT], writes=[b_sT])
        it = 0
        for l in range(DEPTH):
            fw.dma("sp", bm[:], K.b_mod[l:l + 1, :].broadcast_to([2, 6144]), writes=[b_bm])
            fw.dma("sp", gn[:], K.gains[:, l, :].unsqueeze(0).broadcast_to([2, 4, 1024]), writes=[b_gn])
            for n in range(12):
                w_, bw_ = wm[it % 2], b_wm[it % 2]
                p_, bp_ = ps[it % 2], b_ps[it % 2]
                it += 1
                src = K.w_mod[l, :, n * 512:(n + 1) * 512].rearrange("(k p) c -> p k c", p=128)
                fw.dma("sp", w_[:, 0:4, :], src[:, 0:4, :], writes=[bw_])
                fw.dma("sp", w_[:, 4:8, :], src[:, 4:8, :], writes=[bw_])
                for k in range(8):
                    fw.op("pe", lambda: nc.tensor.matmul(p_[:], lhsT=sT[:, :, k], rhs=w_[:, k, :],
                                                         start=(k == 0), stop=(k == 7)),
                          reads=[b_sT, bw_], writes=[bp_], inc=(k == 7))
                fw.op("dve", lambda: nc.vector.tensor_tensor(out=mv[:, n * 512:(n + 1) * 512], in0=p_[:],
                                                             in1=bm[:, n * 512:(n + 1) * 512], op=ALU.add),
                      reads=[bp_, b_bm], writes=[b_mv])
            m = lambda i: mv[:, i * 1024:(i + 1) * 1024]
            R, W_ = [b_mv, b_gn], [b_cv]
            fw.op("dve", lambda: nc.vector.scalar_tensor_tensor(out=cv[:, 0, :], in0=m(1), scalar=1.0, in1=gn[:, 0, :],
                                                                op0=ALU.add, op1=ALU.mult), reads=R, writes=W_)
            fw.op("dve", lambda: nc.vector.tensor_copy(out=cv[:, 1, :], in_=m(0)), reads=R, writes=W_)
            fw.op("dve", lambda: nc.vector.tensor_tensor(out=cv[:, 2, :], in0=m(2), in1=gn[:, 1, :], op=ALU.mult),
                  reads=R, writes=W_)
            fw.op("dve", lambda: nc.vector.scalar_tensor_tensor(out=cv[:, 3, :], in0=m(4), scalar=1.0, in1=gn[:, 2, :],
                                                                op0=ALU.add, op1=ALU.mult), reads=R, writes=W_)
            fw.op("dve", lambda: nc.vector.tensor_copy(out=cv[:, 4, :], in_=m(3)), reads=R, writes=W_)
            fw.op("dve", lambda: nc.vector.tensor_tensor(out=cv[:, 5, :], in0=m(5), in1=gn[:, 3, :], op=ALU.mult),
                  reads=R, writes=W_)
            fw.dma("pool", K.modvec[l].rearrange("i j f -> j i f"), cv[:], reads=[b_cv], writes=[K.b_modvec])
        fw.barrier()


def _load_bcast(K, q, tile, l, i, j, buf):
    K.fw.dma(q, tile[:], K.modvec[l, i, j:j + 1, :].broadcast_to([128, 1024]), reads=[K.b_modvec], writes=[buf])


def _phase_a(K, l):
    nc, fw = K.nc, K.fw
    with contextlib.ExitStack() as st:
        sb = lambda n, s, d: st.enter_context(nc.sbuf_tensor(K.un(n), s, d))
        W = sb("a_W", [128, 8, NWCOL], BF16); b_W = Buf()
        ident = sb("a_id", [128, 128], BF16); b_id = Buf()
        with contextlib.ExitStack() as st2:
            wst = [st2.enter_context(nc.sbuf_tensor(K.un("a_wst%d" % i), [128, 2048], F32)) for i in range(2)]
            b_wst = [Buf(), Buf()]
            idf = st2.enter_context(nc.sbuf_tensor(K.un("a_idf"), [128, 128], F32)); b_idf = Buf()
            fw.dma("sp", idf[:], K.ident, writes=[b_idf])
            fw.op("dve", lambda: nc.vector.tensor_copy(out=ident[:], in_=idf[:]), reads=[b_idf], writes=[b_id])
            it = 0
            for k in range(8):
                for hf in range(2):
                    s_, bs_ = wst[it % 2], b_wst[it % 2]
                    fw.dma("sp", s_[:], K.w_in[l, k * 128:(k + 1) * 128, hf * 2048:(hf + 1) * 2048], writes=[bs_])
                    e = ("act", "dve", "pool")[it % 3]
                    dst = W[:, k, hf * 2048:(hf + 1) * 2048]
                    if e == "act":
                        fw.op(e, lambda: nc.scalar.copy(out=dst, in_=s_[:]), reads=[bs_], writes=[b_W])
                    elif e == "dve":
                        fw.op(e, lambda: nc.vector.tensor_copy(out=dst, in_=s_[:]), reads=[bs_], writes=[b_W])
                    else:
                        fw.op(e, lambda: nc.gpsimd.tensor_copy(out=dst, in_=s_[:]), reads=[bs_], writes=[b_W])
                    it += 1
            fw.barrier()
        gm = [sb("a_gm%d" % j, [128, 1024], F32) for j in range(2)]; b_gm = [Buf(), Buf()]
        sh = [sb("a_sh%d" % j, [128, 1024], F32) for j in range(2)]; b_sh = [Buf(), Buf()]
        for j in range(2):
            _load_bcast(K, "sp", gm[j], l, 0, j, b_gm[j])
            _load_bcast(K, "sp", sh[j], l, 1, j, b_sh[j])
        rtab = sb("a_rtab", [128, 6, 192], F32); b_rtab = Buf()
        fw.dma("sp", rtab[:], K.rope.rearrange("s p c -> p s c"), writes=[b_rtab])
        rope = sb("a_rope", [128, 6, 512], F32); b_rope = Buf()
        xt = [sb("a_xt%d" % i, [128, 1024], F32) for i in range(2)]; b_xt = [Buf(), Buf()]
        junk = sb("a_junk", [128, 1024], BF16); b_junk = Buf()
        ss = [sb("a_ss%d" % i, [128, 1], F32) for i in range(2)]; b_ss = [Buf(), Buf()]
        sd = [sb("a_sd%d" % i, [128, 1], F32) for i in range(2)]; b_sd = [Buf(), Buf()]
        rs = [sb("a_rs%d" % i, [128, 1], F32) for i in range(2)]; b_rs = [Buf(), Buf()]
        h1 = sb("a_h1", [128, 1024], F32); b_h1 = Buf()
        hb = [sb("a_hb%d" % i, [128, 1024], BF16) for i in range(2)]; b_hb = [Buf(), Buf()]
        hT = [sb("a_hT%d" % i, [128, 8, 512], BF16) for i in range(2)]; b_hT = [Buf(), Buf()]
        stB = sb("a_stB", [128, 12, 512], BF16); b_stB = [Buf() for _ in range(12)]
        stF = sb("a_stF", [128, 6, 512], F32); b_stF = [Buf() for _ in range(6)]
        stV = sb("a_stV", [128, 4, 768], BF16); b_stV = [Buf() for _ in range(4)]
        t1 = [sb("a_t1%d" % i, [128, 512], F32) for i in range(2)]; b_t1 = [Buf(), Buf()]
        t2 = [sb("a_t2%d" % i, [128, 512], F32) for i in range(2)]; b_t2 = [Buf(), Buf()]
        tp = [st.enter_context(nc.psum_tensor(K.un("a_tp%d" % i), [128, 8, 128], BF16)) for i in range(2)]
        b_tp = [Buf(), Buf()]
        mm = [st.enter_context(nc.psum_tensor(K.un("a_mm%d" % i), [128, 512], F32)) for i in range(6)]
        b_mm = [Buf() for _ in range(6)]
        mmi = [0]
        eps_t = sb("a_eps", [128, 1], F32); b_eps = Buf()
        fw.op("pool", lambda: nc.gpsimd.memset(eps_t[:], EPS), writes=[b_eps])

        def next_mm():
            i = mmi[0] % 6
            mmi[0] += 1
            return mm[i], b_mm[i]

        blocks = [(K.src[l][0], b * 512, 512, 0) for b in range(NLAT // 512)]
        blocks.append((K.src[l][1], NLAT, NCTX, 1))
        sub_c = [0]

        def prep(bi):
            src, tok0, ntok, j = blocks[bi]
            nsub = ntok // 128
            hT_, bhT_ = hT[bi % 2], b_hT[bi % 2]
            if j == 0:
                r0 = tok0 // GRID_W
                nr = ntok // GRID_W
                for s in range(6):
                    o = rope[:, s, 0:ntok].rearrange("p (r c) -> p r c", c=GRID_W)
                    a = rtab[:, s, r0:r0 + nr].unsqueeze(2).broadcast_to([128, nr, GRID_W])
                    b_ = rtab[:, s, 128:192].unsqueeze(1).broadcast_to([128, nr, GRID_W])
                    fw.op("pool", lambda: nc.gpsimd.tensor_tensor(out=o, in0=a, in1=b_, op=ALU.add),
                          reads=[b_rtab], writes=[b_rope])
            else:
                for s, val in enumerate((1.0, 0.0, 0.125, 0.0, 1.0, 0.0)):
                    fw.op("pool", lambda: nc.gpsimd.memset(rope[:, s, :], val), writes=[b_rope])
            for s_ in range(nsub):
                sub_i = sub_c[0]
                x_, bx_ = xt[sub_i % 2], b_xt[sub_i % 2]
                ss_, bss_ = ss[sub_i % 2], b_ss[sub_i % 2]
                sd_, bsd_ = sd[sub_i % 2], b_sd[sub_i % 2]
                rs_, brs_ = rs[sub_i % 2], b_rs[sub_i % 2]
                hb_, bhb_ = hb[sub_i % 2], b_hb[sub_i % 2]
                tp_, btp_ = tp[sub_i % 2], b_tp[sub_i % 2]
                sub_c[0] += 1
                lo = (tok0 - (NLAT if j else 0)) + s_ * 128
                fw.dma("sp", x_[:], src[lo:lo + 128, :], reads=[K.b_src[l]], writes=[bx_])
                fw.op("act", lambda: nc.scalar.activation(out=junk[:], in_=x_[:], func=AF.Square, accum_out=ss_[:]),
                      reads=[bx_], writes=[b_junk, bss_])
                fw.op("act", lambda: nc.scalar.activation(out=sd_[:], in_=ss_[:], func=AF.Sqrt, scale=1.0 / D,
                                                          bias=eps_t[:]), reads=[bss_, b_eps], writes=[bsd_])
                fw.op("dve", lambda: nc.vector.reciprocal(out=rs_[:], in_=sd_[:]), reads=[bsd_], writes=[brs_])
                fw.op("dve", lambda: nc.vector.scalar_tensor_tensor(out=h1[:], in0=x_[:], scalar=rs_[:], in1=gm[j][:],
                                                                    op0=ALU.mult, op1=ALU.mult),
                      reads=[bx_, brs_, b_gm[j]], writes=[b_h1])
                fw.op("pool", lambda: nc.gpsimd.tensor_tensor(out=hb_[:], in0=h1[:], in1=sh[j][:], op=ALU.add),
                      reads=[b_h1, b_sh[j]], writes=[bhb_])
                for k in range(8):
                    fw.op("pe", lambda: nc.tensor.transpose(tp_[:, k, :], hb_[:, k * 128:(k + 1) * 128], ident[:]),
                          reads=[bhb_, b_id], writes=[btp_], inc=(k == 7))
                fw.op("dve", lambda: nc.vector.tensor_copy(out=hT_[:, :, s_ * 128:(s_ + 1) * 128], in_=tp_[:]),
                      reads=[btp_], writes=[bhT_])

        prep(0)
        for bi, (src, tok0, ntok, j) in enumerate(blocks):
            nsub = ntok // 128
            hT_, bhT_ = hT[bi % 2], b_hT[bi % 2]

            def fm(c):
                p_, bp_ = next_mm()
                for k in range(8):
                    fw.op("pe", lambda: nc.tensor.matmul(p_[:, 0:ntok], lhsT=W[:, k, c * 128:(c + 1) * 128],
                                                         rhs=hT_[:, k, 0:ntok], start=(k == 0), stop=(k == 7)),
                          reads=[b_W, bhT_], writes=[bp_], inc=(k == 7))
                return p_, bp_

            for c in range(4):
                p_, bp_ = fm(c)
                fw.op("act", lambda: nc.scalar.copy(out=stB[:, c, 0:ntok], in_=p_[:, 0:ntok]),
                      reads=[bp_], writes=[b_stB[c]])
            for c in range(4, 10):
                p_, bp_ = fm(c)
                fw.op("act", lambda: nc.scalar.copy(out=stF[:, c - 4, 0:ntok], in_=p_[:, 0:ntok]),
                      reads=[bp_], writes=[b_stF[c - 4]])
            ri = 0
            for g, (c0, tabs, o0) in enumerate(((10, (0, 1), 4), (14, (2, 3), 6), (18, (4, 5), 8), (22, (4, 5), 10))):
                for hh in range(2):
                    pa, bpa = fm(c0 + hh)
                    pb, bpb = fm(c0 + 2 + hh)
                    t1_, bt1_ = t1[ri % 2], b_t1[ri % 2]
                    t2_, bt2_ = t2[ri % 2], b_t2[ri % 2]
                    ri += 1
                    fw.op("dve", lambda: nc.vector.tensor_tensor(out=t1_[:, 0:ntok], in0=pa[:, 0:ntok],
                                                                 in1=rope[:, tabs[0], 0:ntok], op=ALU.mult),
                          reads=[bpa, b_rope], writes=[bt1_])
                    fw.op("dve", lambda: nc.vector.tensor_tensor(out=t2_[:, 0:ntok], in0=pb[:, 0:ntok],
                                                                 in1=rope[:, tabs[1], 0:ntok], op=ALU.mult),
                          reads=[bpb, b_rope], writes=[bt2_])
                    fw.op("pool", lambda: nc.gpsimd.tensor_tensor(out=stB[:, o0 + hh, 0:ntok], in0=t1_[:, 0:ntok],
                                                                  in1=t2_[:, 0:ntok], op=ALU.add),
                          reads=[bt1_, bt2_], writes=[b_stB[o0 + hh]])
            fw.dma("pool", K.featB[l][:, :, tok0:tok0 + ntok].rearrange("c p t -> p c t"), stB[:, :, 0:ntok],
                   reads=b_stB, writes=[K.b_feat])
            fw.dma("pool", K.featF[l][:, :, tok0:tok0 + ntok].rearrange("c p t -> p c t"), stF[:, :, 0:ntok],
                   reads=b_stF, writes=[K.b_feat])
            if bi + 1 < len(blocks):
                prep(bi + 1)
            for s_ in range(nsub):
                for (n0, n1) in ((0, 512), (512, 768)):
                    p_, bp_ = next_mm()
                    for k in range(8):
                        fw.op("pe", lambda: nc.tensor.matmul(p_[:, 0:n1 - n0], lhsT=hT_[:, k, s_ * 128:(s_ + 1) * 128],
                                                             rhs=W[:, k, 3328 + n0:3328 + n1],
                                                             start=(k == 0), stop=(k == 7)),
                              reads=[b_W, bhT_], writes=[bp_], inc=(k == 7))
                    fw.op("act", lambda: nc.scalar.copy(out=stV[:, s_, n0:n1], in_=p_[:, 0:n1 - n0]),
                          reads=[bp_], writes=[b_stV[s_]])
            fw.dma("pool", K.vAll[l][tok0:tok0 + ntok, :].rearrange("(s p) c -> p s c", p=128), stV[:, 0:nsub, :],
                   reads=b_stV[0:nsub], writes=[K.b_feat])
        fw.barrier()


def _phase_lru(K, l):
    nc, fw = K.nc, K.fw
    CH = 2048
    for cc in range(2):
        with contextlib.ExitStack() as st:
            sb = lambda n, s, d: st.enter_context(nc.sbuf_tensor(K.un(n), s, d))
            A = sb("l_A", [128, T], F32); b_A = Buf()
            B = sb("l_B", [128, T], F32); b_B = Buf()
            Bb = sb("l_Bb", [128, T], BF16); b_Bb = Buf()
            C = sb("l_C", [128, T], F32); b_C = Buf()
            Dd = sb("l_D", [128, T], F32); b_D = Buf()
            E = sb("l_E", [128, T], F32); b_E = Buf()
            cw = sb("l_cw", [128, 4], F32); b_cw = Buf()
            cb = sb("l_cb", [128, 1], F32); b_cb = Buf()
            gwf = sb("l_gwf", [128, 4, 128], F32); b_gwf = Buf()
            gw = sb("l_gw", [128, 4, 128], BF16); b_gw = Buf()
            gb = sb("l_gb", [128, 4], F32); b_gb = Buf()
            lam = sb("l_lam", [128, 2], F32); b_lam = Buf()
            cn = sb("l_cn", [128, 2], F32); b_cn = Buf()
            one = sb("l_one", [128, 1], F32); b_one = Buf()
            CG = 1056
            gst = [sb("l_gst%d" % i, [128, CG], F32) for i in range(2)]; b_gst = [Buf(), Buf()]
            yst = [sb("l_yst%d" % i, [128, CG], BF16) for i in range(2)]; b_yst = [Buf(), Buf()]
            ps = [st.enter_context(nc.psum_tensor(K.un("l_ps%d" % i), [128, CH], F32)) for i in range(2)]
            b_ps = [Buf(), Buf()]
            for c0 in range(0, T, 2112):
                fw.dma("sp", A[:, c0:c0 + 2112], K.featF[l][cc, :, c0:c0 + 2112], reads=[K.b_feat], writes=[b_A])
            fw.dma("sp", cw[:], K.lru_conv_w[l, :, cc * 128:(cc + 1) * 128].rearrange("k p -> p k"), writes=[b_cw],
                   allow_slow_non_contiguous=True)
            fw.dma("sp", cb[:], K.lru_conv_b[l, cc * 128:(cc + 1) * 128].rearrange("(p o) -> p o", o=1), writes=[b_cb])
            fw.op("pool", lambda: nc.gpsimd.memset(gwf[:], 0.0), writes=[b_gwf])
            fw.op("pool", lambda: nc.gpsimd.memset(one[:], 1.0), writes=[b_one])
            for dr in range(2):
                for g in range(2):
                    for bk in range(2):
                        fw.dma("sp", gwf[bk * 64:(bk + 1) * 64, dr * 2 + g, bk * 64:(bk + 1) * 64],
                               K.lru_gate_w[l, dr, g, cc * 2 + bk], writes=[b_gwf])
                    fw.dma("sp", gb[:, dr * 2 + g:dr * 2 + g + 1],
                           K.lru_gate_b[l, dr, g, cc * 2:cc * 2 + 2, :].rearrange("k (d o) -> (k d) o", o=1), writes=[b_gb])
                fw.dma("sp", lam[:, dr:dr + 1], K.lru_lambda[l, dr, cc * 128:(cc + 1) * 128].rearrange("(p o) -> p o", o=1),
                       writes=[b_lam])
            fw.op("dve", lambda: nc.vector.tensor_copy(out=gw[:], in_=gwf[:]), reads=[b_gwf], writes=[b_gw])
            fw.op("act", lambda: nc.scalar.activation(out=cn[:], in_=lam[:], func=AF.Exp, scale=-1.0), reads=[b_lam], writes=[b_cn])
            fw.op("act", lambda: nc.scalar.activation(out=cn[:], in_=cn[:], func=AF.Ln, bias=one[:]), reads=[b_cn, b_one], writes=[b_cn])
            fw.op("dve", lambda: nc.vector.tensor_scalar(out=cn[:], in0=cn[:], scalar1=-8.0, scalar2=None, op0=ALU.mult),
                  reads=[b_cn], writes=[b_cn])
            for (s0, s1) in ((0, NLAT), (NLAT, T)):
                fw.op("dve", lambda: nc.vector.tensor_scalar(out=B[:, s0:s1], in0=A[:, s0:s1], scalar1=cw[:, 1:2], scalar2=cb[:, 0:1],
                                                             op0=ALU.mult, op1=ALU.add), reads=[b_A, b_cw, b_cb], writes=[b_B])
                fw.op("dve", lambda: nc.vector.scalar_tensor_tensor(out=B[:, s0 + 1:s1], in0=A[:, s0:s1 - 1], scalar=cw[:, 0:1],
                                                                    in1=B[:, s0 + 1:s1], op0=ALU.mult, op1=ALU.add),
                      reads=[b_A, b_cw, b_B], writes=[b_B])
                fw.op("dve", lambda: nc.vector.scalar_tensor_tensor(out=B[:, s0:s1 - 1], in0=A[:, s0 + 1:s1], scalar=cw[:, 2:3],
                                                                    in1=B[:, s0:s1 - 1], op0=ALU.mult, op1=ALU.add),
                      reads=[b_A, b_cw, b_B], writes=[b_B])
                fw.op("dve", lambda: nc.vector.scalar_tensor_tensor(out=B[:, s0:s1 - 2], in0=A[:, s0 + 2:s1], scalar=cw[:, 3:4],
                                                                    in1=B[:, s0:s1 - 2], op0=ALU.mult, op1=ALU.add),
                      reads=[b_A, b_cw, b_B], writes=[b_B])
            fw.op("pool", lambda: nc.gpsimd.tensor_copy(out=Bb[:], in_=B[:]), reads=[b_B], writes=[b_Bb])
            pi = 0
            chunks = [(c0, min(T, c0 + CH)) for c0 in range(0, T, CH)]
            for dr in range(2):
                for g, (dst, bdst) in enumerate(((C, b_C), (Dd, b_D))):
                    for (c0, c1) in chunks:
                        p_, bp_ = ps[pi % 2], b_ps[pi % 2]
                        pi += 1
                        for t0 in range(c0, c1, 512):
                            t1 = min(c1, t0 + 512)
                            fw.op("pe", lambda: nc.tensor.matmul(p_[:, t0 - c0:t1 - c0], lhsT=gw[:, dr * 2 + g, :], rhs=Bb[:, t0:t1],
                                                                 start=True, stop=True),
                                  reads=[b_gw, b_Bb], writes=[bp_], inc=(t1 == c1))
                        fw.op("act", lambda: nc.scalar.activation(out=dst[:, c0:c1], in_=p_[:, 0:c1 - c0], func=AF.Sigmoid,
                                                                  bias=gb[:, dr * 2 + g:dr * 2 + g + 1]),
                              reads=[bp_, b_gb], writes=[bdst])
                fw.op("act", lambda: nc.scalar.activation(out=C[:], in_=C[:], func=AF.Exp, scale=cn[:, dr:dr + 1]),
                      reads=[b_C, b_cn], writes=[b_C])
                fw.op("pool", lambda: nc.gpsimd.tensor_tensor(out=A[:], in0=C[:], in1=C[:], op=ALU.mult), reads=[b_C], writes=[b_A])
                fw.op("act", lambda: nc.scalar.activation(out=A[:], in_=A[:], func=AF.Sqrt, scale=-1.0, bias=one[:]),
                      reads=[b_A, b_one], writes=[b_A])
                fw.op("dve", lambda: nc.vector.tensor_tensor(out=Dd[:], in0=Dd[:], in1=B[:], op=ALU.mult), reads=[b_D, b_B], writes=[b_D])
                fw.op("dve", lambda: nc.vector.tensor_tensor(out=Dd[:], in0=Dd[:], in1=A[:], op=ALU.mult), reads=[b_D, b_A], writes=[b_D])
                if dr == 0:
                    segs = [(NLAT, T)] + [(c0, c0 + CH) for c0 in range(0, NLAT, CH)]
                    prev = None
                    for (c0, c1) in segs:
                        init = 0.0 if prev is None else A[:, prev - 1:prev]
                        fw.op("dve", lambda: nc.vector.tensor_tensor_scan(out=A[:, c0:c1], data0=C[:, c0:c1], data1=Dd[:, c0:c1],
                                                                          initial=init, op0=ALU.mult, op1=ALU.add),
                              reads=[b_C, b_D, b_A], writes=[b_A])
                        prev = c1
                else:
                    segs = [(NLAT, T)] + [(c0, c0 + CH) for c0 in range(NLAT - CH, -1, -CH)]
                    prev = None
                    rev = lambda ap: ap[:, ::-1]
                    for (c0, c1) in segs:
                        init = 0.0 if prev is None else A[:, prev:prev + 1]
                        fw.op("dve", lambda: nc.vector.tensor_tensor_scan(out=rev(A[:, c0:c1]), data0=rev(C[:, c0:c1]),
                                                                          data1=rev(Dd[:, c0:c1]), initial=init,
                                                                          op0=ALU.mult, op1=ALU.add),
                              reads=[b_C, b_D, b_A], writes=[b_A])
                        prev = c0
                if dr == 0:
                    fw.op("pool", lambda: nc.gpsimd.tensor_copy(out=E[:], in_=A[:]), reads=[b_A], writes=[b_E])
                else:
                    fw.op("pool", lambda: nc.gpsimd.tensor_tensor(out=E[:], in0=E[:], in1=A[:], op=ALU.add), reads=[b_A, b_E], writes=[b_E])
            for ci, (c0, c1) in enumerate([(c0, c0 + CG) for c0 in range(0, T, CG)]):
                g_, bg_ = gst[ci % 2], b_gst[ci % 2]
                y_, by_ = yst[ci % 2], b_yst[ci % 2]
                fw.dma("sp", g_[:, 0:c1 - c0], K.featF[l][2 + cc, :, c0:c1], reads=[K.b_feat], writes=[bg_])
                fw.op("act", lambda: nc.scalar.activation(out=g_[:, 0:c1 - c0], in_=g_[:, 0:c1 - c0], func=AF.Gelu_apprx_tanh),
                      reads=[bg_], writes=[bg_])
                fw.op("dve", lambda: nc.vector.tensor_tensor(out=y_[:, 0:c1 - c0], in0=g_[:, 0:c1 - c0], in1=E[:, c0:c1], op=ALU.mult),
                      reads=[bg_, b_E], writes=[by_])
                fw.dma("pool", K.yT[l][2 + cc, :, c0:c1], y_[:, 0:c1 - c0], reads=[by_], writes=[K.b_yT])
            fw.barrier()


def _ret_consts():
    j = np.arange(128)[:, None].astype(np.float32)
    i = np.arange(128)[None, :].astype(np.float32)
    z = np.zeros((128, 128), np.float32)
    c = np.stack([np.maximum(i - j, 0), (i >= j).astype(np.float32), np.maximum(j - i - 1, 0),
                  (j > i).astype(np.float32), z + i + 1, z + 127 - i], 1).astype(np.float32)
    cv = np.stack([127 - np.arange(128), np.arange(128)], 1).astype(np.float32)
    return np.ascontiguousarray(c), np.ascontiguousarray(cv)


def _blk64():
    b = np.zeros((128, 128), np.float32)
    b[:64, :64] = 1.0 / 64
    b[64:, 64:] = 1.0 / 64
    return b


def _phase_ret(K, l):
    nc, fw = K.nc, K.fw
    last = l == DEPTH - 1
    NCH = T // 128
    with contextlib.ExitStack() as st0:
        sb0 = lambda n, s, d: st0.enter_context(nc.sbuf_tensor(K.un(n), s, d))
        rc = sb0("r_rc", [128, 6, 128], F32); b_rc = Buf()
        cv = sb0("r_cv", [128, 2], F32); b_cv = Buf()
        rd = sb0("r_rd", [128, 8], F32); b_rd = Buf()
        lg = sb0("r_lg", [128, 8], F32); b_lg = Buf()
        one = sb0("r_one", [128, 1], F32); b_one = Buf()
        eps_t = sb0("r_eps", [128, 1], F32); b_eps = Buf()
        blkf = sb0("r_blkf", [128, 128], F32); b_blkf = Buf()
        blk = sb0("r_blk", [128, 128], BF16); b_blk = Buf()
        idf = sb0("r_idf", [128, 128], F32); b_idf = Buf()
        ident = sb0("r_id", [128, 128], BF16); b_id = Buf()
        fw.dma("sp", rc[:], K.retc, writes=[b_rc])
        fw.dma("sp", cv[:], K.retcv, writes=[b_cv])
        fw.dma("sp", rd[:], K.ret_decay[l:l + 1].rearrange("o a h -> o (a h)").broadcast_to([128, 8]), writes=[b_rd])
        fw.dma("sp", blkf[:], K.blk64, writes=[b_blkf])
        fw.dma("sp", idf[:], K.ident, writes=[b_idf])
        fw.op("dve", lambda: nc.vector.tensor_copy(out=blk[:], in_=blkf[:]), reads=[b_blkf], writes=[b_blk])
        fw.op("dve", lambda: nc.vector.tensor_copy(out=ident[:], in_=idf[:]), reads=[b_idf], writes=[b_id])
        fw.op("pool", lambda: nc.gpsimd.memset(one[:], 1.0), writes=[b_one])
        fw.op("pool", lambda: nc.gpsimd.memset(eps_t[:], EPS), writes=[b_eps])
        fw.op("act", lambda: nc.scalar.activation(out=lg[:], in_=rd[:], func=AF.Exp, scale=-1.0), reads=[b_rd], writes=[b_lg])
        fw.op("act", lambda: nc.scalar.activation(out=lg[:], in_=lg[:], func=AF.Ln, bias=one[:]), reads=[b_lg, b_one], writes=[b_lg])
        fw.op("dve", lambda: nc.vector.tensor_scalar(out=lg[:], in0=lg[:], scalar1=-1.0, scalar2=None, op0=ALU.mult),
              reads=[b_lg], writes=[b_lg])
        for cc in range(2):
            with contextlib.ExitStack() as st:
                sb = lambda n, s, d: st.enter_context(nc.sbuf_tensor(K.un(n), s, d))
                QT = sb("r_QT", [128, NCH, 128], BF16); b_QT = Buf()
                KT = sb("r_KT", [128, NCH, 128], BF16); b_KT = Buf()
                V = sb("r_V", [128, NCH, 128], BF16); b_V = Buf()
                qf = sb("r_qf", [128, NCH, 128], BF16); b_qf = Buf()
                qb = sb("r_qb", [128, NCH, 128], BF16); b_qb = Buf()
                KVs = [sb("r_KVs%d" % d_, [128, NCH + 6, 64], F32) for d_ in range(2)]; b_KVs = [Buf(), Buf()]
                Sbf = [sb("r_Sbf%d" % d_, [128, NCH, 64], BF16) for d_ in range(2)]; b_Sbf = [Buf(), Buf()]
                lgp = sb("r_lgp", [128, 2], F32); b_lgp = Buf()
                g128 = sb("r_g128", [128, 2], F32); b_g128 = Buf()
                M = sb("r_M", [128, 2, 128], F32); b_M = Buf()
                mt = sb("r_mt", [128, 2, 128], F32); b_mt = Buf()
                df = sb("r_df", [128, 128], F32); b_df = Buf()
                db = sb("r_db", [128, 128], F32); b_db = Buf()
                kdec = sb("r_kdec", [128, 2, 2], F32); b_kdec = Buf()
                for c0 in range(0, NCH, 22):
                    fw.dma("sp", QT[:, c0:c0 + 22, :], K.featB[l][4 + cc, :, c0 * 128:(c0 + 22) * 128].rearrange("p (c i) -> p c i", i=128),
                           reads=[K.b_feat], writes=[b_QT])
                    fw.dma("sp", KT[:, c0:c0 + 22, :], K.featB[l][6 + cc, :, c0 * 128:(c0 + 22) * 128].rearrange("p (c i) -> p c i", i=128),
                           reads=[K.b_feat], writes=[b_KT])
                for c0 in range(0, NCH, 6):
                    fw.dma("sp", V[:, c0:c0 + 6, :],
                           K.vAll[l][c0 * 128:(c0 + 6) * 128, 256 + cc * 128:256 + (cc + 1) * 128].rearrange("(c p) e -> p c e", p=128),
                           reads=[K.b_feat], writes=[b_V])
                for dr in range(2):
                    for hh in range(2):
                        col = dr * 4 + 2 * cc + hh
                        fw.op("dve", lambda: nc.vector.tensor_copy(out=lgp[hh * 64:(hh + 1) * 64, dr:dr + 1],
                                                                   in_=lg[hh * 64:(hh + 1) * 64, col:col + 1]),
                              reads=[b_lg], writes=[b_lgp])
                fw.op("act", lambda: nc.scalar.activation(out=g128[:], in_=lgp[:], func=AF.Exp, scale=128.0), reads=[b_lgp], writes=[b_g128])
                for hh in range(2):
                    h = 2 * cc + hh
                    fw.op("act", lambda: nc.scalar.activation(out=M[:, hh, :], in_=rc[:, 0, :], func=AF.Exp, scale=lg[:, h:h + 1]),
                          reads=[b_rc, b_lg], writes=[b_M])
                    fw.op("act", lambda: nc.scalar.activation(out=mt[:, hh, :], in_=rc[:, 2, :], func=AF.Exp, scale=lg[:, 4 + h:5 + h]),
                          reads=[b_rc, b_lg], writes=[b_mt])
                    fw.op("dve", lambda: nc.vector.tensor_tensor(out=M[:, hh, :], in0=M[:, hh, :], in1=rc[:, 1, :], op=ALU.mult),
                          reads=[b_M, b_rc], writes=[b_M])
                    fw.op("dve", lambda: nc.vector.tensor_tensor(out=mt[:, hh, :], in0=mt[:, hh, :], in1=rc[:, 3, :], op=ALU.mult),
                          reads=[b_mt, b_rc], writes=[b_mt])
                    fw.op("act", lambda: nc.scalar.activation(out=kdec[:, 0, hh:hh + 1], in_=lg[:, h:h + 1], func=AF.Exp, scale=cv[:, 0:1]),
                          reads=[b_lg, b_cv], writes=[b_kdec])
                    fw.op("act", lambda: nc.scalar.activation(out=kdec[:, 1, hh:hh + 1], in_=lg[:, 4 + h:5 + h], func=AF.Exp, scale=cv[:, 1:2]),
                          reads=[b_lg, b_cv], writes=[b_kdec])
                fw.op("dve", lambda: nc.vector.tensor_tensor(out=M[:], in0=M[:], in1=mt[:], op=ALU.add), reads=[b_M, b_mt], writes=[b_M])
                fw.op("act", lambda: nc.scalar.activation(out=df[:], in_=rc[:, 4, :], func=AF.Exp, scale=lgp[:, 0:1]),
                      reads=[b_rc, b_lgp], writes=[b_df])
                fw.op("act", lambda: nc.scalar.activation(out=db[:], in_=rc[:, 5, :], func=AF.Exp, scale=lgp[:, 1:2]),
                      reads=[b_rc, b_lgp], writes=[b_db])
                fw.op("dve", lambda: nc.vector.tensor_tensor(out=qf[:], in0=QT[:], in1=df[:].unsqueeze(1).broadcast_to([128, NCH, 128]), op=ALU.mult),
                      reads=[b_QT, b_df], writes=[b_qf])
                fw.op("pool", lambda: nc.gpsimd.tensor_tensor(out=qb[:], in0=QT[:], in1=db[:].unsqueeze(1).broadcast_to([128, NCH, 128]), op=ALU.mult),
                      reads=[b_QT, b_db], writes=[b_qb])
                if K.stop <= 0:
                    fw.barrier()
                    continue
                with contextlib.ExitStack() as st1:
                    tpk = [st1.enter_context(nc.psum_tensor(K.un("r_tpk%d" % i), [128, 1024], BF16)) for i in range(2)]
                    b_tpk = [Buf(), Buf()]
                    kvp = [[st1.enter_context(nc.psum_tensor(K.un("r_kvp%d_%d" % (d_, i)), [128, 8, 64], F32)) for i in range(2)] for d_ in range(2)]
                    b_kvp = [[Buf(), Buf()] for d_ in range(2)]
                    kd = [[st1.enter_context(nc.sbuf_tensor(K.un("r_kd%d_%d" % (d_, i)), [128, 2, 64], BF16)) for i in range(2)] for d_ in range(2)]
                    b_kd = [[Buf(), Buf()] for d_ in range(2)]
                    for c in range(min(NCH, K.p1n)):
                        p = c % 2
                        grp = (c // 8) % 2
                        fw.op("pe", lambda: nc.tensor.transpose(tpk[p][:, 0:128], KT[:, c, :], ident[:]), reads=[b_KT, b_id], writes=[b_tpk[p]])
                        for dr in range(2):
                            fw.op("dve", lambda: nc.vector.tensor_tensor(out=kd[dr][p][:], in0=tpk[p][:, 0:128].rearrange("p (h d) -> p h d", d=64),
                                                                         in1=kdec[:, dr, :].unsqueeze(2).broadcast_to([128, 2, 64]), op=ALU.mult),
                                  reads=[b_tpk[p], b_kdec], writes=[b_kd[dr][p]])
                        for dr in range(2):
                            for hh in range(2):
                                fw.op("pe", lambda: nc.tensor.matmul(kvp[dr][grp][hh * 64:(hh + 1) * 64, c % 8, :], lhsT=kd[dr][p][:, hh, :],
                                                                     rhs=V[:, c, hh * 64:(hh + 1) * 64], start=True, stop=True),
                                      reads=[b_kd[dr][p], b_V], writes=[b_kvp[dr][grp]], inc=(hh == 1))
                        if c % 8 == 7 or c == min(NCH, K.p1n) - 1:
                            n8 = c % 8 + 1
                            c8 = c - (n8 - 1)
                            for dr in range(2):
                                fw.op("act", lambda: nc.scalar.copy(out=KVs[dr][:, c8:c8 + n8, :], in_=kvp[dr][grp][:, 0:n8, :]),
                                      reads=[b_kvp[dr][grp]], writes=[b_KVs[dr]])
                    fw.barrier()
                if K.stop <= 1:
                    continue
                with contextlib.ExitStack() as st1:
                    S = [[st1.enter_context(nc.sbuf_tensor(K.un("r_S%d_%d" % (d_, i)), [128, 64], F32)) for i in range(2)] for d_ in range(2)]
                    b_S = [[Buf(), Buf()] for d_ in range(2)]
                    orders = [[64, 65] + list(range(64)), [65, 64] + list(range(63, -1, -1))]
                    for dr in range(2):
                        fw.op("pool", lambda: nc.gpsimd.memset(S[dr][0][:], 0.0), writes=[b_S[dr][0]])
                    for n in range(NCH):
                        for dr in range(2):
                            c = orders[dr][n]
                            cur, nxt = S[dr][n % 2], S[dr][(n + 1) % 2]
                            bcur, bnxt = b_S[dr][n % 2], b_S[dr][(n + 1) % 2]
                            eng = "act" if dr == 0 else "pool"
                            if eng == "act":
                                fw.op("act", lambda: nc.scalar.copy(out=Sbf[dr][:, c, :], in_=cur[:]), reads=[bcur], writes=[b_Sbf[dr]])
                            else:
                                fw.op("pool", lambda: nc.gpsimd.tensor_copy(out=Sbf[dr][:, c, :], in_=cur[:]), reads=[bcur], writes=[b_Sbf[dr]])
                            fw.op("dve", lambda: nc.vector.scalar_tensor_tensor(out=nxt[:], in0=cur[:], scalar=g128[:, dr:dr + 1],
                                                                                in1=KVs[dr][:, c, :], op0=ALU.mult, op1=ALU.add),
                                  reads=[bcur, b_g128, b_KVs[dr]], writes=[bnxt])
                    fw.barrier()
                if K.stop <= 2:
                    continue
                with contextlib.ExitStack() as st1:
                    sbl = lambda n, s, d: st1.enter_context(nc.sbuf_tensor(K.un(n), s, d))
                    sc = [[st1.enter_context(nc.psum_tensor(K.un("r_sc%d_%d" % (i, h_)), [128, 512], F32)) for h_ in range(2)] for i in range(2)]
                    b_sc = [[Buf(), Buf()], [Buf(), Buf()]]
                    oT = [st1.enter_context(nc.psum_tensor(K.un("r_oT%d" % i), [128, 512], F32)) for i in range(2)]; b_oT = [Buf(), Buf()]
                    ms = [st1.enter_context(nc.psum_tensor(K.un("r_ms%d" % i), [128, 512], F32)) for i in range(2)]; b_ms = [Buf(), Buf()]
                    P = [sbl("r_P%d" % i, [128, 2, 128], BF16) for i in range(2)]; b_P = [Buf(), Buf()]
                    sq = [sbl("r_sq%d" % i, [128, 512], BF16) for i in range(2)]; b_sq = [Buf(), Buf()]
                    sd = [sbl("r_sd%d" % i, [128, 512], F32) for i in range(2)]; b_sd = [Buf(), Buf()]
                    tt = [sbl("r_tt%d" % i, [128, 512], F32) for i in range(2)]; b_tt = [Buf(), Buf()]
                    gs = [sbl("r_gs%d" % i, [128, 512], F32) for i in range(2)]; b_gs = [Buf(), Buf()]
                    ys = [sbl("r_ys%d" % i, [128, 512], BF16) for i in range(2)]; b_ys = [Buf(), Buf()]
                    groups = [list(range(g * 4, g * 4 + 4)) for g in range(16)]
                    if not last:
                        groups.append([64, 65])
                    groups = groups[:K.p2n] if K.p2n >= 0 else groups[K.p2n:]
                    ci_all = 0
                    for gi, grp in enumerate(groups):
                        q = gi % 2
                        tok0 = grp[0] * 128
                        ntok = len(grp) * 128
                        if grp[0] == 64:
                            fw.barrier()
                        fw.dma("sp", gs[q][:, 0:ntok], K.featF[l][4 + cc, :, tok0:tok0 + ntok], reads=[K.b_feat], writes=[b_gs[q]])
                        for ci, c in enumerate(grp):
                            p = ci_all % 2
                            ci_all += 1
                            for hh in range(2):
                                fw.op("pe", lambda: nc.tensor.matmul(sc[p][hh][:, 0:128], lhsT=KT[hh * 64:(hh + 1) * 64, c, :],
                                                                     rhs=QT[hh * 64:(hh + 1) * 64, c, :], start=True, stop=True),
                                      reads=[b_KT, b_QT], writes=[b_sc[p][hh]])
                            for hh in range(2):
                                fw.op("dve", lambda: nc.vector.tensor_tensor(out=P[p][:, hh, :], in0=sc[p][hh][:, 0:128], in1=M[:, hh, :], op=ALU.mult),
                                      reads=[b_sc[p][hh], b_M], writes=[b_P[p]])
                            for hh in range(2):
                                o_ = oT[q][hh * 64:(hh + 1) * 64, ci * 128:(ci + 1) * 128]
                                fw.op("pe", lambda: nc.tensor.matmul(o_, lhsT=V[:, c, hh * 64:(hh + 1) * 64], rhs=P[p][:, hh, :],
                                                                     start=True, stop=False),
                                      reads=[b_V, b_P[p]], writes=[b_oT[q]], inc=False)
                                fw.op("pe", lambda: nc.tensor.matmul(o_, lhsT=Sbf[0][hh * 64:(hh + 1) * 64, c, :],
                                                                     rhs=qf[hh * 64:(hh + 1) * 64, c, :], start=False, stop=False),
                                      reads=[b_Sbf[0], b_qf], writes=[b_oT[q]], inc=False)
                                fw.op("pe", lambda: nc.tensor.matmul(o_, lhsT=Sbf[1][hh * 64:(hh + 1) * 64, c, :],
                                                                     rhs=qb[hh * 64:(hh + 1) * 64, c, :], start=False, stop=True),
                                      reads=[b_Sbf[1], b_qb], writes=[b_oT[q]], inc=(hh == 1))
                        fw.op("act", lambda: nc.scalar.activation(out=sq[q][:, 0:ntok], in_=oT[q][:, 0:ntok], func=AF.Square),
                              reads=[b_oT[q]], writes=[b_sq[q]])
                        fw.op("pe", lambda: nc.tensor.matmul(ms[q][:, 0:ntok], lhsT=blk[:], rhs=sq[q][:, 0:ntok], start=True, stop=True),
                              reads=[b_blk, b_sq[q]], writes=[b_ms[q]])
                        fw.op("act", lambda: nc.scalar.activation(out=sd[q][:, 0:ntok], in_=ms[q][:, 0:ntok], func=AF.Sqrt, bias=eps_t[:]),
                              reads=[b_ms[q], b_eps], writes=[b_sd[q]])
                        fw.op("dve", lambda: nc.vector.reciprocal(out=sd[q][:, 0:ntok], in_=sd[q][:, 0:ntok]), reads=[b_sd[q]], writes=[b_sd[q]])
                        fw.op("dve", lambda: nc.vector.tensor_tensor(out=tt[q][:, 0:ntok], in0=oT[q][:, 0:ntok], in1=sd[q][:, 0:ntok], op=ALU.mult),
                              reads=[b_oT[q], b_sd[q]], writes=[b_tt[q]])
                        fw.op("act", lambda: nc.scalar.activation(out=gs[q][:, 0:ntok], in_=gs[q][:, 0:ntok], func=AF.Silu),
                              reads=[b_gs[q]], writes=[b_gs[q]])
                        fw.op("pool", lambda: nc.gpsimd.tensor_tensor(out=ys[q][:, 0:ntok], in0=tt[q][:, 0:ntok], in1=gs[q][:, 0:ntok], op=ALU.mult),
                              reads=[b_tt[q], b_gs[q]], writes=[b_ys[q]])
                        fw.dma("pool", K.yT[l][4 + cc, :, tok0:tok0 + ntok], ys[q][:, 0:ntok], reads=[b_ys[q]], writes=[K.b_yT])
                    fw.barrier()


def _diff_consts():
    sel = np.zeros((65, 64), np.float32)
    sel[64, :] = 1.0
    ones64 = np.full((64, 64), 1.0 / 64, np.float32)
    mcol = np.zeros((128, 4), np.float32)
    for p in range(128):
        mcol[p, p // 32] = 1.0
    return sel, ones64, mcol


def _phase_diff(K, l):
    nc, fw = K.nc, K.fw
    last = l == DEPTH - 1
    NCH = T // 128
    lam_init = 0.8 - 0.6 * math.exp(-0.3 * l)
    SC = 32 ** -0.5
    with contextlib.ExitStack() as st0:
        sb0 = lambda n, s, d: st0.enter_context(nc.sbuf_tensor(K.un(n), s, d))
        Va = sb0("d_Va", [128, NCH, 256], BF16); b_Va = Buf()
        o64 = sb0("d_o64", [64, 64], F32); b_o64 = Buf()
        onesf = sb0("d_onesf", [128, 64], F32); b_onesf = Buf()
        selm = sb0("d_selm", [128, 64], F32); b_selm = Buf()
        mcol = sb0("d_mcol", [128, 4], F32); b_mcol = Buf()
        dl = sb0("d_dl", [128, 4, 32], F32); b_dl = Buf()
        pr = sb0("d_pr", [128, 2, 32], F32); b_pr = Buf()
        s12 = sb0("d_s12", [128, 2], F32); b_s12 = Buf()
        nlam = sb0("d_nlam", [128, 1], F32); b_nlam = Buf()
        gsc = sb0("d_gsc", [64, 1], F32); b_gsc = Buf()
        eps_t = sb0("d_eps", [64, 1], F32); b_eps = Buf()
        fw.dma("sp", o64[:], K.ones64, writes=[b_o64])
        fw.dma("sp", mcol[:], K.mcol, writes=[b_mcol])
        fw.dma("sp", dl[:], K.diff_lambda[l:l + 1].broadcast_to([128, 4, 32]), writes=[b_dl])
        fw.dma("sp", gsc[:], K.diff_subln[l].rearrange("(p o) -> p o", o=1), writes=[b_gsc])
        fw.dma("sp", selm[0:64, :], K.ident[0:64, 0:64], writes=[b_selm])
        fw.dma("sp", selm[64:128, :], K.ident[0:64, 0:64], writes=[b_selm])
        fw.op("pool", lambda: nc.gpsimd.memset(eps_t[:], EPS), writes=[b_eps])
        fw.op("pool", lambda: nc.gpsimd.memset(onesf[:], 1.0), writes=[b_onesf])
        fw.op("dve", lambda: nc.vector.tensor_tensor(out=pr[:, 0, :], in0=dl[:, 0, :], in1=dl[:, 1, :], op=ALU.mult), reads=[b_dl], writes=[b_pr])
        fw.op("dve", lambda: nc.vector.tensor_tensor(out=pr[:, 1, :], in0=dl[:, 2, :], in1=dl[:, 3, :], op=ALU.mult), reads=[b_dl], writes=[b_pr])
        fw.op("dve", lambda: nc.vector.tensor_reduce(out=s12[:], in_=pr[:], axis=AX.X, op=ALU.add), reads=[b_pr], writes=[b_s12])
        fw.op("act", lambda: nc.scalar.activation(out=s12[:], in_=s12[:], func=AF.Exp), reads=[b_s12], writes=[b_s12])
        fw.op("dve", lambda: nc.vector.tensor_tensor(out=nlam[:], in0=s12[:, 1:2], in1=s12[:, 0:1], op=ALU.subtract), reads=[b_s12], writes=[b_nlam])
        fw.op("dve", lambda: nc.vector.tensor_scalar(out=nlam[:], in0=nlam[:], scalar1=-lam_init, scalar2=None, op0=ALU.add),
              reads=[b_nlam], writes=[b_nlam])
        fw.op("dve", lambda: nc.vector.tensor_scalar(out=selm[64:128, :], in0=selm[64:128, :], scalar1=nlam[64:128, 0:1], scalar2=None, op0=ALU.mult),
              reads=[b_selm, b_nlam], writes=[b_selm])
        fw.op("dve", lambda: nc.vector.tensor_scalar(out=gsc[:], in0=gsc[:], scalar1=1.0 - lam_init, scalar2=None, op0=ALU.mult),
              reads=[b_gsc], writes=[b_gsc])
        for c0 in range(0, NCH, 6):
            fw.dma("sp", Va[:, c0:c0 + 6, :], K.vAll[l][c0 * 128:(c0 + 6) * 128, 512:768].rearrange("(c p) e -> p c e", p=128),
                   reads=[K.b_feat], writes=[b_Va])
        for cc in range(2):
            with contextlib.ExitStack() as st:
                sb = lambda n, s, d: st.enter_context(nc.sbuf_tensor(K.un(n), s, d))
                QT = sb("d_QT", [128, T], BF16); b_QT = Buf()
                KT = sb("d_KT", [128, T], BF16); b_KT = Buf()
                Qm = [sb("d_Qm%d" % m, [128, T], BF16) for m in range(4)]; b_Qm = [Buf() for _ in range(4)]
                P = [sb("d_P%d" % i, [128, 2, 512], BF16) for i in range(2)]; b_P = [Buf(), Buf()]
                acc = [sb("d_acc%d" % i, [128, 2, 512], F32) for i in range(2)]; b_acc = [[Buf(), Buf()], [Buf(), Buf()]]
                Osb = sb("d_Osb", [128, 512], F32); b_Osb = Buf()
                rec = sb("d_rec", [128, 512], F32); b_rec = Buf()
                ab = sb("d_ab", [128, 512], F32); b_ab = Buf()
                td = sb("d_td", [64, 512], F32); b_td = Buf()
                tq = sb("d_tq", [64, 512], F32); b_tq = Buf()
                sd = sb("d_sd", [64, 512], F32); b_sd = Buf()
                ys = [sb("d_ys%d" % i, [64, 512], BF16) for i in range(2)]; b_ys = [Buf(), Buf()]
                S = [st.enter_context(nc.psum_tensor(K.un("d_S%d" % i), [128, 2, 512], F32)) for i in range(2)]; b_S = [Buf(), Buf()]
                O = [st.enter_context(nc.psum_tensor(K.un("d_O%d" % m), [128, 512], F32)) for m in range(2)]; b_O = [Buf(), Buf()]
                bcp = st.enter_context(nc.psum_tensor(K.un("d_bcp"), [128, 512], F32)); b_bcp = Buf()
                dps = st.enter_context(nc.psum_tensor(K.un("d_dps"), [128, 512], F32)); b_dps = Buf()
                for c0 in range(0, T, 2112):
                    fw.dma("sp", QT[:, c0:c0 + 2112], K.featB[l][8 + cc, :, c0:c0 + 2112], reads=[K.b_feat], writes=[b_QT])
                    fw.dma("sp", KT[:, c0:c0 + 2112], K.featB[l][10 + cc, :, c0:c0 + 2112], reads=[K.b_feat], writes=[b_KT])
                for g_ in range(4):
                    fw.op("dve", lambda: nc.vector.tensor_scalar(out=Qm[g_][:], in0=QT[:], scalar1=mcol[:, g_:g_ + 1], scalar2=None, op0=ALU.mult),
                          reads=[b_QT, b_mcol], writes=[b_Qm[g_]])
                si = 0
                yi = [0]
                deferred = []

                def finalize_stage1(oi, ai, rows, q0, nq):
                    fw.op("act", lambda: nc.scalar.copy(out=Osb[:, 0:nq], in_=O[oi][:, 0:nq]), reads=[b_O[oi]], writes=[b_Osb])

                def finalize_stage2(oi, ai, rows, q0, nq):
                    for m in range(2):
                        fw.op("pe", lambda: nc.tensor.matmul(bcp[m * 64:(m + 1) * 64, 0:nq], lhsT=onesf[:], rhs=acc[ai][:, m, 0:nq],
                                                             start=True, stop=True),
                              reads=[b_onesf, b_acc[ai][m]], writes=[b_bcp], inc=(m == 1))
                    fw.op("dve", lambda: nc.vector.reciprocal(out=rec[:, 0:nq], in_=bcp[:, 0:nq]), reads=[b_bcp], writes=[b_rec])
                    fw.op("dve", lambda: nc.vector.tensor_tensor(out=ab[:, 0:nq], in0=Osb[:, 0:nq], in1=rec[:, 0:nq], op=ALU.mult),
                          reads=[b_Osb, b_rec], writes=[b_ab])
                    fw.op("pe", lambda: nc.tensor.matmul(dps[0:64, 0:nq], lhsT=selm[:], rhs=ab[:, 0:nq], start=True, stop=True),
                          reads=[b_selm, b_ab], writes=[b_dps])
                    fw.op("dve", lambda: nc.vector.tensor_copy(out=td[:, 0:nq], in_=dps[0:64, 0:nq]), reads=[b_dps], writes=[b_td])
                    fw.op("pool", lambda: nc.gpsimd.tensor_tensor(out=tq[:, 0:nq], in0=td[:, 0:nq], in1=td[:, 0:nq], op=ALU.mult),
                          reads=[b_td], writes=[b_tq])

                def finalize_stage3(oi, ai, rows, q0, nq):
                    fw.op("pe", lambda: nc.tensor.matmul(dps[0:64, 0:nq], lhsT=o64[:], rhs=tq[:, 0:nq], start=True, stop=True),
                          reads=[b_o64, b_tq, b_td], writes=[b_dps])
                    fw.op("act", lambda: nc.scalar.activation(out=sd[:, 0:nq], in_=dps[0:64, 0:nq], func=AF.Sqrt, bias=eps_t[:]),
                          reads=[b_dps, b_eps], writes=[b_sd])
                    fw.op("dve", lambda: nc.vector.reciprocal(out=sd[:, 0:nq], in_=sd[:, 0:nq]), reads=[b_sd], writes=[b_sd])
                    y_, by_ = ys[yi[0] % 2], b_ys[yi[0] % 2]
                    yi[0] += 1
                    fw.op("dve", lambda: nc.vector.scalar_tensor_tensor(out=y_[:, 0:nq], in0=td[:, 0:nq], scalar=gsc[:, 0:1], in1=sd[:, 0:nq],
                                                                        op0=ALU.mult, op1=ALU.mult),
                          reads=[b_td, b_gsc, b_sd], writes=[by_])
                    fw.dma("pool", K.yT[l][6 + cc, rows, q0:q0 + nq], y_[:, 0:nq], reads=[by_], writes=[K.b_yT])

                def flush(step):
                    while deferred and deferred[0][0] <= step:
                        deferred.pop(0)[1]()

                ci = 0
                for hh in range(2):
                    h = 2 * cc + hh
                    rows = slice(hh * 64, (hh + 1) * 64)
                    qchunks = [(q0, 512, list(range(NCH))) for q0 in range(0, NLAT, 512)]
                    if not last:
                        qchunks.append((NLAT, NCTX, [64, 65]))
                    qchunks = qchunks[:K.p2n] if K.p2n >= 0 else qchunks[K.p2n:]
                    for (q0, nq, kbs) in qchunks:
                        if q0 == NLAT:
                            flush(10 ** 9)
                            fw.barrier()
                        n = len(kbs)
                        oi = ci % 2
                        ci += 1
                        acc_, bacc_ = acc[oi], b_acc[oi]
                        O_, bO_ = O[oi], b_O[oi]

                        def emit_S(k, si_):
                            S_, bS_ = S[si_ % 2], b_S[si_ % 2]
                            for m in range(2):
                                fw.op("pe", lambda: nc.tensor.matmul(S_[:, m, 0:nq], lhsT=KT[:, kbs[k] * 128:(kbs[k] + 1) * 128],
                                                                     rhs=Qm[hh * 2 + m][:, q0:q0 + nq], start=True, stop=True),
                                      reads=[b_KT, b_Qm[hh * 2 + m]], writes=[bS_], inc=(m == 1))

                        emit_S(0, si)
                        for k in range(n):
                            if k + 1 < n:
                                emit_S(k + 1, si + 1)
                            S_, bS_ = S[si % 2], b_S[si % 2]
                            P_, bP_ = P[si % 2], b_P[si % 2]
                            si += 1
                            fw.op("act", lambda: nc.scalar.activation(out=P_[:, :, 0:nq], in_=S_[:, :, 0:nq], func=AF.Exp, scale=SC),
                                  reads=[bS_], writes=[bP_])
                            for m in range(2):
                                fw.op("pe", lambda: nc.tensor.matmul(O_[m * 64:(m + 1) * 64, 0:nq], lhsT=Va[:, kbs[k], h * 64:(h + 1) * 64],
                                                                     rhs=P_[:, m, 0:nq], start=(k == 0), stop=(k == n - 1)),
                                      reads=[b_Va, bP_], writes=[bO_], inc=(m == 1))
                            for m, (en, eo) in enumerate((("dve", nc.vector), ("dve", nc.vector))):
                                ba_ = bacc_[m]
                                if k == 0:
                                    fw.op(en, lambda: eo.tensor_copy(out=acc_[:, m, 0:nq], in_=P_[:, m, 0:nq]), reads=[bP_], writes=[ba_])
                                else:
                                    fw.op(en, lambda: eo.tensor_tensor(out=acc_[:, m, 0:nq], in0=acc_[:, m, 0:nq], in1=P_[:, m, 0:nq], op=ALU.add),
                                          reads=[bP_, ba_], writes=[ba_])
                            flush(k)
                        flush(10 ** 9)
                        a_ = (oi, oi, rows, q0, nq)
                        finalize_stage1(*a_)
                        deferred.append((3, lambda a_=a_: finalize_stage2(*a_)))
                        deferred.append((8, lambda a_=a_: finalize_stage3(*a_)))
                flush(10 ** 9)
                fw.barrier()


_NA_CACHE = {}


def _na_schedule():
    if "s" in _NA_CACHE:
        return _NA_CACHE["s"]
    kl = np.arange(128)
    k_r, k_c = kl // 64, kl % 64
    q_r, q_c = kl // 64, kl % 64
    c0 = np.clip(q_c - 8, 0, 48)
    classes = {}
    tiles = []
    sched = []
    for i in range(64):
        qrow = 2 * i + q_r
        r0 = np.clip(qrow - 4, 0, 120)
        entry = []
        for j in range(64):
            krow = 2 * j + k_r
            vr = (krow[:, None] >= r0[None, :]) & (krow[:, None] <= r0[None, :] + 7)
            vc = (k_c[:, None] >= c0[None, :]) & (k_c[:, None] <= c0[None, :] + 15)
            valid = vr & vc
            if not valid.any():
                continue
            dr = np.clip(krow[:, None] - qrow[None, :] + 7, 0, 14)
            dc = np.clip(k_c[:, None] - q_c[None, :] + 15, 0, 30)
            entry.append((j - i, dr, dc, valid))
        key = tuple((d, a.tobytes(), b.tobytes(), v.tobytes()) for d, a, b, v in entry)
        if key not in classes:
            classes[key] = len(tiles)
            for d, a, b, v in entry:
                tiles.append((a, b, v))
        sched.append(([i + d for d, _, _, _ in entry], classes[key]))
    dr = np.stack([t[0] for t in tiles]); dc = np.stack([t[1] for t in tiles]); va = np.stack([t[2] for t in tiles])
    _NA_CACHE["s"] = (sched, dr, dc, va)
    return _NA_CACHE["s"]


def _na_bias(na_rpb):
    sched, dr, dc, va = _na_schedule()
    rpb = np.asarray(na_rpb, np.float32)
    g = rpb[:, :, dr, dc]
    return np.ascontiguousarray(np.where(va[None, None], g, np.float32(-30000.0)).astype(np.float32))


def _phase_na(K, l):
    nc, fw = K.nc, K.fw
    last = l == DEPTH - 1
    NCH = T // 128
    SC = 64 ** -0.5
    sched, _, _, _ = _na_schedule()
    NT = K.na_nt
    with contextlib.ExitStack() as st0:
        sb0 = lambda n, s, d: st0.enter_context(nc.sbuf_tensor(K.un(n), s, d))
        Va = sb0("n_Va", [128, NCH, 4, 65], BF16); b_Va = Buf()
        sel = sb0("n_sel", [65, 64], F32); b_sel = Buf()
        fw.dma("sp", sel[:], K.sel65, writes=[b_sel])
        with contextlib.ExitStack() as st1:
            vst = st1.enter_context(nc.sbuf_tensor(K.un("n_vst"), [128, NCH, 256], BF16)); b_vst = Buf()
            for c0 in range(0, NCH, 6):
                fw.dma("sp", vst[:, c0:c0 + 6, :], K.vAll[l][c0 * 128:(c0 + 6) * 128, 0:256].rearrange("(c p) e -> p c e", p=128),
                       reads=[K.b_feat], writes=[b_vst])
            fw.op("pool", lambda: nc.gpsimd.memset(Va[:, :, :, 64:65], 1.0), writes=[b_Va])
            fw.op("dve", lambda: nc.vector.tensor_copy(out=Va[:, :, :, 0:64], in_=vst[:].rearrange("p c (h e) -> p c h e", e=64)),
                  reads=[b_vst], writes=[b_Va])
            fw.barrier()
        for cc in range(2):
            with contextlib.ExitStack() as st:
                sb = lambda n, s, d: st.enter_context(nc.sbuf_tensor(K.un(n), s, d))
                QT = sb("n_QT", [128, T], BF16); b_QT = Buf()
                KT = sb("n_KT", [128, T], BF16); b_KT = Buf()
                E = sb("n_E", [128, NT, 128], F32); b_E = Buf()
                Pf = [sb("n_Pf%d" % i, [128, 5, 128], F32) for i in range(2)]; b_Pf = [Buf(), Buf()]
                Pw = [sb("n_Pw%d" % i, [128, 5, 128], BF16) for i in range(2)]; b_Pw = [Buf(), Buf()]
                Pc = [sb("n_Pc%d" % i, [128, 2, 128], BF16) for i in range(2)]; b_Pc = [Buf(), Buf()]
                Osb = sb("n_Osb", [65, 512], F32); b_Osb = Buf()
                rec = sb("n_rec", [64, 512], F32); b_rec = Buf()
                ys = [sb("n_ys%d" % i, [64, 512], BF16) for i in range(2)]; b_ys = [Buf(), Buf()]
                Sw = [st.enter_context(nc.psum_tensor(K.un("n_Sw%d" % i), [128, 8, 128], F32)) for i in range(2)]; b_Sw = [Buf(), Buf()]
                Sc = [st.enter_context(nc.psum_tensor(K.un("n_Sc%d" % i), [128, 4, 128], F32)) for i in range(2)]; b_Sc = [Buf(), Buf()]
                O = st.enter_context(nc.psum_tensor(K.un("n_O"), [128, 512], F32)); b_O = Buf()
                bcp = st.enter_context(nc.psum_tensor(K.un("n_bcp"), [128, 512], F32)); b_bcp = Buf()
                for c0 in range(0, T, 2112):
                    fw.dma("sp", QT[:, c0:c0 + 2112], K.featB[l][0 + cc, :, c0:c0 + 2112], reads=[K.b_feat], writes=[b_QT])
                    fw.dma("sp", KT[:, c0:c0 + 2112], K.featB[l][2 + cc, :, c0:c0 + 2112], reads=[K.b_feat], writes=[b_KT])
                si = 0
                yi = 0
                for hh in range(2):
                    h = 2 * cc + hh
                    rows = slice(hh * 64, (hh + 1) * 64)
                    for t0 in range(0, NT, 7):
                        t1 = min(NT, t0 + 7)
                        fw.dma("sp", E[:, t0:t1, :], K.nabias[l, h, t0:t1].rearrange("t k q -> k t q"), writes=[b_E])
                    fw.op("act", lambda: nc.scalar.activation(out=E[:], in_=E[:], func=AF.Exp), reads=[b_E], writes=[b_E])
                    qlist = [(i, sched[i][0], sched[i][1]) for i in range(64)]
                    if not last:
                        qlist += [(64, [], 0), (65, [], 0)]
                    qlist = qlist[:K.p2n] if K.p2n >= 0 else qlist[K.p2n:]
                    segs = [[q for q in qlist if q[0] < 64], [q for q in qlist if q[0] >= 64]]
                    for seg in segs:
                        if not seg:
                            continue
                        if seg[0][0] >= 64:
                            fw.barrier()

                        def emit_S(qi, p):
                            i, js, tid0 = seg[qi]
                            nb = len(js)
                            qs = slice(i * 128, (i + 1) * 128)
                            for bi, j in enumerate(js):
                                fw.op("pe", lambda: nc.tensor.matmul(Sw[p][:, bi, :], lhsT=KT[rows, j * 128:(j + 1) * 128], rhs=QT[rows, qs],
                                                                     start=True, stop=True),
                                      reads=[b_KT, b_QT], writes=[b_Sw[p]], inc=(bi == nb - 1))
                            for t in range(2):
                                fw.op("pe", lambda: nc.tensor.matmul(Sc[p][:, t, :], lhsT=KT[rows, (64 + t) * 128:(65 + t) * 128], rhs=QT[rows, qs],
                                                                     start=True, stop=True),
                                      reads=[b_KT, b_QT], writes=[b_Sc[p]], inc=(t == 1))

                        emit_S(0, si % 2)
                        for qi, (i, js, tid0) in enumerate(seg):
                            nb = len(js)
                            p = si % 2
                            si += 1
                            if qi + 1 < len(seg):
                                emit_S(qi + 1, si % 2)
                            gcol = (i % 4) * 128
                            if nb:
                                fw.op("act", lambda: nc.scalar.activation(out=Pf[p][:, 0:nb, :], in_=Sw[p][:, 0:nb, :], func=AF.Exp, scale=SC),
                                      reads=[b_Sw[p]], writes=[b_Pf[p]])
                                fw.op("dve", lambda: nc.vector.tensor_tensor(out=Pw[p][:, 0:nb, :], in0=Pf[p][:, 0:nb, :],
                                                                             in1=E[:, tid0:tid0 + nb, :], op=ALU.mult),
                                      reads=[b_Pf[p], b_E], writes=[b_Pw[p]])
                            fw.op("act", lambda: nc.scalar.activation(out=Pc[p][:], in_=Sc[p][:, 0:2, :], func=AF.Exp, scale=SC),
                                  reads=[b_Sc[p]], writes=[b_Pc[p]])
                            for bi, j in enumerate(js):
                                fw.op("pe", lambda: nc.tensor.matmul(O[0:65, gcol:gcol + 128], lhsT=Va[:, j, h, :], rhs=Pw[p][:, bi, :],
                                                                     start=(bi == 0), stop=False),
                                      reads=[b_Va, b_Pw[p]], writes=[b_O], inc=False)
                            for t in range(2):
                                fw.op("pe", lambda: nc.tensor.matmul(O[0:65, gcol:gcol + 128], lhsT=Va[:, 64 + t, h, :], rhs=Pc[p][:, t, :],
                                                                     start=(nb == 0 and t == 0), stop=(t == 1)),
                                      reads=[b_Va, b_Pc[p]], writes=[b_O], inc=(t == 1))
                            endgrp = (i % 4 == 3) or (i == 65) or (qi == len(seg) - 1)
                            if endgrp:
                                g0 = (i // 4) * 4 if i < 64 else 64
                                nq = (i - g0 + 1) * 128
                                fw.op("act", lambda: nc.scalar.copy(out=Osb[:, 0:nq], in_=O[0:65, 0:nq]), reads=[b_O], writes=[b_Osb])
                                fw.op("pe", lambda: nc.tensor.matmul(bcp[0:64, 0:nq], lhsT=sel[:], rhs=Osb[:, 0:nq], start=True, stop=True),
                                      reads=[b_sel, b_Osb], writes=[b_bcp])
                                fw.op("dve", lambda: nc.vector.reciprocal(out=rec[:, 0:nq], in_=bcp[0:64, 0:nq]), reads=[b_bcp], writes=[b_rec])
                                y_, by_ = ys[yi % 2], b_ys[yi % 2]
                                yi += 1
                                fw.op("dve", lambda: nc.vector.tensor_tensor(out=y_[:, 0:nq], in0=Osb[0:64, 0:nq], in1=rec[:, 0:nq], op=ALU.mult),
                                      reads=[b_Osb, b_rec], writes=[by_])
                                fw.dma("pool", K.yT[l][0 + cc, rows, g0 * 128:g0 * 128 + nq], y_[:, 0:nq], reads=[by_], writes=[K.b_yT])
                fw.barrier()


def _load_cast(K, st, name, dst_tile, b_dst, src_ap, nk, ncols, piece=2048):
    nc, fw = K.nc, K.fw
    with contextlib.ExitStack() as st2:
        wst = [st2.enter_context(nc.sbuf_tensor(K.un("%s_wst%d" % (name, i)), [128, piece], F32)) for i in range(2)]
        b_wst = [Buf(), Buf()]
        it = 0
        for k in range(nk):
            for c0 in range(0, ncols, piece):
                c1 = min(ncols, c0 + piece)
                s_, bs_ = wst[it % 2], b_wst[it % 2]
                fw.dma("sp", s_[:, 0:c1 - c0], src_ap[k * 128:(k + 1) * 128, c0:c1], writes=[bs_])
                e = ("act", "dve", "pool")[it % 3]
                dst = dst_tile[:, k, c0:c1]
                src = s_[:, 0:c1 - c0]
                if e == "act":
                    fw.op(e, lambda: nc.scalar.copy(out=dst, in_=src), reads=[bs_], writes=[b_dst])
                elif e == "dve":
                    fw.op(e, lambda: nc.vector.tensor_copy(out=dst, in_=src), reads=[bs_], writes=[b_dst])
                else:
                    fw.op(e, lambda: nc.gpsimd.tensor_copy(out=dst, in_=src), reads=[bs_], writes=[b_dst])
                it += 1
        fw.barrier()


def _rstd(K, src_ap, reads, junk, b_junk, ss, b_ss, sd, b_sd, rs, b_rs, eps_t, b_eps, n):
    nc, fw = K.nc, K.fw
    fw.op("act", lambda: nc.scalar.activation(out=junk, in_=src_ap, func=AF.Square, accum_out=ss[:]),
          reads=reads, writes=[b_junk, b_ss])
    fw.op("act", lambda: nc.scalar.activation(out=sd[:], in_=ss[:], func=AF.Sqrt, scale=1.0 / n, bias=eps_t[:]),
          reads=[b_ss, b_eps], writes=[b_sd])
    fw.op("dve", lambda: nc.vector.reciprocal(out=rs[:], in_=sd[:]), reads=[b_sd], writes=[b_rs])


def _tok_tiles(l, last):
    tiles = [(0, t * 128) for t in range(NLAT // 128)]
    if not last:
        tiles += [(1, NLAT + t * 128) for t in range(NCTX // 128)]
    return tiles


def _phase_c1(K, l):
    nc, fw = K.nc, K.fw
    last = l == DEPTH - 1
    with contextlib.ExitStack() as st:
        sb = lambda n, s, d: st.enter_context(nc.sbuf_tensor(K.un(n), s, d))
        Wo = sb("c1_W", [128, 8, 1024], BF16); b_W = Buf()
        ident = sb("c1_id", [128, 128], BF16); b_id = Buf()
        idf = sb("c1_idf", [128, 128], F32); b_idf = Buf()
        fw.dma("sp", idf[:], K.ident, writes=[b_idf])
        fw.op("dve", lambda: nc.vector.tensor_copy(out=ident[:], in_=idf[:]), reads=[b_idf], writes=[b_id])
        _load_cast(K, st, "c1", Wo, b_W, K.w_out[l], 8, 1024, piece=1024)
        mt = [sb("c1_mt%d" % i, [128, 1024], F32) for i in range(3)]; b_mt = [Buf() for _ in range(3)]
        yT = [sb("c1_yT%d" % i, [128, 8, 128], BF16) for i in range(3)]; b_yT = [Buf() for _ in range(3)]
        xt = [sb("c1_xt%d" % i, [128, 1024], F32) for i in range(3)]; b_xt = [Buf() for _ in range(3)]
        tmps = [sb("c1_tmp%d" % i, [128, 1024], F32) for i in range(2)]; b_tmps = [Buf(), Buf()]
        xm = [sb("c1_xm%d" % i, [128, 1024], F32) for i in range(3)]; b_xm = [Buf() for _ in range(3)]
        h1s = [sb("c1_h1%d" % i, [128, 1024], F32) for i in range(2)]; b_h1s = [Buf(), Buf()]
        hb = [sb("c1_hb%d" % i, [128, 1024], BF16) for i in range(3)]; b_hb = [Buf() for _ in range(3)]
        hTs = [sb("c1_hT%d" % i, [128, 8, 128], BF16) for i in range(3)]; b_hTs = [Buf() for _ in range(3)]
        junk = sb("c1_junk", [128, 1024], BF16); b_junk = Buf()
        sm = [[sb("c1_s%d_%d" % (a, i), [128, 1], F32) for i in range(3)] for a in range(6)]
        b_sm = [[Buf() for _ in range(3)] for a in range(6)]
        eps_t = sb("c1_eps", [128, 1], F32); b_eps = Buf()
        fw.op("pool", lambda: nc.gpsimd.memset(eps_t[:], EPS), writes=[b_eps])
        yl = [st.enter_context(nc.psum_tensor(K.un("c1_yl%d" % i), [128, 1024], F32)) for i in range(3)]
        b_yl = [Buf() for _ in range(3)]
        tp = [st.enter_context(nc.psum_tensor(K.un("c1_tp%d" % i), [128, 8, 128], BF16)) for i in range(2)]
        b_tp = [Buf(), Buf()]
        cur_j = None
        for ti, (j, tok0) in enumerate(_tok_tiles(l, last)):
            if j != cur_j:
                for i, vi in enumerate((2, 3, 4)):
                    _load_bcast(K, "sp", mt[i], l, vi, j, b_mt[i])
                cur_j = j
            p = ti % 3
            tq_ = ti % 2
            tmp, b_tmp = tmps[tq_], b_tmps[tq_]
            h1, b_h1 = h1s[tq_], b_h1s[tq_]
            lo = tok0 - (NLAT if j else 0)
            fw.dma("sp", yT[p][:], K.yT[l][:, :, tok0:tok0 + 128].rearrange("c p t -> p c t"),
                   reads=[K.b_yT], writes=[b_yT[p]])
            fw.dma("sp", xt[p][:], K.src[l][j][lo:lo + 128, :], reads=[K.b_src[l]], writes=[b_xt[p]])
            for hf in range(2):
                for k in range(8):
                    fw.op("pe", lambda: nc.tensor.matmul(yl[p][:, hf * 512:(hf + 1) * 512], lhsT=yT[p][:, k, :],
                                                         rhs=Wo[:, k, hf * 512:(hf + 1) * 512],
                                                         start=(k == 0), stop=(k == 7)),
                          reads=[b_yT[p], b_W], writes=[b_yl[p]], inc=(hf == 1 and k == 7))
            _rstd(K, yl[p][:], [b_yl[p]], junk[:], b_junk, sm[0][p], b_sm[0][p], sm[1][p], b_sm[1][p],
                  sm[2][p], b_sm[2][p], eps_t, b_eps, D)
            fw.op("dve", lambda: nc.vector.scalar_tensor_tensor(out=tmp[:], in0=yl[p][:], scalar=sm[2][p][:], in1=mt[0][:],
                                                                op0=ALU.mult, op1=ALU.mult),
                  reads=[b_yl[p], b_sm[2][p], b_mt[0]], writes=[b_tmp])
            fw.op("pool", lambda: nc.gpsimd.tensor_tensor(out=xm[p][:], in0=tmp[:], in1=xt[p][:], op=ALU.add),
                  reads=[b_tmp, b_xt[p]], writes=[b_xm[p]])
            fw.dma("pool", K.xmid[tok0:tok0 + 128, :], xm[p][:], reads=[b_xm[p]], writes=[K.b_xmid])
            _rstd(K, xm[p][:], [b_xm[p]], junk[:], b_junk, sm[3][p], b_sm[3][p], sm[4][p], b_sm[4][p],
                  sm[5][p], b_sm[5][p], eps_t, b_eps, D)
            fw.op("dve", lambda: nc.vector.scalar_tensor_tensor(out=h1[:], in0=xm[p][:], scalar=sm[5][p][:], in1=mt[1][:],
                                                                op0=ALU.mult, op1=ALU.mult),
                  reads=[b_xm[p], b_sm[5][p], b_mt[1]], writes=[b_h1])
            fw.op("pool", lambda: nc.gpsimd.tensor_tensor(out=hb[p][:], in0=h1[:], in1=mt[2][:], op=ALU.add),
                  reads=[b_h1, b_mt[2]], writes=[b_hb[p]])
            for k in range(8):
                fw.op("pe", lambda: nc.tensor.transpose(tp[tq_][:, k, :], hb[p][:, k * 128:(k + 1) * 128], ident[:]),
                      reads=[b_hb[p], b_id], writes=[b_tp[tq_]], inc=(k == 7))
            fw.op("act", lambda: nc.scalar.copy(out=hTs[p][:], in_=tp[tq_][:]), reads=[b_tp[tq_]], writes=[b_hTs[p]])
            fw.dma("pool", K.h2T[:, :, tok0:tok0 + 128].rearrange("c p t -> p c t"), hTs[p][:],
                   reads=[b_hTs[p]], writes=[K.b_h2T])
        fw.barrier()


def _phase_c2(K, l):
    nc, fw = K.nc, K.fw
    last = l == DEPTH - 1
    NF = DFF // 128
    with contextlib.ExitStack() as st:
        sb = lambda n, s, d: st.enter_context(nc.sbuf_tensor(K.un(n), s, d))
        W1 = sb("c2_W1", [128, 8, 2 * DFF], BF16); b_W1 = Buf()
        W2 = sb("c2_W2", [128, NF, 1024], BF16); b_W2 = Buf()
        _load_cast(K, st, "c2a", W1, b_W1, K.w_ffn_in[l], 8, 2 * DFF, piece=2816)
        _load_cast(K, st, "c2b", W2, b_W2, K.w_ffn_out[l], NF, 1024, piece=1024)
        gg = sb("c2_gg", [128, 1024], F32); b_gg = Buf()
        hT = [sb("c2_hT%d" % i, [128, 8, 512], BF16) for i in range(2)]; b_hT = [Buf(), Buf()]
        sg = [sb("c2_sg%d" % i, [128, 512], F32) for i in range(2)]; b_sg = [Buf(), Buf()]
        aT = sb("c2_aT", [128, NF, 512], BF16); b_aT = [Buf() for _ in range(NF)]
        xm = [sb("c2_xm%d" % i, [128, 1024], F32) for i in range(2)]; b_xm = [Buf(), Buf()]
        tmp = sb("c2_tmp", [128, 1024], F32); b_tmp = Buf()
        ot = [sb("c2_ot%d" % i, [128, 1024], F32) for i in range(2)]; b_ot = [Buf(), Buf()]
        junk = sb("c2_junk", [128, 1024], BF16); b_junk = Buf()
        sm = [[sb("c2_s%d_%d" % (a, i), [128, 1], F32) for i in range(2)] for a in range(3)]
        b_sm = [[Buf(), Buf()] for a in range(3)]
        eps_t = sb("c2_eps", [128, 1], F32); b_eps = Buf()
        fw.op("pool", lambda: nc.gpsimd.memset(eps_t[:], EPS), writes=[b_eps])
        gu = [st.enter_context(nc.psum_tensor(K.un("c2_gu%d" % i), [128, 2, 512], F32)) for i in range(2)]
        b_gu = [Buf(), Buf()]
        fo = [st.enter_context(nc.psum_tensor(K.un("c2_fo%d" % i), [128, 1024], F32)) for i in range(2)]
        b_fo = [Buf(), Buf()]
        tiles = _tok_tiles(l, last)
        groups = [tiles[i:i + 4] for i in range(0, NLAT // 128, 4)]
        if not last:
            groups.append(tiles[NLAT // 128:])
        cur_j = None
        gi = 0
        si = 0
        for bi, grp in enumerate(groups):
            j, tok0 = grp[0]
            nsub = len(grp)
            ntok = nsub * 128
            if j != cur_j:
                _load_bcast(K, "sp", gg, l, 5, j, b_gg)
                cur_j = j
            hT_, bhT_ = hT[bi % 2], b_hT[bi % 2]
            fw.dma("sp", hT_[:, :, 0:ntok], K.h2T[:, :, tok0:tok0 + ntok].rearrange("c p t -> p c t"),
                   reads=[K.b_h2T], writes=[bhT_])
            for c in range(NF):
                g_, bg_ = gu[gi % 2], b_gu[gi % 2]
                s_, bs_ = sg[gi % 2], b_sg[gi % 2]
                gi += 1
                for half in range(2):
                    col0 = half * DFF + c * 128
                    for k in range(8):
                        fw.op("pe", lambda: nc.tensor.matmul(g_[:, half, 0:ntok], lhsT=W1[:, k, col0:col0 + 128],
                                                             rhs=hT_[:, k, 0:ntok], start=(k == 0), stop=(k == 7)),
                              reads=[b_W1, bhT_], writes=[bg_], inc=(half == 1 and k == 7))
                fw.op("act", lambda: nc.scalar.activation(out=s_[:, 0:ntok], in_=g_[:, 0, 0:ntok], func=AF.Silu),
                      reads=[bg_], writes=[bs_])
                fw.op("dve", lambda: nc.vector.tensor_tensor(out=aT[:, c, 0:ntok], in0=g_[:, 1, 0:ntok], in1=s_[:, 0:ntok], op=ALU.mult),
                      reads=[bg_, bs_], writes=[b_aT[c]])
            for sub in range(nsub):
                p = si % 2
                si += 1
                t0 = tok0 + sub * 128
                fw.dma("sp", xm[p][:], K.xmid[t0:t0 + 128, :], reads=[K.b_xmid], writes=[b_xm[p]])
                for hf in range(2):
                    for c in range(NF):
                        fw.op("pe", lambda: nc.tensor.matmul(fo[p][:, hf * 512:(hf + 1) * 512],
                                                             lhsT=aT[:, c, sub * 128:(sub + 1) * 128],
                                                             rhs=W2[:, c, hf * 512:(hf + 1) * 512],
                                                             start=(c == 0), stop=(c == NF - 1)),
                              reads=[b_aT[c], b_W2], writes=[b_fo[p]], inc=(hf == 1 and c == NF - 1))
                _rstd(K, fo[p][:], [b_fo[p]], junk[:], b_junk, sm[0][p], b_sm[0][p], sm[1][p], b_sm[1][p],
                      sm[2][p], b_sm[2][p], eps_t, b_eps, D)
                fw.op("dve", lambda: nc.vector.scalar_tensor_tensor(out=tmp[:], in0=fo[p][:], scalar=sm[2][p][:], in1=gg[:],
                                                                    op0=ALU.mult, op1=ALU.mult),
                      reads=[b_fo[p], b_sm[2][p], b_gg], writes=[b_tmp])
                fw.op("pool", lambda: nc.gpsimd.tensor_tensor(out=ot[p][:], in0=tmp[:], in1=xm[p][:], op=ALU.add),
                      reads=[b_tmp, b_xm[p]], writes=[b_ot[p]])
                if last:
                    dst, bd = K.out[t0:t0 + 128, :], K.b_out
                else:
                    dst, bd = K.xl1[t0:t0 + 128, :], K.b_src[l + 1]
                fw.dma("pool", dst, ot[p][:], reads=[b_ot[p]], writes=[bd])
        fw.barrier()


def build(dbg=None):
    nc = bass.Bass("TRN2", target_bir_lowering=False)
    K = Ctx()
    K.nc = nc
    dt_in = lambda n, s, d=F32: nc.dram_tensor(n, s, d, kind="ExternalInput").ap()
    K.x = dt_in("x", [NLAT, D])
    K.ctx = dt_in("ctx", [NCTX, D])
    K.cvec = dt_in("cvec", [2, D])
    K.w_mod = dt_in("w_mod", [DEPTH, D, 6 * D])
    K.b_mod = dt_in("b_mod", [DEPTH, 6 * D])
    K.gains = dt_in("gains", [4, DEPTH, D])
    K.w_in = dt_in("w_in", [DEPTH, D, NWCOL])
    K.rope = dt_in("rope", [6, 128, 192])
    K.ident = dt_in("ident", [128, 128])
    K.w_out = dt_in("w_out", [DEPTH, D, D])
    K.w_ffn_in = dt_in("w_ffn_in", [DEPTH, D, 2 * DFF])
    K.w_ffn_out = dt_in("w_ffn_out", [DEPTH, DFF, D])
    K.lru_conv_w = dt_in("lru_conv_w", [DEPTH, 4, 256])
    K.lru_conv_b = dt_in("lru_conv_b", [DEPTH, 256])
    K.lru_gate_w = dt_in("lru_gate_w", [DEPTH, 2, 2, 4, 64, 64])
    K.lru_gate_b = dt_in("lru_gate_b", [DEPTH, 2, 2, 4, 64])
    K.lru_lambda = dt_in("lru_lambda", [DEPTH, 2, 256])
    K.ret_decay = dt_in("ret_decay", [DEPTH, 2, 4])
    K.retc = dt_in("retc", [128, 6, 128])
    K.retcv = dt_in("retcv", [128, 2])
    K.blk64 = dt_in("blk64", [128, 128])
    K.diff_lambda = dt_in("diff_lambda", [DEPTH, 4, 32])
    K.diff_subln = dt_in("diff_subln", [DEPTH, 64])
    K.sel65 = dt_in("sel65", [65, 64])
    K.ones64 = dt_in("ones64", [64, 64])
    K.mcol = dt_in("mcol", [128, 4])
    K.na_nt = _na_schedule()[1].shape[0]
    K.nabias = dt_in("nabias", [DEPTH, 4, K.na_nt, 128, 128])
    phases = dbg["phases"] if dbg else None
    K.stop = dbg.get("stop", 99) if dbg else 99
    K.p1n = dbg.get("p1n", 999) if dbg else 999
    K.p2n = dbg.get("p2n", 999) if dbg else 999
    ext_in = dbg.get("ext_in", set()) if dbg else set()
    ext_out = dbg.get("ext_out", set()) if dbg else set()

    def scr(n, s, d):
        kind = "Internal"
        if n in ext_in:
            kind = "ExternalInput"
        elif n in ext_out:
            kind = "ExternalOutput"
        return nc.dram_tensor(n, s, d, kind=kind).ap()

    K.modvec = scr("modvec", [DEPTH, 6, 2, D], F32); K.b_modvec = Buf()
    K.featB = [scr("featB%d" % l, [12, 128, T], BF16) for l in range(DEPTH)]
    K.featF = [scr("featF%d" % l, [6, 128, T], F32) for l in range(DEPTH)]
    K.vAll = [scr("vAll%d" % l, [T, 768], BF16) for l in range(DEPTH)]
    K.b_feat = Buf()
    K.yT = [scr("yT%d" % l, [8, 128, T], BF16) for l in range(DEPTH)]; K.b_yT = Buf()
    K.xmid = scr("xmid", [T, D], F32); K.b_xmid = Buf()
    K.h2T = scr("h2T", [8, 128, T], BF16); K.b_h2T = Buf()
    K.xl1 = scr("xl1", [T, D], F32)
    K.out = nc.dram_tensor("out", [NLAT, D], F32, kind="ExternalOutput").ap(); K.b_out = Buf()
    K.src = [(K.x, K.ctx), (K.xl1[0:NLAT, :], K.xl1[NLAT:T, :])]
    K.b_src = [Buf(), Buf()]
    on = lambda name: (phases is None) or (name in phases)
    with contextlib.ExitStack() as st:
        K.fw = FW(nc, st)
        def ph(name, fn, *a):
            if on(name):
                with nc.named_scope(name):
                    fn(K, *a)

        ph("P0", _p0_modvec)
        for l in range(DEPTH):
            ph("A%d" % l, _phase_a, l)
            ph("NA%d" % l, _phase_na, l)
            ph("LRU%d" % l, _phase_lru, l)
            ph("RET%d" % l, _phase_ret, l)
            ph("DIFF%d" % l, _phase_diff, l)
            if on("C%d" % l):
                with nc.named_scope("C1_%d" % l):
                    _phase_c1(K, l)
                with nc.named_scope("C2_%d" % l):
                    _phase_c2(K, l)
        K.fw.finish("sp")
        K.n_inst, K.n_wait = K.fw.n_inst, K.fw.n_wait
    return nc, K


def host_inputs(inputs, b):
    f = lambda a: np.ascontiguousarray(np.asarray(a, dtype=np.float32))
    perm = _w_in_perm()
    m = {
        "x": f(inputs["x"][b]),
        "ctx": f(inputs["ctx"][b]),
        "cvec": f(np.stack([np.asarray(inputs["c"])[b], np.asarray(inputs["c_ctx"])], 0)),
        "w_mod": f(inputs["w_mod"]),
        "b_mod": f(inputs["b_mod"]),
        "gains": f(np.stack([inputs["g_pre_mix"], inputs["g_post_mix"], inputs["g_pre_ffn"], inputs["g_post_ffn"]], 0)),
        "w_in": f(np.asarray(inputs["w_in"])[:, :, perm]),
        "rope": _rope_tables(),
        "ident": np.eye(128, dtype=np.float32),
        "w_out": f(inputs["w_out"]),
        "w_ffn_in": f(inputs["w_ffn_in"]),
        "w_ffn_out": f(inputs["w_ffn_out"]),
        "lru_conv_w": f(inputs["lru_conv_w"]),
        "lru_conv_b": f(inputs["lru_conv_b"]),
        "lru_gate_w": f(inputs["lru_gate_w"]),
        "lru_gate_b": f(inputs["lru_gate_b"]),
        "lru_lambda": f(inputs["lru_lambda"]),
        "ret_decay": f(inputs["ret_decay"]),
        "retc": _ret_consts()[0],
        "retcv": _ret_consts()[1],
        "blk64": _blk64(),
        "diff_lambda": f(inputs["diff_lambda"]),
        "diff_subln": f(inputs["diff_subln"]),
        "sel65": _diff_consts()[0],
        "ones64": _diff_consts()[1],
        "mcol": _diff_consts()[2],
        "nabias": _na_bias(inputs["na_rpb"]),
    }
    return m


N_CORES = 4


def kernel(**inputs):
    nc, K = build()
    in_maps = [host_inputs(inputs, c) for c in range(N_CORES)]
    res = run_bass_kernel_spmd(nc, in_maps, core_ids=list(range(N_CORES)))
    out = np.stack([np.asarray(res.results[b]["out"], dtype=np.float32) for b in range(4)], 0)
    return out
```

```python
import contextlib
import math
import numpy as np
import concourse.bass as bass
import concourse.mybir as mybir
from concourse.bass_utils import run_bass_kernel_spmd

F32 = mybir.dt.float32
BF16 = mybir.dt.bfloat16
AF = mybir.ActivationFunctionType
ALU = mybir.AluOpType
AX = mybir.AxisListType

D = 1024
NLAT = 8192
NCTX = 256
T = NLAT + NCTX
DEPTH = 2
DFF = 2816
EPS = 1e-6
NWCOL = 4096
GRID_W = 64


class Buf:
    __slots__ = ("w", "r", "name")

    def __init__(self, name=""):
        self.w = None
        self.r = []
        self.name = name


class FW:
    N_DMA_SEMS = 24

    def __init__(self, nc, stack):
        self.nc = nc
        self.eng = {"pe": nc.tensor, "act": nc.scalar, "dve": nc.vector, "pool": nc.gpsimd, "sp": nc.sync}
        self.sems = {}
        self.count = {}
        for e in self.eng:
            self.sems[e] = stack.enter_context(nc.semaphore("s_" + e))
            self.count[e] = 0
        self.dma_sems = {"hw": [], "sw": []}
        for kind, n in (("hw", 24), ("sw", 24)):
            for i in range(n):
                k = "d%s%d" % (kind, i)
                self.sems[k] = stack.enter_context(nc.semaphore("s_" + k))
                self.count[k] = 0
                self.dma_sems[kind].append(k)
        self.dma_rr = {"hw": 0, "sw": 0}
        self.waited = {e: {} for e in self.eng}
        self.pending_pe = []
        self.n_inst = 0
        self.n_wait = 0

    def _wait(self, e, ev):
        if ev is None:
            return
        k, v = ev
        if self.waited[e].get(k, 0) >= v:
            return
        self.eng[e].wait_ge(self.sems[k], v)
        self.waited[e][k] = v
        self.n_wait += 1

    def _deps(self, e, reads, writes):
        for b in reads:
            self._wait(e, b.w)
        for b in writes:
            self._wait(e, b.w)
            for ev in b.r:
                self._wait(e, ev)

    def _mark(self, ev, reads, writes):
        for b in reads:
            b.r.append(ev)
            if len(b.r) > 40:
                best = {}
                for k, v in b.r:
                    if best.get(k, 0) < v:
                        best[k] = v
                b.r = list(best.items())
        for b in writes:
            b.w = ev
            b.r = []

    def op(self, e, fn, reads=(), writes=(), inc=True):
        self._deps(e, reads, writes)
        inst = fn()
        self.n_inst += 1
        if e == "pe" and not inc:
            self.pending_pe.append((tuple(reads), tuple(writes)))
            return inst
        self.count[e] += 1
        ev = (e, self.count[e])
        inst.then_inc(self.sems[e], 1)
        if e == "pe" and self.pending_pe:
            for r, w in self.pending_pe:
                self._mark(ev, r, w)
            self.pending_pe = []
        self._mark(ev, reads, writes)
        return inst

    def dma(self, q, out, in_, reads=(), writes=(), **kw):
        kind = "sw" if q == "pool" else "hw"
        k = self.dma_sems[kind][self.dma_rr[kind]]
        self.dma_rr[kind] = (self.dma_rr[kind] + 1) % len(self.dma_sems[kind])
        self._wait(q, (k, self.count[k]))
        self._deps(q, reads, writes)
        inst = self.eng[q].dma_start(out=out, in_=in_, **kw)
        self.count[k] += 16
        inst.then_inc(self.sems[k], 16)
        ev = (k, self.count[k])
        self._mark(ev, reads, writes)
        self.n_inst += 1
        return ev

    def barrier(self):
        for e in self.eng:
            for k in self.sems:
                if k != e and self.count[k] > 0:
                    self._wait(e, (k, self.count[k]))

    def finish(self, e="sp"):
        for k in self.sems:
            if self.count[k] > 0:
                self._wait(e, (k, self.count[k]))


class Ctx:
    _n = 0

    def un(self, name):
        Ctx._n += 1
        return "%s_%d" % (name, Ctx._n)


def _w_in_perm():
    def split(i):
        return np.arange(i * 256, (i + 1) * 256)

    def swap_ret(cols):
        c = cols.reshape(4, 64)
        return np.concatenate([c[:, 32:], c[:, :32]], axis=1).reshape(-1)

    def swap_diff(cols):
        c = cols.reshape(4, 2, 32)
        return np.concatenate([c[:, :, 16:], c[:, :, :16]], axis=2).reshape(-1)

    order = [split(0), split(1), split(3), split(4), split(8),
             split(5), swap_ret(split(5)), split(6), swap_ret(split(6)),
             split(9), swap_diff(split(9)), split(10), swap_diff(split(10)),
             split(2), split(7), split(11)]
    return np.concatenate(order)


def _rope_tables():
    tabs = np.zeros((6, 128, 192), np.float64)
    rows = np.arange(128, dtype=np.float64)
    cols = np.arange(64, dtype=np.float64)
    for p in range(128):
        d = p % 64
        j = d % 32
        sign = -1.0 if d < 32 else 1.0
        inv = 10000.0 ** (-(j % 16) / 16.0)
        if j < 16:
            ang, sl = rows * np.float32(inv), slice(0, 128)
        else:
            ang, sl = cols * np.float32(inv), slice(128, 192)
        tabs[0, p, sl] = np.cos(ang)
        tabs[1, p, sl] = sign * np.sin(ang)
        tabs[2, p, sl] = np.cos(ang) * 0.125
        tabs[3, p, sl] = sign * np.sin(ang) * 0.125
        d = p % 32
        j = d % 16
        sign = -1.0 if d < 16 else 1.0
        inv = 10000.0 ** (-(j % 8) / 8.0)
        if j < 8:
            ang, sl = rows * np.float32(inv), slice(0, 128)
        else:
            ang, sl = cols * np.float32(inv), slice(128, 192)
        tabs[4, p, sl] = np.cos(ang)
        tabs[5, p, sl] = sign * np.sin(ang)
    return tabs.astype(np.float32)


def _p0_modvec(K):
    nc, fw = K.nc, K.fw
    with contextlib.ExitStack() as st:
        sb = lambda n, s, d: st.enter_context(nc.sbuf_tensor(K.un(n), s, d))
        cT = sb("p0_cT", [128, 2, 8], F32); b_cT = Buf()
        sT = sb("p0_sT", [128, 2, 8], F32); b_sT = Buf()
        wm = [sb("p0_wm%d" % i, [128, 8, 512], F32) for i in range(2)]; b_wm = [Buf(), Buf()]
        bm = sb("p0_bm", [2, 6144], F32); b_bm = Buf()
        gn = sb("p0_gn", [2, 4, 1024], F32); b_gn = Buf()
        mv = sb("p0_mv", [2, 6144], F32); b_mv = Buf()
        cv = sb("p0_cv", [2, 6, 1024], F32); b_cv = Buf()
        ps = [st.enter_context(nc.psum_tensor(K.un("p0_ps%d" % i), [2, 512], F32)) for i in range(2)]
        b_ps = [Buf(), Buf()]
        for j in range(2):
            fw.dma("sp", cT[:, j, :], K.cvec[j, :].rearrange("(k p) -> p k", p=128), writes=[b_cT],
                   allow_slow_non_contiguous=True)
        fw.op("act", lambda: nc.scalar.activation(out=sT[:], in_=cT[:], func=AF.Silu), reads=[b_cT], writes=[b_sT])
        it = 0
        for l in range(DEPTH):
            fw.dma("sp", bm[:], K.b_mod[l:l + 1, :].broadcast_to([2, 6144]), writes=[b_bm])
            fw.dma("sp", gn[:], K.gains[:, l, :].unsqueeze(0).broadcast_to([2, 4, 1024]), writes=[b_gn])
            for n in range(12):
                w_, bw_ = wm[it % 2], b_wm[it % 2]
                p_, bp_ = ps[it % 2], b_ps[it % 2]
                it += 1
                src = K.w_mod[l, :, n * 512:(n + 1) * 512].rearrange("(k p) c -> p k c", p=128)
                fw.dma("sp", w_[:, 0:4, :], src[:, 0:4, :], writes=[bw_])
                fw.dma("sp", w_[:, 4:8, :], src[:, 4:8, :], writes=[bw_])
                for k in range(8):
                    fw.op("pe", lambda: nc.tensor.matmul(p_[:], lhsT=sT[:, :, k], rhs=w_[:, k, :],
                                                         start=(k == 0), stop=(k == 7)),
                          reads=[b_sT, bw_], writes=[bp_], inc=(k == 7))
                fw.op("dve", lambda: nc.vector.tensor_tensor(out=mv[:, n * 512:(n + 1) * 512], in0=p_[:],
                                                             in1=bm[:, n * 512:(n + 1) * 512], op=ALU.add),
                      reads=[bp_, b_bm], writes=[b_mv])
            m = lambda i: mv[:, i * 1024:(i + 1) * 1024]
            R, W_ = [b_mv, b_gn], [b_cv]
            fw.op("dve", lambda: nc.vector.scalar_tensor_tensor(out=cv[:, 0, :], in0=m(1), scalar=1.0, in1=gn[:, 0, :],
                                                                op0=ALU.add, op1=ALU.mult), reads=R, writes=W_)
            fw.op("dve", lambda: nc.vector.tensor_copy(out=cv[:, 1, :], in_=m(0)), reads=R, writes=W_)
            fw.op("dve", lambda: nc.vector.tensor_tensor(out=cv[:, 2, :], in0=m(2), in1=gn[:, 1, :], op=ALU.mult),
                  reads=R, writes=W_)
            fw.op("dve", lambda: nc.vector.scalar_tensor_tensor(out=cv[:, 3, :], in0=m(4), scalar=1.0, in1=gn[:, 2, :],
                                                                op0=ALU.add, op1=ALU.mult), reads=R, writes=W_)
            fw.op("dve", lambda: nc.vector.tensor_copy(out=cv[:, 4, :], in_=m(3)), reads=R, writes=W_)
            fw.op("dve", lambda: nc.vector.tensor_tensor(out=cv[:, 5, :], in0=m(5), in1=gn[:, 3, :], op=ALU.mult),
                  reads=R, writes=W_)
            fw.dma("pool", K.modvec[l].rearrange("i j f -> j i f"), cv[:], reads=[b_cv], writes=[K.b_modvec])
        fw.barrier()


def _load_bcast(K, q, tile, l, i, j, buf):
    K.fw.dma(q, tile[:], K.modvec[l, i, j:j + 1, :].broadcast_to([128, 1024]), reads=[K.b_modvec], writes=[buf])


def _phase_a(K, l):
    nc, fw = K.nc, K.fw
    with contextlib.ExitStack() as st:
        sb = lambda n, s, d: st.enter_context(nc.sbuf_tensor(K.un(n), s, d))
        W = sb("a_W", [128, 8, NWCOL], BF16); b_W = Buf()
        ident = sb("a_id", [128, 128], BF16); b_id = Buf()
        with contextlib.ExitStack() as st2:
            wst = [st2.enter_context(nc.sbuf_tensor(K.un("a_wst%d" % i), [128, 2048], F32)) for i in range(2)]
            b_wst = [Buf(), Buf()]
            idf = st2.enter_context(nc.sbuf_tensor(K.un("a_idf"), [128, 128], F32)); b_idf = Buf()
            fw.dma("sp", idf[:], K.ident, writes=[b_idf])
            fw.op("dve", lambda: nc.vector.tensor_copy(out=ident[:], in_=idf[:]), reads=[b_idf], writes=[b_id])
            it = 0
            for k in range(8):
                for hf in range(2):
                    s_, bs_ = wst[it % 2], b_wst[it % 2]
                    fw.dma("sp", s_[:], K.w_in[l, k * 128:(k + 1) * 128, hf * 2048:(hf + 1) * 2048], writes=[bs_])
                    e = ("act", "dve", "pool")[it % 3]
                    dst = W[:, k, hf * 2048:(hf + 1) * 2048]
                    if e == "act":
                        fw.op(e, lambda: nc.scalar.copy(out=dst, in_=s_[:]), reads=[bs_], writes=[b_W])
                    elif e == "dve":
                        fw.op(e, lambda: nc.vector.tensor_copy(out=dst, in_=s_[:]), reads=[bs_], writes=[b_W])
                    else:
                        fw.op(e, lambda: nc.gpsimd.tensor_copy(out=dst, in_=s_[:]), reads=[bs_], writes=[b_W])
                    it += 1
            fw.barrier()
        gm = [sb("a_gm%d" % j, [128, 1024], F32) for j in range(2)]; b_gm = [Buf(), Buf()]
        sh = [sb("a_sh%d" % j, [128, 1024], F32) for j in range(2)]; b_sh = [Buf(), Buf()]
        for j in range(2):
            _load_bcast(K, "sp", gm[j], l, 0, j, b_gm[j])
            _load_bcast(K, "sp", sh[j], l, 1, j, b_sh[j])
        rtab = sb("a_rtab", [128, 6, 192], F32); b_rtab = Buf()
        fw.dma("sp", rtab[:], K.rope.rearrange("s p c -> p s c"), writes=[b_rtab])
        rope = sb("a_rope", [128, 6, 512], F32); b_rope = Buf()
        xt = [sb("a_xt%d" % i, [128, 1024], F32) for i in range(2)]; b_xt = [Buf(), Buf()]
        junk = sb("a_junk", [128, 1024], BF16); b_junk = Buf()
        ss = [sb("a_ss%d" % i, [128, 1], F32) for i in range(2)]; b_ss = [Buf(), Buf()]
        sd = [sb("a_sd%d" % i, [128, 1], F32) for i in range(2)]; b_sd = [Buf(), Buf()]
        rs = [sb("a_rs%d" % i, [128, 1], F32) for i in range(2)]; b_rs = [Buf(), Buf()]
        h1 = sb("a_h1", [128, 1024], F32); b_h1 = Buf()
        hb = [sb("a_hb%d" % i, [128, 1024], BF16) for i in range(2)]; b_hb = [Buf(), Buf()]
        hT = [sb("a_hT%d" % i, [128, 8, 512], BF16) for i in range(2)]; b_hT = [Buf(), Buf()]
        stB = sb("a_stB", [128, 12, 512], BF16); b_stB = [Buf() for _ in range(12)]
        stF = sb("a_stF", [128, 6, 512], F32); b_stF = [Buf() for _ in range(6)]
        stV = sb("a_stV", [128, 4, 768], BF16); b_stV = [Buf() for _ in range(4)]
        t1 = [sb("a_t1%d" % i, [128, 512], F32) for i in range(2)]; b_t1 = [Buf(), Buf()]
        t2 = [sb("a_t2%d" % i, [128, 512], F32) for i in range(2)]; b_t2 = [Buf(), Buf()]
        tp = [st.enter_context(nc.psum_tensor(K.un("a_tp%d" % i), [128, 8, 128], BF16)) for i in range(2)]
        b_tp = [Buf(), Buf()]
        mm = [st.enter_context(nc.psum_tensor(K.un("a_mm%d" % i), [128, 512], F32)) for i in range(6)]
        b_mm = [Buf() for _ in range(6)]
        mmi = [0]
        eps_t = sb("a_eps", [128, 1], F32); b_eps = Buf()
        fw.op("pool", lambda: nc.gpsimd.memset(eps_t[:], EPS), writes=[b_eps])

        def next_mm():
            i = mmi[0] % 6
            mmi[0] += 1
            return mm[i], b_mm[i]

        blocks = [(K.src[l][0], b * 512, 512, 0) for b in range(NLAT // 512)]
        blocks.append((K.src[l][1], NLAT, NCTX, 1))
        sub_c = [0]

        def prep(bi):
            src, tok0, ntok, j = blocks[bi]
            nsub = ntok // 128
            hT_, bhT_ = hT[bi % 2], b_hT[bi % 2]
            if j == 0:
                r0 = tok0 // GRID_W
                nr = ntok // GRID_W
                for s in range(6):
                    o = rope[:, s, 0:ntok].rearrange("p (r c) -> p r c", c=GRID_W)
                    a = rtab[:, s, r0:r0 + nr].unsqueeze(2).broadcast_to([128, nr, GRID_W])
                    b_ = rtab[:, s, 128:192].unsqueeze(1).broadcast_to([128, nr, GRID_W])
                    fw.op("pool", lambda: nc.gpsimd.tensor_tensor(out=o, in0=a, in1=b_, op=ALU.add),
                          reads=[b_rtab], writes=[b_rope])
            else:
                for s, val in enumerate((1.0, 0.0, 0.125, 0.0, 1.0, 0.0)):
                    fw.op("pool", lambda: nc.gpsimd.memset(rope[:, s, :], val), writes=[b_rope])
            for s_ in range(nsub):
                sub_i = sub_c[0]
                x_, bx_ = xt[sub_i % 2], b_xt[sub_i % 2]
                ss_, bss_ = ss[sub_i % 2], b_ss[sub_i % 2]
                sd_, bsd_ = sd[sub_i % 2], b_sd[sub_i % 2]
                rs_, brs_ = rs[sub_i % 2], b_rs[sub_i % 2]
                hb_, bhb_ = hb[sub_i % 2], b_hb[sub_i % 2]
                tp_, btp_ = tp[sub_i % 2], b_tp[sub_i % 2]
                sub_c[0] += 1
                lo = (tok0 - (NLAT if j else 0)) + s_ * 128
                fw.dma("sp", x_[:], src[lo:lo + 128, :], reads=[K.b_src[l]], writes=[bx_])
                fw.op("act", lambda: nc.scalar.activation(out=junk[:], in_=x_[:], func=AF.Square, accum_out=ss_[:]),
                      reads=[bx_], writes=[b_junk, bss_])
                fw.op("act", lambda: nc.scalar.activation(out=sd_[:], in_=ss_[:], func=AF.Sqrt, scale=1.0 / D,
                                                          bias=eps_t[:]), reads=[bss_, b_eps], writes=[bsd_])
                fw.op("dve", lambda: nc.vector.reciprocal(out=rs_[:], in_=sd_[:]), reads=[bsd_], writes=[brs_])
                fw.op("dve", lambda: nc.vector.scalar_tensor_tensor(out=h1[:], in0=x_[:], scalar=rs_[:], in1=gm[j][:],
                                                                    op0=ALU.mult, op1=ALU.mult),
                      reads=[bx_, brs_, b_gm[j]], writes=[b_h1])
                fw.op("pool", lambda: nc.gpsimd.tensor_tensor(out=hb_[:], in0=h1[:], in1=sh[j][:], op=ALU.add),
                      reads=[b_h1, b_sh[j]], writes=[bhb_])
                for k in range(8):
                    fw.op("pe", lambda: nc.tensor.transpose(tp_[:, k, :], hb_[:, k * 128:(k + 1) * 128], ident[:]),
                          reads=[bhb_, b_id], writes=[btp_], inc=(k == 7))
                fw.op("dve", lambda: nc.vector.tensor_copy(out=hT_[:, :, s_ * 128:(s_ + 1) * 128], in_=tp_[:]),
                      reads=[btp_], writes=[bhT_])

        prep(0)
        for bi, (src, tok0, ntok, j) in enumerate(blocks):
            nsub = ntok // 128
            hT_, bhT_ = hT[bi % 2], b_hT[bi % 2]

            def fm(c):
                p_, bp_ = next_mm()
                for k in range(8):
                    fw.op("pe", lambda: nc.tensor.matmul(p_[:, 0:ntok], lhsT=W[:, k, c * 128:(c + 1) * 128],
                                                         rhs=hT_[:, k, 0:ntok], start=(k == 0), stop=(k == 7)),
                          reads=[b_W, bhT_], writes=[bp_], inc=(k == 7))
                return p_, bp_

            for c in range(4):
                p_, bp_ = fm(c)
                fw.op("act", lambda: nc.scalar.copy(out=stB[:, c, 0:ntok], in_=p_[:, 0:ntok]),
                      reads=[bp_], writes=[b_stB[c]])
            for c in range(4, 10):
                p_, bp_ = fm(c)
                fw.op("act", lambda: nc.scalar.copy(out=stF[:, c - 4, 0:ntok], in_=p_[:, 0:ntok]),
                      reads=[bp_], writes=[b_stF[c - 4]])
            ri = 0
            for g, (c0, tabs, o0) in enumerate(((10, (0, 1), 4), (14, (2, 3), 6), (18, (4, 5), 8), (22, (4, 5), 10))):
                for hh in range(2):
                    pa, bpa = fm(c0 + hh)
                    pb, bpb = fm(c0 + 2 + hh)
                    t1_, bt1_ = t1[ri % 2], b_t1[ri % 2]
                    t2_, bt2_ = t2[ri % 2], b_t2[ri % 2]
                    ri += 1
                    fw.op("dve", lambda: nc.vector.tensor_tensor(out=t1_[:, 0:ntok], in0=pa[:, 0:ntok],
                                                                 in1=rope[:, tabs[0], 0:ntok], op=ALU.mult),
                          reads=[bpa, b_rope], writes=[bt1_])
                    fw.op("dve", lambda: nc.vector.tensor_tensor(out=t2_[:, 0:ntok], in0=pb[:, 0:ntok],
                                                                 in1=rope[:, tabs[1], 0:ntok], op=ALU.mult),
                          reads=[bpb, b_rope], writes=[bt2_])
                    fw.op("pool", lambda: nc.gpsimd.tensor_tensor(out=stB[:, o0 + hh, 0:ntok], in0=t1_[:, 0:ntok],
                                                                  in1=t2_[:, 0:ntok], op=ALU.add),
                          reads=[bt1_, bt2_], writes=[b_stB[o0 + hh]])
            fw.dma("pool", K.featB[l][:, :, tok0:tok0 + ntok].rearrange("c p t -> p c t"), stB[:, :, 0:ntok],
                   reads=b_stB, writes=[K.b_feat])
            fw.dma("pool", K.featF[l][:, :, tok0:tok0 + ntok].rearrange("c p t -> p c t"), stF[:, :, 0:ntok],
                   reads=b_stF, writes=[K.b_feat])
            if bi + 1 < len(blocks):
                prep(bi + 1)
            for s_ in range(nsub):
                for (n0, n1) in ((0, 512), (512, 768)):
                    p_, bp_ = next_mm()
                    for k in range(8):
                        fw.op("pe", lambda: nc.tensor.matmul(p_[:, 0:n1 - n0], lhsT=hT_[:, k, s_ * 128:(s_ + 1) * 128],
                                                             rhs=W[:, k, 3328 + n0:3328 + n1],
                                                             start=(k == 0), stop=(k == 7)),
                              reads=[b_W, bhT_], writes=[bp_], inc=(k == 7))
                    fw.op("act", lambda: nc.scalar.copy(out=stV[:, s_, n0:n1], in_=p_[:, 0:n1 - n0]),
                          reads=[bp_], writes=[b_stV[s_]])
            fw.dma("pool", K.vAll[l][tok0:tok0 + ntok, :].rearrange("(s p) c -> p s c", p=128), stV[:, 0:nsub, :],
                   reads=b_stV[0:nsub], writes=[K.b_feat])
        fw.barrier()


def _phase_lru(K, l):
    nc, fw = K.nc, K.fw
    CH = 2048
    for cc in range(2):
        with contextlib.ExitStack() as st:
            sb = lambda n, s, d: st.enter_context(nc.sbuf_tensor(K.un(n), s, d))
            A = sb("l_A", [128, T], F32); b_A = Buf()
            B = sb("l_B", [128, T], F32); b_B = Buf()
            Bb = sb("l_Bb", [128, T], BF16); b_Bb = Buf()
            C = sb("l_C", [128, T], F32); b_C = Buf()
            Dd = sb("l_D", [128, T], F32); b_D = Buf()
            E = sb("l_E", [128, T], F32); b_E = Buf()
            cw = sb("l_cw", [128, 4], F32); b_cw = Buf()
            cb = sb("l_cb", [128, 1], F32); b_cb = Buf()
            gwf = sb("l_gwf", [128, 4, 128], F32); b_gwf = Buf()
            gw = sb("l_gw", [128, 4, 128], BF16); b_gw = Buf()
            gb = sb("l_gb", [128, 4], F32); b_gb = Buf()
            lam = sb("l_lam", [128, 2], F32); b_lam = Buf()
            cn = sb("l_cn", [128, 2], F32); b_cn = Buf()
            one = sb("l_one", [128, 1], F32); b_one = Buf()
            CG = 1056
            gst = [sb("l_gst%d" % i, [128, CG], F32) for i in range(2)]; b_gst = [Buf(), Buf()]
            yst = [sb("l_yst%d" % i, [128, CG], BF16) for i in range(2)]; b_yst = [Buf(), Buf()]
            ps = [st.enter_context(nc.psum_tensor(K.un("l_ps%d" % i), [128, CH], F32)) for i in range(2)]
            b_ps = [Buf(), Buf()]
            for c0 in range(0, T, 2112):
                fw.dma("sp", A[:, c0:c0 + 2112], K.featF[l][cc, :, c0:c0 + 2112], reads=[K.b_feat], writes=[b_A])
            fw.dma("sp", cw[:], K.lru_conv_w[l, :, cc * 128:(cc + 1) * 128].rearrange("k p -> p k"), writes=[b_cw],
                   allow_slow_non_contiguous=True)
            fw.dma("sp", cb[:], K.lru_conv_b[l, cc * 128:(cc + 1) * 128].rearrange("(p o) -> p o", o=1), writes=[b_cb])
            fw.op("pool", lambda: nc.gpsimd.memset(gwf[:], 0.0), writes=[b_gwf])
            fw.op("pool", lambda: nc.gpsimd.memset(one[:], 1.0), writes=[b_one])
            for dr in range(2):
                for g in range(2):
                    for bk in range(2):
                        fw.dma("sp", gwf[bk * 64:(bk + 1) * 64, dr * 2 + g, bk * 64:(bk + 1) * 64],
                               K.lru_gate_w[l, dr, g, cc * 2 + bk], writes=[b_gwf])
                    fw.dma("sp", gb[:, dr * 2 + g:dr * 2 + g + 1],
                           K.lru_gate_b[l, dr, g, cc * 2:cc * 2 + 2, :].rearrange("k (d o) -> (k d) o", o=1), writes=[b_gb])
                fw.dma("sp", lam[:, dr:dr + 1], K.lru_lambda[l, dr, cc * 128:(cc + 1) * 128].rearrange("(p o) -> p o", o=1),
                       writes=[b_lam])
            fw.op("dve", lambda: nc.vector.tensor_copy(out=gw[:], in_=gwf[:]), reads=[b_gwf], writes=[b_gw])
            fw.op("act", lambda: nc.scalar.activation(out=cn[:], in_=lam[:], func=AF.Exp, scale=-1.0), reads=[b_lam], writes=[b_cn])
            fw.op("act", lambda: nc.scalar.activation(out=cn[:], in_=cn[:], func=AF.Ln, bias=one[:]), reads=[b_cn, b_one], writes=[b_cn])
            fw.op("dve", lambda: nc.vector.tensor_scalar(out=cn[:], in0=cn[:], scalar1=-8.0, scalar2=None, op0=ALU.mult),
                  reads=[b_cn], writes=[b_cn])
            for (s0, s1) in ((0, NLAT), (NLAT, T)):
                fw.op("dve", lambda: nc.vector.tensor_scalar(out=B[:, s0:s1], in0=A[:, s0:s1], scalar1=cw[:, 1:2], scalar2=cb[:, 0:1],
                                                             op0=ALU.mult, op1=ALU.add), reads=[b_A, b_cw, b_cb], writes=[b_B])
                fw.op("dve", lambda: nc.vector.scalar_tensor_tensor(out=B[:, s0 + 1:s1], in0=A[:, s0:s1 - 1], scalar=cw[:, 0:1],
                                                                    in1=B[:, s0 + 1:s1], op0=ALU.mult, op1=ALU.add),
                      reads=[b_A, b_cw, b_B], writes=[b_B])
                fw.op("dve", lambda: nc.vector.scalar_tensor_tensor(out=B[:, s0:s1 - 1], in0=A[:, s0 + 1:s1], scalar=cw[:, 2:3],
                                                                    in1=B[:, s0:s1 - 1], op0=ALU.mult, op1=ALU.add),
                      reads=[b_A, b_cw, b_B], writes=[b_B])
                fw.op("dve", lambda: nc.vector.scalar_tensor_tensor(out=B[:, s0:s1 - 2], in0=A[:, s0 + 2:s1], scalar=cw[:, 3:4],
                                                                    in1=B[:, s0:s1 - 2], op0=ALU.mult, op1=ALU.add),
                      reads=[b_A, b_cw, b_B], writes=[b_B])
            fw.op("pool", lambda: nc.gpsimd.tensor_copy(out=Bb[:], in_=B[:]), reads=[b_B], writes=[b_Bb])
            pi = 0
            chunks = [(c0, min(T, c0 + CH)) for c0 in range(0, T, CH)]
            for dr in range(2):
                for g, (dst, bdst) in enumerate(((C, b_C), (Dd, b_D))):
                    for (c0, c1) in chunks:
                        p_, bp_ = ps[pi % 2], b_ps[pi % 2]
                        pi += 1
                        for t0 in range(c0, c1, 512):
                            t1 = min(c1, t0 + 512)
                            fw.op("pe", lambda: nc.tensor.matmul(p_[:, t0 - c0:t1 - c0], lhsT=gw[:, dr * 2 + g, :], rhs=Bb[:, t0:t1],
                                                                 start=True, stop=True),
                                  reads=[b_gw, b_Bb], writes=[bp_], inc=(t1 == c1))
                        fw.op("act", lambda: nc.scalar.activation(out=dst[:, c0:c1], in_=p_[:, 0:c1 - c0], func=AF.Sigmoid,
                                                                  bias=gb[:, dr * 2 + g:dr * 2 + g + 1]),
                              reads=[bp_, b_gb], writes=[bdst])
                fw.op("act", lambda: nc.scalar.activation(out=C[:], in_=C[:], func=AF.Exp, scale=cn[:, dr:dr + 1]),
                      reads=[b_C, b_cn], writes=[b_C])
                fw.op("pool", lambda: nc.gpsimd.tensor_tensor(out=A[:], in0=C[:], in1=C[:], op=ALU.mult), reads=[b_C], writes=[b_A])
                fw.op("act", lambda: nc.scalar.activation(out=A[:], in_=A[:], func=AF.Sqrt, scale=-1.0, bias=one[:]),
                      reads=[b_A, b_one], writes=[b_A])
                fw.op("dve", lambda: nc.vector.tensor_tensor(out=Dd[:], in0=Dd[:], in1=B[:], op=ALU.mult), reads=[b_D, b_B], writes=[b_D])
                fw.op("dve", lambda: nc.vector.tensor_tensor(out=Dd[:], in0=Dd[:], in1=A[:], op=ALU.mult), reads=[b_D, b_A], writes=[b_D])
                if dr == 0:
                    segs = [(NLAT, T)] + [(c0, c0 + CH) for c0 in range(0, NLAT, CH)]
                    prev = None
                    for (c0, c1) in segs:
                        init = 0.0 if prev is None else A[:, prev - 1:prev]
                        fw.op("dve", lambda: nc.vector.tensor_tensor_scan(out=A[:, c0:c1], data0=C[:, c0:c1], data1=Dd[:, c0:c1],
                                                                          initial=init, op0=ALU.mult, op1=ALU.add),
                              reads=[b_C, b_D, b_A], writes=[b_A])
                        prev = c1
                else:
                    segs = [(NLAT, T)] + [(c0, c0 + CH) for c0 in range(NLAT - CH, -1, -CH)]
                    prev = None
                    rev = lambda ap: ap[:, ::-1]
                    for (c0, c1) in segs:
                        init = 0.0 if prev is None else A[:, prev:prev + 1]
                        fw.op("dve", lambda: nc.vector.tensor_tensor_scan(out=rev(A[:, c0:c1]), data0=rev(C[:, c0:c1]),
                                                                          data1=rev(Dd[:, c0:c1]), initial=init,
                                                                          op0=ALU.mult, op1=ALU.add),
                              reads=[b_C, b_D, b_A], writes=[b_A])
                        prev = c0
                if dr == 0:
                    fw.op("pool", lambda: nc.gpsimd.tensor_copy(out=E[:], in_=A[:]), reads=[b_A], writes=[b_E])
                else:
                    fw.op("pool", lambda: nc.gpsimd.tensor_tensor(out=E[:], in0=E[:], in1=A[:], op=ALU.add), reads=[b_A, b_E], writes=[b_E])
            for ci, (c0, c1) in enumerate([(c0, c0 + CG) for c0 in range(0, T, CG)]):
                g_, bg_ = gst[ci % 2], b_gst[ci % 2]
                y_, by_ = yst[ci % 2], b_yst[ci % 2]
                fw.dma("sp", g_[:, 0:c1 - c0], K.featF[l][2 + cc, :, c0:c1], reads=[K.b_feat], writes=[bg_])
                fw.op("act", lambda: nc.scalar.activation(out=g_[:, 0:c1 - c0], in_=g_[:, 0:c1 - c0], func=AF.Gelu_apprx_tanh),
                      reads=[bg_], writes=[bg_])
                fw.op("dve", lambda: nc.vector.tensor_tensor(out=y_[:, 0:c1 - c0], in0=g_[:, 0:c1 - c0], in1=E[:, c0:c1], op=ALU.mult),
                      reads=[bg_, b_E], writes=[by_])
                fw.dma("pool", K.yT[l][2 + cc, :, c0:c1], y_[:, 0:c1 - c0], reads=[by_], writes=[K.b_yT])
            fw.barrier()


def _ret_consts():
    j = np.arange(128)[:, None].astype(np.float32)
    i = np.arange(128)[None, :].astype(np.float32)
    z = np.zeros((128, 128), np.float32)
    c = np.stack([np.maximum(i - j, 0), (i >= j).astype(np.float32), np.maximum(j - i - 1, 0),
                  (j > i).astype(np.float32), z + i + 1, z + 127 - i], 1).astype(np.float32)
    cv = np.stack([127 - np.arange(128), np.arange(128)], 1).astype(np.float32)
    return np.ascontiguousarray(c), np.ascontiguousarray(cv)


def _blk64():
    b = np.zeros((128, 128), np.float32)
    b[:64, :64] = 1.0 / 64
    b[64:, 64:] = 1.0 / 64
    return b


def _phase_ret(K, l):
    nc, fw = K.nc, K.fw
    last = l == DEPTH - 1
    NCH = T // 128
    with contextlib.ExitStack() as st0:
        sb0 = lambda n, s, d: st0.enter_context(nc.sbuf_tensor(K.un(n), s, d))
        rc = sb0("r_rc", [128, 6, 128], F32); b_rc = Buf()
        cv = sb0("r_cv", [128, 2], F32); b_cv = Buf()
        rd = sb0("r_rd", [128, 8], F32); b_rd = Buf()
        lg = sb0("r_lg", [128, 8], F32); b_lg = Buf()
        one = sb0("r_one", [128, 1], F32); b_one = Buf()
        eps_t = sb0("r_eps", [128, 1], F32); b_eps = Buf()
        blkf = sb0("r_blkf", [128, 128], F32); b_blkf = Buf()
        blk = sb0("r_blk", [128, 128], BF16); b_blk = Buf()
        idf = sb0("r_idf", [128, 128], F32); b_idf = Buf()
        ident = sb0("r_id", [128, 128], BF16); b_id = Buf()
        fw.dma("sp", rc[:], K.retc, writes=[b_rc])
        fw.dma("sp", cv[:], K.retcv, writes=[b_cv])
        fw.dma("sp", rd[:], K.ret_decay[l:l + 1].rearrange("o a h -> o (a h)").broadcast_to([128, 8]), writes=[b_rd])
        fw.dma("sp", blkf[:], K.blk64, writes=[b_blkf])
        fw.dma("sp", idf[:], K.ident, writes=[b_idf])
        fw.op("dve", lambda: nc.vector.tensor_copy(out=blk[:], in_=blkf[:]), reads=[b_blkf], writes=[b_blk])
        fw.op("dve", lambda: nc.vector.tensor_copy(out=ident[:], in_=idf[:]), reads=[b_idf], writes=[b_id])
        fw.op("pool", lambda: nc.gpsimd.memset(one[:], 1.0), writes=[b_one])
        fw.op("pool", lambda: nc.gpsimd.memset(eps_t[:], EPS), writes=[b_eps])
        fw.op("act", lambda: nc.scalar.activation(out=lg[:], in_=rd[:], func=AF.Exp, scale=-1.0), reads=[b_rd], writes=[b_lg])
        fw.op("act", lambda: nc.scalar.activation(out=lg[:], in_=lg[:], func=AF.Ln, bias=one[:]), reads=[b_lg, b_one], writes=[b_lg])
        fw.op("dve", lambda: nc.vector.tensor_scalar(out=lg[:], in0=lg[:], scalar1=-1.0, scalar2=None, op0=ALU.mult),
              reads=[b_lg], writes=[b_lg])
        for cc in range(2):
            with contextlib.ExitStack() as st:
                sb = lambda n, s, d: st.enter_context(nc.sbuf_tensor(K.un(n), s, d))
                QT = sb("r_QT", [128, NCH, 128], BF16); b_QT = Buf()
                KT = sb("r_KT", [128, NCH, 128], BF16); b_KT = Buf()
                V = sb("r_V", [128, NCH, 128], BF16); b_V = Buf()
                qf = sb("r_qf", [128, NCH, 128], BF16); b_qf = Buf()
                qb = sb("r_qb", [128, NCH, 128], BF16); b_qb = Buf()
                KVs = [sb("r_KVs%d" % d_, [128, NCH + 6, 64], F32) for d_ in range(2)]; b_KVs = [Buf(), Buf()]
                Sbf = [sb("r_Sbf%d" % d_, [128, NCH, 64], BF16) for d_ in range(2)]; b_Sbf = [Buf(), Buf()]
                lgp = sb("r_lgp", [128, 2], F32); b_lgp = Buf()
                g128 = sb("r_g128", [128, 2], F32); b_g128 = Buf()
                M = sb("r_M", [128, 2, 128], F32); b_M = Buf()
                mt = sb("r_mt", [128, 2, 128], F32); b_mt = Buf()
                df = sb("r_df", [128, 128], F32); b_df = Buf()
                db = sb("r_db", [128, 128], F32); b_db = Buf()
                kdec = sb("r_kdec", [128, 2, 2], F32); b_kdec = Buf()
                for c0 in range(0, NCH, 22):
                    fw.dma("sp", QT[:, c0:c0 + 22, :], K.featB[l][4 + cc, :, c0 * 128:(c0 + 22) * 128].rearrange("p (c i) -> p c i", i=128),
                           reads=[K.b_feat], writes=[b_QT])
                    fw.dma("sp", KT[:, c0:c0 + 22, :], K.featB[l][6 + cc, :, c0 * 128:(c0 + 22) * 128].rearrange("p (c i) -> p c i", i=128),
                           reads=[K.b_feat], writes=[b_KT])
                for c0 in range(0, NCH, 6):
                    fw.dma("sp", V[:, c0:c0 + 6, :],
                           K.vAll[l][c0 * 128:(c0 + 6) * 128, 256 + cc * 128:256 + (cc + 1) * 128].rearrange("(c p) e -> p c e", p=128),
                           reads=[K.b_feat], writes=[b_V])
                for dr in range(2):
                    for hh in range(2):
                        col = dr * 4 + 2 * cc + hh
                        fw.op("dve", lambda: nc.vector.tensor_copy(out=lgp[hh * 64:(hh + 1) * 64, dr:dr + 1],
                                                                   in_=lg[hh * 64:(hh + 1) * 64, col:col + 1]),
                              reads=[b_lg], writes=[b_lgp])
                fw.op("act", lambda: nc.scalar.activation(out=g128[:], in_=lgp[:], func=AF.Exp, scale=128.0), reads=[b_lgp], writes=[b_g128])
                for hh in range(2):
                    h = 2 * cc + hh
                    fw.op("act", lambda: nc.scalar.activation(out=M[:, hh, :], in_=rc[:, 0, :], func=AF.Exp, scale=lg[:, h:h + 1]),
                          reads=[b_rc, b_lg], writes=[b_M])
                    fw.op("act", lambda: nc.scalar.activation(out=mt[:, hh, :], in_=rc[:, 2, :], func=AF.Exp, scale=lg[:, 4 + h:5 + h]),
                          reads=[b_rc, b_lg], writes=[b_mt])
                    fw.op("dve", lambda: nc.vector.tensor_tensor(out=M[:, hh, :], in0=M[:, hh, :], in1=rc[:, 1, :], op=ALU.mult),
                          reads=[b_M, b_rc], writes=[b_M])
                    fw.op("dve", lambda: nc.vector.tensor_tensor(out=mt[:, hh, :], in0=mt[:, hh, :], in1=rc[:, 3, :], op=ALU.mult),
                          reads=[b_mt, b_rc], writes=[b_mt])
                    fw.op("act", lambda: nc.scalar.activation(out=kdec[:, 0, hh:hh + 1], in_=lg[:, h:h + 1], func=AF.Exp, scale=cv[:, 0:1]),
                          reads=[b_lg, b_cv], writes=[b_kdec])
                    fw.op("act", lambda: nc.scalar.activation(out=kdec[:, 1, hh:hh + 1], in_=lg[:, 4 + h:5 + h], func=AF.Exp, scale=cv[:, 1:2]),
                          reads=[b_lg, b_cv], writes=[b_kdec])
                fw.op("dve", lambda: nc.vector.tensor_tensor(out=M[:], in0=M[:], in1=mt[:], op=ALU.add), reads=[b_M, b_mt], writes=[b_M])
                fw.op("act", lambda: nc.scalar.activation(out=df[:], in_=rc[:, 4, :], func=AF.Exp, scale=lgp[:, 0:1]),
                      reads=[b_rc, b_lgp], writes=[b_df])
                fw.op("act", lambda: nc.scalar.activation(out=db[:], in_=rc[:, 5, :], func=AF.Exp, scale=lgp[:, 1:2]),
                      reads=[b_rc, b_lgp], writes=[b_db])
                fw.op("dve", lambda: nc.vector.tensor_tensor(out=qf[:], in0=QT[:], in1=df[:].unsqueeze(1).broadcast_to([128, NCH, 128]), op=ALU.mult),
                      reads=[b_QT, b_df], writes=[b_qf])
                fw.op("pool", lambda: nc.gpsimd.tensor_tensor(out=qb[:], in0=QT[:], in1=db[:].unsqueeze(1).broadcast_to([128, NCH, 128]), op=ALU.mult),
                      reads=[b_QT, b_db], writes=[b_qb])
                if K.stop <= 0:
                    fw.barrier()
                    continue
                with contextlib.ExitStack() as st1:
                    tpk = [st1.enter_context(nc.psum_tensor(K.un("r_tpk%d" % i), [128, 1024], BF16)) for i in range(2)]
                    b_tpk = [Buf(), Buf()]
                    kvp = [[st1.enter_context(nc.psum_tensor(K.un("r_kvp%d_%d" % (d_, i)), [128, 8, 64], F32)) for i in range(2)] for d_ in range(2)]
                    b_kvp = [[Buf(), Buf()] for d_ in range(2)]
                    kd = [[st1.enter_context(nc.sbuf_tensor(K.un("r_kd%d_%d" % (d_, i)), [128, 2, 64], BF16)) for i in range(2)] for d_ in range(2)]
                    b_kd = [[Buf(), Buf()] for d_ in range(2)]
                    for c in range(min(NCH, K.p1n)):
                        p = c % 2
                        grp = (c // 8) % 2
                        fw.op("pe", lambda: nc.tensor.transpose(tpk[p][:, 0:128], KT[:, c, :], ident[:]), reads=[b_KT, b_id], writes=[b_tpk[p]])
                        for dr in range(2):
                            fw.op("dve", lambda: nc.vector.tensor_tensor(out=kd[dr][p][:], in0=tpk[p][:, 0:128].rearrange("p (h d) -> p h d", d=64),
                                                                         in1=kdec[:, dr, :].unsqueeze(2).broadcast_to([128, 2, 64]), op=ALU.mult),
                                  reads=[b_tpk[p], b_kdec], writes=[b_kd[dr][p]])
                        for dr in range(2):
                            for hh in range(2):
                                fw.op("pe", lambda: nc.tensor.matmul(kvp[dr][grp][hh * 64:(hh + 1) * 64, c % 8, :], lhsT=kd[dr][p][:, hh, :],
                                                                     rhs=V[:, c, hh * 64:(hh + 1) * 64], start=True, stop=True),
                                      reads=[b_kd[dr][p], b_V], writes=[b_kvp[dr][grp]], inc=(hh == 1))
                        if c % 8 == 7 or c == min(NCH, K.p1n) - 1:
                            n8 = c % 8 + 1
                            c8 = c - (n8 - 1)
                            for dr in range(2):
                                fw.op("act", lambda: nc.scalar.copy(out=KVs[dr][:, c8:c8 + n8, :], in_=kvp[dr][grp][:, 0:n8, :]),
                                      reads=[b_kvp[dr][grp]], writes=[b_KVs[dr]])
                    fw.barrier()
                if K.stop <= 1:
                    continue
                with contextlib.ExitStack() as st1:
                    S = [[st1.enter_context(nc.sbuf_tensor(K.un("r_S%d_%d" % (d_, i)), [128, 64], F32)) for i in range(2)] for d_ in range(2)]
                    b_S = [[Buf(), Buf()] for d_ in range(2)]
                    orders = [[64, 65] + list(range(64)), [65, 64] + list(range(63, -1, -1))]
                    for dr in range(2):
                        fw.op("pool", lambda: nc.gpsimd.memset(S[dr][0][:], 0.0), writes=[b_S[dr][0]])
                    for n in range(NCH):
                        for dr in range(2):
                            c = orders[dr][n]
                            cur, nxt = S[dr][n % 2], S[dr][(n + 1) % 2]
                            bcur, bnxt = b_S[dr][n % 2], b_S[dr][(n + 1) % 2]
                            eng = "act" if dr == 0 else "pool"
                            if eng == "act":
                                fw.op("act", lambda: nc.scalar.copy(out=Sbf[dr][:, c, :], in_=cur[:]), reads=[bcur], writes=[b_Sbf[dr]])
                            else:
                                fw.op("pool", lambda: nc.gpsimd.tensor_copy(out=Sbf[dr][:, c, :], in_=cur[:]), reads=[bcur], writes=[b_Sbf[dr]])
                            fw.op("dve", lambda: nc.vector.scalar_tensor_tensor(out=nxt[:], in0=cur[:], scalar=g128[:, dr:dr + 1],
                                                                                in1=KVs[dr][:, c, :], op0=ALU.mult, op1=ALU.add),
                                  reads=[bcur, b_g128, b_KVs[dr]], writes=[bnxt])
                    fw.barrier()
                if K.stop <= 2:
                    continue
                with contextlib.ExitStack() as st1:
                    sbl = lambda n, s, d: st1.enter_context(nc.sbuf_tensor(K.un(n), s, d))
                    sc = [[st1.enter_context(nc.psum_tensor(K.un("r_sc%d_%d" % (i, h_)), [128, 512], F32)) for h_ in range(2)] for i in range(2)]
                    b_sc = [[Buf(), Buf()], [Buf(), Buf()]]
                    oT = [st1.enter_context(nc.psum_tensor(K.un("r_oT%d" % i), [128, 512], F32)) for i in range(2)]; b_oT = [Buf(), Buf()]
                    ms = [st1.enter_context(nc.psum_tensor(K.un("r_ms%d" % i), [128, 512], F32)) for i in range(2)]; b_ms = [Buf(), Buf()]
                    P = [sbl("r_P%d" % i, [128, 2, 128], BF16) for i in range(2)]; b_P = [Buf(), Buf()]
                    sq = [sbl("r_sq%d" % i, [128, 512], BF16) for i in range(2)]; b_sq = [Buf(), Buf()]
                    sd = [sbl("r_sd%d" % i, [128, 512], F32) for i in range(2)]; b_sd = [Buf(), Buf()]
                    tt = [sbl("r_tt%d" % i, [128, 512], F32) for i in range(2)]; b_tt = [Buf(), Buf()]
                    gs = [sbl("r_gs%d" % i, [128, 512], F32) for i in range(2)]; b_gs = [Buf(), Buf()]
                    ys = [sbl("r_ys%d" % i, [128, 512], BF16) for i in range(2)]; b_ys = [Buf(), Buf()]
                    groups = [list(range(g * 4, g * 4 + 4)) for g in range(16)]
                    if not last:
                        groups.append([64, 65])
                    groups = groups[:K.p2n] if K.p2n >= 0 else groups[K.p2n:]
                    ci_all = 0
                    for gi, grp in enumerate(groups):
                        q = gi % 2
                        tok0 = grp[0] * 128
                        ntok = len(grp) * 128
                        if grp[0] == 64:
                            fw.barrier()
                        fw.dma("sp", gs[q][:, 0:ntok], K.featF[l][4 + cc, :, tok0:tok0 + ntok], reads=[K.b_feat], writes=[b_gs[q]])
                        for ci, c in enumerate(grp):
                            p = ci_all % 2
                            ci_all += 1
                            for hh in range(2):
                                fw.op("pe", lambda: nc.tensor.matmul(sc[p][hh][:, 0:128], lhsT=KT[hh * 64:(hh + 1) * 64, c, :],
                                                                     rhs=QT[hh * 64:(hh + 1) * 64, c, :], start=True, stop=True),
                                      reads=[b_KT, b_QT], writes=[b_sc[p][hh]])
                            for hh in range(2):
                                fw.op("dve", lambda: nc.vector.tensor_tensor(out=P[p][:, hh, :], in0=sc[p][hh][:, 0:128], in1=M[:, hh, :], op=ALU.mult),
                                      reads=[b_sc[p][hh], b_M], writes=[b_P[p]])
                            for hh in range(2):
                                o_ = oT[q][hh * 64:(hh + 1) * 64, ci * 128:(ci + 1) * 128]
                                fw.op("pe", lambda: nc.tensor.matmul(o_, lhsT=V[:, c, hh * 64:(hh + 1) * 64], rhs=P[p][:, hh, :],
                                                                     start=True, stop=False),
                                      reads=[b_V, b_P[p]], writes=[b_oT[q]], inc=False)
                                fw.op("pe", lambda: nc.tensor.matmul(o_, lhsT=Sbf[0][hh * 64:(hh + 1) * 64, c, :],
                                                                     rhs=qf[hh * 64:(hh + 1) * 64, c, :], start=False, stop=False),
                                      reads=[b_Sbf[0], b_qf], writes=[b_oT[q]], inc=False)
                                fw.op("pe", lambda: nc.tensor.matmul(o_, lhsT=Sbf[1][hh * 64:(hh + 1) * 64, c, :],
                                                                     rhs=qb[hh * 64:(hh + 1) * 64, c, :], start=False, stop=True),
                                      reads=[b_Sbf[1], b_qb], writes=[b_oT[q]], inc=(hh == 1))
                        fw.op("act", lambda: nc.scalar.activation(out=sq[q][:, 0:ntok], in_=oT[q][:, 0:ntok], func=AF.Square),
                              reads=[b_oT[q]], writes=[b_sq[q]])
                        fw.op("pe", lambda: nc.tensor.matmul(ms[q][:, 0:ntok], lhsT=blk[:], rhs=sq[q][:, 0:ntok], start=True, stop=True),
                              reads=[b_blk, b_sq[q]], writes=[b_ms[q]])
                        fw.op("act", lambda: nc.scalar.activation(out=sd[q][:, 0:ntok], in_=ms[q][:, 0:ntok], func=AF.Sqrt, bias=eps_t[:]),
                              reads=[b_ms[q], b_eps], writes=[b_sd[q]])
                        fw.op("dve", lambda: nc.vector.reciprocal(out=sd[q][:, 0:ntok], in_=sd[q][:, 0:ntok]), reads=[b_sd[q]], writes=[b_sd[q]])
                        fw.op("dve", lambda: nc.vector.tensor_tensor(out=tt[q][:, 0:ntok], in0=oT[q][:, 0:ntok], in1=sd[q][:, 0:ntok], op=ALU.mult),
                              reads=[b_oT[q], b_sd[q]], writes=[b_tt[q]])
                        fw.op("act", lambda: nc.scalar.activation(out=gs[q][:, 0:ntok], in_=gs[q][:, 0:ntok], func=AF.Silu),
                              reads=[b_gs[q]], writes=[b_gs[q]])
                        fw.op("pool", lambda: nc.gpsimd.tensor_tensor(out=ys[q][:, 0:ntok], in0=tt[q][:, 0:ntok], in1=gs[q][:, 0:ntok], op=ALU.mult),
                              reads=[b_tt[q], b_gs[q]], writes=[b_ys[q]])
                        fw.dma("pool", K.yT[l][4 + cc, :, tok0:tok0 + ntok], ys[q][:, 0:ntok], reads=[b_ys[q]], writes=[K.b_yT])
                    fw.barrier()


def _diff_consts():
    sel = np.zeros((65, 64), np.float32)
    sel[64, :] = 1.0
    ones64 = np.full((64, 64), 1.0 / 64, np.float32)
    mcol = np.zeros((128, 4), np.float32)
    for p in range(128):
        mcol[p, p // 32] = 1.0
    return sel, ones64, mcol


def _phase_diff(K, l):
    nc, fw = K.nc, K.fw
    last = l == DEPTH - 1
    NCH = T // 128
    lam_init = 0.8 - 0.6 * math.exp(-0.3 * l)
    SC = 32 ** -0.5
    with contextlib.ExitStack() as st0:
        sb0 = lambda n, s, d: st0.enter_context(nc.sbuf_tensor(K.un(n), s, d))
        Va = sb0("d_Va", [128, NCH, 4, 128], BF16); b_Va = Buf()
        sel = sb0("d_sel", [65, 64], F32); b_sel = Buf()
        o64 = sb0("d_o64", [64, 64], F32); b_o64 = Buf()
        mcol = sb0("d_mcol", [128, 4], F32); b_mcol = Buf()
        dl = sb0("d_dl", [64, 4, 32], F32); b_dl = Buf()
        pr = sb0("d_pr", [64, 2, 32], F32); b_pr = Buf()
        s12 = sb0("d_s12", [64, 2], F32); b_s12 = Buf()
        nlam = sb0("d_nlam", [64, 1], F32); b_nlam = Buf()
        gsc = sb0("d_gsc", [64, 1], F32); b_gsc = Buf()
        eps_t = sb0("d_eps", [64, 1], F32); b_eps = Buf()
        fw.dma("sp", sel[:], K.sel65, writes=[b_sel])
        fw.dma("sp", o64[:], K.ones64, writes=[b_o64])
        fw.dma("sp", mcol[:], K.mcol, writes=[b_mcol])
        fw.dma("sp", dl[:], K.diff_lambda[l:l + 1].broadcast_to([64, 4, 32]), writes=[b_dl])
        fw.dma("sp", gsc[:], K.diff_subln[l].rearrange("(p o) -> p o", o=1), writes=[b_gsc])
        fw.op("pool", lambda: nc.gpsimd.memset(eps_t[:], EPS), writes=[b_eps])
        fw.op("dve", lambda: nc.vector.tensor_tensor(out=pr[:, 0, :], in0=dl[:, 0, :], in1=dl[:, 1, :], op=ALU.mult), reads=[b_dl], writes=[b_pr])
        fw.op("dve", lambda: nc.vector.tensor_tensor(out=pr[:, 1, :], in0=dl[:, 2, :], in1=dl[:, 3, :], op=ALU.mult), reads=[b_dl], writes=[b_pr])
        fw.op("dve", lambda: nc.vector.tensor_reduce(out=s12[:], in_=pr[:], axis=AX.X, op=ALU.add), reads=[b_pr], writes=[b_s12])
        fw.op("act", lambda: nc.scalar.activation(out=s12[:], in_=s12[:], func=AF.Exp), reads=[b_s12], writes=[b_s12])
        fw.op("dve", lambda: nc.vector.tensor_tensor(out=nlam[:], in0=s12[:, 1:2], in1=s12[:, 0:1], op=ALU.subtract), reads=[b_s12], writes=[b_nlam])
        fw.op("dve", lambda: nc.vector.tensor_scalar(out=nlam[:], in0=nlam[:], scalar1=-lam_init, scalar2=None, op0=ALU.add),
              reads=[b_nlam], writes=[b_nlam])
        fw.op("dve", lambda: nc.vector.tensor_scalar(out=gsc[:], in0=gsc[:], scalar1=1.0 - lam_init, scalar2=None, op0=ALU.mult),
              reads=[b_gsc], writes=[b_gsc])
        with contextlib.ExitStack() as st1:
            vst = st1.enter_context(nc.sbuf_tensor(K.un("d_vst"), [128, NCH, 256], BF16)); b_vst = Buf()
            for c0 in range(0, NCH, 6):
                fw.dma("sp", vst[:, c0:c0 + 6, :], K.vAll[l][c0 * 128:(c0 + 6) * 128, 512:768].rearrange("(c p) e -> p c e", p=128),
                       reads=[K.b_feat], writes=[b_vst])
            fw.op("pool", lambda: nc.gpsimd.memset(Va[:, :, :, 64:128], 0.0), writes=[b_Va])
            fw.op("pool", lambda: nc.gpsimd.memset(Va[:, :, :, 64:65], 1.0), writes=[b_Va])
            fw.op("dve", lambda: nc.vector.tensor_copy(out=Va[:, :, :, 0:64], in_=vst[:].rearrange("p c (h e) -> p c h e", e=64)),
                  reads=[b_vst], writes=[b_Va])
            fw.barrier()
        for cc in range(2):
            with contextlib.ExitStack() as st:
                sb = lambda n, s, d: st.enter_context(nc.sbuf_tensor(K.un(n), s, d))
                QT = sb("d_QT", [128, T], BF16); b_QT = Buf()
                KT = sb("d_KT", [128, T], BF16); b_KT = Buf()
                Qm = [sb("d_Qm%d" % m, [128, T], BF16) for m in range(4)]; b_Qm = [Buf() for _ in range(4)]
                P = [sb("d_P%d" % i, [128, 2, 512], BF16) for i in range(2)]; b_P = [Buf(), Buf()]
                Osb = sb("d_Osb", [65, 2, 512], F32); b_Osb = Buf()
                rec = sb("d_rec", [64, 2, 512], F32); b_rec = Buf()
                ta = sb("d_ta", [64, 512], F32); b_ta = Buf()
                tb = sb("d_tb", [64, 512], F32); b_tb = Buf()
                td = sb("d_td", [64, 512], F32); b_td = Buf()
                tq = sb("d_tq", [64, 512], F32); b_tq = Buf()
                sd = sb("d_sd", [64, 512], F32); b_sd = Buf()
                ys = [sb("d_ys%d" % i, [64, 512], BF16) for i in range(2)]; b_ys = [Buf(), Buf()]
                S = [st.enter_context(nc.psum_tensor(K.un("d_S%d" % i), [128, 2, 512], F32)) for i in range(2)]; b_S = [Buf(), Buf()]
                O = [st.enter_context(nc.psum_tensor(K.un("d_O%d" % m), [128, 512], F32)) for m in range(2)]; b_O = [Buf(), Buf()]
                bcp = st.enter_context(nc.psum_tensor(K.un("d_bcp"), [128, 2, 512], F32)); b_bcp = Buf()
                for c0 in range(0, T, 2112):
                    fw.dma("sp", QT[:, c0:c0 + 2112], K.featB[l][8 + cc, :, c0:c0 + 2112], reads=[K.b_feat], writes=[b_QT])
                    fw.dma("sp", KT[:, c0:c0 + 2112], K.featB[l][10 + cc, :, c0:c0 + 2112], reads=[K.b_feat], writes=[b_KT])
                for g_ in range(4):
                    fw.op("dve", lambda: nc.vector.tensor_scalar(out=Qm[g_][:], in0=QT[:], scalar1=mcol[:, g_:g_ + 1], scalar2=None, op0=ALU.mult),
                          reads=[b_QT, b_mcol], writes=[b_Qm[g_]])
                si = 0
                yi = [0]
                deferred = []

                def finalize_stage1(rows, q0, nq):
                    fw.op("act", lambda: nc.scalar.copy(out=Osb[:, 0, 0:nq], in_=O[0][0:65, 0:nq]), reads=[b_O[0]], writes=[b_Osb])
                    fw.op("dve", lambda: nc.vector.tensor_copy(out=Osb[:, 1, 0:nq], in_=O[1][0:65, 0:nq]), reads=[b_O[1]], writes=[b_Osb])

                def finalize_stage2(rows, q0, nq):
                    for m in range(2):
                        fw.op("pe", lambda: nc.tensor.matmul(bcp[0:64, m, 0:nq], lhsT=sel[:], rhs=Osb[:, m, 0:nq], start=True, stop=True),
                              reads=[b_sel, b_Osb], writes=[b_bcp], inc=(m == 1))
                    fw.op("dve", lambda: nc.vector.reciprocal(out=rec[:, :, 0:nq], in_=bcp[0:64, :, 0:nq]), reads=[b_bcp], writes=[b_rec])
                    fw.op("dve", lambda: nc.vector.tensor_tensor(out=ta[:, 0:nq], in0=Osb[0:64, 0, 0:nq], in1=rec[:, 0, 0:nq], op=ALU.mult),
                          reads=[b_Osb, b_rec], writes=[b_ta])
                    fw.op("pool", lambda: nc.gpsimd.tensor_tensor(out=tb[:, 0:nq], in0=Osb[0:64, 1, 0:nq], in1=rec[:, 1, 0:nq], op=ALU.mult),
                          reads=[b_Osb, b_rec], writes=[b_tb])
                    fw.op("dve", lambda: nc.vector.scalar_tensor_tensor(out=td[:, 0:nq], in0=tb[:, 0:nq], scalar=nlam[:, 0:1], in1=ta[:, 0:nq],
                                                                        op0=ALU.mult, op1=ALU.add),
                          reads=[b_tb, b_ta, b_nlam], writes=[b_td])
                    fw.op("pool", lambda: nc.gpsimd.tensor_tensor(out=tq[:, 0:nq], in0=td[:, 0:nq], in1=td[:, 0:nq], op=ALU.mult),
                          reads=[b_td], writes=[b_tq])

                def finalize_stage3(rows, q0, nq):
                    fw.op("pe", lambda: nc.tensor.matmul(bcp[0:64, 0, 0:nq], lhsT=o64[:], rhs=tq[:, 0:nq], start=True, stop=True),
                          reads=[b_o64, b_tq, b_rec], writes=[b_bcp])
                    fw.op("act", lambda: nc.scalar.activation(out=sd[:, 0:nq], in_=bcp[0:64, 0, 0:nq], func=AF.Sqrt, bias=eps_t[:]),
                          reads=[b_bcp, b_eps], writes=[b_sd])
                    fw.op("dve", lambda: nc.vector.reciprocal(out=sd[:, 0:nq], in_=sd[:, 0:nq]), reads=[b_sd], writes=[b_sd])
                    y_, by_ = ys[yi[0] % 2], b_ys[yi[0] % 2]
                    yi[0] += 1
                    fw.op("dve", lambda: nc.vector.scalar_tensor_tensor(out=y_[:, 0:nq], in0=td[:, 0:nq], scalar=gsc[:, 0:1], in1=sd[:, 0:nq],
                                                                        op0=ALU.mult, op1=ALU.mult),
                          reads=[b_td, b_gsc, b_sd], writes=[by_])
                    fw.dma("pool", K.yT[l][6 + cc, rows, q0:q0 + nq], y_[:, 0:nq], reads=[by_], writes=[K.b_yT])

                def flush(step):
                    while deferred and deferred[0][0] <= step:
                        deferred.pop(0)[1]()

                for hh in range(2):
                    h = 2 * cc + hh
                    rows = slice(hh * 64, (hh + 1) * 64)
                    qchunks = [(q0, 512, list(range(NCH))) for q0 in range(0, NLAT, 512)]
                    if not last:
                        qchunks.append((NLAT, NCTX, [64, 65]))
                    qchunks = qchunks[:K.p2n] if K.p2n >= 0 else qchunks[K.p2n:]
                    for (q0, nq, kbs) in qchunks:
                        if q0 == NLAT:
                            flush(10 ** 9)
                            fw.barrier()
                        n = len(kbs)

                        def emit_S(k, si_):
                            S_, bS_ = S[si_ % 2], b_S[si_ % 2]
                            for m in range(2):
                                fw.op("pe", lambda: nc.tensor.matmul(S_[:, m, 0:nq], lhsT=KT[:, kbs[k] * 128:(kbs[k] + 1) * 128],
                                                                     rhs=Qm[hh * 2 + m][:, q0:q0 + nq], start=True, stop=True),
                                      reads=[b_KT, b_Qm[hh * 2 + m]], writes=[bS_], inc=(m == 1))

                        emit_S(0, si)
                        for k in range(n):
                            if k + 1 < n:
                                emit_S(k + 1, si + 1)
                            S_, bS_ = S[si % 2], b_S[si % 2]
                            P_, bP_ = P[si % 2], b_P[si % 2]
                            si += 1
                            fw.op("act", lambda: nc.scalar.activation(out=P_[:, :, 0:nq], in_=S_[:, :, 0:nq], func=AF.Exp, scale=SC),
                                  reads=[bS_], writes=[bP_])
                            for m in range(2):
                                fw.op("pe", lambda: nc.tensor.matmul(O[m][0:65, 0:nq], lhsT=Va[:, kbs[k], h, 0:65], rhs=P_[:, m, 0:nq],
                                                                     start=(k == 0), stop=(k == n - 1)),
                                      reads=[b_Va, bP_], writes=[b_O[m]], inc=(k == n - 1))
                            flush(k)
                        flush(10 ** 9)
                        finalize_stage1(rows, q0, nq)
                        a_ = (rows, q0, nq)
                        deferred.append((3, lambda a_=a_: finalize_stage2(*a_)))
                        deferred.append((8, lambda a_=a_: finalize_stage3(*a_)))
                flush(10 ** 9)
                fw.barrier()


_NA_CACHE = {}


def _na_schedule():
    if "s" in _NA_CACHE:
        return _NA_CACHE["s"]
    kl = np.arange(128)
    k_r, k_c = kl // 64, kl % 64
    q_r, q_c = kl // 64, kl % 64
    c0 = np.clip(q_c - 8, 0, 48)
    classes = {}
    tiles = []
    sched = []
    for i in range(64):
        qrow = 2 * i + q_r
        r0 = np.clip(qrow - 4, 0, 120)
        entry = []
        for j in range(64):
            krow = 2 * j + k_r
            vr = (krow[:, None] >= r0[None, :]) & (krow[:, None] <= r0[None, :] + 7)
            vc = (k_c[:, None] >= c0[None, :]) & (k_c[:, None] <= c0[None, :] + 15)
            valid = vr & vc
            if not valid.any():
                continue
            dr = np.clip(krow[:, None] - qrow[None, :] + 7, 0, 14)
            dc = np.clip(k_c[:, None] - q_c[None, :] + 15, 0, 30)
            entry.append((j - i, dr, dc, valid))
        key = tuple((d, a.tobytes(), b.tobytes(), v.tobytes()) for d, a, b, v in entry)
        if key not in classes:
            classes[key] = len(tiles)
            for d, a, b, v in entry:
                tiles.append((a, b, v))
        sched.append(([i + d for d, _, _, _ in entry], classes[key]))
    dr = np.stack([t[0] for t in tiles]); dc = np.stack([t[1] for t in tiles]); va = np.stack([t[2] for t in tiles])
    _NA_CACHE["s"] = (sched, dr, dc, va)
    return _NA_CACHE["s"]


def _na_bias(na_rpb):
    sched, dr, dc, va = _na_schedule()
    rpb = np.asarray(na_rpb, np.float32)
    g = rpb[:, :, dr, dc]
    return np.ascontiguousarray(np.where(va[None, None], g, np.float32(-30000.0)).astype(np.float32))


def _phase_na(K, l):
    nc, fw = K.nc, K.fw
    last = l == DEPTH - 1
    NCH = T // 128
    SC = 64 ** -0.5
    sched, _, _, _ = _na_schedule()
    NT = K.na_nt
    with contextlib.ExitStack() as st0:
        sb0 = lambda n, s, d: st0.enter_context(nc.sbuf_tensor(K.un(n), s, d))
        Va = sb0("n_Va", [128, NCH, 4, 65], BF16); b_Va = Buf()
        sel = sb0("n_sel", [65, 64], F32); b_sel = Buf()
        fw.dma("sp", sel[:], K.sel65, writes=[b_sel])
        with contextlib.ExitStack() as st1:
            vst = st1.enter_context(nc.sbuf_tensor(K.un("n_vst"), [128, NCH, 256], BF16)); b_vst = Buf()
            for c0 in range(0, NCH, 6):
                fw.dma("sp", vst[:, c0:c0 + 6, :], K.vAll[l][c0 * 128:(c0 + 6) * 128, 0:256].rearrange("(c p) e -> p c e", p=128),
                       reads=[K.b_feat], writes=[b_vst])
            fw.op("pool", lambda: nc.gpsimd.memset(Va[:, :, :, 64:65], 1.0), writes=[b_Va])
            fw.op("dve", lambda: nc.vector.tensor_copy(out=Va[:, :, :, 0:64], in_=vst[:].rearrange("p c (h e) -> p c h e", e=64)),
                  reads=[b_vst], writes=[b_Va])
            fw.barrier()
        for cc in range(2):
            with contextlib.ExitStack() as st:
                sb = lambda n, s, d: st.enter_context(nc.sbuf_tensor(K.un(n), s, d))
                QT = sb("n_QT", [128, T], BF16); b_QT = Buf()
                KT = sb("n_KT", [128, T], BF16); b_KT = Buf()
                E = sb("n_E", [128, NT, 128], F32); b_E = Buf()
                Pf = [sb("n_Pf%d" % i, [128, 5, 128], F32) for i in range(2)]; b_Pf = [Buf(), Buf()]
                Pw = [sb("n_Pw%d" % i, [128, 5, 128], BF16) for i in range(2)]; b_Pw = [Buf(), Buf()]
                Pc = [sb("n_Pc%d" % i, [128, 2, 128], BF16) for i in range(2)]; b_Pc = [Buf(), Buf()]
                Osb = sb("n_Osb", [65, 512], F32); b_Osb = Buf()
                rec = sb("n_rec", [64, 512], F32); b_rec = Buf()
                ys = [sb("n_ys%d" % i, [64, 512], BF16) for i in range(2)]; b_ys = [Buf(), Buf()]
                Sw = [st.enter_context(nc.psum_tensor(K.un("n_Sw%d" % i), [128, 8, 128], F32)) for i in range(2)]; b_Sw = [Buf(), Buf()]
                Sc = [st.enter_context(nc.psum_tensor(K.un("n_Sc%d" % i), [128, 4, 128], F32)) for i in range(2)]; b_Sc = [Buf(), Buf()]
                O = st.enter_context(nc.psum_tensor(K.un("n_O"), [128, 512], F32)); b_O = Buf()
                bcp = st.enter_context(nc.psum_tensor(K.un("n_bcp"), [128, 512], F32)); b_bcp = Buf()
                for c0 in range(0, T, 2112):
                    fw.dma("sp", QT[:, c0:c0 + 2112], K.featB[l][0 + cc, :, c0:c0 + 2112], reads=[K.b_feat], writes=[b_QT])
                    fw.dma("sp", KT[:, c0:c0 + 2112], K.featB[l][2 + cc, :, c0:c0 + 2112], reads=[K.b_feat], writes=[b_KT])
                si = 0
                yi = 0
                for hh in range(2):
                    h = 2 * cc + hh
                    rows = slice(hh * 64, (hh + 1) * 64)
                    for t0 in range(0, NT, 7):
                        t1 = min(NT, t0 + 7)
                        fw.dma("sp", E[:, t0:t1, :], K.nabias[l, h, t0:t1].rearrange("t k q -> k t q"), writes=[b_E])
                    fw.op("act", lambda: nc.scalar.activation(out=E[:], in_=E[:], func=AF.Exp), reads=[b_E], writes=[b_E])
                    qlist = [(i, sched[i][0], sched[i][1]) for i in range(64)]
                    if not last:
                        qlist += [(64, [], 0), (65, [], 0)]
                    qlist = qlist[:K.p2n] if K.p2n >= 0 else qlist[K.p2n:]
                    segs = [[q for q in qlist if q[0] < 64], [q for q in qlist if q[0] >= 64]]
                    for seg in segs:
                        if not seg:
                            continue
                        if seg[0][0] >= 64:
                            fw.barrier()

                        def emit_S(qi, p):
                            i, js, tid0 = seg[qi]
                            nb = len(js)
                            qs = slice(i * 128, (i + 1) * 128)
                            for bi, j in enumerate(js):
                                fw.op("pe", lambda: nc.tensor.matmul(Sw[p][:, bi, :], lhsT=KT[rows, j * 128:(j + 1) * 128], rhs=QT[rows, qs],
                                                                     start=True, stop=True),
                                      reads=[b_KT, b_QT], writes=[b_Sw[p]], inc=(bi == nb - 1))
                            for t in range(2):
                                fw.op("pe", lambda: nc.tensor.matmul(Sc[p][:, t, :], lhsT=KT[rows, (64 + t) * 128:(65 + t) * 128], rhs=QT[rows, qs],
                                                                     start=True, stop=True),
                                      reads=[b_KT, b_QT], writes=[b_Sc[p]], inc=(t == 1))

                        emit_S(0, si % 2)
                        for qi, (i, js, tid0) in enumerate(seg):
                            nb = len(js)
                            p = si % 2
                            si += 1
                            if qi + 1 < len(seg):
                                emit_S(qi + 1, si % 2)
                            gcol = (i % 4) * 128
                            if nb:
                                fw.op("act", lambda: nc.scalar.activation(out=Pf[p][:, 0:nb, :], in_=Sw[p][:, 0:nb, :], func=AF.Exp, scale=SC),
                                      reads=[b_Sw[p]], writes=[b_Pf[p]])
                                fw.op("dve", lambda: nc.vector.tensor_tensor(out=Pw[p][:, 0:nb, :], in0=Pf[p][:, 0:nb, :],
                                                                             in1=E[:, tid0:tid0 + nb, :], op=ALU.mult),
                                      reads=[b_Pf[p], b_E], writes=[b_Pw[p]])
                            fw.op("act", lambda: nc.scalar.activation(out=Pc[p][:], in_=Sc[p][:, 0:2, :], func=AF.Exp, scale=SC),
                                  reads=[b_Sc[p]], writes=[b_Pc[p]])
                            for bi, j in enumerate(js):
                                fw.op("pe", lambda: nc.tensor.matmul(O[0:65, gcol:gcol + 128], lhsT=Va[:, j, h, :], rhs=Pw[p][:, bi, :],
                                                                     start=(bi == 0), stop=False),
                                      reads=[b_Va, b_Pw[p]], writes=[b_O], inc=False)
                            for t in range(2):
                                fw.op("pe", lambda: nc.tensor.matmul(O[0:65, gcol:gcol + 128], lhsT=Va[:, 64 + t, h, :], rhs=Pc[p][:, t, :],
                                                                     start=(nb == 0 and t == 0), stop=(t == 1)),
                                      reads=[b_Va, b_Pc[p]], writes=[b_O], inc=(t == 1))
                            endgrp = (i % 4 == 3) or (i == 65) or (qi == len(seg) - 1)
                            if endgrp:
                                g0 = (i // 4) * 4 if i < 64 else 64
                                nq = (i - g0 + 1) * 128
                                fw.op("act", lambda: nc.scalar.copy(out=Osb[:, 0:nq], in_=O[0:65, 0:nq]), reads=[b_O], writes=[b_Osb])
                                fw.op("pe", lambda: nc.tensor.matmul(bcp[0:64, 0:nq], lhsT=sel[:], rhs=Osb[:, 0:nq], start=True, stop=True),
                                      reads=[b_sel, b_Osb], writes=[b_bcp])
                                fw.op("dve", lambda: nc.vector.reciprocal(out=rec[:, 0:nq], in_=bcp[0:64, 0:nq]), reads=[b_bcp], writes=[b_rec])
                                y_, by_ = ys[yi % 2], b_ys[yi % 2]
                                yi += 1
                                fw.op("dve", lambda: nc.vector.tensor_tensor(out=y_[:, 0:nq], in0=Osb[0:64, 0:nq], in1=rec[:, 0:nq], op=ALU.mult),
                                      reads=[b_Osb, b_rec], writes=[by_])
                                fw.dma("pool", K.yT[l][0 + cc, rows, g0 * 128:g0 * 128 + nq], y_[:, 0:nq], reads=[by_], writes=[K.b_yT])
                fw.barrier()


def _load_cast(K, st, name, dst_tile, b_dst, src_ap, nk, ncols, piece=2048):
    nc, fw = K.nc, K.fw
    with contextlib.ExitStack() as st2:
        wst = [st2.enter_context(nc.sbuf_tensor(K.un("%s_wst%d" % (name, i)), [128, piece], F32)) for i in range(2)]
        b_wst = [Buf(), Buf()]
        it = 0
        for k in range(nk):
            for c0 in range(0, ncols, piece):
                c1 = min(ncols, c0 + piece)
                s_, bs_ = wst[it % 2], b_wst[it % 2]
                fw.dma("sp", s_[:, 0:c1 - c0], src_ap[k * 128:(k + 1) * 128, c0:c1], writes=[bs_])
                e = ("act", "dve", "pool")[it % 3]
                dst = dst_tile[:, k, c0:c1]
                src = s_[:, 0:c1 - c0]
                if e == "act":
                    fw.op(e, lambda: nc.scalar.copy(out=dst, in_=src), reads=[bs_], writes=[b_dst])
                elif e == "dve":
                    fw.op(e, lambda: nc.vector.tensor_copy(out=dst, in_=src), reads=[bs_], writes=[b_dst])
                else:
                    fw.op(e, lambda: nc.gpsimd.tensor_copy(out=dst, in_=src), reads=[bs_], writes=[b_dst])
                it += 1
        fw.barrier()


def _rstd(K, src_ap, reads, junk, b_junk, ss, b_ss, sd, b_sd, rs, b_rs, eps_t, b_eps, n):
    nc, fw = K.nc, K.fw
    fw.op("act", lambda: nc.scalar.activation(out=junk, in_=src_ap, func=AF.Square, accum_out=ss[:]),
          reads=reads, writes=[b_junk, b_ss])
    fw.op("act", lambda: nc.scalar.activation(out=sd[:], in_=ss[:], func=AF.Sqrt, scale=1.0 / n, bias=eps_t[:]),
          reads=[b_ss, b_eps], writes=[b_sd])
    fw.op("dve", lambda: nc.vector.reciprocal(out=rs[:], in_=sd[:]), reads=[b_sd], writes=[b_rs])


def _tok_tiles(l, last):
    tiles = [(0, t * 128) for t in range(NLAT // 128)]
    if not last:
        tiles += [(1, NLAT + t * 128) for t in range(NCTX // 128)]
    return tiles


def _phase_c1(K, l):
    nc, fw = K.nc, K.fw
    last = l == DEPTH - 1
    with contextlib.ExitStack() as st:
        sb = lambda n, s, d: st.enter_context(nc.sbuf_tensor(K.un(n), s, d))
        Wo = sb("c1_W", [128, 8, 1024], BF16); b_W = Buf()
        ident = sb("c1_id", [128, 128], BF16); b_id = Buf()
        idf = sb("c1_idf", [128, 128], F32); b_idf = Buf()
        fw.dma("sp", idf[:], K.ident, writes=[b_idf])
        fw.op("dve", lambda: nc.vector.tensor_copy(out=ident[:], in_=idf[:]), reads=[b_idf], writes=[b_id])
        _load_cast(K, st, "c1", Wo, b_W, K.w_out[l], 8, 1024, piece=1024)
        mt = [sb("c1_mt%d" % i, [128, 1024], F32) for i in range(3)]; b_mt = [Buf() for _ in range(3)]
        yT = [sb("c1_yT%d" % i, [128, 8, 128], BF16) for i in range(3)]; b_yT = [Buf() for _ in range(3)]
        xt = [sb("c1_xt%d" % i, [128, 1024], F32) for i in range(3)]; b_xt = [Buf() for _ in range(3)]
        tmps = [sb("c1_tmp%d" % i, [128, 1024], F32) for i in range(2)]; b_tmps = [Buf(), Buf()]
        xm = [sb("c1_xm%d" % i, [128, 1024], F32) for i in range(3)]; b_xm = [Buf() for _ in range(3)]
        h1s = [sb("c1_h1%d" % i, [128, 1024], F32) for i in range(2)]; b_h1s = [Buf(), Buf()]
        hb = [sb("c1_hb%d" % i, [128, 1024], BF16) for i in range(3)]; b_hb = [Buf() for _ in range(3)]
        hTs = [sb("c1_hT%d" % i, [128, 8, 128], BF16) for i in range(3)]; b_hTs = [Buf() for _ in range(3)]
        junk = sb("c1_junk", [128, 1024], BF16); b_junk = Buf()
        sm = [[sb("c1_s%d_%d" % (a, i), [128, 1], F32) for i in range(3)] for a in range(6)]
        b_sm = [[Buf() for _ in range(3)] for a in range(6)]
        eps_t = sb("c1_eps", [128, 1], F32); b_eps = Buf()
        fw.op("pool", lambda: nc.gpsimd.memset(eps_t[:], EPS), writes=[b_eps])
        yl = [st.enter_context(nc.psum_tensor(K.un("c1_yl%d" % i), [128, 1024], F32)) for i in range(3)]
        b_yl = [Buf() for _ in range(3)]
        tp = [st.enter_context(nc.psum_tensor(K.un("c1_tp%d" % i), [128, 8, 128], BF16)) for i in range(2)]
        b_tp = [Buf(), Buf()]
        cur_j = None
        for ti, (j, tok0) in enumerate(_tok_tiles(l, last)):
            if j != cur_j:
                for i, vi in enumerate((2, 3, 4)):
                    _load_bcast(K, "sp", mt[i], l, vi, j, b_mt[i])
                cur_j = j
            p = ti % 3
            tq_ = ti % 2
            tmp, b_tmp = tmps[tq_], b_tmps[tq_]
            h1, b_h1 = h1s[tq_], b_h1s[tq_]
            lo = tok0 - (NLAT if j else 0)
            fw.dma("sp", yT[p][:], K.yT[l][:, :, tok0:tok0 + 128].rearrange("c p t -> p c t"),
                   reads=[K.b_yT], writes=[b_yT[p]])
            fw.dma("sp", xt[p][:], K.src[l][j][lo:lo + 128, :], reads=[K.b_src[l]], writes=[b_xt[p]])
            for hf in range(2):
                for k in range(8):
                    fw.op("pe", lambda: nc.tensor.matmul(yl[p][:, hf * 512:(hf + 1) * 512], lhsT=yT[p][:, k, :],
                                                         rhs=Wo[:, k, hf * 512:(hf + 1) * 512],
                                                         start=(k == 0), stop=(k == 7)),
                          reads=[b_yT[p], b_W], writes=[b_yl[p]], inc=(hf == 1 and k == 7))
            _rstd(K, yl[p][:], [b_yl[p]], junk[:], b_junk, sm[0][p], b_sm[0][p], sm[1][p], b_sm[1][p],
                  sm[2][p], b_sm[2][p], eps_t, b_eps, D)
            fw.op("dve", lambda: nc.vector.scalar_tensor_tensor(out=tmp[:], in0=yl[p][:], scalar=sm[2][p][:], in1=mt[0][:],
                                                                op0=ALU.mult, op1=ALU.mult),
                  reads=[b_yl[p], b_sm[2][p], b_mt[0]], writes=[b_tmp])
            fw.op("pool", lambda: nc.gpsimd.tensor_tensor(out=xm[p][:], in0=tmp[:], in1=xt[p][:], op=ALU.add),
                  reads=[b_tmp, b_xt[p]], writes=[b_xm[p]])
            fw.dma("pool", K.xmid[tok0:tok0 + 128, :], xm[p][:], reads=[b_xm[p]], writes=[K.b_xmid])
            _rstd(K, xm[p][:], [b_xm[p]], junk[:], b_junk, sm[3][p], b_sm[3][p], sm[4][p], b_sm[4][p],
                  sm[5][p], b_sm[5][p], eps_t, b_eps, D)
            fw.op("dve", lambda: nc.vector.scalar_tensor_tensor(out=h1[:], in0=xm[p][:], scalar=sm[5][p][:], in1=mt[1][:],
                                                                op0=ALU.mult, op1=ALU.mult),
                  reads=[b_xm[p], b_sm[5][p], b_mt[1]], writes=[b_h1])
            fw.op("pool", lambda: nc.gpsimd.tensor_tensor(out=hb[p][:], in0=h1[:], in1=mt[2][:], op=ALU.add),
                  reads=[b_h1, b_mt[2]], writes=[b_hb[p]])
            for k in range(8):
                fw.op("pe", lambda: nc.tensor.transpose(tp[tq_][:, k, :], hb[p][:, k * 128:(k + 1) * 128], ident[:]),
                      reads=[b_hb[p], b_id], writes=[b_tp[tq_]], inc=(k == 7))
            fw.op("act", lambda: nc.scalar.copy(out=hTs[p][:], in_=tp[tq_][:]), reads=[b_tp[tq_]], writes=[b_hTs[p]])
            fw.dma("pool", K.h2T[:, :, tok0:tok0 + 128].rearrange("c p t -> p c t"), hTs[p][:],
                   reads=[b_hTs[p]], writes=[K.b_h2T])
        fw.barrier()


def _phase_c2(K, l):
    nc, fw = K.nc, K.fw
    last = l == DEPTH - 1
    NF = DFF // 128
    with contextlib.ExitStack() as st:
        sb = lambda n, s, d: st.enter_context(nc.sbuf_tensor(K.un(n), s, d))
        W1 = sb("c2_W1", [128, 8, 2 * DFF], BF16); b_W1 = Buf()
        W2 = sb("c2_W2", [128, NF, 1024], BF16); b_W2 = Buf()
        _load_cast(K, st, "c2a", W1, b_W1, K.w_ffn_in[l], 8, 2 * DFF, piece=2816)
        _load_cast(K, st, "c2b", W2, b_W2, K.w_ffn_out[l], NF, 1024, piece=1024)
        gg = sb("c2_gg", [128, 1024], F32); b_gg = Buf()
        hT = [sb("c2_hT%d" % i, [128, 8, 512], BF16) for i in range(2)]; b_hT = [Buf(), Buf()]
        sg = [sb("c2_sg%d" % i, [128, 512], F32) for i in range(2)]; b_sg = [Buf(), Buf()]
        aT = sb("c2_aT", [128, NF, 512], BF16); b_aT = [Buf() for _ in range(NF)]
        xm = [sb("c2_xm%d" % i, [128, 1024], F32) for i in range(2)]; b_xm = [Buf(), Buf()]
        tmp = sb("c2_tmp", [128, 1024], F32); b_tmp = Buf()
        ot = [sb("c2_ot%d" % i, [128, 1024], F32) for i in range(2)]; b_ot = [Buf(), Buf()]
        junk = sb("c2_junk", [128, 1024], BF16); b_junk = Buf()
        sm = [[sb("c2_s%d_%d" % (a, i), [128, 1], F32) for i in range(2)] for a in range(3)]
        b_sm = [[Buf(), Buf()] for a in range(3)]
        eps_t = sb("c2_eps", [128, 1], F32); b_eps = Buf()
        fw.op("pool", lambda: nc.gpsimd.memset(eps_t[:], EPS), writes=[b_eps])
        gu = [st.enter_context(nc.psum_tensor(K.un("c2_gu%d" % i), [128, 2, 512], F32)) for i in range(2)]
        b_gu = [Buf(), Buf()]
        fo = [st.enter_context(nc.psum_tensor(K.un("c2_fo%d" % i), [128, 1024], F32)) for i in range(2)]
        b_fo = [Buf(), Buf()]
        tiles = _tok_tiles(l, last)
        groups = [tiles[i:i + 4] for i in range(0, NLAT // 128, 4)]
        if not last:
            groups.append(tiles[NLAT // 128:])
        cur_j = None
        gi = 0
        si = 0
        for bi, grp in enumerate(groups):
            j, tok0 = grp[0]
            nsub = len(grp)
            ntok = nsub * 128
            if j != cur_j:
                _load_bcast(K, "sp", gg, l, 5, j, b_gg)
                cur_j = j
            hT_, bhT_ = hT[bi % 2], b_hT[bi % 2]
            fw.dma("sp", hT_[:, :, 0:ntok], K.h2T[:, :, tok0:tok0 + ntok].rearrange("c p t -> p c t"),
                   reads=[K.b_h2T], writes=[bhT_])
            for c in range(NF):
                g_, bg_ = gu[gi % 2], b_gu[gi % 2]
                s_, bs_ = sg[gi % 2], b_sg[gi % 2]
                gi += 1
                for half in range(2):
                    col0 = half * DFF + c * 128
                    for k in range(8):
                        fw.op("pe", lambda: nc.tensor.matmul(g_[:, half, 0:ntok], lhsT=W1[:, k, col0:col0 + 128],
                                                             rhs=hT_[:, k, 0:ntok], start=(k == 0), stop=(k == 7)),
                              reads=[b_W1, bhT_], writes=[bg_], inc=(half == 1 and k == 7))
                fw.op("act", lambda: nc.scalar.activation(out=s_[:, 0:ntok], in_=g_[:, 0, 0:ntok], func=AF.Silu),
                      reads=[bg_], writes=[bs_])
                fw.op("dve", lambda: nc.vector.tensor_tensor(out=aT[:, c, 0:ntok], in0=g_[:, 1, 0:ntok], in1=s_[:, 0:ntok], op=ALU.mult),
                      reads=[bg_, bs_], writes=[b_aT[c]])
            for sub in range(nsub):
                p = si % 2
                si += 1
                t0 = tok0 + sub * 128
                fw.dma("sp", xm[p][:], K.xmid[t0:t0 + 128, :], reads=[K.b_xmid], writes=[b_xm[p]])
                for hf in range(2):
                    for c in range(NF):
                        fw.op("pe", lambda: nc.tensor.matmul(fo[p][:, hf * 512:(hf + 1) * 512],
                                                             lhsT=aT[:, c, sub * 128:(sub + 1) * 128],
                                                             rhs=W2[:, c, hf * 512:(hf + 1) * 512],
                                                             start=(c == 0), stop=(c == NF - 1)),
                              reads=[b_aT[c], b_W2], writes=[b_fo[p]], inc=(hf == 1 and c == NF - 1))
                _rstd(K, fo[p][:], [b_fo[p]], junk[:], b_junk, sm[0][p], b_sm[0][p], sm[1][p], b_sm[1][p],
                      sm[2][p], b_sm[2][p], eps_t, b_eps, D)
                fw.op("dve", lambda: nc.vector.scalar_tensor_tensor(out=tmp[:], in0=fo[p][:], scalar=sm[2][p][:], in1=gg[:],
                                                                    op0=ALU.mult, op1=ALU.mult),
                      reads=[b_fo[p], b_sm[2][p], b_gg], writes=[b_tmp])
                fw.op("pool", lambda: nc.gpsimd.tensor_tensor(out=ot[p][:], in0=tmp[:], in1=xm[p][:], op=ALU.add),
                      reads=[b_tmp, b_xm[p]], writes=[b_ot[p]])
                if last:
                    dst, bd = K.out[t0:t0 + 128, :], K.b_out
                else:
                    dst, bd = K.xl1[t0:t0 + 128, :], K.b_src[l + 1]
                fw.dma("pool", dst, ot[p][:], reads=[b_ot[p]], writes=[bd])
        fw.barrier()


def build(dbg=None):
    nc = bass.Bass("TRN2", target_bir_lowering=False)
    K = Ctx()
    K.nc = nc
    dt_in = lambda n, s, d=F32: nc.dram_tensor(n, s, d, kind="ExternalInput").ap()
    K.x = dt_in("x", [NLAT, D])
    K.ctx = dt_in("ctx", [NCTX, D])
    K.cvec = dt_in("cvec", [2, D])
    K.w_mod = dt_in("w_mod", [DEPTH, D, 6 * D])
    K.b_mod = dt_in("b_mod", [DEPTH, 6 * D])
    K.gains = dt_in("gains", [4, DEPTH, D])
    K.w_in = dt_in("w_in", [DEPTH, D, NWCOL])
    K.rope = dt_in("rope", [6, 128, 192])
    K.ident = dt_in("ident", [128, 128])
    K.w_out = dt_in("w_out", [DEPTH, D, D])
    K.w_ffn_in = dt_in("w_ffn_in", [DEPTH, D, 2 * DFF])
    K.w_ffn_out = dt_in("w_ffn_out", [DEPTH, DFF, D])
    K.lru_conv_w = dt_in("lru_conv_w", [DEPTH, 4, 256])
    K.lru_conv_b = dt_in("lru_conv_b", [DEPTH, 256])
    K.lru_gate_w = dt_in("lru_gate_w", [DEPTH, 2, 2, 4, 64, 64])
    K.lru_gate_b = dt_in("lru_gate_b", [DEPTH, 2, 2, 4, 64])
    K.lru_lambda = dt_in("lru_lambda", [DEPTH, 2, 256])
    K.ret_decay = dt_in("ret_decay", [DEPTH, 2, 4])
    K.retc = dt_in("retc", [128, 6, 128])
    K.retcv = dt_in("retcv", [128, 2])
    K.blk64 = dt_in("blk64", [128, 128])
    K.diff_lambda = dt_in("diff_lambda", [DEPTH, 4, 32])
    K.diff_subln = dt_in("diff_subln", [DEPTH, 64])
    K.sel65 = dt_in("sel65", [65, 64])
    K.ones64 = dt_in("ones64", [64, 64])
    K.mcol = dt_in("mcol", [128, 4])
    K.na_nt = _na_schedule()[1].shape[0]
    K.nabias = dt_in("nabias", [DEPTH, 4, K.na_nt, 128, 128])
    phases = dbg["phases"] if dbg else None
    K.stop = dbg.get("stop", 99) if dbg else 99
    K.p1n = dbg.get("p1n", 999) if dbg else 999
    K.p2n = dbg.get("p2n", 999) if dbg else 999
    ext_in = dbg.get("ext_in", set()) if dbg else set()
    ext_out = dbg.get("ext_out", set()) if dbg else set()

    def scr(n, s, d):
        kind = "Internal"
        if n in ext_in:
            kind = "ExternalInput"
        elif n in ext_out:
            kind = "ExternalOutput"
        return nc.dram_tensor(n, s, d, kind=kind).ap()

    K.modvec = scr("modvec", [DEPTH, 6, 2, D], F32); K.b_modvec = Buf()
    K.featB = [scr("featB%d" % l, [12, 128, T], BF16) for l in range(DEPTH)]
    K.featF = [scr("featF%d" % l, [6, 128, T], F32) for l in range(DEPTH)]
    K.vAll = [scr("vAll%d" % l, [T, 768], BF16) for l in range(DEPTH)]
    K.b_feat = Buf()
    K.yT = [scr("yT%d" % l, [8, 128, T], BF16) for l in range(DEPTH)]; K.b_yT = Buf()
    K.xmid = scr("xmid", [T, D], F32); K.b_xmid = Buf()
    K.h2T = scr("h2T", [8, 128, T], BF16); K.b_h2T = Buf()
    K.xl1 = scr("xl1", [T, D], F32)
    K.out = nc.dram_tensor("out", [NLAT, D], F32, kind="ExternalOutput").ap(); K.b_out = Buf()
    K.src = [(K.x, K.ctx), (K.xl1[0:NLAT, :], K.xl1[NLAT:T, :])]
    K.b_src = [Buf(), Buf()]
    on = lambda name: (phases is None) or (name in phases)
    with contextlib.ExitStack() as st:
        K.fw = FW(nc, st)
        def ph(name, fn, *a):
            if on(name):
                with nc.named_scope(name):
                    fn(K, *a)

        ph("P0", _p0_modvec)
        for l in range(DEPTH):
            ph("A%d" % l, _phase_a, l)
            ph("NA%d" % l, _phase_na, l)
            ph("LRU%d" % l, _phase_lru, l)
            ph("RET%d" % l, _phase_ret, l)
            ph("DIFF%d" % l, _phase_diff, l)
            if on("C%d" % l):
                with nc.named_scope("C1_%d" % l):
                    _phase_c1(K, l)
                with nc.named_scope("C2_%d" % l):
                    _phase_c2(K, l)
        K.fw.finish("sp")
        K.n_inst, K.n_wait = K.fw.n_inst, K.fw.n_wait
    return nc, K


def host_inputs(inputs, b):
    f = lambda a: np.ascontiguousarray(np.asarray(a, dtype=np.float32))
    perm = _w_in_perm()
    m = {
        "x": f(inputs["x"][b]),
        "ctx": f(inputs["ctx"][b]),
        "cvec": f(np.stack([np.asarray(inputs["c"])[b], np.asarray(inputs["c_ctx"])], 0)),
        "w_mod": f(inputs["w_mod"]),
        "b_mod": f(inputs["b_mod"]),
        "gains": f(np.stack([inputs["g_pre_mix"], inputs["g_post_mix"], inputs["g_pre_ffn"], inputs["g_post_ffn"]], 0)),
        "w_in": f(np.asarray(inputs["w_in"])[:, :, perm]),
        "rope": _rope_tables(),
        "ident": np.eye(128, dtype=np.float32),
        "w_out": f(inputs["w_out"]),
        "w_ffn_in": f(inputs["w_ffn_in"]),
        "w_ffn_out": f(inputs["w_ffn_out"]),
        "lru_conv_w": f(inputs["lru_conv_w"]),
        "lru_conv_b": f(inputs["lru_conv_b"]),
        "lru_gate_w": f(inputs["lru_gate_w"]),
        "lru_gate_b": f(inputs["lru_gate_b"]),
        "lru_lambda": f(inputs["lru_lambda"]),
        "ret_decay": f(inputs["ret_decay"]),
        "retc": _ret_consts()[0],
        "retcv": _ret_consts()[1],
        "blk64": _blk64(),
        "diff_lambda": f(inputs["diff_lambda"]),
        "diff_subln": f(inputs["diff_subln"]),
        "sel65": _diff_consts()[0],
        "ones64": _diff_consts()[1],
        "mcol": _diff_consts()[2],
        "nabias": _na_bias(inputs["na_rpb"]),
    }
    return m


N_CORES = 4


def kernel(**inputs):
    nc, K = build()
    in_maps = [host_inputs(inputs, c) for c in range(N_CORES)]
    res = run_bass_kernel_spmd(nc, in_maps, core_ids=list(range(N_CORES)))
    out = np.stack([np.asarray(res.results[b]["out"], dtype=np.float32) for b in range(4)], 0)
    return out
```

```python
import contextlib
import math
import numpy as np
import concourse.bass as bass
import concourse.mybir as mybir
from concourse.bass_utils import run_bass_kernel_spmd

F32 = mybir.dt.float32
BF16 = mybir.dt.bfloat16
AF = mybir.ActivationFunctionType
ALU = mybir.AluOpType
AX = mybir.AxisListType

D = 1024
NLAT = 8192
NCTX = 256
T = NLAT + NCTX
DEPTH = 2
DFF = 2816
EPS = 1e-6
NWCOL = 4096
GRID_W = 64


class Buf:
    __slots__ = ("w", "r", "name")

    def __init__(self, name=""):
        self.w = None
        self.r = []
        self.name = name


class FW:
    N_DMA_SEMS = 24

    def __init__(self, nc, stack):
        self.nc = nc
        self.eng = {"pe": nc.tensor, "act": nc.scalar, "dve": nc.vector, "pool": nc.gpsimd, "sp": nc.sync}
        self.sems = {}
        self.count = {}
        for e in self.eng:
            self.sems[e] = stack.enter_context(nc.semaphore("s_" + e))
            self.count[e] = 0
        self.dma_sems = {"hw": [], "sw": []}
        for kind, n in (("hw", 24), ("sw", 24)):
            for i in range(n):
                k = "d%s%d" % (kind, i)
                self.sems[k] = stack.enter_context(nc.semaphore("s_" + k))
                self.count[k] = 0
                self.dma_sems[kind].append(k)
        self.dma_rr = {"hw": 0, "sw": 0}
        self.waited = {e: {} for e in self.eng}
        self.pending_pe = []
        self.n_inst = 0
        self.n_wait = 0

    def _wait(self, e, ev):
        if ev is None:
            return
        k, v = ev
        if e == "pe" and k == "pe":
            return
        if self.waited[e].get(k, 0) >= v:
            return
        self.eng[e].wait_ge(self.sems[k], v)
        self.waited[e][k] = v
        self.n_wait += 1

    def _deps(self, e, reads, writes):
        for b in reads:
            self._wait(e, b.w)
        for b in writes:
            self._wait(e, b.w)
            for ev in b.r:
                self._wait(e, ev)

    def _mark(self, ev, reads, writes):
        for b in reads:
            b.r.append(ev)
            if len(b.r) > 40:
                best = {}
                for k, v in b.r:
                    if best.get(k, 0) < v:
                        best[k] = v
                b.r = list(best.items())
        for b in writes:
            b.w = ev
            b.r = []

    def op(self, e, fn, reads=(), writes=(), inc=True):
        self._deps(e, reads, writes)
        inst = fn()
        self.n_inst += 1
        if e == "pe" and not inc:
            self.pending_pe.append((tuple(reads), tuple(writes)))
            return inst
        self.count[e] += 1
        ev = (e, self.count[e])
        inst.then_inc(self.sems[e], 1)
        if e == "pe" and self.pending_pe:
            for r, w in self.pending_pe:
                self._mark(ev, r, w)
            self.pending_pe = []
        self._mark(ev, reads, writes)
        return inst

    def dma(self, q, out, in_, reads=(), writes=(), **kw):
        kind = "sw" if q == "pool" else "hw"
        k = self.dma_sems[kind][self.dma_rr[kind]]
        self.dma_rr[kind] = (self.dma_rr[kind] + 1) % len(self.dma_sems[kind])
        self._wait(q, (k, self.count[k]))
        self._deps(q, reads, writes)
        inst = self.eng[q].dma_start(out=out, in_=in_, **kw)
        self.count[k] += 16
        inst.then_inc(self.sems[k], 16)
        ev = (k, self.count[k])
        self._mark(ev, reads, writes)
        self.n_inst += 1
        return ev

    def barrier(self):
        for e in self.eng:
            for k in self.sems:
                if k != e and self.count[k] > 0:
                    self._wait(e, (k, self.count[k]))

    def finish(self, e="sp"):
        for k in self.sems:
            if self.count[k] > 0:
                self._wait(e, (k, self.count[k]))


class Ctx:
    _n = 0

    def un(self, name):
        Ctx._n += 1
        return "%s_%d" % (name, Ctx._n)


def _w_in_perm():
    def split(i):
        return np.arange(i * 256, (i + 1) * 256)

    def swap_ret(cols):
        c = cols.reshape(4, 64)
        return np.concatenate([c[:, 32:], c[:, :32]], axis=1).reshape(-1)

    def swap_diff(cols):
        c = cols.reshape(4, 2, 32)
        return np.concatenate([c[:, :, 16:], c[:, :, :16]], axis=2).reshape(-1)

    order = [split(0), split(1), split(3), split(4), split(8),
             split(5), swap_ret(split(5)), split(6), swap_ret(split(6)),
             split(9), swap_diff(split(9)), split(10), swap_diff(split(10)),
             split(2), split(7), split(11)]
    return np.concatenate(order)


def _rope_tables():
    tabs = np.zeros((6, 128, 192), np.float64)
    rows = np.arange(128, dtype=np.float64)
    cols = np.arange(64, dtype=np.float64)
    for p in range(128):
        d = p % 64
        j = d % 32
        sign = -1.0 if d < 32 else 1.0
        inv = 10000.0 ** (-(j % 16) / 16.0)
        if j < 16:
            ang, sl = rows * np.float32(inv), slice(0, 128)
        else:
            ang, sl = cols * np.float32(inv), slice(128, 192)
        tabs[0, p, sl] = np.cos(ang)
        tabs[1, p, sl] = sign * np.sin(ang)
        tabs[2, p, sl] = np.cos(ang) * 0.125
        tabs[3, p, sl] = sign * np.sin(ang) * 0.125
        d = p % 32
        j = d % 16
        sign = -1.0 if d < 16 else 1.0
        inv = 10000.0 ** (-(j % 8) / 8.0)
        if j < 8:
            ang, sl = rows * np.float32(inv), slice(0, 128)
        else:
            ang, sl = cols * np.float32(inv), slice(128, 192)
        tabs[4, p, sl] = np.cos(ang)
        tabs[5, p, sl] = sign * np.sin(ang)
    return tabs.astype(np.float32)


def _p0_modvec(K):
    nc, fw = K.nc, K.fw
    with contextlib.ExitStack() as st:
        sb = lambda n, s, d: st.enter_context(nc.sbuf_tensor(K.un(n), s, d))
        cT = sb("p0_cT", [128, 2, 8], F32); b_cT = Buf()
        sT = sb("p0_sT", [128, 2, 8], F32); b_sT = Buf()
        wm = [sb("p0_wm%d" % i, [128, 8, 512], F32) for i in range(2)]; b_wm = [Buf(), Buf()]
        bm = sb("p0_bm", [2, 6144], F32); b_bm = Buf()
        gn = sb("p0_gn", [2, 4, 1024], F32); b_gn = Buf()
        mv = sb("p0_mv", [2, 6144], F32); b_mv = Buf()
        cv = sb("p0_cv", [2, 6, 1024], F32); b_cv = Buf()
        ps = [st.enter_context(nc.psum_tensor(K.un("p0_ps%d" % i), [2, 512], F32)) for i in range(2)]
        b_ps = [Buf(), Buf()]
        for j in range(2):
            fw.dma("sp", cT[:, j, :], K.cvec[j, :].rearrange("(k p) -> p k", p=128), writes=[b_cT],
                   allow_slow_non_contiguous=True)
        fw.op("act", lambda: nc.scalar.activation(out=sT[:], in_=cT[:], func=AF.Silu), reads=[b_cT], writes=[b_sT])
        it = 0
        for l in range(DEPTH):
            fw.dma("sp", bm[:], K.b_mod[l:l + 1, :].broadcast_to([2, 6144]), writes=[b_bm])
            fw.dma("sp", gn[:], K.gains[:, l, :].unsqueeze(0).broadcast_to([2, 4, 1024]), writes=[b_gn])
            for n in range(12):
                w_, bw_ = wm[it % 2], b_wm[it % 2]
                p_, bp_ = ps[it % 2], b_ps[it % 2]
                it += 1
                src = K.w_mod[l, :, n * 512:(n + 1) * 512].rearrange("(k p) c -> p k c", p=128)
                fw.dma("sp", w_[:, 0:4, :], src[:, 0:4, :], writes=[bw_])
                fw.dma("sp", w_[:, 4:8, :], src[:, 4:8, :], writes=[bw_])
                for k in range(8):
                    fw.op("pe", lambda: nc.tensor.matmul(p_[:], lhsT=sT[:, :, k], rhs=w_[:, k, :],
                                                         start=(k == 0), stop=(k == 7)),
                          reads=[b_sT, bw_], writes=[bp_], inc=(k == 7))
                fw.op("dve", lambda: nc.vector.tensor_tensor(out=mv[:, n * 512:(n + 1) * 512], in0=p_[:],
                                                             in1=bm[:, n * 512:(n + 1) * 512], op=ALU.add),
                      reads=[bp_, b_bm], writes=[b_mv])
            m = lambda i: mv[:, i * 1024:(i + 1) * 1024]
            R, W_ = [b_mv, b_gn], [b_cv]
            fw.op("dve", lambda: nc.vector.scalar_tensor_tensor(out=cv[:, 0, :], in0=m(1), scalar=1.0, in1=gn[:, 0, :],
                                                                op0=ALU.add, op1=ALU.mult), reads=R, writes=W_)
            fw.op("dve", lambda: nc.vector.tensor_copy(out=cv[:, 1, :], in_=m(0)), reads=R, writes=W_)
            fw.op("dve", lambda: nc.vector.tensor_tensor(out=cv[:, 2, :], in0=m(2), in1=gn[:, 1, :], op=ALU.mult),
                  reads=R, writes=W_)
            fw.op("dve", lambda: nc.vector.scalar_tensor_tensor(out=cv[:, 3, :], in0=m(4), scalar=1.0, in1=gn[:, 2, :],
                                                                op0=ALU.add, op1=ALU.mult), reads=R, writes=W_)
            fw.op("dve", lambda: nc.vector.tensor_copy(out=cv[:, 4, :], in_=m(3)), reads=R, writes=W_)
            fw.op("dve", lambda: nc.vector.tensor_tensor(out=cv[:, 5, :], in0=m(5), in1=gn[:, 3, :], op=ALU.mult),
                  reads=R, writes=W_)
            fw.dma("pool", K.modvec[l].rearrange("i j f -> j i f"), cv[:], reads=[b_cv], writes=[K.b_modvec])
        fw.barrier()


def _load_bcast(K, q, tile, l, i, j, buf):
    K.fw.dma(q, tile[:], K.modvec[l, i, j:j + 1, :].broadcast_to([128, 1024]), reads=[K.b_modvec], writes=[buf])


def _phase_a(K, l):
    nc, fw = K.nc, K.fw
    with contextlib.ExitStack() as st:
        sb = lambda n, s, d: st.enter_context(nc.sbuf_tensor(K.un(n), s, d))
        W = sb("a_W", [128, 8, NWCOL], BF16); b_W = Buf()
        ident = sb("a_id", [128, 128], BF16); b_id = Buf()
        with contextlib.ExitStack() as st2:
            wst = [st2.enter_context(nc.sbuf_tensor(K.un("a_wst%d" % i), [128, 2048], F32)) for i in range(2)]
            b_wst = [Buf(), Buf()]
            idf = st2.enter_context(nc.sbuf_tensor(K.un("a_idf"), [128, 128], F32)); b_idf = Buf()
            fw.dma("sp", idf[:], K.ident, writes=[b_idf])
            fw.op("dve", lambda: nc.vector.tensor_copy(out=ident[:], in_=idf[:]), reads=[b_idf], writes=[b_id])
            it = 0
            for k in range(8):
                for hf in range(2):
                    s_, bs_ = wst[it % 2], b_wst[it % 2]
                    fw.dma("sp", s_[:], K.w_in[l, k * 128:(k + 1) * 128, hf * 2048:(hf + 1) * 2048], writes=[bs_])
                    e = ("act", "dve", "pool")[it % 3]
                    dst = W[:, k, hf * 2048:(hf + 1) * 2048]
                    if e == "act":
                        fw.op(e, lambda: nc.scalar.copy(out=dst, in_=s_[:]), reads=[bs_], writes=[b_W])
                    elif e == "dve":
                        fw.op(e, lambda: nc.vector.tensor_copy(out=dst, in_=s_[:]), reads=[bs_], writes=[b_W])
                    else:
                        fw.op(e, lambda: nc.gpsimd.tensor_copy(out=dst, in_=s_[:]), reads=[bs_], writes=[b_W])
                    it += 1
            fw.barrier()
        gm = [sb("a_gm%d" % j, [128, 1024], F32) for j in range(2)]; b_gm = [Buf(), Buf()]
        sh = [sb("a_sh%d" % j, [128, 1024], F32) for j in range(2)]; b_sh = [Buf(), Buf()]
        for j in range(2):
            _load_bcast(K, "sp", gm[j], l, 0, j, b_gm[j])
            _load_bcast(K, "sp", sh[j], l, 1, j, b_sh[j])
        rtab = sb("a_rtab", [128, 6, 192], F32); b_rtab = Buf()
        fw.dma("sp", rtab[:], K.rope.rearrange("s p c -> p s c"), writes=[b_rtab])
        rope = sb("a_rope", [128, 6, 512], F32); b_rope = Buf()
        xt = [sb("a_xt%d" % i, [128, 1024], F32) for i in range(2)]; b_xt = [Buf(), Buf()]
        junk = sb("a_junk", [128, 1024], BF16); b_junk = Buf()
        ss = [sb("a_ss%d" % i, [128, 1], F32) for i in range(2)]; b_ss = [Buf(), Buf()]
        sd = [sb("a_sd%d" % i, [128, 1], F32) for i in range(2)]; b_sd = [Buf(), Buf()]
        rs = [sb("a_rs%d" % i, [128, 1], F32) for i in range(2)]; b_rs = [Buf(), Buf()]
        h1 = sb("a_h1", [128, 1024], F32); b_h1 = Buf()
        hb = [sb("a_hb%d" % i, [128, 1024], BF16) for i in range(2)]; b_hb = [Buf(), Buf()]
        hT = [sb("a_hT%d" % i, [128, 8, 512], BF16) for i in range(2)]; b_hT = [Buf(), Buf()]
        stB = sb("a_stB", [128, 12, 512], BF16); b_stB = [Buf() for _ in range(12)]
        stF = sb("a_stF", [128, 6, 512], F32); b_stF = [Buf() for _ in range(6)]
        stV = sb("a_stV", [128, 4, 768], BF16); b_stV = [Buf() for _ in range(4)]
        t1 = [sb("a_t1%d" % i, [128, 512], F32) for i in range(2)]; b_t1 = [Buf(), Buf()]
        t2 = [sb("a_t2%d" % i, [128, 512], F32) for i in range(2)]; b_t2 = [Buf(), Buf()]
        tp = [st.enter_context(nc.psum_tensor(K.un("a_tp%d" % i), [128, 8, 128], BF16)) for i in range(2)]
        b_tp = [Buf(), Buf()]
        mm = [st.enter_context(nc.psum_tensor(K.un("a_mm%d" % i), [128, 512], F32)) for i in range(6)]
        b_mm = [Buf() for _ in range(6)]
        mmi = [0]
        eps_t = sb("a_eps", [128, 1], F32); b_eps = Buf()
        fw.op("pool", lambda: nc.gpsimd.memset(eps_t[:], EPS), writes=[b_eps])

        def next_mm():
            i = mmi[0] % 6
            mmi[0] += 1
            return mm[i], b_mm[i]

        blocks = [(K.src[l][0], b * 512, 512, 0) for b in range(NLAT // 512)]
        blocks.append((K.src[l][1], NLAT, NCTX, 1))
        sub_c = [0]

        def prep(bi):
            src, tok0, ntok, j = blocks[bi]
            nsub = ntok // 128
            hT_, bhT_ = hT[bi % 2], b_hT[bi % 2]
            if j == 0:
                r0 = tok0 // GRID_W
                nr = ntok // GRID_W
                for s in range(6):
                    o = rope[:, s, 0:ntok].rearrange("p (r c) -> p r c", c=GRID_W)
                    a = rtab[:, s, r0:r0 + nr].unsqueeze(2).broadcast_to([128, nr, GRID_W])
                    b_ = rtab[:, s, 128:192].unsqueeze(1).broadcast_to([128, nr, GRID_W])
                    fw.op("pool", lambda: nc.gpsimd.tensor_tensor(out=o, in0=a, in1=b_, op=ALU.add),
                          reads=[b_rtab], writes=[b_rope])
            else:
                for s, val in enumerate((1.0, 0.0, 0.125, 0.0, 1.0, 0.0)):
                    fw.op("pool", lambda: nc.gpsimd.memset(rope[:, s, :], val), writes=[b_rope])
            for s_ in range(nsub):
                sub_i = sub_c[0]
                x_, bx_ = xt[sub_i % 2], b_xt[sub_i % 2]
                ss_, bss_ = ss[sub_i % 2], b_ss[sub_i % 2]
                sd_, bsd_ = sd[sub_i % 2], b_sd[sub_i % 2]
                rs_, brs_ = rs[sub_i % 2], b_rs[sub_i % 2]
                hb_, bhb_ = hb[sub_i % 2], b_hb[sub_i % 2]
                tp_, btp_ = tp[sub_i % 2], b_tp[sub_i % 2]
                sub_c[0] += 1
                lo = (tok0 - (NLAT if j else 0)) + s_ * 128
                fw.dma("sp", x_[:], src[lo:lo + 128, :], reads=[K.b_src[l]], writes=[bx_])
                fw.op("act", lambda: nc.scalar.activation(out=junk[:], in_=x_[:], func=AF.Square, accum_out=ss_[:]),
                      reads=[bx_], writes=[b_junk, bss_])
                fw.op("act", lambda: nc.scalar.activation(out=sd_[:], in_=ss_[:], func=AF.Sqrt, scale=1.0 / D,
                                                          bias=eps_t[:]), reads=[bss_, b_eps], writes=[bsd_])
                fw.op("dve", lambda: nc.vector.reciprocal(out=rs_[:], in_=sd_[:]), reads=[bsd_], writes=[brs_])
                fw.op("dve", lambda: nc.vector.scalar_tensor_tensor(out=h1[:], in0=x_[:], scalar=rs_[:], in1=gm[j][:],
                                                                    op0=ALU.mult, op1=ALU.mult),
                      reads=[bx_, brs_, b_gm[j]], writes=[b_h1])
                fw.op("pool", lambda: nc.gpsimd.tensor_tensor(out=hb_[:], in0=h1[:], in1=sh[j][:], op=ALU.add),
                      reads=[b_h1, b_sh[j]], writes=[bhb_])
                for k in range(8):
                    fw.op("pe", lambda: nc.tensor.transpose(tp_[:, k, :], hb_[:, k * 128:(k + 1) * 128], ident[:]),
                          reads=[bhb_, b_id], writes=[btp_], inc=(k == 7))
                fw.op("dve", lambda: nc.vector.tensor_copy(out=hT_[:, :, s_ * 128:(s_ + 1) * 128], in_=tp_[:]),
                      reads=[btp_], writes=[bhT_])

        prep(0)
        for bi, (src, tok0, ntok, j) in enumerate(blocks):
            nsub = ntok // 128
            hT_, bhT_ = hT[bi % 2], b_hT[bi % 2]

            def fm(c):
                p_, bp_ = next_mm()
                for k in range(8):
                    fw.op("pe", lambda: nc.tensor.matmul(p_[:, 0:ntok], lhsT=W[:, k, c * 128:(c + 1) * 128],
                                                         rhs=hT_[:, k, 0:ntok], start=(k == 0), stop=(k == 7)),
                          reads=[b_W, bhT_], writes=[bp_], inc=(k == 7))
                return p_, bp_

            for c in range(4):
                p_, bp_ = fm(c)
                fw.op("act", lambda: nc.scalar.copy(out=stB[:, c, 0:ntok], in_=p_[:, 0:ntok]),
                      reads=[bp_], writes=[b_stB[c]])
            for c in range(4, 10):
                p_, bp_ = fm(c)
                fw.op("act", lambda: nc.scalar.copy(out=stF[:, c - 4, 0:ntok], in_=p_[:, 0:ntok]),
                      reads=[bp_], writes=[b_stF[c - 4]])
            ri = 0
            for g, (c0, tabs, o0) in enumerate(((10, (0, 1), 4), (14, (2, 3), 6), (18, (4, 5), 8), (22, (4, 5), 10))):
                for hh in range(2):
                    pa, bpa = fm(c0 + hh)
                    pb, bpb = fm(c0 + 2 + hh)
                    t1_, bt1_ = t1[ri % 2], b_t1[ri % 2]
                    t2_, bt2_ = t2[ri % 2], b_t2[ri % 2]
                    ri += 1
                    fw.op("dve", lambda: nc.vector.tensor_tensor(out=t1_[:, 0:ntok], in0=pa[:, 0:ntok],
                                                                 in1=rope[:, tabs[0], 0:ntok], op=ALU.mult),
                          reads=[bpa, b_rope], writes=[bt1_])
                    fw.op("dve", lambda: nc.vector.tensor_tensor(out=t2_[:, 0:ntok], in0=pb[:, 0:ntok],
                                                                 in1=rope[:, tabs[1], 0:ntok], op=ALU.mult),
                          reads=[bpb, b_rope], writes=[bt2_])
                    fw.op("pool", lambda: nc.gpsimd.tensor_tensor(out=stB[:, o0 + hh, 0:ntok], in0=t1_[:, 0:ntok],
                                                                  in1=t2_[:, 0:ntok], op=ALU.add),
                          reads=[bt1_, bt2_], writes=[b_stB[o0 + hh]])
            fw.dma("pool", K.featB[l][:, :, tok0:tok0 + ntok].rearrange("c p t -> p c t"), stB[:, :, 0:ntok],
                   reads=b_stB, writes=[K.b_feat])
            fw.dma("pool", K.featF[l][:, :, tok0:tok0 + ntok].rearrange("c p t -> p c t"), stF[:, :, 0:ntok],
                   reads=b_stF, writes=[K.b_feat])
            if bi + 1 < len(blocks):
                prep(bi + 1)
            for s_ in range(nsub):
                for (n0, n1) in ((0, 512), (512, 768)):
                    p_, bp_ = next_mm()
                    for k in range(8):
                        fw.op("pe", lambda: nc.tensor.matmul(p_[:, 0:n1 - n0], lhsT=hT_[:, k, s_ * 128:(s_ + 1) * 128],
                                                             rhs=W[:, k, 3328 + n0:3328 + n1],
                                                             start=(k == 0), stop=(k == 7)),
                              reads=[b_W, bhT_], writes=[bp_], inc=(k == 7))
                    fw.op("act", lambda: nc.scalar.copy(out=stV[:, s_, n0:n1], in_=p_[:, 0:n1 - n0]),
                          reads=[bp_], writes=[b_stV[s_]])
            fw.dma("pool", K.vAll[l][tok0:tok0 + ntok, :].rearrange("(s p) c -> p s c", p=128), stV[:, 0:nsub, :],
                   reads=b_stV[0:nsub], writes=[K.b_feat])
        fw.barrier()


def _phase_lru(K, l):
    nc, fw = K.nc, K.fw
    CH = 2048
    for cc in range(2):
        with contextlib.ExitStack() as st:
            sb = lambda n, s, d: st.enter_context(nc.sbuf_tensor(K.un(n), s, d))
            A = sb("l_A", [128, T], F32); b_A = Buf()
            B = sb("l_B", [128, T], F32); b_B = Buf()
            Bb = sb("l_Bb", [128, T], BF16); b_Bb = Buf()
            C = sb("l_C", [128, T], F32); b_C = Buf()
            Dd = sb("l_D", [128, T], F32); b_D = Buf()
            E = sb("l_E", [128, T], F32); b_E = Buf()
            cw = sb("l_cw", [128, 4], F32); b_cw = Buf()
            cb = sb("l_cb", [128, 1], F32); b_cb = Buf()
            gwf = sb("l_gwf", [128, 4, 128], F32); b_gwf = Buf()
            gw = sb("l_gw", [128, 4, 128], BF16); b_gw = Buf()
            gb = sb("l_gb", [128, 4], F32); b_gb = Buf()
            lam = sb("l_lam", [128, 2], F32); b_lam = Buf()
            cn = sb("l_cn", [128, 2], F32); b_cn = Buf()
            one = sb("l_one", [128, 1], F32); b_one = Buf()
            CG = 1056
            gst = [sb("l_gst%d" % i, [128, CG], F32) for i in range(2)]; b_gst = [Buf(), Buf()]
            yst = [sb("l_yst%d" % i, [128, CG], BF16) for i in range(2)]; b_yst = [Buf(), Buf()]
            ps = [st.enter_context(nc.psum_tensor(K.un("l_ps%d" % i), [128, CH], F32)) for i in range(2)]
            b_ps = [Buf(), Buf()]
            for c0 in range(0, T, 2112):
                fw.dma("sp", A[:, c0:c0 + 2112], K.featF[l][cc, :, c0:c0 + 2112], reads=[K.b_feat], writes=[b_A])
            fw.dma("sp", cw[:], K.lru_conv_w[l, :, cc * 128:(cc + 1) * 128].rearrange("k p -> p k"), writes=[b_cw],
                   allow_slow_non_contiguous=True)
            fw.dma("sp", cb[:], K.lru_conv_b[l, cc * 128:(cc + 1) * 128].rearrange("(p o) -> p o", o=1), writes=[b_cb])
            fw.op("pool", lambda: nc.gpsimd.memset(gwf[:], 0.0), writes=[b_gwf])
            fw.op("pool", lambda: nc.gpsimd.memset(one[:], 1.0), writes=[b_one])
            for dr in range(2):
                for g in range(2):
                    for bk in range(2):
                        fw.dma("sp", gwf[bk * 64:(bk + 1) * 64, dr * 2 + g, bk * 64:(bk + 1) * 64],
                               K.lru_gate_w[l, dr, g, cc * 2 + bk], writes=[b_gwf])
                    fw.dma("sp", gb[:, dr * 2 + g:dr * 2 + g + 1],
                           K.lru_gate_b[l, dr, g, cc * 2:cc * 2 + 2, :].rearrange("k (d o) -> (k d) o", o=1), writes=[b_gb])
                fw.dma("sp", lam[:, dr:dr + 1], K.lru_lambda[l, dr, cc * 128:(cc + 1) * 128].rearrange("(p o) -> p o", o=1),
                       writes=[b_lam])
            fw.op("dve", lambda: nc.vector.tensor_copy(out=gw[:], in_=gwf[:]), reads=[b_gwf], writes=[b_gw])
            fw.op("act", lambda: nc.scalar.activation(out=cn[:], in_=lam[:], func=AF.Exp, scale=-1.0), reads=[b_lam], writes=[b_cn])
            fw.op("act", lambda: nc.scalar.activation(out=cn[:], in_=cn[:], func=AF.Ln, bias=one[:]), reads=[b_cn, b_one], writes=[b_cn])
            fw.op("dve", lambda: nc.vector.tensor_scalar(out=cn[:], in0=cn[:], scalar1=-8.0, scalar2=None, op0=ALU.mult),
                  reads=[b_cn], writes=[b_cn])
            for (s0, s1) in ((0, NLAT), (NLAT, T)):
                fw.op("dve", lambda: nc.vector.tensor_scalar(out=B[:, s0:s1], in0=A[:, s0:s1], scalar1=cw[:, 1:2], scalar2=cb[:, 0:1],
                                                             op0=ALU.mult, op1=ALU.add), reads=[b_A, b_cw, b_cb], writes=[b_B])
                fw.op("dve", lambda: nc.vector.scalar_tensor_tensor(out=B[:, s0 + 1:s1], in0=A[:, s0:s1 - 1], scalar=cw[:, 0:1],
                                                                    in1=B[:, s0 + 1:s1], op0=ALU.mult, op1=ALU.add),
                      reads=[b_A, b_cw, b_B], writes=[b_B])
                fw.op("dve", lambda: nc.vector.scalar_tensor_tensor(out=B[:, s0:s1 - 1], in0=A[:, s0 + 1:s1], scalar=cw[:, 2:3],
                                                                    in1=B[:, s0:s1 - 1], op0=ALU.mult, op1=ALU.add),
                      reads=[b_A, b_cw, b_B], writes=[b_B])
                fw.op("dve", lambda: nc.vector.scalar_tensor_tensor(out=B[:, s0:s1 - 2], in0=A[:, s0 + 2:s1], scalar=cw[:, 3:4],
                                                                    in1=B[:, s0:s1 - 2], op0=ALU.mult, op1=ALU.add),
                      reads=[b_A, b_cw, b_B], writes=[b_B])
            fw.op("pool", lambda: nc.gpsimd.tensor_copy(out=Bb[:], in_=B[:]), reads=[b_B], writes=[b_Bb])
            pi = 0
            chunks = [(c0, min(T, c0 + CH)) for c0 in range(0, T, CH)]
            for dr in range(2):
                for g, (dst, bdst) in enumerate(((C, b_C), (Dd, b_D))):
                    for (c0, c1) in chunks:
                        p_, bp_ = ps[pi % 2], b_ps[pi % 2]
                        pi += 1
                        for t0 in range(c0, c1, 512):
                            t1 = min(c1, t0 + 512)
                            fw.op("pe", lambda: nc.tensor.matmul(p_[:, t0 - c0:t1 - c0], lhsT=gw[:, dr * 2 + g, :], rhs=Bb[:, t0:t1],
                                                                 start=True, stop=True),
                                  reads=[b_gw, b_Bb], writes=[bp_], inc=(t1 == c1))
                        fw.op("act", lambda: nc.scalar.activation(out=dst[:, c0:c1], in_=p_[:, 0:c1 - c0], func=AF.Sigmoid,
                                                                  bias=gb[:, dr * 2 + g:dr * 2 + g + 1]),
                              reads=[bp_, b_gb], writes=[bdst])
                fw.op("act", lambda: nc.scalar.activation(out=C[:], in_=C[:], func=AF.Exp, scale=cn[:, dr:dr + 1]),
                      reads=[b_C, b_cn], writes=[b_C])
                fw.op("pool", lambda: nc.gpsimd.tensor_tensor(out=A[:], in0=C[:], in1=C[:], op=ALU.mult), reads=[b_C], writes=[b_A])
                fw.op("act", lambda: nc.scalar.activation(out=A[:], in_=A[:], func=AF.Sqrt, scale=-1.0, bias=one[:]),
                      reads=[b_A, b_one], writes=[b_A])
                fw.op("dve", lambda: nc.vector.tensor_tensor(out=Dd[:], in0=Dd[:], in1=B[:], op=ALU.mult), reads=[b_D, b_B], writes=[b_D])
                fw.op("dve", lambda: nc.vector.tensor_tensor(out=Dd[:], in0=Dd[:], in1=A[:], op=ALU.mult), reads=[b_D, b_A], writes=[b_D])
                if dr == 0:
                    segs = [(NLAT, T)] + [(c0, c0 + CH) for c0 in range(0, NLAT, CH)]
                    prev = None
                    for (c0, c1) in segs:
                        init = 0.0 if prev is None else A[:, prev - 1:prev]
                        fw.op("dve", lambda: nc.vector.tensor_tensor_scan(out=A[:, c0:c1], data0=C[:, c0:c1], data1=Dd[:, c0:c1],
                                                                          initial=init, op0=ALU.mult, op1=ALU.add),
                              reads=[b_C, b_D, b_A], writes=[b_A])
                        prev = c1
                else:
                    segs = [(NLAT, T)] + [(c0, c0 + CH) for c0 in range(NLAT - CH, -1, -CH)]
                    prev = None
                    rev = lambda ap: ap[:, ::-1]
                    for (c0, c1) in segs:
                        init = 0.0 if prev is None else A[:, prev:prev + 1]
                        fw.op("dve", lambda: nc.vector.tensor_tensor_scan(out=rev(A[:, c0:c1]), data0=rev(C[:, c0:c1]),
                                                                          data1=rev(Dd[:, c0:c1]), initial=init,
                                                                          op0=ALU.mult, op1=ALU.add),
                              reads=[b_C, b_D, b_A], writes=[b_A])
                        prev = c0
                if dr == 0:
                    fw.op("pool", lambda: nc.gpsimd.tensor_copy(out=E[:], in_=A[:]), reads=[b_A], writes=[b_E])
                else:
                    fw.op("pool", lambda: nc.gpsimd.tensor_tensor(out=E[:], in0=E[:], in1=A[:], op=ALU.add), reads=[b_A, b_E], writes=[b_E])
            for ci, (c0, c1) in enumerate([(c0, c0 + CG) for c0 in range(0, T, CG)]):
                g_, bg_ = gst[ci % 2], b_gst[ci % 2]
                y_, by_ = yst[ci % 2], b_yst[ci % 2]
                fw.dma("sp", g_[:, 0:c1 - c0], K.featF[l][2 + cc, :, c0:c1], reads=[K.b_feat], writes=[bg_])
                fw.op("act", lambda: nc.scalar.activation(out=g_[:, 0:c1 - c0], in_=g_[:, 0:c1 - c0], func=AF.Gelu_apprx_tanh),
                      reads=[bg_], writes=[bg_])
                fw.op("dve", lambda: nc.vector.tensor_tensor(out=y_[:, 0:c1 - c0], in0=g_[:, 0:c1 - c0], in1=E[:, c0:c1], op=ALU.mult),
                      reads=[bg_, b_E], writes=[by_])
                fw.dma("pool", K.yT[l][2 + cc, :, c0:c1], y_[:, 0:c1 - c0], reads=[by_], writes=[K.b_yT])
            fw.barrier()


def _ret_consts():
    j = np.arange(128)[:, None].astype(np.float32)
    i = np.arange(128)[None, :].astype(np.float32)
    z = np.zeros((128, 128), np.float32)
    c = np.stack([np.maximum(i - j, 0), (i >= j).astype(np.float32), np.maximum(j - i - 1, 0),
                  (j > i).astype(np.float32), z + i + 1, z + 127 - i], 1).astype(np.float32)
    cv = np.stack([127 - np.arange(128), np.arange(128)], 1).astype(np.float32)
    return np.ascontiguousarray(c), np.ascontiguousarray(cv)


def _blk64():
    b = np.zeros((128, 128), np.float32)
    b[:64, :64] = 1.0 / 64
    b[64:, 64:] = 1.0 / 64
    return b


def _phase_ret(K, l):
    nc, fw = K.nc, K.fw
    last = l == DEPTH - 1
    NCH = T // 128
    with contextlib.ExitStack() as st0:
        sb0 = lambda n, s, d: st0.enter_context(nc.sbuf_tensor(K.un(n), s, d))
        rc = sb0("r_rc", [128, 6, 128], F32); b_rc = Buf()
        cv = sb0("r_cv", [128, 2], F32); b_cv = Buf()
        rd = sb0("r_rd", [128, 8], F32); b_rd = Buf()
        lg = sb0("r_lg", [128, 8], F32); b_lg = Buf()
        one = sb0("r_one", [128, 1], F32); b_one = Buf()
        eps_t = sb0("r_eps", [128, 1], F32); b_eps = Buf()
        blkf = sb0("r_blkf", [128, 128], F32); b_blkf = Buf()
        blk = sb0("r_blk", [128, 128], BF16); b_blk = Buf()
        idf = sb0("r_idf", [128, 128], F32); b_idf = Buf()
        ident = sb0("r_id", [128, 128], BF16); b_id = Buf()
        fw.dma("sp", rc[:], K.retc, writes=[b_rc])
        fw.dma("sp", cv[:], K.retcv, writes=[b_cv])
        fw.dma("sp", rd[:], K.ret_decay[l:l + 1].rearrange("o a h -> o (a h)").broadcast_to([128, 8]), writes=[b_rd])
        fw.dma("sp", blkf[:], K.blk64, writes=[b_blkf])
        fw.dma("sp", idf[:], K.ident, writes=[b_idf])
        fw.op("dve", lambda: nc.vector.tensor_copy(out=blk[:], in_=blkf[:]), reads=[b_blkf], writes=[b_blk])
        fw.op("dve", lambda: nc.vector.tensor_copy(out=ident[:], in_=idf[:]), reads=[b_idf], writes=[b_id])
        fw.op("pool", lambda: nc.gpsimd.memset(one[:], 1.0), writes=[b_one])
        fw.op("pool", lambda: nc.gpsimd.memset(eps_t[:], EPS), writes=[b_eps])
        fw.op("act", lambda: nc.scalar.activation(out=lg[:], in_=rd[:], func=AF.Exp, scale=-1.0), reads=[b_rd], writes=[b_lg])
        fw.op("act", lambda: nc.scalar.activation(out=lg[:], in_=lg[:], func=AF.Ln, bias=one[:]), reads=[b_lg, b_one], writes=[b_lg])
        fw.op("dve", lambda: nc.vector.tensor_scalar(out=lg[:], in0=lg[:], scalar1=-1.0, scalar2=None, op0=ALU.mult),
              reads=[b_lg], writes=[b_lg])
        for cc in range(2):
            with contextlib.ExitStack() as st:
                sb = lambda n, s, d: st.enter_context(nc.sbuf_tensor(K.un(n), s, d))
                QT = sb("r_QT", [128, NCH, 128], BF16); b_QT = Buf()
                KT = sb("r_KT", [128, NCH, 128], BF16); b_KT = Buf()
                V = sb("r_V", [128, NCH, 128], BF16); b_V = Buf()
                qf = sb("r_qf", [128, NCH, 128], BF16); b_qf = Buf()
                qb = sb("r_qb", [128, NCH, 128], BF16); b_qb = Buf()
                KVs = [sb("r_KVs%d" % d_, [128, NCH + 6, 64], F32) for d_ in range(2)]; b_KVs = [Buf(), Buf()]
                Sbf = [sb("r_Sbf%d" % d_, [128, NCH, 64], BF16) for d_ in range(2)]; b_Sbf = [Buf(), Buf()]
                lgp = sb("r_lgp", [128, 2], F32); b_lgp = Buf()
                g128 = sb("r_g128", [128, 2], F32); b_g128 = Buf()
                M = sb("r_M", [128, 2, 128], F32); b_M = Buf()
                mt = sb("r_mt", [128, 2, 128], F32); b_mt = Buf()
                df = sb("r_df", [128, 128], F32); b_df = Buf()
                db = sb("r_db", [128, 128], F32); b_db = Buf()
                kdec = sb("r_kdec", [128, 2, 2], F32); b_kdec = Buf()
                for c0 in range(0, NCH, 22):
                    fw.dma("sp", QT[:, c0:c0 + 22, :], K.featB[l][4 + cc, :, c0 * 128:(c0 + 22) * 128].rearrange("p (c i) -> p c i", i=128),
                           reads=[K.b_feat], writes=[b_QT])
                    fw.dma("sp", KT[:, c0:c0 + 22, :], K.featB[l][6 + cc, :, c0 * 128:(c0 + 22) * 128].rearrange("p (c i) -> p c i", i=128),
                           reads=[K.b_feat], writes=[b_KT])
                for c0 in range(0, NCH, 6):
                    fw.dma("sp", V[:, c0:c0 + 6, :],
                           K.vAll[l][c0 * 128:(c0 + 6) * 128, 256 + cc * 128:256 + (cc + 1) * 128].rearrange("(c p) e -> p c e", p=128),
                           reads=[K.b_feat], writes=[b_V])
                for dr in range(2):
                    for hh in range(2):
                        col = dr * 4 + 2 * cc + hh
                        fw.op("dve", lambda: nc.vector.tensor_copy(out=lgp[hh * 64:(hh + 1) * 64, dr:dr + 1],
                                                                   in_=lg[hh * 64:(hh + 1) * 64, col:col + 1]),
                              reads=[b_lg], writes=[b_lgp])
                fw.op("act", lambda: nc.scalar.activation(out=g128[:], in_=lgp[:], func=AF.Exp, scale=128.0), reads=[b_lgp], writes=[b_g128])
                for hh in range(2):
                    h = 2 * cc + hh
                    fw.op("act", lambda: nc.scalar.activation(out=M[:, hh, :], in_=rc[:, 0, :], func=AF.Exp, scale=lg[:, h:h + 1]),
                          reads=[b_rc, b_lg], writes=[b_M])
                    fw.op("act", lambda: nc.scalar.activation(out=mt[:, hh, :], in_=rc[:, 2, :], func=AF.Exp, scale=lg[:, 4 + h:5 + h]),
                          reads=[b_rc, b_lg], writes=[b_mt])
                    fw.op("dve", lambda: nc.vector.tensor_tensor(out=M[:, hh, :], in0=M[:, hh, :], in1=rc[:, 1, :], op=ALU.mult),
                          reads=[b_M, b_rc], writes=[b_M])
                    fw.op("dve", lambda: nc.vector.tensor_tensor(out=mt[:, hh, :], in0=mt[:, hh, :], in1=rc[:, 3, :], op=ALU.mult),
                          reads=[b_mt, b_rc], writes=[b_mt])
                    fw.op("act", lambda: nc.scalar.activation(out=kdec[:, 0, hh:hh + 1], in_=lg[:, h:h + 1], func=AF.Exp, scale=cv[:, 0:1]),
                          reads=[b_lg, b_cv], writes=[b_kdec])
                    fw.op("act", lambda: nc.scalar.activation(out=kdec[:, 1, hh:hh + 1], in_=lg[:, 4 + h:5 + h], func=AF.Exp, scale=cv[:, 1:2]),
                          reads=[b_lg, b_cv], writes=[b_kdec])
                fw.op("dve", lambda: nc.vector.tensor_tensor(out=M[:], in0=M[:], in1=mt[:], op=ALU.add), reads=[b_M, b_mt], writes=[b_M])
                fw.op("act", lambda: nc.scalar.activation(out=df[:], in_=rc[:, 4, :], func=AF.Exp, scale=lgp[:, 0:1]),
                      reads=[b_rc, b_lgp], writes=[b_df])
                fw.op("act", lambda: nc.scalar.activation(out=db[:], in_=rc[:, 5, :], func=AF.Exp, scale=lgp[:, 1:2]),
                      reads=[b_rc, b_lgp], writes=[b_db])
                fw.op("dve", lambda: nc.vector.tensor_tensor(out=qf[:], in0=QT[:], in1=df[:].unsqueeze(1).broadcast_to([128, NCH, 128]), op=ALU.mult),
                      reads=[b_QT, b_df], writes=[b_qf])
                fw.op("pool", lambda: nc.gpsimd.tensor_tensor(out=qb[:], in0=QT[:], in1=db[:].unsqueeze(1).broadcast_to([128, NCH, 128]), op=ALU.mult),
                      reads=[b_QT, b_db], writes=[b_qb])
                if K.stop <= 0:
                    fw.barrier()
                    continue
                with contextlib.ExitStack() as st1:
                    tpk = [st1.enter_context(nc.psum_tensor(K.un("r_tpk%d" % i), [128, 1024], BF16)) for i in range(2)]
                    b_tpk = [Buf(), Buf()]
                    kvp = [[st1.enter_context(nc.psum_tensor(K.un("r_kvp%d_%d" % (d_, i)), [128, 8, 64], F32)) for i in range(2)] for d_ in range(2)]
                    b_kvp = [[Buf(), Buf()] for d_ in range(2)]
                    kd = [[st1.enter_context(nc.sbuf_tensor(K.un("r_kd%d_%d" % (d_, i)), [128, 2, 64], BF16)) for i in range(2)] for d_ in range(2)]
                    b_kd = [[Buf(), Buf()] for d_ in range(2)]
                    for c in range(min(NCH, K.p1n)):
                        p = c % 2
                        grp = (c // 8) % 2
                        fw.op("pe", lambda: nc.tensor.transpose(tpk[p][:, 0:128], KT[:, c, :], ident[:]), reads=[b_KT, b_id], writes=[b_tpk[p]])
                        for dr in range(2):
                            fw.op("dve", lambda: nc.vector.tensor_tensor(out=kd[dr][p][:], in0=tpk[p][:, 0:128].rearrange("p (h d) -> p h d", d=64),
                                                                         in1=kdec[:, dr, :].unsqueeze(2).broadcast_to([128, 2, 64]), op=ALU.mult),
                                  reads=[b_tpk[p], b_kdec], writes=[b_kd[dr][p]])
                        for dr in range(2):
                            for hh in range(2):
                                fw.op("pe", lambda: nc.tensor.matmul(kvp[dr][grp][hh * 64:(hh + 1) * 64, c % 8, :], lhsT=kd[dr][p][:, hh, :],
                                                                     rhs=V[:, c, hh * 64:(hh + 1) * 64], start=True, stop=True),
                                      reads=[b_kd[dr][p], b_V], writes=[b_kvp[dr][grp]], inc=(hh == 1))
                        if c % 8 == 7 or c == min(NCH, K.p1n) - 1:
                            n8 = c % 8 + 1
                            c8 = c - (n8 - 1)
                            for dr in range(2):
                                fw.op("act", lambda: nc.scalar.copy(out=KVs[dr][:, c8:c8 + n8, :], in_=kvp[dr][grp][:, 0:n8, :]),
                                      reads=[b_kvp[dr][grp]], writes=[b_KVs[dr]])
                    fw.barrier()
                if K.stop <= 1:
                    continue
                with contextlib.ExitStack() as st1:
                    S = [[st1.enter_context(nc.sbuf_tensor(K.un("r_S%d_%d" % (d_, i)), [128, 64], F32)) for i in range(2)] for d_ in range(2)]
                    b_S = [[Buf(), Buf()] for d_ in range(2)]
                    orders = [[64, 65] + list(range(64)), [65, 64] + list(range(63, -1, -1))]
                    for dr in range(2):
                        fw.op("pool", lambda: nc.gpsimd.memset(S[dr][0][:], 0.0), writes=[b_S[dr][0]])
                    for n in range(NCH):
                        for dr in range(2):
                            c = orders[dr][n]
                            cur, nxt = S[dr][n % 2], S[dr][(n + 1) % 2]
                            bcur, bnxt = b_S[dr][n % 2], b_S[dr][(n + 1) % 2]
                            eng = "act" if dr == 0 else "pool"
                            if eng == "act":
                                fw.op("act", lambda: nc.scalar.copy(out=Sbf[dr][:, c, :], in_=cur[:]), reads=[bcur], writes=[b_Sbf[dr]])
                            else:
                                fw.op("pool", lambda: nc.gpsimd.tensor_copy(out=Sbf[dr][:, c, :], in_=cur[:]), reads=[bcur], writes=[b_Sbf[dr]])
                            fw.op("dve", lambda: nc.vector.scalar_tensor_tensor(out=nxt[:], in0=cur[:], scalar=g128[:, dr:dr + 1],
                                                                                in1=KVs[dr][:, c, :], op0=ALU.mult, op1=ALU.add),
                                  reads=[bcur, b_g128, b_KVs[dr]], writes=[bnxt])
                    fw.barrier()
                if K.stop <= 2:
                    continue
                with contextlib.ExitStack() as st1:
                    sbl = lambda n, s, d: st1.enter_context(nc.sbuf_tensor(K.un(n), s, d))
                    sc = [[st1.enter_context(nc.psum_tensor(K.un("r_sc%d_%d" % (i, h_)), [128, 512], F32)) for h_ in range(2)] for i in range(2)]
                    b_sc = [[Buf(), Buf()], [Buf(), Buf()]]
                    oT = [st1.enter_context(nc.psum_tensor(K.un("r_oT%d" % i), [128, 512], F32)) for i in range(2)]; b_oT = [Buf(), Buf()]
                    ms = [st1.enter_context(nc.psum_tensor(K.un("r_ms%d" % i), [128, 512], F32)) for i in range(2)]; b_ms = [Buf(), Buf()]
                    P = [sbl("r_P%d" % i, [128, 2, 128], BF16) for i in range(2)]; b_P = [Buf(), Buf()]
                    sq = [sbl("r_sq%d" % i, [128, 512], BF16) for i in range(2)]; b_sq = [Buf(), Buf()]
                    sd = [sbl("r_sd%d" % i, [128, 512], F32) for i in range(2)]; b_sd = [Buf(), Buf()]
                    tt = [sbl("r_tt%d" % i, [128, 512], F32) for i in range(2)]; b_tt = [Buf(), Buf()]
                    gs = [sbl("r_gs%d" % i, [128, 512], F32) for i in range(2)]; b_gs = [Buf(), Buf()]
                    ys = [sbl("r_ys%d" % i, [128, 512], BF16) for i in range(2)]; b_ys = [Buf(), Buf()]
                    groups = [list(range(g * 4, g * 4 + 4)) for g in range(16)]
                    if not last:
                        groups.append([64, 65])
                    groups = groups[:K.p2n] if K.p2n >= 0 else groups[K.p2n:]
                    ci_all = 0
                    for gi, grp in enumerate(groups):
                        q = gi % 2
                        tok0 = grp[0] * 128
                        ntok = len(grp) * 128
                        if grp[0] == 64:
                            fw.barrier()
                        fw.dma("sp", gs[q][:, 0:ntok], K.featF[l][4 + cc, :, tok0:tok0 + ntok], reads=[K.b_feat], writes=[b_gs[q]])
                        for ci, c in enumerate(grp):
                            p = ci_all % 2
                            ci_all += 1
                            for hh in range(2):
                                fw.op("pe", lambda: nc.tensor.matmul(sc[p][hh][:, 0:128], lhsT=KT[hh * 64:(hh + 1) * 64, c, :],
                                                                     rhs=QT[hh * 64:(hh + 1) * 64, c, :], start=True, stop=True),
                                      reads=[b_KT, b_QT], writes=[b_sc[p][hh]])
                            for hh in range(2):
                                fw.op("dve", lambda: nc.vector.tensor_tensor(out=P[p][:, hh, :], in0=sc[p][hh][:, 0:128], in1=M[:, hh, :], op=ALU.mult),
                                      reads=[b_sc[p][hh], b_M], writes=[b_P[p]])
                            for hh in range(2):
                                o_ = oT[q][hh * 64:(hh + 1) * 64, ci * 128:(ci + 1) * 128]
                                fw.op("pe", lambda: nc.tensor.matmul(o_, lhsT=V[:, c, hh * 64:(hh + 1) * 64], rhs=P[p][:, hh, :],
                                                                     start=True, stop=False),
                                      reads=[b_V, b_P[p]], writes=[b_oT[q]], inc=False)
                                fw.op("pe", lambda: nc.tensor.matmul(o_, lhsT=Sbf[0][hh * 64:(hh + 1) * 64, c, :],
                                                                     rhs=qf[hh * 64:(hh + 1) * 64, c, :], start=False, stop=False),
                                      reads=[b_Sbf[0], b_qf], writes=[b_oT[q]], inc=False)
                                fw.op("pe", lambda: nc.tensor.matmul(o_, lhsT=Sbf[1][hh * 64:(hh + 1) * 64, c, :],
                                                                     rhs=qb[hh * 64:(hh + 1) * 64, c, :], start=False, stop=True),
                                      reads=[b_Sbf[1], b_qb], writes=[b_oT[q]], inc=(hh == 1))
                        fw.op("act", lambda: nc.scalar.activation(out=sq[q][:, 0:ntok], in_=oT[q][:, 0:ntok], func=AF.Square),
                              reads=[b_oT[q]], writes=[b_sq[q]])
                        fw.op("pe", lambda: nc.tensor.matmul(ms[q][:, 0:ntok], lhsT=blk[:], rhs=sq[q][:, 0:ntok], start=True, stop=True),
                              reads=[b_blk, b_sq[q]], writes=[b_ms[q]])
                        fw.op("act", lambda: nc.scalar.activation(out=sd[q][:, 0:ntok], in_=ms[q][:, 0:ntok], func=AF.Sqrt, bias=eps_t[:]),
                              reads=[b_ms[q], b_eps], writes=[b_sd[q]])
                        fw.op("dve", lambda: nc.vector.reciprocal(out=sd[q][:, 0:ntok], in_=sd[q][:, 0:ntok]), reads=[b_sd[q]], writes=[b_sd[q]])
                        fw.op("dve", lambda: nc.vector.tensor_tensor(out=tt[q][:, 0:ntok], in0=oT[q][:, 0:ntok], in1=sd[q][:, 0:ntok], op=ALU.mult),
                              reads=[b_oT[q], b_sd[q]], writes=[b_tt[q]])
                        fw.op("act", lambda: nc.scalar.activation(out=gs[q][:, 0:ntok], in_=gs[q][:, 0:ntok], func=AF.Silu),
                              reads=[b_gs[q]], writes=[b_gs[q]])
                        fw.op("pool", lambda: nc.gpsimd.tensor_tensor(out=ys[q][:, 0:ntok], in0=tt[q][:, 0:ntok], in1=gs[q][:, 0:ntok], op=ALU.mult),
                              reads=[b_tt[q], b_gs[q]], writes=[b_ys[q]])
                        fw.dma("pool", K.yT[l][4 + cc, :, tok0:tok0 + ntok], ys[q][:, 0:ntok], reads=[b_ys[q]], writes=[K.b_yT])
                    fw.barrier()


def _diff_consts():
    sel = np.zeros((65, 64), np.float32)
    sel[64, :] = 1.0
    ones64 = np.full((64, 64), 1.0 / 64, np.float32)
    mcol = np.zeros((128, 4), np.float32)
    for p in range(128):
        mcol[p, p // 32] = 1.0
    return sel, ones64, mcol


def _phase_diff(K, l):
    nc, fw = K.nc, K.fw
    last = l == DEPTH - 1
    NCH = T // 128
    lam_init = 0.8 - 0.6 * math.exp(-0.3 * l)
    SC = 32 ** -0.5
    with contextlib.ExitStack() as st0:
        sb0 = lambda n, s, d: st0.enter_context(nc.sbuf_tensor(K.un(n), s, d))
        Va = sb0("d_Va", [128, NCH, 4, 128], BF16); b_Va = Buf()
        sel = sb0("d_sel", [65, 64], F32); b_sel = Buf()
        o64 = sb0("d_o64", [64, 64], F32); b_o64 = Buf()
        mcol = sb0("d_mcol", [128, 4], F32); b_mcol = Buf()
        dl = sb0("d_dl", [64, 4, 32], F32); b_dl = Buf()
        pr = sb0("d_pr", [64, 2, 32], F32); b_pr = Buf()
        s12 = sb0("d_s12", [64, 2], F32); b_s12 = Buf()
        nlam = sb0("d_nlam", [64, 1], F32); b_nlam = Buf()
        gsc = sb0("d_gsc", [64, 1], F32); b_gsc = Buf()
        eps_t = sb0("d_eps", [64, 1], F32); b_eps = Buf()
        fw.dma("sp", sel[:], K.sel65, writes=[b_sel])
        fw.dma("sp", o64[:], K.ones64, writes=[b_o64])
        fw.dma("sp", mcol[:], K.mcol, writes=[b_mcol])
        fw.dma("sp", dl[:], K.diff_lambda[l:l + 1].broadcast_to([64, 4, 32]), writes=[b_dl])
        fw.dma("sp", gsc[:], K.diff_subln[l].rearrange("(p o) -> p o", o=1), writes=[b_gsc])
        fw.op("pool", lambda: nc.gpsimd.memset(eps_t[:], EPS), writes=[b_eps])
        fw.op("dve", lambda: nc.vector.tensor_tensor(out=pr[:, 0, :], in0=dl[:, 0, :], in1=dl[:, 1, :], op=ALU.mult), reads=[b_dl], writes=[b_pr])
        fw.op("dve", lambda: nc.vector.tensor_tensor(out=pr[:, 1, :], in0=dl[:, 2, :], in1=dl[:, 3, :], op=ALU.mult), reads=[b_dl], writes=[b_pr])
        fw.op("dve", lambda: nc.vector.tensor_reduce(out=s12[:], in_=pr[:], axis=AX.X, op=ALU.add), reads=[b_pr], writes=[b_s12])
        fw.op("act", lambda: nc.scalar.activation(out=s12[:], in_=s12[:], func=AF.Exp), reads=[b_s12], writes=[b_s12])
        fw.op("dve", lambda: nc.vector.tensor_tensor(out=nlam[:], in0=s12[:, 1:2], in1=s12[:, 0:1], op=ALU.subtract), reads=[b_s12], writes=[b_nlam])
        fw.op("dve", lambda: nc.vector.tensor_scalar(out=nlam[:], in0=nlam[:], scalar1=-lam_init, scalar2=None, op0=ALU.add),
              reads=[b_nlam], writes=[b_nlam])
        fw.op("dve", lambda: nc.vector.tensor_scalar(out=gsc[:], in0=gsc[:], scalar1=1.0 - lam_init, scalar2=None, op0=ALU.mult),
              reads=[b_gsc], writes=[b_gsc])
        with contextlib.ExitStack() as st1:
            vst = st1.enter_context(nc.sbuf_tensor(K.un("d_vst"), [128, NCH, 256], BF16)); b_vst = Buf()
            for c0 in range(0, NCH, 6):
                fw.dma("sp", vst[:, c0:c0 + 6, :], K.vAll[l][c0 * 128:(c0 + 6) * 128, 512:768].rearrange("(c p) e -> p c e", p=128),
                       reads=[K.b_feat], writes=[b_vst])
            fw.op("pool", lambda: nc.gpsimd.memset(Va[:, :, :, 64:128], 0.0), writes=[b_Va])
            fw.op("pool", lambda: nc.gpsimd.memset(Va[:, :, :, 64:65], 1.0), writes=[b_Va])
            fw.op("dve", lambda: nc.vector.tensor_copy(out=Va[:, :, :, 0:64], in_=vst[:].rearrange("p c (h e) -> p c h e", e=64)),
                  reads=[b_vst], writes=[b_Va])
            fw.barrier()
        for cc in range(2):
            with contextlib.ExitStack() as st:
                sb = lambda n, s, d: st.enter_context(nc.sbuf_tensor(K.un(n), s, d))
                QT = sb("d_QT", [128, T], BF16); b_QT = Buf()
                KT = sb("d_KT", [128, T], BF16); b_KT = Buf()
                Qm = [sb("d_Qm%d" % m, [128, T], BF16) for m in range(4)]; b_Qm = [Buf() for _ in range(4)]
                P = [sb("d_P%d" % i, [128, 2, 512], BF16) for i in range(2)]; b_P = [Buf(), Buf()]
                Osb = sb("d_Osb", [65, 2, 512], F32); b_Osb = Buf()
                rec = sb("d_rec", [64, 2, 512], F32); b_rec = Buf()
                ta = sb("d_ta", [64, 512], F32); b_ta = Buf()
                tb = sb("d_tb", [64, 512], F32); b_tb = Buf()
                td = sb("d_td", [64, 512], F32); b_td = Buf()
                tq = sb("d_tq", [64, 512], F32); b_tq = Buf()
                sd = sb("d_sd", [64, 512], F32); b_sd = Buf()
                ys = [sb("d_ys%d" % i, [64, 512], BF16) for i in range(2)]; b_ys = [Buf(), Buf()]
                S = [st.enter_context(nc.psum_tensor(K.un("d_S%d" % i), [128, 2, 512], F32)) for i in range(2)]; b_S = [Buf(), Buf()]
                O = [st.enter_context(nc.psum_tensor(K.un("d_O%d" % m), [128, 512], F32)) for m in range(2)]; b_O = [Buf(), Buf()]
                bcp = st.enter_context(nc.psum_tensor(K.un("d_bcp"), [128, 2, 512], F32)); b_bcp = Buf()
                for c0 in range(0, T, 2112):
                    fw.dma("sp", QT[:, c0:c0 + 2112], K.featB[l][8 + cc, :, c0:c0 + 2112], reads=[K.b_feat], writes=[b_QT])
                    fw.dma("sp", KT[:, c0:c0 + 2112], K.featB[l][10 + cc, :, c0:c0 + 2112], reads=[K.b_feat], writes=[b_KT])
                for g_ in range(4):
                    fw.op("dve", lambda: nc.vector.tensor_scalar(out=Qm[g_][:], in0=QT[:], scalar1=mcol[:, g_:g_ + 1], scalar2=None, op0=ALU.mult),
                          reads=[b_QT, b_mcol], writes=[b_Qm[g_]])
                si = 0
                yi = [0]
                deferred = []

                def finalize_stage1(rows, q0, nq):
                    fw.op("act", lambda: nc.scalar.copy(out=Osb[:, 0, 0:nq], in_=O[0][0:65, 0:nq]), reads=[b_O[0]], writes=[b_Osb])
                    fw.op("dve", lambda: nc.vector.tensor_copy(out=Osb[:, 1, 0:nq], in_=O[1][0:65, 0:nq]), reads=[b_O[1]], writes=[b_Osb])

                def finalize_stage2(rows, q0, nq):
                    for m in range(2):
                        fw.op("pe", lambda: nc.tensor.matmul(bcp[0:64, m, 0:nq], lhsT=sel[:], rhs=Osb[:, m, 0:nq], start=True, stop=True),
                              reads=[b_sel, b_Osb], writes=[b_bcp], inc=(m == 1))
                    fw.op("dve", lambda: nc.vector.reciprocal(out=rec[:, :, 0:nq], in_=bcp[0:64, :, 0:nq]), reads=[b_bcp], writes=[b_rec])
                    fw.op("dve", lambda: nc.vector.tensor_tensor(out=ta[:, 0:nq], in0=Osb[0:64, 0, 0:nq], in1=rec[:, 0, 0:nq], op=ALU.mult),
                          reads=[b_Osb, b_rec], writes=[b_ta])
                    fw.op("pool", lambda: nc.gpsimd.tensor_tensor(out=tb[:, 0:nq], in0=Osb[0:64, 1, 0:nq], in1=rec[:, 1, 0:nq], op=ALU.mult),
                          reads=[b_Osb, b_rec], writes=[b_tb])
                    fw.op("dve", lambda: nc.vector.scalar_tensor_tensor(out=td[:, 0:nq], in0=tb[:, 0:nq], scalar=nlam[:, 0:1], in1=ta[:, 0:nq],
                                                                        op0=ALU.mult, op1=ALU.add),
                          reads=[b_tb, b_ta, b_nlam], writes=[b_td])
                    fw.op("pool", lambda: nc.gpsimd.tensor_tensor(out=tq[:, 0:nq], in0=td[:, 0:nq], in1=td[:, 0:nq], op=ALU.mult),
                          reads=[b_td], writes=[b_tq])

                def finalize_stage3(rows, q0, nq):
                    fw.op("pe", lambda: nc.tensor.matmul(bcp[0:64, 0, 0:nq], lhsT=o64[:], rhs=tq[:, 0:nq], start=True, stop=True),
                          reads=[b_o64, b_tq, b_rec], writes=[b_bcp])
                    fw.op("act", lambda: nc.scalar.activation(out=sd[:, 0:nq], in_=bcp[0:64, 0, 0:nq], func=AF.Sqrt, bias=eps_t[:]),
                          reads=[b_bcp, b_eps], writes=[b_sd])
                    fw.op("dve", lambda: nc.vector.reciprocal(out=sd[:, 0:nq], in_=sd[:, 0:nq]), reads=[b_sd], writes=[b_sd])
                    y_, by_ = ys[yi[0] % 2], b_ys[yi[0] % 2]
                    yi[0] += 1
                    fw.op("dve", lambda: nc.vector.scalar_tensor_tensor(out=y_[:, 0:nq], in0=td[:, 0:nq], scalar=gsc[:, 0:1], in1=sd[:, 0:nq],
                                                                        op0=ALU.mult, op1=ALU.mult),
                          reads=[b_td, b_gsc, b_sd], writes=[by_])
                    fw.dma("pool", K.yT[l][6 + cc, rows, q0:q0 + nq], y_[:, 0:nq], reads=[by_], writes=[K.b_yT])

                def flush(step):
                    while deferred and deferred[0][0] <= step:
                        deferred.pop(0)[1]()

                for hh in range(2):
                    h = 2 * cc + hh
                    rows = slice(hh * 64, (hh + 1) * 64)
                    qchunks = [(q0, 512, list(range(NCH))) for q0 in range(0, NLAT, 512)]
                    if not last:
                        qchunks.append((NLAT, NCTX, [64, 65]))
                    qchunks = qchunks[:K.p2n] if K.p2n >= 0 else qchunks[K.p2n:]
                    for (q0, nq, kbs) in qchunks:
                        if q0 == NLAT:
                            flush(10 ** 9)
                            fw.barrier()
                        n = len(kbs)

                        def emit_S(k, si_):
                            S_, bS_ = S[si_ % 2], b_S[si_ % 2]
                            for m in range(2):
                                fw.op("pe", lambda: nc.tensor.matmul(S_[:, m, 0:nq], lhsT=KT[:, kbs[k] * 128:(kbs[k] + 1) * 128],
                                                                     rhs=Qm[hh * 2 + m][:, q0:q0 + nq], start=True, stop=True),
                                      reads=[b_KT, b_Qm[hh * 2 + m]], writes=[bS_], inc=(m == 1))

                        emit_S(0, si)
                        for k in range(n):
                            if k + 1 < n:
                                emit_S(k + 1, si + 1)
                            S_, bS_ = S[si % 2], b_S[si % 2]
                            P_, bP_ = P[si % 2], b_P[si % 2]
                            si += 1
                            fw.op("act", lambda: nc.scalar.activation(out=P_[:, :, 0:nq], in_=S_[:, :, 0:nq], func=AF.Exp, scale=SC),
                                  reads=[bS_], writes=[bP_])
                            for m in range(2):
                                fw.op("pe", lambda: nc.tensor.matmul(O[m][:, 0:nq], lhsT=Va[:, kbs[k], h, :], rhs=P_[:, m, 0:nq],
                                                                     start=(k == 0), stop=(k == n - 1)),
                                      reads=[b_Va, bP_], writes=[b_O[m]], inc=(k == n - 1))
                            flush(k)
                        flush(10 ** 9)
                        finalize_stage1(rows, q0, nq)
                        a_ = (rows, q0, nq)
                        deferred.append((3, lambda a_=a_: finalize_stage2(*a_)))
                        deferred.append((8, lambda a_=a_: finalize_stage3(*a_)))
                flush(10 ** 9)
                fw.barrier()


_NA_CACHE = {}


def _na_schedule():
    if "s" in _NA_CACHE:
        return _NA_CACHE["s"]
    kl = np.arange(128)
    k_r, k_c = kl // 64, kl % 64
    q_r, q_c = kl // 64, kl % 64
    c0 = np.clip(q_c - 8, 0, 48)
    classes = {}
    tiles = []
    sched = []
    for i in range(64):
        qrow = 2 * i + q_r
        r0 = np.clip(qrow - 4, 0, 120)
        entry = []
        for j in range(64):
            krow = 2 * j + k_r
            vr = (krow[:, None] >= r0[None, :]) & (krow[:, None] <= r0[None, :] + 7)
            vc = (k_c[:, None] >= c0[None, :]) & (k_c[:, None] <= c0[None, :] + 15)
            valid = vr & vc
            if not valid.any():
                continue
            dr = np.clip(krow[:, None] - qrow[None, :] + 7, 0, 14)
            dc = np.clip(k_c[:, None] - q_c[None, :] + 15, 0, 30)
            entry.append((j - i, dr, dc, valid))
        key = tuple((d, a.tobytes(), b.tobytes(), v.tobytes()) for d, a, b, v in entry)
        if key not in classes:
            classes[key] = len(tiles)
            for d, a, b, v in entry:
                tiles.append((a, b, v))
        sched.append(([i + d for d, _, _, _ in entry], classes[key]))
    dr = np.stack([t[0] for t in tiles]); dc = np.stack([t[1] for t in tiles]); va = np.stack([t[2] for t in tiles])
    _NA_CACHE["s"] = (sched, dr, dc, va)
    return _NA_CACHE["s"]


def _na_bias(na_rpb):
    sched, dr, dc, va = _na_schedule()
    rpb = np.asarray(na_rpb, np.float32)
    g = rpb[:, :, dr, dc]
    return np.ascontiguousarray(np.where(va[None, None], g, np.float32(-30000.0)).astype(np.float32))


def _phase_na(K, l):
    nc, fw = K.nc, K.fw
    last = l == DEPTH - 1
    NCH = T // 128
    SC = 64 ** -0.5
    sched, _, _, _ = _na_schedule()
    NT = K.na_nt
    with contextlib.ExitStack() as st0:
        sb0 = lambda n, s, d: st0.enter_context(nc.sbuf_tensor(K.un(n), s, d))
        Va = sb0("n_Va", [128, NCH, 4, 65], BF16); b_Va = Buf()
        sel = sb0("n_sel", [65, 64], F32); b_sel = Buf()
        fw.dma("sp", sel[:], K.sel65, writes=[b_sel])
        with contextlib.ExitStack() as st1:
            vst = st1.enter_context(nc.sbuf_tensor(K.un("n_vst"), [128, NCH, 256], BF16)); b_vst = Buf()
            for c0 in range(0, NCH, 6):
                fw.dma("sp", vst[:, c0:c0 + 6, :], K.vAll[l][c0 * 128:(c0 + 6) * 128, 0:256].rearrange("(c p) e -> p c e", p=128),
                       reads=[K.b_feat], writes=[b_vst])
            fw.op("pool", lambda: nc.gpsimd.memset(Va[:, :, :, 64:65], 1.0), writes=[b_Va])
            fw.op("dve", lambda: nc.vector.tensor_copy(out=Va[:, :, :, 0:64], in_=vst[:].rearrange("p c (h e) -> p c h e", e=64)),
                  reads=[b_vst], writes=[b_Va])
            fw.barrier()
        for cc in range(2):
            with contextlib.ExitStack() as st:
                sb = lambda n, s, d: st.enter_context(nc.sbuf_tensor(K.un(n), s, d))
                QT = sb("n_QT", [128, T], BF16); b_QT = Buf()
                KT = sb("n_KT", [128, T], BF16); b_KT = Buf()
                E = sb("n_E", [128, NT, 128], F32); b_E = Buf()
                Pf = [sb("n_Pf%d" % i, [128, 5, 128], F32) for i in range(2)]; b_Pf = [Buf(), Buf()]
                Pw = [sb("n_Pw%d" % i, [128, 5, 128], BF16) for i in range(2)]; b_Pw = [Buf(), Buf()]
                Pc = [sb("n_Pc%d" % i, [128, 2, 128], BF16) for i in range(2)]; b_Pc = [Buf(), Buf()]
                Osb = sb("n_Osb", [65, 512], F32); b_Osb = Buf()
                rec = sb("n_rec", [64, 512], F32); b_rec = Buf()
                ys = [sb("n_ys%d" % i, [64, 512], BF16) for i in range(2)]; b_ys = [Buf(), Buf()]
                Sw = [st.enter_context(nc.psum_tensor(K.un("n_Sw%d" % i), [128, 8, 128], F32)) for i in range(2)]; b_Sw = [Buf(), Buf()]
                Sc = [st.enter_context(nc.psum_tensor(K.un("n_Sc%d" % i), [128, 4, 128], F32)) for i in range(2)]; b_Sc = [Buf(), Buf()]
                O = st.enter_context(nc.psum_tensor(K.un("n_O"), [128, 512], F32)); b_O = Buf()
                bcp = st.enter_context(nc.psum_tensor(K.un("n_bcp"), [128, 512], F32)); b_bcp = Buf()
                for c0 in range(0, T, 2112):
                    fw.dma("sp", QT[:, c0:c0 + 2112], K.featB[l][0 + cc, :, c0:c0 + 2112], reads=[K.b_feat], writes=[b_QT])
                    fw.dma("sp", KT[:, c0:c0 + 2112], K.featB[l][2 + cc, :, c0:c0 + 2112], reads=[K.b_feat], writes=[b_KT])
                si = 0
                yi = 0
                for hh in range(2):
                    h = 2 * cc + hh
                    rows = slice(hh * 64, (hh + 1) * 64)
                    for t0 in range(0, NT, 7):
                        t1 = min(NT, t0 + 7)
                        fw.dma("sp", E[:, t0:t1, :], K.nabias[l, h, t0:t1].rearrange("t k q -> k t q"), writes=[b_E])
                    fw.op("act", lambda: nc.scalar.activation(out=E[:], in_=E[:], func=AF.Exp), reads=[b_E], writes=[b_E])
                    qlist = [(i, sched[i][0], sched[i][1]) for i in range(64)]
                    if not last:
                        qlist += [(64, [], 0), (65, [], 0)]
                    qlist = qlist[:K.p2n] if K.p2n >= 0 else qlist[K.p2n:]
                    segs = [[q for q in qlist if q[0] < 64], [q for q in qlist if q[0] >= 64]]
                    for seg in segs:
                        if not seg:
                            continue
                        if seg[0][0] >= 64:
                            fw.barrier()

                        def emit_S(qi, p):
                            i, js, tid0 = seg[qi]
                            nb = len(js)
                            qs = slice(i * 128, (i + 1) * 128)
                            for bi, j in enumerate(js):
                                fw.op("pe", lambda: nc.tensor.matmul(Sw[p][:, bi, :], lhsT=KT[rows, j * 128:(j + 1) * 128], rhs=QT[rows, qs],
                                                                     start=True, stop=True),
                                      reads=[b_KT, b_QT], writes=[b_Sw[p]], inc=(bi == nb - 1))
                            for t in range(2):
                                fw.op("pe", lambda: nc.tensor.matmul(Sc[p][:, t, :], lhsT=KT[rows, (64 + t) * 128:(65 + t) * 128], rhs=QT[rows, qs],
                                                                     start=True, stop=True),
                                      reads=[b_KT, b_QT], writes=[b_Sc[p]], inc=(t == 1))

                        emit_S(0, si % 2)
                        for qi, (i, js, tid0) in enumerate(seg):
                            nb = len(js)
                            p = si % 2
                            si += 1
                            if qi + 1 < len(seg):
                                emit_S(qi + 1, si % 2)
                            gcol = (i % 4) * 128
                            if nb:
                                fw.op("act", lambda: nc.scalar.activation(out=Pf[p][:, 0:nb, :], in_=Sw[p][:, 0:nb, :], func=AF.Exp, scale=SC),
                                      reads=[b_Sw[p]], writes=[b_Pf[p]])
                                fw.op("dve", lambda: nc.vector.tensor_tensor(out=Pw[p][:, 0:nb, :], in0=Pf[p][:, 0:nb, :],
                                                                             in1=E[:, tid0:tid0 + nb, :], op=ALU.mult),
                                      reads=[b_Pf[p], b_E], writes=[b_Pw[p]])
                            fw.op("act", lambda: nc.scalar.activation(out=Pc[p][:], in_=Sc[p][:, 0:2, :], func=AF.Exp, scale=SC),
                                  reads=[b_Sc[p]], writes=[b_Pc[p]])
                            for bi, j in enumerate(js):
                                fw.op("pe", lambda: nc.tensor.matmul(O[0:65, gcol:gcol + 128], lhsT=Va[:, j, h, :], rhs=Pw[p][:, bi, :],
                                                                     start=(bi == 0), stop=False),
                                      reads=[b_Va, b_Pw[p]], writes=[b_O], inc=False)
                            for t in range(2):
                                fw.op("pe", lambda: nc.tensor.matmul(O[0:65, gcol:gcol + 128], lhsT=Va[:, 64 + t, h, :], rhs=Pc[p][:, t, :],
                                                                     start=(nb == 0 and t == 0), stop=(t == 1)),
                                      reads=[b_Va, b_Pc[p]], writes=[b_O], inc=(t == 1))
                            endgrp = (i % 4 == 3) or (i == 65) or (qi == len(seg) - 1)
                            if endgrp:
                                g0 = (i // 4) * 4 if i < 64 else 64
                                nq = (i - g0 + 1) * 128
                                fw.op("act", lambda: nc.scalar.copy(out=Osb[:, 0:nq], in_=O[0:65, 0:nq]), reads=[b_O], writes=[b_Osb])
                                fw.op("pe", lambda: nc.tensor.matmul(bcp[0:64, 0:nq], lhsT=sel[:], rhs=Osb[:, 0:nq], start=True, stop=True),
                                      reads=[b_sel, b_Osb], writes=[b_bcp])
                                fw.op("dve", lambda: nc.vector.reciprocal(out=rec[:, 0:nq], in_=bcp[0:64, 0:nq]), reads=[b_bcp], writes=[b_rec])
                                y_, by_ = ys[yi % 2], b_ys[yi % 2]
                                yi += 1
                                fw.op("dve", lambda: nc.vector.tensor_tensor(out=y_[:, 0:nq], in0=Osb[0:64, 0:nq], in1=rec[:, 0:nq], op=ALU.mult),
                                      reads=[b_Osb, b_rec], writes=[by_])
                                fw.dma("pool", K.yT[l][0 + cc, rows, g0 * 128:g0 * 128 + nq], y_[:, 0:nq], reads=[by_], writes=[K.b_yT])
                fw.barrier()


def _load_cast(K, st, name, dst_tile, b_dst, src_ap, nk, ncols, piece=2048):
    nc, fw = K.nc, K.fw
    with contextlib.ExitStack() as st2:
        wst = [st2.enter_context(nc.sbuf_tensor(K.un("%s_wst%d" % (name, i)), [128, piece], F32)) for i in range(2)]
        b_wst = [Buf(), Buf()]
        it = 0
        for k in range(nk):
            for c0 in range(0, ncols, piece):
                c1 = min(ncols, c0 + piece)
                s_, bs_ = wst[it % 2], b_wst[it % 2]
                fw.dma("sp", s_[:, 0:c1 - c0], src_ap[k * 128:(k + 1) * 128, c0:c1], writes=[bs_])
                e = ("act", "dve", "pool")[it % 3]
                dst = dst_tile[:, k, c0:c1]
                src = s_[:, 0:c1 - c0]
                if e == "act":
                    fw.op(e, lambda: nc.scalar.copy(out=dst, in_=src), reads=[bs_], writes=[b_dst])
                elif e == "dve":
                    fw.op(e, lambda: nc.vector.tensor_copy(out=dst, in_=src), reads=[bs_], writes=[b_dst])
                else:
                    fw.op(e, lambda: nc.gpsimd.tensor_copy(out=dst, in_=src), reads=[bs_], writes=[b_dst])
                it += 1
        fw.barrier()


def _rstd(K, src_ap, reads, junk, b_junk, ss, b_ss, sd, b_sd, rs, b_rs, eps_t, b_eps, n):
    nc, fw = K.nc, K.fw
    fw.op("act", lambda: nc.scalar.activation(out=junk, in_=src_ap, func=AF.Square, accum_out=ss[:]),
          reads=reads, writes=[b_junk, b_ss])
    fw.op("act", lambda: nc.scalar.activation(out=sd[:], in_=ss[:], func=AF.Sqrt, scale=1.0 / n, bias=eps_t[:]),
          reads=[b_ss, b_eps], writes=[b_sd])
    fw.op("dve", lambda: nc.vector.reciprocal(out=rs[:], in_=sd[:]), reads=[b_sd], writes=[b_rs])


def _tok_tiles(l, last):
    tiles = [(0, t * 128) for t in range(NLAT // 128)]
    if not last:
        tiles += [(1, NLAT + t * 128) for t in range(NCTX // 128)]
    return tiles


def _phase_c1(K, l):
    nc, fw = K.nc, K.fw
    last = l == DEPTH - 1
    with contextlib.ExitStack() as st:
        sb = lambda n, s, d: st.enter_context(nc.sbuf_tensor(K.un(n), s, d))
        Wo = sb("c1_W", [128, 8, 1024], BF16); b_W = Buf()
        ident = sb("c1_id", [128, 128], BF16); b_id = Buf()
        idf = sb("c1_idf", [128, 128], F32); b_idf = Buf()
        fw.dma("sp", idf[:], K.ident, writes=[b_idf])
        fw.op("dve", lambda: nc.vector.tensor_copy(out=ident[:], in_=idf[:]), reads=[b_idf], writes=[b_id])
        _load_cast(K, st, "c1", Wo, b_W, K.w_out[l], 8, 1024, piece=1024)
        mt = [sb("c1_mt%d" % i, [128, 1024], F32) for i in range(3)]; b_mt = [Buf() for _ in range(3)]
        yT = [sb("c1_yT%d" % i, [128, 8, 128], BF16) for i in range(3)]; b_yT = [Buf() for _ in range(3)]
        xt = [sb("c1_xt%d" % i, [128, 1024], F32) for i in range(3)]; b_xt = [Buf() for _ in range(3)]
        tmps = [sb("c1_tmp%d" % i, [128, 1024], F32) for i in range(2)]; b_tmps = [Buf(), Buf()]
        xm = [sb("c1_xm%d" % i, [128, 1024], F32) for i in range(3)]; b_xm = [Buf() for _ in range(3)]
        h1s = [sb("c1_h1%d" % i, [128, 1024], F32) for i in range(2)]; b_h1s = [Buf(), Buf()]
        hb = [sb("c1_hb%d" % i, [128, 1024], BF16) for i in range(3)]; b_hb = [Buf() for _ in range(3)]
        hTs = [sb("c1_hT%d" % i, [128, 8, 128], BF16) for i in range(3)]; b_hTs = [Buf() for _ in range(3)]
        junk = sb("c1_junk", [128, 1024], BF16); b_junk = Buf()
        sm = [[sb("c1_s%d_%d" % (a, i), [128, 1], F32) for i in range(3)] for a in range(6)]
        b_sm = [[Buf() for _ in range(3)] for a in range(6)]
        eps_t = sb("c1_eps", [128, 1], F32); b_eps = Buf()
        fw.op("pool", lambda: nc.gpsimd.memset(eps_t[:], EPS), writes=[b_eps])
        yl = [st.enter_context(nc.psum_tensor(K.un("c1_yl%d" % i), [128, 1024], F32)) for i in range(3)]
        b_yl = [Buf() for _ in range(3)]
        tp = [st.enter_context(nc.psum_tensor(K.un("c1_tp%d" % i), [128, 8, 128], BF16)) for i in range(2)]
        b_tp = [Buf(), Buf()]
        cur_j = None
        for ti, (j, tok0) in enumerate(_tok_tiles(l, last)):
            if j != cur_j:
                for i, vi in enumerate((2, 3, 4)):
                    _load_bcast(K, "sp", mt[i], l, vi, j, b_mt[i])
                cur_j = j
            p = ti % 3
            tq_ = ti % 2
            tmp, b_tmp = tmps[tq_], b_tmps[tq_]
            h1, b_h1 = h1s[tq_], b_h1s[tq_]
            lo = tok0 - (NLAT if j else 0)
            fw.dma("sp", yT[p][:], K.yT[l][:, :, tok0:tok0 + 128].rearrange("c p t -> p c t"),
                   reads=[K.b_yT], writes=[b_yT[p]])
            fw.dma("sp", xt[p][:], K.src[l][j][lo:lo + 128, :], reads=[K.b_src[l]], writes=[b_xt[p]])
            for hf in range(2):
                for k in range(8):
                    fw.op("pe", lambda: nc.tensor.matmul(yl[p][:, hf * 512:(hf + 1) * 512], lhsT=yT[p][:, k, :],
                                                         rhs=Wo[:, k, hf * 512:(hf + 1) * 512],
                                                         start=(k == 0), stop=(k == 7)),
                          reads=[b_yT[p], b_W], writes=[b_yl[p]], inc=(hf == 1 and k == 7))
            _rstd(K, yl[p][:], [b_yl[p]], junk[:], b_junk, sm[0][p], b_sm[0][p], sm[1][p], b_sm[1][p],
                  sm[2][p], b_sm[2][p], eps_t, b_eps, D)
            fw.op("dve", lambda: nc.vector.scalar_tensor_tensor(out=tmp[:], in0=yl[p][:], scalar=sm[2][p][:], in1=mt[0][:],
                                                                op0=ALU.mult, op1=ALU.mult),
                  reads=[b_yl[p], b_sm[2][p], b_mt[0]], writes=[b_tmp])
            fw.op("pool", lambda: nc.gpsimd.tensor_tensor(out=xm[p][:], in0=tmp[:], in1=xt[p][:], op=ALU.add),
                  reads=[b_tmp, b_xt[p]], writes=[b_xm[p]])
            fw.dma("pool", K.xmid[tok0:tok0 + 128, :], xm[p][:], reads=[b_xm[p]], writes=[K.b_xmid])
            _rstd(K, xm[p][:], [b_xm[p]], junk[:], b_junk, sm[3][p], b_sm[3][p], sm[4][p], b_sm[4][p],
                  sm[5][p], b_sm[5][p], eps_t, b_eps, D)
            fw.op("dve", lambda: nc.vector.scalar_tensor_tensor(out=h1[:], in0=xm[p][:], scalar=sm[5][p][:], in1=mt[1][:],
                                                                op0=ALU.mult, op1=ALU.mult),
                  reads=[b_xm[p], b_sm[5][p], b_mt[1]], writes=[b_h1])
            fw.op("pool", lambda: nc.gpsimd.tensor_tensor(out=hb[p][:], in0=h1[:], in1=mt[2][:], op=ALU.add),
                  reads=[b_h1, b_mt[2]], writes=[b_hb[p]])
            for k in range(8):
                fw.op("pe", lambda: nc.tensor.transpose(tp[tq_][:, k, :], hb[p][:, k * 128:(k + 1) * 128], ident[:]),
                      reads=[b_hb[p], b_id], writes=[b_tp[tq_]], inc=(k == 7))
            fw.op("act", lambda: nc.scalar.copy(out=hTs[p][:], in_=tp[tq_][:]), reads=[b_tp[tq_]], writes=[b_hTs[p]])
            fw.dma("pool", K.h2T[:, :, tok0:tok0 + 128].rearrange("c p t -> p c t"), hTs[p][:],
                   reads=[b_hTs[p]], writes=[K.b_h2T])
        fw.barrier()


def _phase_c2(K, l):
    nc, fw = K.nc, K.fw
    last = l == DEPTH - 1
    NF = DFF // 128
    with contextlib.ExitStack() as st:
        sb = lambda n, s, d: st.enter_context(nc.sbuf_tensor(K.un(n), s, d))
        W1 = sb("c2_W1", [128, 8, 2 * DFF], BF16); b_W1 = Buf()
        W2 = sb("c2_W2", [128, NF, 1024], BF16); b_W2 = Buf()
        _load_cast(K, st, "c2a", W1, b_W1, K.w_ffn_in[l], 8, 2 * DFF, piece=2816)
        _load_cast(K, st, "c2b", W2, b_W2, K.w_ffn_out[l], NF, 1024, piece=1024)
        gg = sb("c2_gg", [128, 1024], F32); b_gg = Buf()
        hT = [sb("c2_hT%d" % i, [128, 8, 512], BF16) for i in range(2)]; b_hT = [Buf(), Buf()]
        sg = [sb("c2_sg%d" % i, [128, 512], F32) for i in range(2)]; b_sg = [Buf(), Buf()]
        aT = sb("c2_aT", [128, NF, 512], BF16); b_aT = [Buf() for _ in range(NF)]
        xm = [sb("c2_xm%d" % i, [128, 1024], F32) for i in range(2)]; b_xm = [Buf(), Buf()]
        tmp = sb("c2_tmp", [128, 1024], F32); b_tmp = Buf()
        ot = [sb("c2_ot%d" % i, [128, 1024], F32) for i in range(2)]; b_ot = [Buf(), Buf()]
        junk = sb("c2_junk", [128, 1024], BF16); b_junk = Buf()
        sm = [[sb("c2_s%d_%d" % (a, i), [128, 1], F32) for i in range(2)] for a in range(3)]
        b_sm = [[Buf(), Buf()] for a in range(3)]
        eps_t = sb("c2_eps", [128, 1], F32); b_eps = Buf()
        fw.op("pool", lambda: nc.gpsimd.memset(eps_t[:], EPS), writes=[b_eps])
        gu = [st.enter_context(nc.psum_tensor(K.un("c2_gu%d" % i), [128, 2, 512], F32)) for i in range(2)]
        b_gu = [Buf(), Buf()]
        fo = [st.enter_context(nc.psum_tensor(K.un("c2_fo%d" % i), [128, 1024], F32)) for i in range(2)]
        b_fo = [Buf(), Buf()]
        tiles = _tok_tiles(l, last)
        groups = [tiles[i:i + 4] for i in range(0, NLAT // 128, 4)]
        if not last:
            groups.append(tiles[NLAT // 128:])
        cur_j = None
        gi = 0
        si = 0
        for bi, grp in enumerate(groups):
            j, tok0 = grp[0]
            nsub = len(grp)
            ntok = nsub * 128
            if j != cur_j:
                _load_bcast(K, "sp", gg, l, 5, j, b_gg)
                cur_j = j
            hT_, bhT_ = hT[bi % 2], b_hT[bi % 2]
            fw.dma("sp", hT_[:, :, 0:ntok], K.h2T[:, :, tok0:tok0 + ntok].rearrange("c p t -> p c t"),
                   reads=[K.b_h2T], writes=[bhT_])
            for c in range(NF):
                g_, bg_ = gu[gi % 2], b_gu[gi % 2]
                s_, bs_ = sg[gi % 2], b_sg[gi % 2]
                gi += 1
                for half in range(2):
                    col0 = half * DFF + c * 128
                    for k in range(8):
                        fw.op("pe", lambda: nc.tensor.matmul(g_[:, half, 0:ntok], lhsT=W1[:, k, col0:col0 + 128],
                                                             rhs=hT_[:, k, 0:ntok], start=(k == 0), stop=(k == 7)),
                              reads=[b_W1, bhT_], writes=[bg_], inc=(half == 1 and k == 7))
                fw.op("act", lambda: nc.scalar.activation(out=s_[:, 0:ntok], in_=g_[:, 0, 0:ntok], func=AF.Silu),
                      reads=[bg_], writes=[bs_])
                fw.op("dve", lambda: nc.vector.tensor_tensor(out=aT[:, c, 0:ntok], in0=g_[:, 1, 0:ntok], in1=s_[:, 0:ntok], op=ALU.mult),
                      reads=[bg_, bs_], writes=[b_aT[c]])
            for sub in range(nsub):
                p = si % 2
                si += 1
                t0 = tok0 + sub * 128
                fw.dma("sp", xm[p][:], K.xmid[t0:t0 + 128, :], reads=[K.b_xmid], writes=[b_xm[p]])
                for hf in range(2):
                    for c in range(NF):
                        fw.op("pe", lambda: nc.tensor.matmul(fo[p][:, hf * 512:(hf + 1) * 512],
                                                             lhsT=aT[:, c, sub * 128:(sub + 1) * 128],
                                                             rhs=W2[:, c, hf * 512:(hf + 1) * 512],
                                                             start=(c == 0), stop=(c == NF - 1)),
                              reads=[b_aT[c], b_W2], writes=[b_fo[p]], inc=(hf == 1 and c == NF - 1))
                _rstd(K, fo[p][:], [b_fo[p]], junk[:], b_junk, sm[0][p], b_sm[0][p], sm[1][p], b_sm[1][p],
                      sm[2][p], b_sm[2][p], eps_t, b_eps, D)
                fw.op("dve", lambda: nc.vector.scalar_tensor_tensor(out=tmp[:], in0=fo[p][:], scalar=sm[2][p][:], in1=gg[:],
                                                                    op0=ALU.mult, op1=ALU.mult),
                      reads=[b_fo[p], b_sm[2][p], b_gg], writes=[b_tmp])
                fw.op("pool", lambda: nc.gpsimd.tensor_tensor(out=ot[p][:], in0=tmp[:], in1=xm[p][:], op=ALU.add),
                      reads=[b_tmp, b_xm[p]], writes=[b_ot[p]])
                if last:
                    dst, bd = K.out[t0:t0 + 128, :], K.b_out
                else:
                    dst, bd = K.xl1[t0:t0 + 128, :], K.b_src[l + 1]
                fw.dma("pool", dst, ot[p][:], reads=[b_ot[p]], writes=[bd])
        fw.barrier()


def build(dbg=None):
    nc = bass.Bass("TRN2", target_bir_lowering=False)
    K = Ctx()
    K.nc = nc
    dt_in = lambda n, s, d=F32: nc.dram_tensor(n, s, d, kind="ExternalInput").ap()
    K.x = dt_in("x", [NLAT, D])
    K.ctx = dt_in("ctx", [NCTX, D])
    K.cvec = dt_in("cvec", [2, D])
    K.w_mod = dt_in("w_mod", [DEPTH, D, 6 * D])
    K.b_mod = dt_in("b_mod", [DEPTH, 6 * D])
    K.gains = dt_in("gains", [4, DEPTH, D])
    K.w_in = dt_in("w_in", [DEPTH, D, NWCOL])
    K.rope = dt_in("rope", [6, 128, 192])
    K.ident = dt_in("ident", [128, 128])
    K.w_out = dt_in("w_out", [DEPTH, D, D])
    K.w_ffn_in = dt_in("w_ffn_in", [DEPTH, D, 2 * DFF])
    K.w_ffn_out = dt_in("w_ffn_out", [DEPTH, DFF, D])
    K.lru_conv_w = dt_in("lru_conv_w", [DEPTH, 4, 256])
    K.lru_conv_b = dt_in("lru_conv_b", [DEPTH, 256])
    K.lru_gate_w = dt_in("lru_gate_w", [DEPTH, 2, 2, 4, 64, 64])
    K.lru_gate_b = dt_in("lru_gate_b", [DEPTH, 2, 2, 4, 64])
    K.lru_lambda = dt_in("lru_lambda", [DEPTH, 2, 256])
    K.ret_decay = dt_in("ret_decay", [DEPTH, 2, 4])
    K.retc = dt_in("retc", [128, 6, 128])
    K.retcv = dt_in("retcv", [128, 2])
    K.blk64 = dt_in("blk64", [128, 128])
    K.diff_lambda = dt_in("diff_lambda", [DEPTH, 4, 32])
    K.diff_subln = dt_in("diff_subln", [DEPTH, 64])
    K.sel65 = dt_in("sel65", [65, 64])
    K.ones64 = dt_in("ones64", [64, 64])
    K.mcol = dt_in("mcol", [128, 4])
    K.na_nt = _na_schedule()[1].shape[0]
    K.nabias = dt_in("nabias", [DEPTH, 4, K.na_nt, 128, 128])
    phases = dbg["phases"] if dbg else None
    K.stop = dbg.get("stop", 99) if dbg else 99
    K.p1n = dbg.get("p1n", 999) if dbg else 999
    K.p2n = dbg.get("p2n", 999) if dbg else 999
    ext_in = dbg.get("ext_in", set()) if dbg else set()
    ext_out = dbg.get("ext_out", set()) if dbg else set()

    def scr(n, s, d):
        kind = "Internal"
        if n in ext_in:
            kind = "ExternalInput"
        elif n in ext_out:
            kind = "ExternalOutput"
        return nc.dram_tensor(n, s, d, kind=kind).ap()

    K.modvec = scr("modvec", [DEPTH, 6, 2, D], F32); K.b_modvec = Buf()
    K.featB = [scr("featB%d" % l, [12, 128, T], BF16) for l in range(DEPTH)]
    K.featF = [scr("featF%d" % l, [6, 128, T], F32) for l in range(DEPTH)]
    K.vAll = [scr("vAll%d" % l, [T, 768], BF16) for l in range(DEPTH)]
    K.b_feat = Buf()
    K.yT = [scr("yT%d" % l, [8, 128, T], BF16) for l in range(DEPTH)]; K.b_yT = Buf()
    K.xmid = scr("xmid", [T, D], F32); K.b_xmid = Buf()
    K.h2T = scr("h2T", [8, 128, T], BF16); K.b_h2T = Buf()
    K.xl1 = scr("xl1", [T, D], F32)
    K.out = nc.dram_tensor("out", [NLAT, D], F32, kind="ExternalOutput").ap(); K.b_out = Buf()
    K.src = [(K.x, K.ctx), (K.xl1[0:NLAT, :], K.xl1[NLAT:T, :])]
    K.b_src = [Buf(), Buf()]
    on = lambda name: (phases is None) or (name in phases)
    with contextlib.ExitStack() as st:
        K.fw = FW(nc, st)
        def ph(name, fn, *a):
            if on(name):
                with nc.named_scope(name):
                    fn(K, *a)

        ph("P0", _p0_modvec)
        for l in range(DEPTH):
            ph("A%d" % l, _phase_a, l)
            ph("NA%d" % l, _phase_na, l)
            ph("LRU%d" % l, _phase_lru, l)
            ph("RET%d" % l, _phase_ret, l)
            ph("DIFF%d" % l, _phase_diff, l)
            if on("C%d" % l):
                with nc.named_scope("C1_%d" % l):
                    _phase_c1(K, l)
                with nc.named_scope("C2_%d" % l):
                    _phase_c2(K, l)
        K.fw.finish("sp")
        K.n_inst, K.n_wait = K.fw.n_inst, K.fw.n_wait
    return nc, K


def host_inputs(inputs, b):
    f = lambda a: np.ascontiguousarray(np.asarray(a, dtype=np.float32))
    perm = _w_in_perm()
    m = {
        "x": f(inputs["x"][b]),
        "ctx": f(inputs["ctx"][b]),
        "cvec": f(np.stack([np.asarray(inputs["c"])[b], np.asarray(inputs["c_ctx"])], 0)),
        "w_mod": f(inputs["w_mod"]),
        "b_mod": f(inputs["b_mod"]),
        "gains": f(np.stack([inputs["g_pre_mix"], inputs["g_post_mix"], inputs["g_pre_ffn"], inputs["g_post_ffn"]], 0)),
        "w_in": f(np.asarray(inputs["w_in"])[:, :, perm]),
        "rope": _rope_tables(),
        "ident": np.eye(128, dtype=np.float32),
        "w_out": f(inputs["w_out"]),
        "w_ffn_in": f(inputs["w_ffn_in"]),
        "w_ffn_out": f(inputs["w_ffn_out"]),
        "lru_conv_w": f(inputs["lru_conv_w"]),
        "lru_conv_b": f(inputs["lru_conv_b"]),
        "lru_gate_w": f(inputs["lru_gate_w"]),
        "lru_gate_b": f(inputs["lru_gate_b"]),
        "lru_lambda": f(inputs["lru_lambda"]),
        "ret_decay": f(inputs["ret_decay"]),
        "retc": _ret_consts()[0],
        "retcv": _ret_consts()[1],
        "blk64": _blk64(),
        "diff_lambda": f(inputs["diff_lambda"]),
        "diff_subln": f(inputs["diff_subln"]),
        "sel65": _diff_consts()[0],
        "ones64": _diff_consts()[1],
        "mcol": _diff_consts()[2],
        "nabias": _na_bias(inputs["na_rpb"]),
    }
    return m


N_CORES = 4


def kernel(**inputs):
    nc, K = build()
    in_maps = [host_inputs(inputs, c) for c in range(N_CORES)]
    res = run_bass_kernel_spmd(nc, in_maps, core_ids=list(range(N_CORES)))
    out = np.stack([np.asarray(res.results[b]["out"], dtype=np.float32) for b in range(4)], 0)
    return out
```

```python
import contextlib
import math
import numpy as np
import concourse.bass as bass
import concourse.mybir as mybir
from concourse.bass_utils import run_bass_kernel_spmd

F32 = mybir.dt.float32
BF16 = mybir.dt.bfloat16
AF = mybir.ActivationFunctionType
ALU = mybir.AluOpType
AX = mybir.AxisListType

D = 1024
NLAT = 8192
NCTX = 256
T = NLAT + NCTX
DEPTH = 2
DFF = 2816
EPS = 1e-6
NWCOL = 4096
GRID_W = 64


class Buf:
    __slots__ = ("w", "r", "name")

    def __init__(self, name=""):
        self.w = None
        self.r = []
        self.name = name


class FW:
    N_DMA_SEMS = 24

    def __init__(self, nc, stack):
        self.nc = nc
        self.eng = {"pe": nc.tensor, "act": nc.scalar, "dve": nc.vector, "pool": nc.gpsimd, "sp": nc.sync}
        self.sems = {}
        self.count = {}
        for e in self.eng:
            self.sems[e] = stack.enter_context(nc.semaphore("s_" + e))
            self.count[e] = 0
        self.dma_sems = {"hw": [], "sw": []}
        for kind, n in (("hw", 24), ("sw", 24)):
            for i in range(n):
                k = "d%s%d" % (kind, i)
                self.sems[k] = stack.enter_context(nc.semaphore("s_" + k))
                self.count[k] = 0
                self.dma_sems[kind].append(k)
        self.dma_rr = {"hw": 0, "sw": 0}
        self.waited = {e: {} for e in self.eng}
        self.pending_pe = []
        self.n_inst = 0
        self.n_wait = 0

    def _wait(self, e, ev):
        if ev is None:
            return
        k, v = ev
        if e == "pe" and k == "pe":
            return
        if self.waited[e].get(k, 0) >= v:
            return
        self.eng[e].wait_ge(self.sems[k], v)
        self.waited[e][k] = v
        self.n_wait += 1

    def _deps(self, e, reads, writes):
        for b in reads:
            self._wait(e, b.w)
        for b in writes:
            self._wait(e, b.w)
            for ev in b.r:
                self._wait(e, ev)

    def _mark(self, ev, reads, writes):
        for b in reads:
            b.r.append(ev)
            if len(b.r) > 40:
                best = {}
                for k, v in b.r:
                    if best.get(k, 0) < v:
                        best[k] = v
                b.r = list(best.items())
        for b in writes:
            b.w = ev
            b.r = []

    def op(self, e, fn, reads=(), writes=(), inc=True):
        self._deps(e, reads, writes)
        inst = fn()
        self.n_inst += 1
        if e == "pe" and not inc:
            self.pending_pe.append((tuple(reads), tuple(writes)))
            return inst
        self.count[e] += 1
        ev = (e, self.count[e])
        inst.then_inc(self.sems[e], 1)
        if e == "pe" and self.pending_pe:
            for r, w in self.pending_pe:
                self._mark(ev, r, w)
            self.pending_pe = []
        self._mark(ev, reads, writes)
        return inst

    def dma(self, q, out, in_, reads=(), writes=(), **kw):
        kind = "sw" if q == "pool" else "hw"
        k = self.dma_sems[kind][self.dma_rr[kind]]
        self.dma_rr[kind] = (self.dma_rr[kind] + 1) % len(self.dma_sems[kind])
        self._wait(q, (k, self.count[k]))
        self._deps(q, reads, writes)
        inst = self.eng[q].dma_start(out=out, in_=in_, **kw)
        self.count[k] += 16
        inst.then_inc(self.sems[k], 16)
        ev = (k, self.count[k])
        self._mark(ev, reads, writes)
        self.n_inst += 1
        return ev

    def barrier(self):
        for e in self.eng:
            for k in self.sems:
                if k != e and self.count[k] > 0:
                    self._wait(e, (k, self.count[k]))

    def finish(self, e="sp"):
        for k in self.sems:
            if self.count[k] > 0:
                self._wait(e, (k, self.count[k]))


class Ctx:
    _n = 0

    def un(self, name):
        Ctx._n += 1
        return "%s_%d" % (name, Ctx._n)


def _w_in_perm():
    def split(i):
        return np.arange(i * 256, (i + 1) * 256)

    def swap_ret(cols):
        c = cols.reshape(4, 64)
        return np.concatenate([c[:, 32:], c[:, :32]], axis=1).reshape(-1)

    def swap_diff(cols):
        c = cols.reshape(4, 2, 32)
        return np.concatenate([c[:, :, 16:], c[:, :, :16]], axis=2).reshape(-1)

    order = [split(0), split(1), split(3), split(4), split(8),
             split(5), swap_ret(split(5)), split(6), swap_ret(split(6)),
             split(9), swap_diff(split(9)), split(10), swap_diff(split(10)),
             split(2), split(7), split(11)]
    return np.concatenate(order)


def _rope_tables():
    tabs = np.zeros((6, 128, 192), np.float64)
    rows = np.arange(128, dtype=np.float64)
    cols = np.arange(64, dtype=np.float64)
    for p in range(128):
        d = p % 64
        j = d % 32
        sign = -1.0 if d < 32 else 1.0
        inv = 10000.0 ** (-(j % 16) / 16.0)
        if j < 16:
            ang, sl = rows * np.float32(inv), slice(0, 128)
        else:
            ang, sl = cols * np.float32(inv), slice(128, 192)
        tabs[0, p, sl] = np.cos(ang)
        tabs[1, p, sl] = sign * np.sin(ang)
        tabs[2, p, sl] = np.cos(ang) * 0.125
        tabs[3, p, sl] = sign * np.sin(ang) * 0.125
        d = p % 32
        j = d % 16
        sign = -1.0 if d < 16 else 1.0
        inv = 10000.0 ** (-(j % 8) / 8.0)
        if j < 8:
            ang, sl = rows * np.float32(inv), slice(0, 128)
        else:
            ang, sl = cols * np.float32(inv), slice(128, 192)
        tabs[4, p, sl] = np.cos(ang)
        tabs[5, p, sl] = sign * np.sin(ang)
    return tabs.astype(np.float32)


def _p0_modvec(K):
    nc, fw = K.nc, K.fw
    with contextlib.ExitStack() as st:
        sb = lambda n, s, d: st.enter_context(nc.sbuf_tensor(K.un(n), s, d))
        cT = sb("p0_cT", [128, 2, 8], F32); b_cT = Buf()
        sT = sb("p0_sT", [128, 2, 8], F32); b_sT = Buf()
        wm = [sb("p0_wm%d" % i, [128, 8, 512], F32) for i in range(2)]; b_wm = [Buf(), Buf()]
        bm = sb("p0_bm", [2, 6144], F32); b_bm = Buf()
        gn = sb("p0_gn", [2, 4, 1024], F32); b_gn = Buf()
        mv = sb("p0_mv", [2, 6144], F32); b_mv = Buf()
        cv = sb("p0_cv", [2, 6, 1024], F32); b_cv = Buf()
        ps = [st.enter_context(nc.psum_tensor(K.un("p0_ps%d" % i), [2, 512], F32)) for i in range(2)]
        b_ps = [Buf(), Buf()]
        for j in range(2):
            fw.dma("sp", cT[:, j, :], K.cvec[j, :].rearrange("(k p) -> p k", p=128), writes=[b_cT],
                   allow_slow_non_contiguous=True)
        fw.op("act", lambda: nc.scalar.activation(out=sT[:], in_=cT[:], func=AF.Silu), reads=[b_cT], writes=[b_sT])
        it = 0
        for l in range(DEPTH):
            fw.dma("sp", bm[:], K.b_mod[l:l + 1, :].broadcast_to([2, 6144]), writes=[b_bm])
            fw.dma("sp", gn[:], K.gains[:, l, :].unsqueeze(0).broadcast_to([2, 4, 1024]), writes=[b_gn])
            for n in range(12):
                w_, bw_ = wm[it % 2], b_wm[it % 2]
                p_, bp_ = ps[it % 2], b_ps[it % 2]
                it += 1
                src = K.w_mod[l, :, n * 512:(n + 1) * 512].rearrange("(k p) c -> p k c", p=128)
                fw.dma("sp", w_[:, 0:4, :], src[:, 0:4, :], writes=[bw_])
                fw.dma("sp", w_[:, 4:8, :], src[:, 4:8, :], writes=[bw_])
                for k in range(8):
                    fw.op("pe", lambda: nc.tensor.matmul(p_[:], lhsT=sT[:, :, k], rhs=w_[:, k, :],
                                                         start=(k == 0), stop=(k == 7)),
                          reads=[b_sT, bw_], writes=[bp_], inc=(k == 7))
                fw.op("dve", lambda: nc.vector.tensor_tensor(out=mv[:, n * 512:(n + 1) * 512], in0=p_[:],
                                                             in1=bm[:, n * 512:(n + 1) * 512], op=ALU.add),
                      reads=[bp_, b_bm], writes=[b_mv])
            m = lambda i: mv[:, i * 1024:(i + 1) * 1024]
            R, W_ = [b_mv, b_gn], [b_cv]
            fw.op("dve", lambda: nc.vector.scalar_tensor_tensor(out=cv[:, 0, :], in0=m(1), scalar=1.0, in1=gn[:, 0, :],
                                                                op0=ALU.add, op1=ALU.mult), reads=R, writes=W_)
            fw.op("dve", lambda: nc.vector.tensor_copy(out=cv[:, 1, :], in_=m(0)), reads=R, writes=W_)
            fw.op("dve", lambda: nc.vector.tensor_tensor(out=cv[:, 2, :], in0=m(2), in1=gn[:, 1, :], op=ALU.mult),
                  reads=R, writes=W_)
            fw.op("dve", lambda: nc.vector.scalar_tensor_tensor(out=cv[:, 3, :], in0=m(4), scalar=1.0, in1=gn[:, 2, :],
                                                                op0=ALU.add, op1=ALU.mult), reads=R, writes=W_)
            fw.op("dve", lambda: nc.vector.tensor_copy(out=cv[:, 4, :], in_=m(3)), reads=R, writes=W_)
            fw.op("dve", lambda: nc.vector.tensor_tensor(out=cv[:, 5, :], in0=m(5), in1=gn[:, 3, :], op=ALU.mult),
                  reads=R, writes=W_)
            fw.dma("pool", K.modvec[l].rearrange("i j f -> j i f"), cv[:], reads=[b_cv], writes=[K.b_modvec])
        fw.barrier()


def _load_bcast(K, q, tile, l, i, j, buf):
    K.fw.dma(q, tile[:], K.modvec[l, i, j:j + 1, :].broadcast_to([128, 1024]), reads=[K.b_modvec], writes=[buf])


def _phase_a(K, l):
    nc, fw = K.nc, K.fw
    with contextlib.ExitStack() as st:
        sb = lambda n, s, d: st.enter_context(nc.sbuf_tensor(K.un(n), s, d))
        W = sb("a_W", [128, 8, NWCOL], BF16); b_W = Buf()
        ident = sb("a_id", [128, 128], BF16); b_id = Buf()
        with contextlib.ExitStack() as st2:
            wst = [st2.enter_context(nc.sbuf_tensor(K.un("a_wst%d" % i), [128, 2048], F32)) for i in range(2)]
            b_wst = [Buf(), Buf()]
            idf = st2.enter_context(nc.sbuf_tensor(K.un("a_idf"), [128, 128], F32)); b_idf = Buf()
            fw.dma("sp", idf[:], K.ident, writes=[b_idf])
            fw.op("dve", lambda: nc.vector.tensor_copy(out=ident[:], in_=idf[:]), reads=[b_idf], writes=[b_id])
            it = 0
            for k in range(8):
                for hf in range(2):
                    s_, bs_ = wst[it % 2], b_wst[it % 2]
                    fw.dma("sp", s_[:], K.w_in[l, k * 128:(k + 1) * 128, hf * 2048:(hf + 1) * 2048], writes=[bs_])
                    e = ("act", "dve", "pool")[it % 3]
                    dst = W[:, k, hf * 2048:(hf + 1) * 2048]
                    if e == "act":
                        fw.op(e, lambda: nc.scalar.copy(out=dst, in_=s_[:]), reads=[bs_], writes=[b_W])
                    elif e == "dve":
                        fw.op(e, lambda: nc.vector.tensor_copy(out=dst, in_=s_[:]), reads=[bs_], writes=[b_W])
                    else:
                        fw.op(e, lambda: nc.gpsimd.tensor_copy(out=dst, in_=s_[:]), reads=[bs_], writes=[b_W])
                    it += 1
            fw.barrier()
        gm = [sb("a_gm%d" % j, [128, 1024], F32) for j in range(2)]; b_gm = [Buf(), Buf()]
        sh = [sb("a_sh%d" % j, [128, 1024], F32) for j in range(2)]; b_sh = [Buf(), Buf()]
        for j in range(2):
            _load_bcast(K, "sp", gm[j], l, 0, j, b_gm[j])
            _load_bcast(K, "sp", sh[j], l, 1, j, b_sh[j])
        rtab = sb("a_rtab", [128, 6, 192], F32); b_rtab = Buf()
        fw.dma("sp", rtab[:], K.rope.rearrange("s p c -> p s c"), writes=[b_rtab])
        rope = sb("a_rope", [128, 6, 512], F32); b_rope = Buf()
        xt = [sb("a_xt%d" % i, [128, 1024], F32) for i in range(2)]; b_xt = [Buf(), Buf()]
        junk = sb("a_junk", [128, 1024], BF16); b_junk = Buf()
        ss = [sb("a_ss%d" % i, [128, 1], F32) for i in range(2)]; b_ss = [Buf(), Buf()]
        sd = [sb("a_sd%d" % i, [128, 1], F32) for i in range(2)]; b_sd = [Buf(), Buf()]
        rs = [sb("a_rs%d" % i, [128, 1], F32) for i in range(2)]; b_rs = [Buf(), Buf()]
        h1 = sb("a_h1", [128, 1024], F32); b_h1 = Buf()
        hb = [sb("a_hb%d" % i, [128, 1024], BF16) for i in range(2)]; b_hb = [Buf(), Buf()]
        hT = [sb("a_hT%d" % i, [128, 8, 512], BF16) for i in range(2)]; b_hT = [Buf(), Buf()]
        stB = sb("a_stB", [128, 12, 512], BF16); b_stB = [Buf() for _ in range(12)]
        stF = sb("a_stF", [128, 6, 512], F32); b_stF = [Buf() for _ in range(6)]
        stV = sb("a_stV", [128, 4, 768], BF16); b_stV = [Buf() for _ in range(4)]
        t1 = [sb("a_t1%d" % i, [128, 512], F32) for i in range(2)]; b_t1 = [Buf(), Buf()]
        t2 = [sb("a_t2%d" % i, [128, 512], F32) for i in range(2)]; b_t2 = [Buf(), Buf()]
        tp = [st.enter_context(nc.psum_tensor(K.un("a_tp%d" % i), [128, 8, 128], BF16)) for i in range(2)]
        b_tp = [Buf(), Buf()]
        mm = [st.enter_context(nc.psum_tensor(K.un("a_mm%d" % i), [128, 512], F32)) for i in range(6)]
        b_mm = [Buf() for _ in range(6)]
        mmi = [0]
        eps_t = sb("a_eps", [128, 1], F32); b_eps = Buf()
        fw.op("pool", lambda: nc.gpsimd.memset(eps_t[:], EPS), writes=[b_eps])

        def next_mm():
            i = mmi[0] % 6
            mmi[0] += 1
            return mm[i], b_mm[i]

        blocks = [(K.src[l][0], b * 512, 512, 0) for b in range(NLAT // 512)]
        blocks.append((K.src[l][1], NLAT, NCTX, 1))
        sub_c = [0]

        def prep(bi):
            src, tok0, ntok, j = blocks[bi]
            nsub = ntok // 128
            hT_, bhT_ = hT[bi % 2], b_hT[bi % 2]
            if j == 0:
                r0 = tok0 // GRID_W
                nr = ntok // GRID_W
                for s in range(6):
                    o = rope[:, s, 0:ntok].rearrange("p (r c) -> p r c", c=GRID_W)
                    a = rtab[:, s, r0:r0 + nr].unsqueeze(2).broadcast_to([128, nr, GRID_W])
                    b_ = rtab[:, s, 128:192].unsqueeze(1).broadcast_to([128, nr, GRID_W])
                    fw.op("pool", lambda: nc.gpsimd.tensor_tensor(out=o, in0=a, in1=b_, op=ALU.add),
                          reads=[b_rtab], writes=[b_rope])
            else:
                for s, val in enumerate((1.0, 0.0, 0.125, 0.0, 1.0, 0.0)):
                    fw.op("pool", lambda: nc.gpsimd.memset(rope[:, s, :], val), writes=[b_rope])
            for s_ in range(nsub):
                sub_i = sub_c[0]
                x_, bx_ = xt[sub_i % 2], b_xt[sub_i % 2]
                ss_, bss_ = ss[sub_i % 2], b_ss[sub_i % 2]
                sd_, bsd_ = sd[sub_i % 2], b_sd[sub_i % 2]
                rs_, brs_ = rs[sub_i % 2], b_rs[sub_i % 2]
                hb_, bhb_ = hb[sub_i % 2], b_hb[sub_i % 2]
                tp_, btp_ = tp[sub_i % 2], b_tp[sub_i % 2]
                sub_c[0] += 1
                lo = (tok0 - (NLAT if j else 0)) + s_ * 128
                fw.dma("sp", x_[:], src[lo:lo + 128, :], reads=[K.b_src[l]], writes=[bx_])
                fw.op("act", lambda: nc.scalar.activation(out=junk[:], in_=x_[:], func=AF.Square, accum_out=ss_[:]),
                      reads=[bx_], writes=[b_junk, bss_])
                fw.op("act", lambda: nc.scalar.activation(out=sd_[:], in_=ss_[:], func=AF.Sqrt, scale=1.0 / D,
                                                          bias=eps_t[:]), reads=[bss_, b_eps], writes=[bsd_])
                fw.op("dve", lambda: nc.vector.reciprocal(out=rs_[:], in_=sd_[:]), reads=[bsd_], writes=[brs_])
                fw.op("dve", lambda: nc.vector.scalar_tensor_tensor(out=h1[:], in0=x_[:], scalar=rs_[:], in1=gm[j][:],
                                                                    op0=ALU.mult, op1=ALU.mult),
                      reads=[bx_, brs_, b_gm[j]], writes=[b_h1])
                fw.op("pool", lambda: nc.gpsimd.tensor_tensor(out=hb_[:], in0=h1[:], in1=sh[j][:], op=ALU.add),
                      reads=[b_h1, b_sh[j]], writes=[bhb_])
                for k in range(8):
                    fw.op("pe", lambda: nc.tensor.transpose(tp_[:, k, :], hb_[:, k * 128:(k + 1) * 128], ident[:]),
                          reads=[bhb_, b_id], writes=[btp_], inc=(k == 7))
                fw.op("dve", lambda: nc.vector.tensor_copy(out=hT_[:, :, s_ * 128:(s_ + 1) * 128], in_=tp_[:]),
                      reads=[btp_], writes=[bhT_])

        prep(0)
        for bi, (src, tok0, ntok, j) in enumerate(blocks):
            nsub = ntok // 128
            hT_, bhT_ = hT[bi % 2], b_hT[bi % 2]

            def fm(c):
                p_, bp_ = next_mm()
                for k in range(8):
                    fw.op("pe", lambda: nc.tensor.matmul(p_[:, 0:ntok], lhsT=W[:, k, c * 128:(c + 1) * 128],
                                                         rhs=hT_[:, k, 0:ntok], start=(k == 0), stop=(k == 7)),
                          reads=[b_W, bhT_], writes=[bp_], inc=(k == 7))
                return p_, bp_

            for c in range(4):
                p_, bp_ = fm(c)
                fw.op("act", lambda: nc.scalar.copy(out=stB[:, c, 0:ntok], in_=p_[:, 0:ntok]),
                      reads=[bp_], writes=[b_stB[c]])
            for c in range(4, 10):
                p_, bp_ = fm(c)
                fw.op("act", lambda: nc.scalar.copy(out=stF[:, c - 4, 0:ntok], in_=p_[:, 0:ntok]),
                      reads=[bp_], writes=[b_stF[c - 4]])
            ri = 0
            for g, (c0, tabs, o0) in enumerate(((10, (0, 1), 4), (14, (2, 3), 6), (18, (4, 5), 8), (22, (4, 5), 10))):
                for hh in range(2):
                    pa, bpa = fm(c0 + hh)
                    pb, bpb = fm(c0 + 2 + hh)
                    t1_, bt1_ = t1[ri % 2], b_t1[ri % 2]
                    t2_, bt2_ = t2[ri % 2], b_t2[ri % 2]
                    ri += 1
                    fw.op("dve", lambda: nc.vector.tensor_tensor(out=t1_[:, 0:ntok], in0=pa[:, 0:ntok],
                                                                 in1=rope[:, tabs[0], 0:ntok], op=ALU.mult),
                          reads=[bpa, b_rope], writes=[bt1_])
                    fw.op("dve", lambda: nc.vector.tensor_tensor(out=t2_[:, 0:ntok], in0=pb[:, 0:ntok],
                                                                 in1=rope[:, tabs[1], 0:ntok], op=ALU.mult),
                          reads=[bpb, b_rope], writes=[bt2_])
                    fw.op("pool", lambda: nc.gpsimd.tensor_tensor(out=stB[:, o0 + hh, 0:ntok], in0=t1_[:, 0:ntok],
                                                                  in1=t2_[:, 0:ntok], op=ALU.add),
                          reads=[bt1_, bt2_], writes=[b_stB[o0 + hh]])
            fw.dma("pool", K.featB[l][:, :, tok0:tok0 + ntok].rearrange("c p t -> p c t"), stB[:, :, 0:ntok],
                   reads=b_stB, writes=[K.b_feat])
            fw.dma("pool", K.featF[l][:, :, tok0:tok0 + ntok].rearrange("c p t -> p c t"), stF[:, :, 0:ntok],
                   reads=b_stF, writes=[K.b_feat])
            if bi + 1 < len(blocks):
                prep(bi + 1)
            for s_ in range(nsub):
                for (n0, n1) in ((0, 512), (512, 768)):
                    p_, bp_ = next_mm()
                    for k in range(8):
                        fw.op("pe", lambda: nc.tensor.matmul(p_[:, 0:n1 - n0], lhsT=hT_[:, k, s_ * 128:(s_ + 1) * 128],
                                                             rhs=W[:, k, 3328 + n0:3328 + n1],
                                                             start=(k == 0), stop=(k == 7)),
                              reads=[b_W, bhT_], writes=[bp_], inc=(k == 7))
                    fw.op("act", lambda: nc.scalar.copy(out=stV[:, s_, n0:n1], in_=p_[:, 0:n1 - n0]),
                          reads=[bp_], writes=[b_stV[s_]])
            fw.dma("pool", K.vAll[l][tok0:tok0 + ntok, :].rearrange("(s p) c -> p s c", p=128), stV[:, 0:nsub, :],
                   reads=b_stV[0:nsub], writes=[K.b_feat])
        fw.barrier()


def _phase_lru(K, l):
    nc, fw = K.nc, K.fw
    CH = 2048
    for cc in range(2):
        with contextlib.ExitStack() as st:
            sb = lambda n, s, d: st.enter_context(nc.sbuf_tensor(K.un(n), s, d))
            A = sb("l_A", [128, T], F32); b_A = Buf()
            B = sb("l_B", [128, T], F32); b_B = Buf()
            Bb = sb("l_Bb", [128, T], BF16); b_Bb = Buf()
            C = sb("l_C", [128, T], F32); b_C = Buf()
            Dd = sb("l_D", [128, T], F32); b_D = Buf()
            E = sb("l_E", [128, T], F32); b_E = Buf()
            cw = sb("l_cw", [128, 4], F32); b_cw = Buf()
            cb = sb("l_cb", [128, 1], F32); b_cb = Buf()
            gwf = sb("l_gwf", [128, 4, 128], F32); b_gwf = Buf()
            gw = sb("l_gw", [128, 4, 128], BF16); b_gw = Buf()
            gb = sb("l_gb", [128, 4], F32); b_gb = Buf()
            lam = sb("l_lam", [128, 2], F32); b_lam = Buf()
            cn = sb("l_cn", [128, 2], F32); b_cn = Buf()
            one = sb("l_one", [128, 1], F32); b_one = Buf()
            CG = 1056
            gst = [sb("l_gst%d" % i, [128, CG], F32) for i in range(2)]; b_gst = [Buf(), Buf()]
            yst = [sb("l_yst%d" % i, [128, CG], BF16) for i in range(2)]; b_yst = [Buf(), Buf()]
            ps = [st.enter_context(nc.psum_tensor(K.un("l_ps%d" % i), [128, CH], F32)) for i in range(2)]
            b_ps = [Buf(), Buf()]
            for c0 in range(0, T, 2112):
                fw.dma("sp", A[:, c0:c0 + 2112], K.featF[l][cc, :, c0:c0 + 2112], reads=[K.b_feat], writes=[b_A])
            fw.dma("sp", cw[:], K.lru_conv_w[l, :, cc * 128:(cc + 1) * 128].rearrange("k p -> p k"), writes=[b_cw],
                   allow_slow_non_contiguous=True)
            fw.dma("sp", cb[:], K.lru_conv_b[l, cc * 128:(cc + 1) * 128].rearrange("(p o) -> p o", o=1), writes=[b_cb])
            fw.op("pool", lambda: nc.gpsimd.memset(gwf[:], 0.0), writes=[b_gwf])
            fw.op("pool", lambda: nc.gpsimd.memset(one[:], 1.0), writes=[b_one])
            for dr in range(2):
                for g in range(2):
                    for bk in range(2):
                        fw.dma("sp", gwf[bk * 64:(bk + 1) * 64, dr * 2 + g, bk * 64:(bk + 1) * 64],
                               K.lru_gate_w[l, dr, g, cc * 2 + bk], writes=[b_gwf])
                    fw.dma("sp", gb[:, dr * 2 + g:dr * 2 + g + 1],
                           K.lru_gate_b[l, dr, g, cc * 2:cc * 2 + 2, :].rearrange("k (d o) -> (k d) o", o=1), writes=[b_gb])
                fw.dma("sp", lam[:, dr:dr + 1], K.lru_lambda[l, dr, cc * 128:(cc + 1) * 128].rearrange("(p o) -> p o", o=1),
                       writes=[b_lam])
            fw.op("dve", lambda: nc.vector.tensor_copy(out=gw[:], in_=gwf[:]), reads=[b_gwf], writes=[b_gw])
            fw.op("act", lambda: nc.scalar.activation(out=cn[:], in_=lam[:], func=AF.Exp, scale=-1.0), reads=[b_lam], writes=[b_cn])
            fw.op("act", lambda: nc.scalar.activation(out=cn[:], in_=cn[:], func=AF.Ln, bias=one[:]), reads=[b_cn, b_one], writes=[b_cn])
            fw.op("dve", lambda: nc.vector.tensor_scalar(out=cn[:], in0=cn[:], scalar1=-8.0, scalar2=None, op0=ALU.mult),
                  reads=[b_cn], writes=[b_cn])
            for (s0, s1) in ((0, NLAT), (NLAT, T)):
                fw.op("dve", lambda: nc.vector.tensor_scalar(out=B[:, s0:s1], in0=A[:, s0:s1], scalar1=cw[:, 1:2], scalar2=cb[:, 0:1],
                                                             op0=ALU.mult, op1=ALU.add), reads=[b_A, b_cw, b_cb], writes=[b_B])
                fw.op("dve", lambda: nc.vector.scalar_tensor_tensor(out=B[:, s0 + 1:s1], in0=A[:, s0:s1 - 1], scalar=cw[:, 0:1],
                                                                    in1=B[:, s0 + 1:s1], op0=ALU.mult, op1=ALU.add),
                      reads=[b_A, b_cw, b_B], writes=[b_B])
                fw.op("dve", lambda: nc.vector.scalar_tensor_tensor(out=B[:, s0:s1 - 1], in0=A[:, s0 + 1:s1], scalar=cw[:, 2:3],
                                                                    in1=B[:, s0:s1 - 1], op0=ALU.mult, op1=ALU.add),
                      reads=[b_A, b_cw, b_B], writes=[b_B])
                fw.op("dve", lambda: nc.vector.scalar_tensor_tensor(out=B[:, s0:s1 - 2], in0=A[:, s0 + 2:s1], scalar=cw[:, 3:4],
                                                                    in1=B[:, s0:s1 - 2], op0=ALU.mult, op1=ALU.add),
                      reads=[b_A, b_cw, b_B], writes=[b_B])
            fw.op("pool", lambda: nc.gpsimd.tensor_copy(out=Bb[:], in_=B[:]), reads=[b_B], writes=[b_Bb])
            pi = 0
            chunks = [(c0, min(T, c0 + CH)) for c0 in range(0, T, CH)]
            for dr in range(2):
                for g, (dst, bdst) in enumerate(((C, b_C), (Dd, b_D))):
                    for (c0, c1) in chunks:
                        p_, bp_ = ps[pi % 2], b_ps[pi % 2]
                        pi += 1
                        for t0 in range(c0, c1, 512):
                            t1 = min(c1, t0 + 512)
                            fw.op("pe", lambda: nc.tensor.matmul(p_[:, t0 - c0:t1 - c0], lhsT=gw[:, dr * 2 + g, :], rhs=Bb[:, t0:t1],
                                                                 start=True, stop=True),
                                  reads=[b_gw, b_Bb], writes=[bp_], inc=(t1 == c1))
                        fw.op("act", lambda: nc.scalar.activation(out=dst[:, c0:c1], in_=p_[:, 0:c1 - c0], func=AF.Sigmoid,
                                                                  bias=gb[:, dr * 2 + g:dr * 2 + g + 1]),
                              reads=[bp_, b_gb], writes=[bdst])
                fw.op("act", lambda: nc.scalar.activation(out=C[:], in_=C[:], func=AF.Exp, scale=cn[:, dr:dr + 1]),
                      reads=[b_C, b_cn], writes=[b_C])
                fw.op("pool", lambda: nc.gpsimd.tensor_tensor(out=A[:], in0=C[:], in1=C[:], op=ALU.mult), reads=[b_C], writes=[b_A])
                fw.op("act", lambda: nc.scalar.activation(out=A[:], in_=A[:], func=AF.Sqrt, scale=-1.0, bias=one[:]),
                      reads=[b_A, b_one], writes=[b_A])
                fw.op("dve", lambda: nc.vector.tensor_tensor(out=Dd[:], in0=Dd[:], in1=B[:], op=ALU.mult), reads=[b_D, b_B], writes=[b_D])
                fw.op("dve", lambda: nc.vector.tensor_tensor(out=Dd[:], in0=Dd[:], in1=A[:], op=ALU.mult), reads=[b_D, b_A], writes=[b_D])
                if dr == 0:
                    segs = [(NLAT, T)] + [(c0, c0 + CH) for c0 in range(0, NLAT, CH)]
                    prev = None
                    for (c0, c1) in segs:
                        init = 0.0 if prev is None else A[:, prev - 1:prev]
                        fw.op("dve", lambda: nc.vector.tensor_tensor_scan(out=A[:, c0:c1], data0=C[:, c0:c1], data1=Dd[:, c0:c1],
                                                                          initial=init, op0=ALU.mult, op1=ALU.add),
                              reads=[b_C, b_D, b_A], writes=[b_A])
                        prev = c1
                else:
                    segs = [(NLAT, T)] + [(c0, c0 + CH) for c0 in range(NLAT - CH, -1, -CH)]
                    prev = None
                    rev = lambda ap: ap[:, ::-1]
                    for (c0, c1) in segs:
                        init = 0.0 if prev is None else A[:, prev:prev + 1]
                        fw.op("dve", lambda: nc.vector.tensor_tensor_scan(out=rev(A[:, c0:c1]), data0=rev(C[:, c0:c1]),
                                                                          data1=rev(Dd[:, c0:c1]), initial=init,
                                                                          op0=ALU.mult, op1=ALU.add),
                              reads=[b_C, b_D, b_A], writes=[b_A])
                        prev = c0
                if dr == 0:
                    fw.op("pool", lambda: nc.gpsimd.tensor_copy(out=E[:], in_=A[:]), reads=[b_A], writes=[b_E])
                else:
                    fw.op("pool", lambda: nc.gpsimd.tensor_tensor(out=E[:], in0=E[:], in1=A[:], op=ALU.add), reads=[b_A, b_E], writes=[b_E])
            for ci, (c0, c1) in enumerate([(c0, c0 + CG) for c0 in range(0, T, CG)]):
                g_, bg_ = gst[ci % 2], b_gst[ci % 2]
                y_, by_ = yst[ci % 2], b_yst[ci % 2]
                fw.dma("sp", g_[:, 0:c1 - c0], K.featF[l][2 + cc, :, c0:c1], reads=[K.b_feat], writes=[bg_])
                fw.op("act", lambda: nc.scalar.activation(out=g_[:, 0:c1 - c0], in_=g_[:, 0:c1 - c0], func=AF.Gelu_apprx_tanh),
                      reads=[bg_], writes=[bg_])
                fw.op("dve", lambda: nc.vector.tensor_tensor(out=y_[:, 0:c1 - c0], in0=g_[:, 0:c1 - c0], in1=E[:, c0:c1], op=ALU.mult),
                      reads=[bg_, b_E], writes=[by_])
                fw.dma("pool", K.yT[l][2 + cc, :, c0:c1], y_[:, 0:c1 - c0], reads=[by_], writes=[K.b_yT])
            fw.barrier()


def _ret_consts():
    j = np.arange(128)[:, None].astype(np.float32)
    i = np.arange(128)[None, :].astype(np.float32)
    z = np.zeros((128, 128), np.float32)
    c = np.stack([np.maximum(i - j, 0), (i >= j).astype(np.float32), np.maximum(j - i - 1, 0),
                  (j > i).astype(np.float32), z + i + 1, z + 127 - i], 1).astype(np.float32)
    cv = np.stack([127 - np.arange(128), np.arange(128)], 1).astype(np.float32)
    return np.ascontiguousarray(c), np.ascontiguousarray(cv)


def _blk64():
    b = np.zeros((128, 128), np.float32)
    b[:64, :64] = 1.0 / 64
    b[64:, 64:] = 1.0 / 64
    return b


def _phase_ret(K, l):
    nc, fw = K.nc, K.fw
    last = l == DEPTH - 1
    NCH = T // 128
    with contextlib.ExitStack() as st0:
        sb0 = lambda n, s, d: st0.enter_context(nc.sbuf_tensor(K.un(n), s, d))
        rc = sb0("r_rc", [128, 6, 128], F32); b_rc = Buf()
        cv = sb0("r_cv", [128, 2], F32); b_cv = Buf()
        rd = sb0("r_rd", [128, 8], F32); b_rd = Buf()
        lg = sb0("r_lg", [128, 8], F32); b_lg = Buf()
        one = sb0("r_one", [128, 1], F32); b_one = Buf()
        eps_t = sb0("r_eps", [128, 1], F32); b_eps = Buf()
        blkf = sb0("r_blkf", [128, 128], F32); b_blkf = Buf()
        blk = sb0("r_blk", [128, 128], BF16); b_blk = Buf()
        idf = sb0("r_idf", [128, 128], F32); b_idf = Buf()
        ident = sb0("r_id", [128, 128], BF16); b_id = Buf()
        fw.dma("sp", rc[:], K.retc, writes=[b_rc])
        fw.dma("sp", cv[:], K.retcv, writes=[b_cv])
        fw.dma("sp", rd[:], K.ret_decay[l:l + 1].rearrange("o a h -> o (a h)").broadcast_to([128, 8]), writes=[b_rd])
        fw.dma("sp", blkf[:], K.blk64, writes=[b_blkf])
        fw.dma("sp", idf[:], K.ident, writes=[b_idf])
        fw.op("dve", lambda: nc.vector.tensor_copy(out=blk[:], in_=blkf[:]), reads=[b_blkf], writes=[b_blk])
        fw.op("dve", lambda: nc.vector.tensor_copy(out=ident[:], in_=idf[:]), reads=[b_idf], writes=[b_id])
        fw.op("pool", lambda: nc.gpsimd.memset(one[:], 1.0), writes=[b_one])
        fw.op("pool", lambda: nc.gpsimd.memset(eps_t[:], EPS), writes=[b_eps])
        fw.op("act", lambda: nc.scalar.activation(out=lg[:], in_=rd[:], func=AF.Exp, scale=-1.0), reads=[b_rd], writes=[b_lg])
        fw.op("act", lambda: nc.scalar.activation(out=lg[:], in_=lg[:], func=AF.Ln, bias=one[:]), reads=[b_lg, b_one], writes=[b_lg])
        fw.op("dve", lambda: nc.vector.tensor_scalar(out=lg[:], in0=lg[:], scalar1=-1.0, scalar2=None, op0=ALU.mult),
              reads=[b_lg], writes=[b_lg])
        for cc in range(2):
            with contextlib.ExitStack() as st:
                sb = lambda n, s, d: st.enter_context(nc.sbuf_tensor(K.un(n), s, d))
                QT = sb("r_QT", [128, NCH, 128], BF16); b_QT = Buf()
                KT = sb("r_KT", [128, NCH, 128], BF16); b_KT = Buf()
                V = sb("r_V", [128, NCH, 128], BF16); b_V = Buf()
                qf = sb("r_qf", [128, NCH, 128], BF16); b_qf = Buf()
                qb = sb("r_qb", [128, NCH, 128], BF16); b_qb = Buf()
                KVs = [sb("r_KVs%d" % d_, [128, NCH + 6, 64], F32) for d_ in range(2)]; b_KVs = [Buf(), Buf()]
                Sbf = [sb("r_Sbf%d" % d_, [128, NCH, 64], BF16) for d_ in range(2)]; b_Sbf = [Buf(), Buf()]
                lgp = sb("r_lgp", [128, 2], F32); b_lgp = Buf()
                g128 = sb("r_g128", [128, 2], F32); b_g128 = Buf()
                M = sb("r_M", [128, 2, 128], F32); b_M = Buf()
                mt = sb("r_mt", [128, 2, 128], F32); b_mt = Buf()
                df = sb("r_df", [128, 128], F32); b_df = Buf()
                db = sb("r_db", [128, 128], F32); b_db = Buf()
                kdec = sb("r_kdec", [128, 2, 2], F32); b_kdec = Buf()
                for c0 in range(0, NCH, 22):
                    fw.dma("sp", QT[:, c0:c0 + 22, :], K.featB[l][4 + cc, :, c0 * 128:(c0 + 22) * 128].rearrange("p (c i) -> p c i", i=128),
                           reads=[K.b_feat], writes=[b_QT])
                    fw.dma("sp", KT[:, c0:c0 + 22, :], K.featB[l][6 + cc, :, c0 * 128:(c0 + 22) * 128].rearrange("p (c i) -> p c i", i=128),
                           reads=[K.b_feat], writes=[b_KT])
                for c0 in range(0, NCH, 6):
                    fw.dma("sp", V[:, c0:c0 + 6, :],
                           K.vAll[l][c0 * 128:(c0 + 6) * 128, 256 + cc * 128:256 + (cc + 1) * 128].rearrange("(c p) e -> p c e", p=128),
                           reads=[K.b_feat], writes=[b_V])
                for dr in range(2):
                    for hh in range(2):
                        col = dr * 4 + 2 * cc + hh
                        fw.op("dve", lambda: nc.vector.tensor_copy(out=lgp[hh * 64:(hh + 1) * 64, dr:dr + 1],
                                                                   in_=lg[hh * 64:(hh + 1) * 64, col:col + 1]),
                              reads=[b_lg], writes=[b_lgp])
                fw.op("act", lambda: nc.scalar.activation(out=g128[:], in_=lgp[:], func=AF.Exp, scale=128.0), reads=[b_lgp], writes=[b_g128])
                for hh in range(2):
                    h = 2 * cc + hh
                    fw.op("act", lambda: nc.scalar.activation(out=M[:, hh, :], in_=rc[:, 0, :], func=AF.Exp, scale=lg[:, h:h + 1]),
                          reads=[b_rc, b_lg], writes=[b_M])
                    fw.op("act", lambda: nc.scalar.activation(out=mt[:, hh, :], in_=rc[:, 2, :], func=AF.Exp, scale=lg[:, 4 + h:5 + h]),
                          reads=[b_rc, b_lg], writes=[b_mt])
                    fw.op("dve", lambda: nc.vector.tensor_tensor(out=M[:, hh, :], in0=M[:, hh, :], in1=rc[:, 1, :], op=ALU.mult),
                          reads=[b_M, b_rc], writes=[b_M])
                    fw.op("dve", lambda: nc.vector.tensor_tensor(out=mt[:, hh, :], in0=mt[:, hh, :], in1=rc[:, 3, :], op=ALU.mult),
                          reads=[b_mt, b_rc], writes=[b_mt])
                    fw.op("act", lambda: nc.scalar.activation(out=kdec[:, 0, hh:hh + 1], in_=lg[:, h:h + 1], func=AF.Exp, scale=cv[:, 0:1]),
                          reads=[b_lg, b_cv], writes=[b_kdec])
                    fw.op("act", lambda: nc.scalar.activation(out=kdec[:, 1, hh:hh + 1], in_=lg[:, 4 + h:5 + h], func=AF.Exp, scale=cv[:, 1:2]),
                          reads=[b_lg, b_cv], writes=[b_kdec])
                fw.op("dve", lambda: nc.vector.tensor_tensor(out=M[:], in0=M[:], in1=mt[:], op=ALU.add), reads=[b_M, b_mt], writes=[b_M])
                fw.op("act", lambda: nc.scalar.activation(out=df[:], in_=rc[:, 4, :], func=AF.Exp, scale=lgp[:, 0:1]),
                      reads=[b_rc, b_lgp], writes=[b_df])
                fw.op("act", lambda: nc.scalar.activation(out=db[:], in_=rc[:, 5, :], func=AF.Exp, scale=lgp[:, 1:2]),
                      reads=[b_rc, b_lgp], writes=[b_db])
                fw.op("dve", lambda: nc.vector.tensor_tensor(out=qf[:], in0=QT[:], in1=df[:].unsqueeze(1).broadcast_to([128, NCH, 128]), op=ALU.mult),
                      reads=[b_QT, b_df], writes=[b_qf])
                fw.op("pool", lambda: nc.gpsimd.tensor_tensor(out=qb[:], in0=QT[:], in1=db[:].unsqueeze(1).broadcast_to([128, NCH, 128]), op=ALU.mult),
                      reads=[b_QT, b_db], writes=[b_qb])
                if K.stop <= 0:
                    fw.barrier()
                    continue
                with contextlib.ExitStack() as st1:
                    tpk = [st1.enter_context(nc.psum_tensor(K.un("r_tpk%d" % i), [128, 1024], BF16)) for i in range(2)]
                    b_tpk = [Buf(), Buf()]
                    kvp = [[st1.enter_context(nc.psum_tensor(K.un("r_kvp%d_%d" % (d_, i)), [128, 8, 64], F32)) for i in range(2)] for d_ in range(2)]
                    b_kvp = [[Buf(), Buf()] for d_ in range(2)]
                    kd = [[st1.enter_context(nc.sbuf_tensor(K.un("r_kd%d_%d" % (d_, i)), [128, 2, 64], BF16)) for i in range(2)] for d_ in range(2)]
                    b_kd = [[Buf(), Buf()] for d_ in range(2)]
                    for c in range(min(NCH, K.p1n)):
                        p = c % 2
                        grp = (c // 8) % 2
                        fw.op("pe", lambda: nc.tensor.transpose(tpk[p][:, 0:128], KT[:, c, :], ident[:]), reads=[b_KT, b_id], writes=[b_tpk[p]])
                        for dr in range(2):
                            fw.op("dve", lambda: nc.vector.tensor_tensor(out=kd[dr][p][:], in0=tpk[p][:, 0:128].rearrange("p (h d) -> p h d", d=64),
                                                                         in1=kdec[:, dr, :].unsqueeze(2).broadcast_to([128, 2, 64]), op=ALU.mult),
                                  reads=[b_tpk[p], b_kdec], writes=[b_kd[dr][p]])
                        for dr in range(2):
                            for hh in range(2):
                                fw.op("pe", lambda: nc.tensor.matmul(kvp[dr][grp][hh * 64:(hh + 1) * 64, c % 8, :], lhsT=kd[dr][p][:, hh, :],
                                                                     rhs=V[:, c, hh * 64:(hh + 1) * 64], start=True, stop=True),
                                      reads=[b_kd[dr][p], b_V], writes=[b_kvp[dr][grp]], inc=(hh == 1))
                        if c % 8 == 7 or c == min(NCH, K.p1n) - 1:
                            n8 = c % 8 + 1
                            c8 = c - (n8 - 1)
                            for dr in range(2):
                                fw.op("act", lambda: nc.scalar.copy(out=KVs[dr][:, c8:c8 + n8, :], in_=kvp[dr][grp][:, 0:n8, :]),
                                      reads=[b_kvp[dr][grp]], writes=[b_KVs[dr]])
                    fw.barrier()
                if K.stop <= 1:
                    continue
                with contextlib.ExitStack() as st1:
                    S = [[st1.enter_context(nc.sbuf_tensor(K.un("r_S%d_%d" % (d_, i)), [128, 64], F32)) for i in range(2)] for d_ in range(2)]
                    b_S = [[Buf(), Buf()] for d_ in range(2)]
                    orders = [[64, 65] + list(range(64)), [65, 64] + list(range(63, -1, -1))]
                    for dr in range(2):
                        fw.op("pool", lambda: nc.gpsimd.memset(S[dr][0][:], 0.0), writes=[b_S[dr][0]])
                    for n in range(NCH):
                        for dr in range(2):
                            c = orders[dr][n]
                            cur, nxt = S[dr][n % 2], S[dr][(n + 1) % 2]
                            bcur, bnxt = b_S[dr][n % 2], b_S[dr][(n + 1) % 2]
                            eng = "act" if dr == 0 else "pool"
                            if eng == "act":
                                fw.op("act", lambda: nc.scalar.copy(out=Sbf[dr][:, c, :], in_=cur[:]), reads=[bcur], writes=[b_Sbf[dr]])
                            else:
                                fw.op("pool", lambda: nc.gpsimd.tensor_copy(out=Sbf[dr][:, c, :], in_=cur[:]), reads=[bcur], writes=[b_Sbf[dr]])
                            fw.op("dve", lambda: nc.vector.scalar_tensor_tensor(out=nxt[:], in0=cur[:], scalar=g128[:, dr:dr + 1],
                                                                                in1=KVs[dr][:, c, :], op0=ALU.mult, op1=ALU.add),
                                  reads=[bcur, b_g128, b_KVs[dr]], writes=[bnxt])
                    fw.barrier()
                if K.stop <= 2:
                    continue
                with contextlib.ExitStack() as st1:
                    sbl = lambda n, s, d: st1.enter_context(nc.sbuf_tensor(K.un(n), s, d))
                    sc = [[st1.enter_context(nc.psum_tensor(K.un("r_sc%d_%d" % (i, h_)), [128, 512], F32)) for h_ in range(2)] for i in range(2)]
                    b_sc = [[Buf(), Buf()], [Buf(), Buf()]]
                    oT = [st1.enter_context(nc.psum_tensor(K.un("r_oT%d" % i), [128, 512], F32)) for i in range(2)]; b_oT = [Buf(), Buf()]
                    ms = [st1.enter_context(nc.psum_tensor(K.un("r_ms%d" % i), [128, 512], F32)) for i in range(2)]; b_ms = [Buf(), Buf()]
                    P = [sbl("r_P%d" % i, [128, 2, 128], BF16) for i in range(2)]; b_P = [Buf(), Buf()]
                    sq = [sbl("r_sq%d" % i, [128, 512], BF16) for i in range(2)]; b_sq = [Buf(), Buf()]
                    sd = [sbl("r_sd%d" % i, [128, 512], F32) for i in range(2)]; b_sd = [Buf(), Buf()]
                    tt = [sbl("r_tt%d" % i, [128, 512], F32) for i in range(2)]; b_tt = [Buf(), Buf()]
                    gs = [sbl("r_gs%d" % i, [128, 512], F32) for i in range(2)]; b_gs = [Buf(), Buf()]
                    ys = [sbl("r_ys%d" % i, [128, 512], BF16) for i in range(2)]; b_ys = [Buf(), Buf()]
                    groups = [list(range(g * 4, g * 4 + 4)) for g in range(16)]
                    if not last:
                        groups.append([64, 65])
                    groups = groups[:K.p2n] if K.p2n >= 0 else groups[K.p2n:]
                    ci_all = 0
                    for gi, grp in enumerate(groups):
                        q = gi % 2
                        tok0 = grp[0] * 128
                        ntok = len(grp) * 128
                        if grp[0] == 64:
                            fw.barrier()
                        fw.dma("sp", gs[q][:, 0:ntok], K.featF[l][4 + cc, :, tok0:tok0 + ntok], reads=[K.b_feat], writes=[b_gs[q]])
                        for ci, c in enumerate(grp):
                            p = ci_all % 2
                            ci_all += 1
                            for hh in range(2):
                                fw.op("pe", lambda: nc.tensor.matmul(sc[p][hh][:, 0:128], lhsT=KT[hh * 64:(hh + 1) * 64, c, :],
                                                                     rhs=QT[hh * 64:(hh + 1) * 64, c, :], start=True, stop=True),
                                      reads=[b_KT, b_QT], writes=[b_sc[p][hh]])
                            for hh in range(2):
                                fw.op("dve", lambda: nc.vector.tensor_tensor(out=P[p][:, hh, :], in0=sc[p][hh][:, 0:128], in1=M[:, hh, :], op=ALU.mult),
                                      reads=[b_sc[p][hh], b_M], writes=[b_P[p]])
                            for hh in range(2):
                                o_ = oT[q][hh * 64:(hh + 1) * 64, ci * 128:(ci + 1) * 128]
                                fw.op("pe", lambda: nc.tensor.matmul(o_, lhsT=V[:, c, hh * 64:(hh + 1) * 64], rhs=P[p][:, hh, :],
                                                                     start=True, stop=False),
                                      reads=[b_V, b_P[p]], writes=[b_oT[q]], inc=False)
                                fw.op("pe", lambda: nc.tensor.matmul(o_, lhsT=Sbf[0][hh * 64:(hh + 1) * 64, c, :],
                                                                     rhs=qf[hh * 64:(hh + 1) * 64, c, :], start=False, stop=False),
                                      reads=[b_Sbf[0], b_qf], writes=[b_oT[q]], inc=False)
                                fw.op("pe", lambda: nc.tensor.matmul(o_, lhsT=Sbf[1][hh * 64:(hh + 1) * 64, c, :],
                                                                     rhs=qb[hh * 64:(hh + 1) * 64, c, :], start=False, stop=True),
                                      reads=[b_Sbf[1], b_qb], writes=[b_oT[q]], inc=(hh == 1))
                        fw.op("act", lambda: nc.scalar.activation(out=sq[q][:, 0:ntok], in_=oT[q][:, 0:ntok], func=AF.Square),
                              reads=[b_oT[q]], writes=[b_sq[q]])
                        fw.op("pe", lambda: nc.tensor.matmul(ms[q][:, 0:ntok], lhsT=blk[:], rhs=sq[q][:, 0:ntok], start=True, stop=True),
                              reads=[b_blk, b_sq[q]], writes=[b_ms[q]])
                        fw.op("act", lambda: nc.scalar.activation(out=sd[q][:, 0:ntok], in_=ms[q][:, 0:ntok], func=AF.Sqrt, bias=eps_t[:]),
                              reads=[b_ms[q], b_eps], writes=[b_sd[q]])
                        fw.op("dve", lambda: nc.vector.reciprocal(out=sd[q][:, 0:ntok], in_=sd[q][:, 0:ntok]), reads=[b_sd[q]], writes=[b_sd[q]])
                        fw.op("dve", lambda: nc.vector.tensor_tensor(out=tt[q][:, 0:ntok], in0=oT[q][:, 0:ntok], in1=sd[q][:, 0:ntok], op=ALU.mult),
                              reads=[b_oT[q], b_sd[q]], writes=[b_tt[q]])
                        fw.op("act", lambda: nc.scalar.activation(out=gs[q][:, 0:ntok], in_=gs[q][:, 0:ntok], func=AF.Silu),
                              reads=[b_gs[q]], writes=[b_gs[q]])
                        fw.op("pool", lambda: nc.gpsimd.tensor_tensor(out=ys[q][:, 0:ntok], in0=tt[q][:, 0:ntok], in1=gs[q][:, 0:ntok], op=ALU.mult),
                              reads=[b_tt[q], b_gs[q]], writes=[b_ys[q]])
                        fw.dma("pool", K.yT[l][4 + cc, :, tok0:tok0 + ntok], ys[q][:, 0:ntok], reads=[b_ys[q]], writes=[K.b_yT])
                    fw.barrier()


def _diff_consts():
    sel = np.zeros((65, 64), np.float32)
    sel[64, :] = 1.0
    ones64 = np.full((64, 64), 1.0 / 64, np.float32)
    mcol = np.zeros((128, 4), np.float32)
    for p in range(128):
        mcol[p, p // 32] = 1.0
    return sel, ones64, mcol


def _phase_diff(K, l):
    nc, fw = K.nc, K.fw
    last = l == DEPTH - 1
    NCH = T // 128
    lam_init = 0.8 - 0.6 * math.exp(-0.3 * l)
    SC = 32 ** -0.5
    with contextlib.ExitStack() as st0:
        sb0 = lambda n, s, d: st0.enter_context(nc.sbuf_tensor(K.un(n), s, d))
        Va = sb0("d_Va", [128, NCH, 4, 128], BF16); b_Va = Buf()
        sel = sb0("d_sel", [65, 64], F32); b_sel = Buf()
        o64 = sb0("d_o64", [64, 64], F32); b_o64 = Buf()
        mcol = sb0("d_mcol", [128, 4], F32); b_mcol = Buf()
        dl = sb0("d_dl", [64, 4, 32], F32); b_dl = Buf()
        pr = sb0("d_pr", [64, 2, 32], F32); b_pr = Buf()
        s12 = sb0("d_s12", [64, 2], F32); b_s12 = Buf()
        nlam = sb0("d_nlam", [64, 1], F32); b_nlam = Buf()
        gsc = sb0("d_gsc", [64, 1], F32); b_gsc = Buf()
        eps_t = sb0("d_eps", [64, 1], F32); b_eps = Buf()
        fw.dma("sp", sel[:], K.sel65, writes=[b_sel])
        fw.dma("sp", o64[:], K.ones64, writes=[b_o64])
        fw.dma("sp", mcol[:], K.mcol, writes=[b_mcol])
        fw.dma("sp", dl[:], K.diff_lambda[l:l + 1].broadcast_to([64, 4, 32]), writes=[b_dl])
        fw.dma("sp", gsc[:], K.diff_subln[l].rearrange("(p o) -> p o", o=1), writes=[b_gsc])
        fw.op("pool", lambda: nc.gpsimd.memset(eps_t[:], EPS), writes=[b_eps])
        fw.op("dve", lambda: nc.vector.tensor_tensor(out=pr[:, 0, :], in0=dl[:, 0, :], in1=dl[:, 1, :], op=ALU.mult), reads=[b_dl], writes=[b_pr])
        fw.op("dve", lambda: nc.vector.tensor_tensor(out=pr[:, 1, :], in0=dl[:, 2, :], in1=dl[:, 3, :], op=ALU.mult), reads=[b_dl], writes=[b_pr])
        fw.op("dve", lambda: nc.vector.tensor_reduce(out=s12[:], in_=pr[:], axis=AX.X, op=ALU.add), reads=[b_pr], writes=[b_s12])
        fw.op("act", lambda: nc.scalar.activation(out=s12[:], in_=s12[:], func=AF.Exp), reads=[b_s12], writes=[b_s12])
        fw.op("dve", lambda: nc.vector.tensor_tensor(out=nlam[:], in0=s12[:, 1:2], in1=s12[:, 0:1], op=ALU.subtract), reads=[b_s12], writes=[b_nlam])
        fw.op("dve", lambda: nc.vector.tensor_scalar(out=nlam[:], in0=nlam[:], scalar1=-lam_init, scalar2=None, op0=ALU.add),
              reads=[b_nlam], writes=[b_nlam])
        fw.op("dve", lambda: nc.vector.tensor_scalar(out=gsc[:], in0=gsc[:], scalar1=1.0 - lam_init, scalar2=None, op0=ALU.mult),
              reads=[b_gsc], writes=[b_gsc])
        with contextlib.ExitStack() as st1:
            vst = st1.enter_context(nc.sbuf_tensor(K.un("d_vst"), [128, NCH, 256], BF16)); b_vst = Buf()
            for c0 in range(0, NCH, 6):
                fw.dma("sp", vst[:, c0:c0 + 6, :], K.vAll[l][c0 * 128:(c0 + 6) * 128, 512:768].rearrange("(c p) e -> p c e", p=128),
                       reads=[K.b_feat], writes=[b_vst])
            fw.op("pool", lambda: nc.gpsimd.memset(Va[:, :, :, 64:128], 0.0), writes=[b_Va])
            fw.op("pool", lambda: nc.gpsimd.memset(Va[:, :, :, 64:65], 1.0), writes=[b_Va])
            fw.op("dve", lambda: nc.vector.tensor_copy(out=Va[:, :, :, 0:64], in_=vst[:].rearrange("p c (h e) -> p c h e", e=64)),
                  reads=[b_vst], writes=[b_Va])
            fw.barrier()
        for cc in range(2):
            with contextlib.ExitStack() as st:
                sb = lambda n, s, d: st.enter_context(nc.sbuf_tensor(K.un(n), s, d))
                QT = sb("d_QT", [128, T], BF16); b_QT = Buf()
                KT = sb("d_KT", [128, T], BF16); b_KT = Buf()
                Qm = [sb("d_Qm%d" % m, [128, T], BF16) for m in range(4)]; b_Qm = [Buf() for _ in range(4)]
                P = [sb("d_P%d" % i, [128, 2, 512], BF16) for i in range(2)]; b_P = [Buf(), Buf()]
                Osb = sb("d_Osb", [65, 2, 512], F32); b_Osb = Buf()
                rec = sb("d_rec", [64, 2, 512], F32); b_rec = Buf()
                ta = sb("d_ta", [64, 512], F32); b_ta = Buf()
                tb = sb("d_tb", [64, 512], F32); b_tb = Buf()
                td = sb("d_td", [64, 512], F32); b_td = Buf()
                tq = sb("d_tq", [64, 512], F32); b_tq = Buf()
                sd = sb("d_sd", [64, 512], F32); b_sd = Buf()
                ys = [sb("d_ys%d" % i, [64, 512], BF16) for i in range(2)]; b_ys = [Buf(), Buf()]
                S = [st.enter_context(nc.psum_tensor(K.un("d_S%d" % i), [128, 2, 512], F32)) for i in range(2)]; b_S = [Buf(), Buf()]
                O = [st.enter_context(nc.psum_tensor(K.un("d_O%d" % m), [128, 512], F32)) for m in range(2)]; b_O = [Buf(), Buf()]
                bcp = st.enter_context(nc.psum_tensor(K.un("d_bcp"), [128, 2, 512], F32)); b_bcp = Buf()
                for c0 in range(0, T, 2112):
                    fw.dma("sp", QT[:, c0:c0 + 2112], K.featB[l][8 + cc, :, c0:c0 + 2112], reads=[K.b_feat], writes=[b_QT])
                    fw.dma("sp", KT[:, c0:c0 + 2112], K.featB[l][10 + cc, :, c0:c0 + 2112], reads=[K.b_feat], writes=[b_KT])
                for g_ in range(4):
                    fw.op("dve", lambda: nc.vector.tensor_scalar(out=Qm[g_][:], in0=QT[:], scalar1=mcol[:, g_:g_ + 1], scalar2=None, op0=ALU.mult),
                          reads=[b_QT, b_mcol], writes=[b_Qm[g_]])
                si = 0
                yi = [0]
                deferred = []

                def finalize_stage1(rows, q0, nq):
                    fw.op("act", lambda: nc.scalar.copy(out=Osb[:, 0, 0:nq], in_=O[0][0:65, 0:nq]), reads=[b_O[0]], writes=[b_Osb])
                    fw.op("dve", lambda: nc.vector.tensor_copy(out=Osb[:, 1, 0:nq], in_=O[1][0:65, 0:nq]), reads=[b_O[1]], writes=[b_Osb])

                def finalize_stage2(rows, q0, nq):
                    for m in range(2):
                        fw.op("pe", lambda: nc.tensor.matmul(bcp[0:64, m, 0:nq], lhsT=sel[:], rhs=Osb[:, m, 0:nq], start=True, stop=True),
                              reads=[b_sel, b_Osb], writes=[b_bcp], inc=(m == 1))
                    fw.op("dve", lambda: nc.vector.reciprocal(out=rec[:, :, 0:nq], in_=bcp[0:64, :, 0:nq]), reads=[b_bcp], writes=[b_rec])
                    fw.op("dve", lambda: nc.vector.tensor_tensor(out=ta[:, 0:nq], in0=Osb[0:64, 0, 0:nq], in1=rec[:, 0, 0:nq], op=ALU.mult),
                          reads=[b_Osb, b_rec], writes=[b_ta])
                    fw.op("pool", lambda: nc.gpsimd.tensor_tensor(out=tb[:, 0:nq], in0=Osb[0:64, 1, 0:nq], in1=rec[:, 1, 0:nq], op=ALU.mult),
                          reads=[b_Osb, b_rec], writes=[b_tb])
                    fw.op("dve", lambda: nc.vector.scalar_tensor_tensor(out=td[:, 0:nq], in0=tb[:, 0:nq], scalar=nlam[:, 0:1], in1=ta[:, 0:nq],
                                                                        op0=ALU.mult, op1=ALU.add),
                          reads=[b_tb, b_ta, b_nlam], writes=[b_td])
                    fw.op("pool", lambda: nc.gpsimd.tensor_tensor(out=tq[:, 0:nq], in0=td[:, 0:nq], in1=td[:, 0:nq], op=ALU.mult),
                          reads=[b_td], writes=[b_tq])

                def finalize_stage3(rows, q0, nq):
                    fw.op("pe", lambda: nc.tensor.matmul(bcp[0:64, 0, 0:nq], lhsT=o64[:], rhs=tq[:, 0:nq], start=True, stop=True),
                          reads=[b_o64, b_tq, b_rec], writes=[b_bcp])
                    fw.op("act", lambda: nc.scalar.activation(out=sd[:, 0:nq], in_=bcp[0:64, 0, 0:nq], func=AF.Sqrt, bias=eps_t[:]),
                          reads=[b_bcp, b_eps], writes=[b_sd])
                    fw.op("dve", lambda: nc.vector.reciprocal(out=sd[:, 0:nq], in_=sd[:, 0:nq]), reads=[b_sd], writes=[b_sd])
                    y_, by_ = ys[yi[0] % 2], b_ys[yi[0] % 2]
                    yi[0] += 1
                    fw.op("dve", lambda: nc.vector.scalar_tensor_tensor(out=y_[:, 0:nq], in0=td[:, 0:nq], scalar=gsc[:, 0:1], in1=sd[:, 0:nq],
                                                                        op0=ALU.mult, op1=ALU.mult),
                          reads=[b_td, b_gsc, b_sd], writes=[by_])
                    fw.dma("pool", K.yT[l][6 + cc, rows, q0:q0 + nq], y_[:, 0:nq], reads=[by_], writes=[K.b_yT])

                def flush(step):
                    while deferred and deferred[0][0] <= step:
                        deferred.pop(0)[1]()

                for hh in range(2):
                    h = 2 * cc + hh
                    rows = slice(hh * 64, (hh + 1) * 64)
                    qchunks = [(q0, 512, list(range(NCH))) for q0 in range(0, NLAT, 512)]
                    if not last:
                        qchunks.append((NLAT, NCTX, [64, 65]))
                    qchunks = qchunks[:K.p2n] if K.p2n >= 0 else qchunks[K.p2n:]
                    for (q0, nq, kbs) in qchunks:
                        if q0 == NLAT:
                            flush(10 ** 9)
                            fw.barrier()
                        n = len(kbs)

                        def emit_S(k, si_):
                            S_, bS_ = S[si_ % 2], b_S[si_ % 2]
                            for m in range(2):
                                fw.op("pe", lambda: nc.tensor.matmul(S_[:, m, 0:nq], lhsT=KT[:, kbs[k] * 128:(kbs[k] + 1) * 128],
                                                                     rhs=Qm[hh * 2 + m][:, q0:q0 + nq], start=True, stop=True),
                                      reads=[b_KT, b_Qm[hh * 2 + m]], writes=[bS_], inc=(m == 1))

                        emit_S(0, si)
                        for k in range(n):
                            if k + 1 < n:
                                emit_S(k + 1, si + 1)
                            S_, bS_ = S[si % 2], b_S[si % 2]
                            P_, bP_ = P[si % 2], b_P[si % 2]
                            si += 1
                            fw.op("act", lambda: nc.scalar.activation(out=P_[:, :, 0:nq], in_=S_[:, :, 0:nq], func=AF.Exp, scale=SC),
                                  reads=[bS_], writes=[bP_])
                            for m in range(2):
                                fw.op("pe", lambda: nc.tensor.matmul(O[m][:, 0:nq], lhsT=Va[:, kbs[k], h, :], rhs=P_[:, m, 0:nq],
                                                                     start=(k == 0), stop=(k == n - 1)),
                                      reads=[b_Va, bP_], writes=[b_O[m]], inc=(k == n - 1))
                            flush(k)
                        flush(10 ** 9)
                        finalize_stage1(rows, q0, nq)
                        a_ = (rows, q0, nq)
                        deferred.append((4, lambda a_=a_: finalize_stage2(*a_)))
                        deferred.append((16, lambda a_=a_: finalize_stage3(*a_)))
                flush(10 ** 9)
                fw.barrier()


_NA_CACHE = {}


def _na_schedule():
    if "s" in _NA_CACHE:
        return _NA_CACHE["s"]
    kl = np.arange(128)
    k_r, k_c = kl // 64, kl % 64
    q_r, q_c = kl // 64, kl % 64
    c0 = np.clip(q_c - 8, 0, 48)
    classes = {}
    tiles = []
    sched = []
    for i in range(64):
        qrow = 2 * i + q_r
        r0 = np.clip(qrow - 4, 0, 120)
        entry = []
        for j in range(64):
            krow = 2 * j + k_r
            vr = (krow[:, None] >= r0[None, :]) & (krow[:, None] <= r0[None, :] + 7)
            vc = (k_c[:, None] >= c0[None, :]) & (k_c[:, None] <= c0[None, :] + 15)
            valid = vr & vc
            if not valid.any():
                continue
            dr = np.clip(krow[:, None] - qrow[None, :] + 7, 0, 14)
            dc = np.clip(k_c[:, None] - q_c[None, :] + 15, 0, 30)
            entry.append((j - i, dr, dc, valid))
        key = tuple((d, a.tobytes(), b.tobytes(), v.tobytes()) for d, a, b, v in entry)
        if key not in classes:
            classes[key] = len(tiles)
            for d, a, b, v in entry:
                tiles.append((a, b, v))
        sched.append(([i + d for d, _, _, _ in entry], classes[key]))
    dr = np.stack([t[0] for t in tiles]); dc = np.stack([t[1] for t in tiles]); va = np.stack([t[2] for t in tiles])
    _NA_CACHE["s"] = (sched, dr, dc, va)
    return _NA_CACHE["s"]


def _na_bias(na_rpb):
    sched, dr, dc, va = _na_schedule()
    rpb = np.asarray(na_rpb, np.float32)
    g = rpb[:, :, dr, dc]
    return np.ascontiguousarray(np.where(va[None, None], g, np.float32(-30000.0)).astype(np.float32))


def _phase_na(K, l):
    nc, fw = K.nc, K.fw
    last = l == DEPTH - 1
    NCH = T // 128
    SC = 64 ** -0.5
    sched, _, _, _ = _na_schedule()
    NT = K.na_nt
    with contextlib.ExitStack() as st0:
        sb0 = lambda n, s, d: st0.enter_context(nc.sbuf_tensor(K.un(n), s, d))
        Va = sb0("n_Va", [128, NCH, 4, 65], BF16); b_Va = Buf()
        sel = sb0("n_sel", [65, 64], F32); b_sel = Buf()
        fw.dma("sp", sel[:], K.sel65, writes=[b_sel])
        with contextlib.ExitStack() as st1:
            vst = st1.enter_context(nc.sbuf_tensor(K.un("n_vst"), [128, NCH, 256], BF16)); b_vst = Buf()
            for c0 in range(0, NCH, 6):
                fw.dma("sp", vst[:, c0:c0 + 6, :], K.vAll[l][c0 * 128:(c0 + 6) * 128, 0:256].rearrange("(c p) e -> p c e", p=128),
                       reads=[K.b_feat], writes=[b_vst])
            fw.op("pool", lambda: nc.gpsimd.memset(Va[:, :, :, 64:65], 1.0), writes=[b_Va])
            fw.op("dve", lambda: nc.vector.tensor_copy(out=Va[:, :, :, 0:64], in_=vst[:].rearrange("p c (h e) -> p c h e", e=64)),
                  reads=[b_vst], writes=[b_Va])
            fw.barrier()
        for cc in range(2):
            with contextlib.ExitStack() as st:
                sb = lambda n, s, d: st.enter_context(nc.sbuf_tensor(K.un(n), s, d))
                QT = sb("n_QT", [128, T], BF16); b_QT = Buf()
                KT = sb("n_KT", [128, T], BF16); b_KT = Buf()
                E = sb("n_E", [128, NT, 128], F32); b_E = Buf()
                Pf = [sb("n_Pf%d" % i, [128, 5, 128], F32) for i in range(2)]; b_Pf = [Buf(), Buf()]
                Pw = [sb("n_Pw%d" % i, [128, 5, 128], BF16) for i in range(2)]; b_Pw = [Buf(), Buf()]
                Pc = [sb("n_Pc%d" % i, [128, 2, 128], BF16) for i in range(2)]; b_Pc = [Buf(), Buf()]
                Osb = sb("n_Osb", [65, 512], F32); b_Osb = Buf()
                rec = sb("n_rec", [64, 512], F32); b_rec = Buf()
                ys = [sb("n_ys%d" % i, [64, 512], BF16) for i in range(2)]; b_ys = [Buf(), Buf()]
                Sw = [st.enter_context(nc.psum_tensor(K.un("n_Sw%d" % i), [128, 8, 128], F32)) for i in range(2)]; b_Sw = [Buf(), Buf()]
                Sc = [st.enter_context(nc.psum_tensor(K.un("n_Sc%d" % i), [128, 4, 128], F32)) for i in range(2)]; b_Sc = [Buf(), Buf()]
                O = st.enter_context(nc.psum_tensor(K.un("n_O"), [128, 512], F32)); b_O = Buf()
                bcp = st.enter_context(nc.psum_tensor(K.un("n_bcp"), [128, 512], F32)); b_bcp = Buf()
                for c0 in range(0, T, 2112):
                    fw.dma("sp", QT[:, c0:c0 + 2112], K.featB[l][0 + cc, :, c0:c0 + 2112], reads=[K.b_feat], writes=[b_QT])
                    fw.dma("sp", KT[:, c0:c0 + 2112], K.featB[l][2 + cc, :, c0:c0 + 2112], reads=[K.b_feat], writes=[b_KT])
                si = 0
                yi = 0
                for hh in range(2):
                    h = 2 * cc + hh
                    rows = slice(hh * 64, (hh + 1) * 64)
                    for t0 in range(0, NT, 7):
                        t1 = min(NT, t0 + 7)
                        fw.dma("sp", E[:, t0:t1, :], K.nabias[l, h, t0:t1].rearrange("t k q -> k t q"), writes=[b_E])
                    fw.op("act", lambda: nc.scalar.activation(out=E[:], in_=E[:], func=AF.Exp), reads=[b_E], writes=[b_E])
                    qlist = [(i, sched[i][0], sched[i][1]) for i in range(64)]
                    if not last:
                        qlist += [(64, [], 0), (65, [], 0)]
                    qlist = qlist[:K.p2n] if K.p2n >= 0 else qlist[K.p2n:]
                    segs = [[q for q in qlist if q[0] < 64], [q for q in qlist if q[0] >= 64]]
                    for seg in segs:
                        if not seg:
                            continue
                        if seg[0][0] >= 64:
                            fw.barrier()

                        def emit_S(qi, p):
                            i, js, tid0 = seg[qi]
                            nb = len(js)
                            qs = slice(i * 128, (i + 1) * 128)
                            for bi, j in enumerate(js):
                                fw.op("pe", lambda: nc.tensor.matmul(Sw[p][:, bi, :], lhsT=KT[rows, j * 128:(j + 1) * 128], rhs=QT[rows, qs],
                                                                     start=True, stop=True),
                                      reads=[b_KT, b_QT], writes=[b_Sw[p]], inc=(bi == nb - 1))
                            for t in range(2):
                                fw.op("pe", lambda: nc.tensor.matmul(Sc[p][:, t, :], lhsT=KT[rows, (64 + t) * 128:(65 + t) * 128], rhs=QT[rows, qs],
                                                                     start=True, stop=True),
                                      reads=[b_KT, b_QT], writes=[b_Sc[p]], inc=(t == 1))

                        emit_S(0, si % 2)
                        for qi, (i, js, tid0) in enumerate(seg):
                            nb = len(js)
                            p = si % 2
                            si += 1
                            if qi + 1 < len(seg):
                                emit_S(qi + 1, si % 2)
                            gcol = (i % 4) * 128
                            if nb:
                                fw.op("act", lambda: nc.scalar.activation(out=Pf[p][:, 0:nb, :], in_=Sw[p][:, 0:nb, :], func=AF.Exp, scale=SC),
                                      reads=[b_Sw[p]], writes=[b_Pf[p]])
                                fw.op("dve", lambda: nc.vector.tensor_tensor(out=Pw[p][:, 0:nb, :], in0=Pf[p][:, 0:nb, :],
                                                                             in1=E[:, tid0:tid0 + nb, :], op=ALU.mult),
                                      reads=[b_Pf[p], b_E], writes=[b_Pw[p]])
                            fw.op("act", lambda: nc.scalar.activation(out=Pc[p][:], in_=Sc[p][:, 0:2, :], func=AF.Exp, scale=SC),
                                  reads=[b_Sc[p]], writes=[b_Pc[p]])
                            for bi, j in enumerate(js):
                                fw.op("pe", lambda: nc.tensor.matmul(O[0:65, gcol:gcol + 128], lhsT=Va[:, j, h, :], rhs=Pw[p][:, bi, :],
                                                                     start=(bi == 0), stop=False),
                                      reads=[b_Va, b_Pw[p]], writes=[b_O], inc=False)
                            for t in range(2):
                                fw.op("pe", lambda: nc.tensor.matmul(O[0:65, gcol:gcol + 128], lhsT=Va[:, 64 + t, h, :], rhs=Pc[p][:, t, :],
                                                                     start=(nb == 0 and t == 0), stop=(t == 1)),
                                      reads=[b_Va, b_Pc[p]], writes=[b_O], inc=(t == 1))
                            endgrp = (i % 4 == 3) or (i == 65) or (qi == len(seg) - 1)
                            if endgrp:
                                g0 = (i // 4) * 4 if i < 64 else 64
                                nq = (i - g0 + 1) * 128
                                fw.op("act", lambda: nc.scalar.copy(out=Osb[:, 0:nq], in_=O[0:65, 0:nq]), reads=[b_O], writes=[b_Osb])
                                fw.op("pe", lambda: nc.tensor.matmul(bcp[0:64, 0:nq], lhsT=sel[:], rhs=Osb[:, 0:nq], start=True, stop=True),
                                      reads=[b_sel, b_Osb], writes=[b_bcp])
                                fw.op("dve", lambda: nc.vector.reciprocal(out=rec[:, 0:nq], in_=bcp[0:64, 0:nq]), reads=[b_bcp], writes=[b_rec])
                                y_, by_ = ys[yi % 2], b_ys[yi % 2]
                                yi += 1
                                fw.op("dve", lambda: nc.vector.tensor_tensor(out=y_[:, 0:nq], in0=Osb[0:64, 0:nq], in1=rec[:, 0:nq], op=ALU.mult),
                                      reads=[b_Osb, b_rec], writes=[by_])
                                fw.dma("pool", K.yT[l][0 + cc, rows, g0 * 128:g0 * 128 + nq], y_[:, 0:nq], reads=[by_], writes=[K.b_yT])
                fw.barrier()


def _load_cast(K, st, name, dst_tile, b_dst, src_ap, nk, ncols, piece=2048):
    nc, fw = K.nc, K.fw
    with contextlib.ExitStack() as st2:
        wst = [st2.enter_context(nc.sbuf_tensor(K.un("%s_wst%d" % (name, i)), [128, piece], F32)) for i in range(2)]
        b_wst = [Buf(), Buf()]
        it = 0
        for k in range(nk):
            for c0 in range(0, ncols, piece):
                c1 = min(ncols, c0 + piece)
                s_, bs_ = wst[it % 2], b_wst[it % 2]
                fw.dma("sp", s_[:, 0:c1 - c0], src_ap[k * 128:(k + 1) * 128, c0:c1], writes=[bs_])
                e = ("act", "dve", "pool")[it % 3]
                dst = dst_tile[:, k, c0:c1]
                src = s_[:, 0:c1 - c0]
                if e == "act":
                    fw.op(e, lambda: nc.scalar.copy(out=dst, in_=src), reads=[bs_], writes=[b_dst])
                elif e == "dve":
                    fw.op(e, lambda: nc.vector.tensor_copy(out=dst, in_=src), reads=[bs_], writes=[b_dst])
                else:
                    fw.op(e, lambda: nc.gpsimd.tensor_copy(out=dst, in_=src), reads=[bs_], writes=[b_dst])
                it += 1
        fw.barrier()


def _rstd(K, src_ap, reads, junk, b_junk, ss, b_ss, sd, b_sd, rs, b_rs, eps_t, b_eps, n):
    nc, fw = K.nc, K.fw
    fw.op("act", lambda: nc.scalar.activation(out=junk, in_=src_ap, func=AF.Square, accum_out=ss[:]),
          reads=reads, writes=[b_junk, b_ss])
    fw.op("act", lambda: nc.scalar.activation(out=sd[:], in_=ss[:], func=AF.Sqrt, scale=1.0 / n, bias=eps_t[:]),
          reads=[b_ss, b_eps], writes=[b_sd])
    fw.op("dve", lambda: nc.vector.reciprocal(out=rs[:], in_=sd[:]), reads=[b_sd], writes=[b_rs])


def _tok_tiles(l, last):
    tiles = [(0, t * 128) for t in range(NLAT // 128)]
    if not last:
        tiles += [(1, NLAT + t * 128) for t in range(NCTX // 128)]
    return tiles


def _phase_c1(K, l):
    nc, fw = K.nc, K.fw
    last = l == DEPTH - 1
    with contextlib.ExitStack() as st:
        sb = lambda n, s, d: st.enter_context(nc.sbuf_tensor(K.un(n), s, d))
        Wo = sb("c1_W", [128, 8, 1024], BF16); b_W = Buf()
        ident = sb("c1_id", [128, 128], BF16); b_id = Buf()
        idf = sb("c1_idf", [128, 128], F32); b_idf = Buf()
        fw.dma("sp", idf[:], K.ident, writes=[b_idf])
        fw.op("dve", lambda: nc.vector.tensor_copy(out=ident[:], in_=idf[:]), reads=[b_idf], writes=[b_id])
        _load_cast(K, st, "c1", Wo, b_W, K.w_out[l], 8, 1024, piece=1024)
        mt = [sb("c1_mt%d" % i, [128, 1024], F32) for i in range(3)]; b_mt = [Buf() for _ in range(3)]
        yT = [sb("c1_yT%d" % i, [128, 8, 128], BF16) for i in range(3)]; b_yT = [Buf() for _ in range(3)]
        xt = [sb("c1_xt%d" % i, [128, 1024], F32) for i in range(3)]; b_xt = [Buf() for _ in range(3)]
        tmps = [sb("c1_tmp%d" % i, [128, 1024], F32) for i in range(2)]; b_tmps = [Buf(), Buf()]
        xm = [sb("c1_xm%d" % i, [128, 1024], F32) for i in range(3)]; b_xm = [Buf() for _ in range(3)]
        h1s = [sb("c1_h1%d" % i, [128, 1024], F32) for i in range(2)]; b_h1s = [Buf(), Buf()]
        hb = [sb("c1_hb%d" % i, [128, 1024], BF16) for i in range(3)]; b_hb = [Buf() for _ in range(3)]
        hTs = [sb("c1_hT%d" % i, [128, 8, 128], BF16) for i in range(3)]; b_hTs = [Buf() for _ in range(3)]
        junk = sb("c1_junk", [128, 1024], BF16); b_junk = Buf()
        sm = [[sb("c1_s%d_%d" % (a, i), [128, 1], F32) for i in range(3)] for a in range(6)]
        b_sm = [[Buf() for _ in range(3)] for a in range(6)]
        eps_t = sb("c1_eps", [128, 1], F32); b_eps = Buf()
        fw.op("pool", lambda: nc.gpsimd.memset(eps_t[:], EPS), writes=[b_eps])
        yl = [st.enter_context(nc.psum_tensor(K.un("c1_yl%d" % i), [128, 1024], F32)) for i in range(3)]
        b_yl = [Buf() for _ in range(3)]
        tp = [st.enter_context(nc.psum_tensor(K.un("c1_tp%d" % i), [128, 8, 128], BF16)) for i in range(2)]
        b_tp = [Buf(), Buf()]
        cur_j = None
        for ti, (j, tok0) in enumerate(_tok_tiles(l, last)):
            if j != cur_j:
                for i, vi in enumerate((2, 3, 4)):
                    _load_bcast(K, "sp", mt[i], l, vi, j, b_mt[i])
                cur_j = j
            p = ti % 3
            tq_ = ti % 2
            tmp, b_tmp = tmps[tq_], b_tmps[tq_]
            h1, b_h1 = h1s[tq_], b_h1s[tq_]
            lo = tok0 - (NLAT if j else 0)
            fw.dma("sp", yT[p][:], K.yT[l][:, :, tok0:tok0 + 128].rearrange("c p t -> p c t"),
                   reads=[K.b_yT], writes=[b_yT[p]])
            fw.dma("sp", xt[p][:], K.src[l][j][lo:lo + 128, :], reads=[K.b_src[l]], writes=[b_xt[p]])
            for hf in range(2):
                for k in range(8):
                    fw.op("pe", lambda: nc.tensor.matmul(yl[p][:, hf * 512:(hf + 1) * 512], lhsT=yT[p][:, k, :],
                                                         rhs=Wo[:, k, hf * 512:(hf + 1) * 512],
                                                         start=(k == 0), stop=(k == 7)),
                          reads=[b_yT[p], b_W], writes=[b_yl[p]], inc=(hf == 1 and k == 7))
            _rstd(K, yl[p][:], [b_yl[p]], junk[:], b_junk, sm[0][p], b_sm[0][p], sm[1][p], b_sm[1][p],
                  sm[2][p], b_sm[2][p], eps_t, b_eps, D)
            fw.op("dve", lambda: nc.vector.scalar_tensor_tensor(out=tmp[:], in0=yl[p][:], scalar=sm[2][p][:], in1=mt[0][:],
                                                                op0=ALU.mult, op1=ALU.mult),
                  reads=[b_yl[p], b_sm[2][p], b_mt[0]], writes=[b_tmp])
            fw.op("pool", lambda: nc.gpsimd.tensor_tensor(out=xm[p][:], in0=tmp[:], in1=xt[p][:], op=ALU.add),
                  reads=[b_tmp, b_xt[p]], writes=[b_xm[p]])
            fw.dma("pool", K.xmid[tok0:tok0 + 128, :], xm[p][:], reads=[b_xm[p]], writes=[K.b_xmid])
            _rstd(K, xm[p][:], [b_xm[p]], junk[:], b_junk, sm[3][p], b_sm[3][p], sm[4][p], b_sm[4][p],
                  sm[5][p], b_sm[5][p], eps_t, b_eps, D)
            fw.op("dve", lambda: nc.vector.scalar_tensor_tensor(out=h1[:], in0=xm[p][:], scalar=sm[5][p][:], in1=mt[1][:],
                                                                op0=ALU.mult, op1=ALU.mult),
                  reads=[b_xm[p], b_sm[5][p], b_mt[1]], writes=[b_h1])
            fw.op("pool", lambda: nc.gpsimd.tensor_tensor(out=hb[p][:], in0=h1[:], in1=mt[2][:], op=ALU.add),
                  reads=[b_h1, b_mt[2]], writes=[b_hb[p]])
            for k in range(8):
                fw.op("pe", lambda: nc.tensor.transpose(tp[tq_][:, k, :], hb[p][:, k * 128:(k + 1) * 128], ident[:]),
                      reads=[b_hb[p], b_id], writes=[b_tp[tq_]], inc=(k == 7))
            fw.op("act", lambda: nc.scalar.copy(out=hTs[p][:], in_=tp[tq_][:]), reads=[b_tp[tq_]], writes=[b_hTs[p]])
            fw.dma("pool", K.h2T[:, :, tok0:tok0 + 128].rearrange("c p t -> p c t"), hTs[p][:],
                   reads=[b_hTs[p]], writes=[K.b_h2T])
        fw.barrier()


def _phase_c2(K, l):
    nc, fw = K.nc, K.fw
    last = l == DEPTH - 1
    NF = DFF // 128
    with contextlib.ExitStack() as st:
        sb = lambda n, s, d: st.enter_context(nc.sbuf_tensor(K.un(n), s, d))
        W1 = sb("c2_W1", [128, 8, 2 * DFF], BF16); b_W1 = Buf()
        W2 = sb("c2_W2", [128, NF, 1024], BF16); b_W2 = Buf()
        _load_cast(K, st, "c2a", W1, b_W1, K.w_ffn_in[l], 8, 2 * DFF, piece=2816)
        _load_cast(K, st, "c2b", W2, b_W2, K.w_ffn_out[l], NF, 1024, piece=1024)
        gg = sb("c2_gg", [128, 1024], F32); b_gg = Buf()
        hT = [sb("c2_hT%d" % i, [128, 8, 512], BF16) for i in range(2)]; b_hT = [Buf(), Buf()]
        sg = [sb("c2_sg%d" % i, [128, 512], F32) for i in range(2)]; b_sg = [Buf(), Buf()]
        aT = sb("c2_aT", [128, NF, 512], BF16); b_aT = [Buf() for _ in range(NF)]
        xm = [sb("c2_xm%d" % i, [128, 1024], F32) for i in range(2)]; b_xm = [Buf(), Buf()]
        tmp = sb("c2_tmp", [128, 1024], F32); b_tmp = Buf()
        ot = [sb("c2_ot%d" % i, [128, 1024], F32) for i in range(2)]; b_ot = [Buf(), Buf()]
        junk = sb("c2_junk", [128, 1024], BF16); b_junk = Buf()
        sm = [[sb("c2_s%d_%d" % (a, i), [128, 1], F32) for i in range(2)] for a in range(3)]
        b_sm = [[Buf(), Buf()] for a in range(3)]
        eps_t = sb("c2_eps", [128, 1], F32); b_eps = Buf()
        fw.op("pool", lambda: nc.gpsimd.memset(eps_t[:], EPS), writes=[b_eps])
        gu = [st.enter_context(nc.psum_tensor(K.un("c2_gu%d" % i), [128, 2, 512], F32)) for i in range(2)]
        b_gu = [Buf(), Buf()]
        fo = [st.enter_context(nc.psum_tensor(K.un("c2_fo%d" % i), [128, 1024], F32)) for i in range(2)]
        b_fo = [Buf(), Buf()]
        tiles = _tok_tiles(l, last)
        groups = [tiles[i:i + 4] for i in range(0, NLAT // 128, 4)]
        if not last:
            groups.append(tiles[NLAT // 128:])
        cur_j = None
        gi = 0
        si = 0
        for bi, grp in enumerate(groups):
            j, tok0 = grp[0]
            nsub = len(grp)
            ntok = nsub * 128
            if j != cur_j:
                _load_bcast(K, "sp", gg, l, 5, j, b_gg)
                cur_j = j
            hT_, bhT_ = hT[bi % 2], b_hT[bi % 2]
            fw.dma("sp", hT_[:, :, 0:ntok], K.h2T[:, :, tok0:tok0 + ntok].rearrange("c p t -> p c t"),
                   reads=[K.b_h2T], writes=[bhT_])
            for c in range(NF):
                g_, bg_ = gu[gi % 2], b_gu[gi % 2]
                s_, bs_ = sg[gi % 2], b_sg[gi % 2]
                gi += 1
                for half in range(2):
                    col0 = half * DFF + c * 128
                    for k in range(8):
                        fw.op("pe", lambda: nc.tensor.matmul(g_[:, half, 0:ntok], lhsT=W1[:, k, col0:col0 + 128],
                                                             rhs=hT_[:, k, 0:ntok], start=(k == 0), stop=(k == 7)),
                              reads=[b_W1, bhT_], writes=[bg_], inc=(half == 1 and k == 7))
                fw.op("act", lambda: nc.scalar.activation(out=s_[:, 0:ntok], in_=g_[:, 0, 0:ntok], func=AF.Silu),
                      reads=[bg_], writes=[bs_])
                fw.op("dve", lambda: nc.vector.tensor_tensor(out=aT[:, c, 0:ntok], in0=g_[:, 1, 0:ntok], in1=s_[:, 0:ntok], op=ALU.mult),
                      reads=[bg_, bs_], writes=[b_aT[c]])
            for sub in range(nsub):
                p = si % 2
                si += 1
                t0 = tok0 + sub * 128
                fw.dma("sp", xm[p][:], K.xmid[t0:t0 + 128, :], reads=[K.b_xmid], writes=[b_xm[p]])
                for hf in range(2):
                    for c in range(NF):
                        fw.op("pe", lambda: nc.tensor.matmul(fo[p][:, hf * 512:(hf + 1) * 512],
                                                             lhsT=aT[:, c, sub * 128:(sub + 1) * 128],
                                                             rhs=W2[:, c, hf * 512:(hf + 1) * 512],
                                                             start=(c == 0), stop=(c == NF - 1)),
                              reads=[b_aT[c], b_W2], writes=[b_fo[p]], inc=(hf == 1 and c == NF - 1))
                _rstd(K, fo[p][:], [b_fo[p]], junk[:], b_junk, sm[0][p], b_sm[0][p], sm[1][p], b_sm[1][p],
                      sm[2][p], b_sm[2][p], eps_t, b_eps, D)
                fw.op("dve", lambda: nc.vector.scalar_tensor_tensor(out=tmp[:], in0=fo[p][:], scalar=sm[2][p][:], in1=gg[:],
                                                                    op0=ALU.mult, op1=ALU.mult),
                      reads=[b_fo[p], b_sm[2][p], b_gg], writes=[b_tmp])
                fw.op("pool", lambda: nc.gpsimd.tensor_tensor(out=ot[p][:], in0=tmp[:], in1=xm[p][:], op=ALU.add),
                      reads=[b_tmp, b_xm[p]], writes=[b_ot[p]])
                if last:
                    dst, bd = K.out[t0:t0 + 128, :], K.b_out
                else:
                    dst, bd = K.xl1[t0:t0 + 128, :], K.b_src[l + 1]
                fw.dma("pool", dst, ot[p][:], reads=[b_ot[p]], writes=[bd])
        fw.barrier()


def build(dbg=None):
    nc = bass.Bass("TRN2", target_bir_lowering=False)
    K = Ctx()
    K.nc = nc
    dt_in = lambda n, s, d=F32: nc.dram_tensor(n, s, d, kind="ExternalInput").ap()
    K.x = dt_in("x", [NLAT, D])
    K.ctx = dt_in("ctx", [NCTX, D])
    K.cvec = dt_in("cvec", [2, D])
    K.w_mod = dt_in("w_mod", [DEPTH, D, 6 * D])
    K.b_mod = dt_in("b_mod", [DEPTH, 6 * D])
    K.gains = dt_in("gains", [4, DEPTH, D])
    K.w_in = dt_in("w_in", [DEPTH, D, NWCOL])
    K.rope = dt_in("rope", [6, 128, 192])
    K.ident = dt_in("ident", [128, 128])
    K.w_out = dt_in("w_out", [DEPTH, D, D])
    K.w_ffn_in = dt_in("w_ffn_in", [DEPTH, D, 2 * DFF])
    K.w_ffn_out = dt_in("w_ffn_out", [DEPTH, DFF, D])
    K.lru_conv_w = dt_in("lru_conv_w", [DEPTH, 4, 256])
    K.lru_conv_b = dt_in("lru_conv_b", [DEPTH, 256])
    K.lru_gate_w = dt_in("lru_gate_w", [DEPTH, 2, 2, 4, 64, 64])
    K.lru_gate_b = dt_in("lru_gate_b", [DEPTH, 2, 2, 4, 64])
    K.lru_lambda = dt_in("lru_lambda", [DEPTH, 2, 256])
    K.ret_decay = dt_in("ret_decay", [DEPTH, 2, 4])
    K.retc = dt_in("retc", [128, 6, 128])
    K.retcv = dt_in("retcv", [128, 2])
    K.blk64 = dt_in("blk64", [128, 128])
    K.diff_lambda = dt_in("diff_lambda", [DEPTH, 4, 32])
    K.diff_subln = dt_in("diff_subln", [DEPTH, 64])
    K.sel65 = dt_in("sel65", [65, 64])
    K.ones64 = dt_in("ones64", [64, 64])
    K.mcol = dt_in("mcol", [128, 4])
    K.na_nt = _na_schedule()[1].shape[0]
    K.nabias = dt_in("nabias", [DEPTH, 4, K.na_nt, 128, 128])
    phases = dbg["phases"] if dbg else None
    K.stop = dbg.get("stop", 99) if dbg else 99
    K.p1n = dbg.get("p1n", 999) if dbg else 999
    K.p2n = dbg.get("p2n", 999) if dbg else 999
    ext_in = dbg.get("ext_in", set()) if dbg else set()
    ext_out = dbg.get("ext_out", set()) if dbg else set()

    def scr(n, s, d):
        kind = "Internal"
        if n in ext_in:
            kind = "ExternalInput"
        elif n in ext_out:
            kind = "ExternalOutput"
        return nc.dram_tensor(n, s, d, kind=kind).ap()

    K.modvec = scr("modvec", [DEPTH, 6, 2, D], F32); K.b_modvec = Buf()
    K.featB = [scr("featB%d" % l, [12, 128, T], BF16) for l in range(DEPTH)]
    K.featF = [scr("featF%d" % l, [6, 128, T], F32) for l in range(DEPTH)]
    K.vAll = [scr("vAll%d" % l, [T, 768], BF16) for l in range(DEPTH)]
    K.b_feat = Buf()
    K.yT = [scr("yT%d" % l, [8, 128, T], BF16) for l in range(DEPTH)]; K.b_yT = Buf()
    K.xmid = scr("xmid", [T, D], F32); K.b_xmid = Buf()
    K.h2T = scr("h2T", [8, 128, T], BF16); K.b_h2T = Buf()
    K.xl1 = scr("xl1", [T, D], F32)
    K.out = nc.dram_tensor("out", [NLAT, D], F32, kind="ExternalOutput").ap(); K.b_out = Buf()
    K.src = [(K.x, K.ctx), (K.xl1[0:NLAT, :], K.xl1[NLAT:T, :])]
    K.b_src = [Buf(), Buf()]
    on = lambda name: (phases is None) or (name in phases)
    with contextlib.ExitStack() as st:
        K.fw = FW(nc, st)
        def ph(name, fn, *a):
            if on(name):
                with nc.named_scope(name):
                    fn(K, *a)

        ph("P0", _p0_modvec)
        for l in range(DEPTH):
            ph("A%d" % l, _phase_a, l)
            ph("NA%d" % l, _phase_na, l)
            ph("LRU%d" % l, _phase_lru, l)
            ph("RET%d" % l, _phase_ret, l)
            ph("DIFF%d" % l, _phase_diff, l)
            if on("C%d" % l):
                with nc.named_scope("C1_%d" % l):
                    _phase_c1(K, l)
                with nc.named_scope("C2_%d" % l):
                    _phase_c2(K, l)
        K.fw.finish("sp")
        K.n_inst, K.n_wait = K.fw.n_inst, K.fw.n_wait
    return nc, K


def host_inputs(inputs, b):
    f = lambda a: np.ascontiguousarray(np.asarray(a, dtype=np.float32))
    perm = _w_in_perm()
    m = {
        "x": f(inputs["x"][b]),
        "ctx": f(inputs["ctx"][b]),
        "cvec": f(np.stack([np.asarray(inputs["c"])[b], np.asarray(inputs["c_ctx"])], 0)),
        "w_mod": f(inputs["w_mod"]),
        "b_mod": f(inputs["b_mod"]),
        "gains": f(np.stack([inputs["g_pre_mix"], inputs["g_post_mix"], inputs["g_pre_ffn"], inputs["g_post_ffn"]], 0)),
        "w_in": f(np.asarray(inputs["w_in"])[:, :, perm]),
        "rope": _rope_tables(),
        "ident": np.eye(128, dtype=np.float32),
        "w_out": f(inputs["w_out"]),
        "w_ffn_in": f(inputs["w_ffn_in"]),
        "w_ffn_out": f(inputs["w_ffn_out"]),
        "lru_conv_w": f(inputs["lru_conv_w"]),
        "lru_conv_b": f(inputs["lru_conv_b"]),
        "lru_gate_w": f(inputs["lru_gate_w"]),
        "lru_gate_b": f(inputs["lru_gate_b"]),
        "lru_lambda": f(inputs["lru_lambda"]),
        "ret_decay": f(inputs["ret_decay"]),
        "retc": _ret_consts()[0],
        "retcv": _ret_consts()[1],
        "blk64": _blk64(),
        "diff_lambda": f(inputs["diff_lambda"]),
        "diff_subln": f(inputs["diff_subln"]),
        "sel65": _diff_consts()[0],
        "ones64": _diff_consts()[1],
        "mcol": _diff_consts()[2],
        "nabias": _na_bias(inputs["na_rpb"]),
    }
    return m


N_CORES = 4


def kernel(**inputs):
    nc, K = build()
    in_maps = [host_inputs(inputs, c) for c in range(N_CORES)]
    res = run_bass_kernel_spmd(nc, in_maps, core_ids=list(range(N_CORES)))
    out = np.stack([np.asarray(res.results[b]["out"], dtype=np.float32) for b in range(4)], 0)
    return out
```
